# Optimizing a Trainium2 kernel written in Bass

```python
import math
import jax
import jax.numpy as jnp
from jax import lax
import numpy as np

D_MODEL = 1024
BATCH = 16
SEQ = 256
DEPTH = 2
DEC_BATCH = 4
DEC_SEQ = 1024
PAST_LEN = 512

GRID_W = 64
EPS = 1e-6
GLA_HEADS = 4
GLA_DK = 64
GLA_DV = 128
GLA_WIDTH = GLA_HEADS * GLA_DV
GLA_GATE_RANK = 16
GLA_GATE_NORM = 16.0
GLA_CHUNK = 64
MLA_HEADS = 4
MLA_Q_LORA = 384
MLA_KV_LORA = 256
MLA_NOPE = 64
MLA_ROPE = 32
MLA_QK = MLA_NOPE + MLA_ROPE
MLA_DV = 128
MLA_WIDTH = MLA_HEADS * MLA_DV
ROPE_THETA = 10000.0
Q_BLOCK = 128
S5_WIDTH = 512
S5_GROUP = 16
S5_GROUPS = S5_WIDTH // S5_GROUP
S5_STATE = 64
DT_MIN = 1e-3
DT_MAX = 1e-1
N_BRANCH = 3
IN_SPLITS = (GLA_HEADS * GLA_DK, GLA_HEADS * GLA_DK, GLA_WIDTH, GLA_GATE_RANK, GLA_GATE_RANK, GLA_WIDTH,
             MLA_Q_LORA, MLA_KV_LORA, MLA_ROPE, MLA_WIDTH,
             S5_WIDTH, S5_WIDTH,
             N_BRANCH * D_MODEL)
D_IN = sum(IN_SPLITS)

kernel_name = 'hybrid_gla_mla_s5_diffusion_step'


def rms_norm(x, w):
    xf = x.astype(jnp.float32)
    y = xf * lax.rsqrt(jnp.mean(xf * xf, axis=-1, keepdims=True) + EPS)
    return (y * w.astype(jnp.float32)).astype(x.dtype)


def split_cols(z, sizes):
    return jnp.split(z, np.cumsum(np.array(sizes))[:-1].tolist(), axis=-1)


def axial_rope_tables(n_tok):
    rows = n_tok // GRID_W
    r = jnp.repeat(jnp.arange(rows, dtype=jnp.float32), GRID_W)
    col = jnp.tile(jnp.arange(GRID_W, dtype=jnp.float32), rows)
    n_freq = MLA_ROPE // 4
    inv = ROPE_THETA ** (-jnp.arange(n_freq, dtype=jnp.float32) / n_freq)
    ang = jnp.concatenate([r[:, None] * inv, col[:, None] * inv], axis=-1)
    return jnp.cos(ang), jnp.sin(ang)


def apply_rope_tail(x, rope):
    cos, sin = rope
    cos = cos[:, None, :].astype(x.dtype)
    sin = sin[:, None, :].astype(x.dtype)
    x_nope, x1, x2 = jnp.split(x, [MLA_NOPE, MLA_NOPE + MLA_ROPE // 2], axis=-1)
    return jnp.concatenate([x_nope, x1 * cos - x2 * sin, x1 * sin + x2 * cos], axis=-1)


def block_attention(q, k, v):
    bsz, heads, nq, dk = q.shape
    scale = dk ** -0.5
    qb = q.reshape(bsz, heads, nq // Q_BLOCK, Q_BLOCK, dk).transpose(2, 0, 1, 3, 4)

    def one_block(q_blk):
        s = jnp.einsum('bhqd,bhkd->bhqk', q_blk, k).astype(jnp.float32) * scale
        p = jax.nn.softmax(s, axis=-1).astype(v.dtype)
        return jnp.einsum('bhqk,bhkd->bhqd', p, v)

    o = lax.map(one_block, qb)
    return o.transpose(1, 2, 0, 3, 4).reshape(bsz, heads, nq, v.shape[-1])


def gla_chunk_scan(q, k, v, log_a, s0):
    bsz, heads, n, dk = q.shape
    nc = n // GLA_CHUNK

    def chunks(t):
        return t.reshape(bsz, heads, nc, GLA_CHUNK, t.shape[-1]).transpose(2, 0, 1, 3, 4)

    mask = jnp.tril(jnp.ones((GLA_CHUNK, GLA_CHUNK), dtype=bool))

    def step(s, inp):
        qi, ki, vi, ai = inp
        bcum = jnp.cumsum(ai, axis=-2)
        blast = bcum[..., -1:, :]
        qd = qi * jnp.exp(bcum).astype(qi.dtype)
        kd = ki * jnp.exp(-bcum).astype(ki.dtype)
        att = jnp.where(mask, jnp.einsum('bhcd,bhsd->bhcs', qd, kd), 0)
        o = jnp.einsum('bhcd,bhde->bhce', qd, s) + jnp.einsum('bhcs,bhse->bhce', att, vi)
        kr = ki * jnp.exp(blast - bcum).astype(ki.dtype)
        s_new = jnp.exp(blast[..., 0, :])[..., None].astype(s.dtype) * s + jnp.einsum('bhcd,bhce->bhde', kr, vi)
        return s_new, o

    s_fin, o = lax.scan(step, s0, (chunks(q), chunks(k), chunks(v), chunks(log_a)))
    return o.transpose(1, 2, 0, 3, 4).reshape(bsz, heads, n, v.shape[-1]), s_fin


def gla_mixer(q, k, v, a_low_f, a_low_b, w_a2, b_a, o_norm, ctx):
    bsz, n, _ = q.shape

    def heads(t, d):
        return t.reshape(bsz, n, GLA_HEADS, d).transpose(0, 2, 1, 3)

    qh = heads(q, GLA_DK) * (GLA_DK ** -0.5)
    kh = heads(k, GLA_DK)
    vh = heads(v, GLA_DV)
    log_af = heads(jax.nn.log_sigmoid((a_low_f @ w_a2[0] + b_a[0]).astype(jnp.float32)) / GLA_GATE_NORM, GLA_DK)
    log_ab = heads(jax.nn.log_sigmoid((a_low_b @ w_a2[1] + b_a[1]).astype(jnp.float32)) / GLA_GATE_NORM, GLA_DK)
    if ctx is None:
        s_f0 = jnp.zeros((bsz, GLA_HEADS, GLA_DK, GLA_DV), v.dtype)
        s_b0 = s_f0
    else:
        s_f0, s_b0 = ctx
    o_f, s_f = gla_chunk_scan(qh, kh, vh, log_af, s_f0)

    def flip(t):
        return jnp.flip(t, axis=2)

    o_b, s_b = gla_chunk_scan(flip(qh), flip(kh), flip(vh), flip(log_ab), s_b0)
    o = rms_norm((o_f + flip(o_b)).transpose(0, 2, 1, 3), o_norm)
    return o.reshape(bsz, n, GLA_WIDTH), jnp.stack([s_f, s_b], axis=1)


def mla_keys_values(ckv, k_rope, w_uk, w_uv, kh_norm, rope):
    bsz, n, _ = ckv.shape
    k_nope = (ckv @ w_uk).reshape(bsz, n, MLA_HEADS, MLA_NOPE)
    v = (ckv @ w_uv).reshape(bsz, n, MLA_HEADS, MLA_DV)
    k_pe = jnp.broadcast_to(k_rope[:, :, None, :], (bsz, n, MLA_HEADS, MLA_ROPE))
    k = rms_norm(jnp.concatenate([k_nope, k_pe], axis=-1), kh_norm)
    if rope is not None:
        k = apply_rope_tail(k, rope)
    return k.transpose(0, 2, 1, 3), v.transpose(0, 2, 1, 3)


def mla_mixer(cq_raw, ckv_raw, k_rope, q_norm, w_uq, kv_norm, w_uk, w_uv, qh_norm, kh_norm, rope, ctx):
    bsz, n, _ = cq_raw.shape
    q = (rms_norm(cq_raw, q_norm) @ w_uq).reshape(bsz, n, MLA_HEADS, MLA_QK)
    q = rms_norm(q, qh_norm)
    if rope is not None:
        q = apply_rope_tail(q, rope)
    ckv = rms_norm(ckv_raw, kv_norm)
    k, v = mla_keys_values(ckv, k_rope, w_uk, w_uv, kh_norm, rope)
    if ctx is not None:
        kc, vc = mla_keys_values(ctx[0], ctx[1], w_uk, w_uv, kh_norm, None)
        k = jnp.concatenate([kc, k], axis=2)
        v = jnp.concatenate([vc, v], axis=2)
    o = block_attention(q.transpose(0, 2, 1, 3), k, v)
    return o.transpose(0, 2, 1, 3).reshape(bsz, n, MLA_WIDTH), (ckv, k_rope)


def s5_discretise(a_re, a_im, log_dt):
    dt = jnp.exp(log_dt)[:, None]
    mag = jnp.exp(a_re * dt)
    ab_re = mag * jnp.cos(a_im * dt)
    ab_im = mag * jnp.sin(a_im * dt)
    den = a_re * a_re + a_im * a_im
    n_re = ab_re - 1.0
    coef_re = (n_re * a_re + ab_im * a_im) / den
    coef_im = (ab_im * a_re - n_re * a_im) / den
    return ab_re, ab_im, coef_re, coef_im


def complex_linear_scan(ab_re, ab_im, u_re, u_im, x0_re, x0_im):
    u_re = u_re.at[:, 0].add(ab_re * x0_re - ab_im * x0_im)
    u_im = u_im.at[:, 0].add(ab_re * x0_im + ab_im * x0_re)
    a_re = jnp.broadcast_to(ab_re, u_re.shape)
    a_im = jnp.broadcast_to(ab_im, u_im.shape)

    def combine(e1, e2):
        a1r, a1i, b1r, b1i = e1
        a2r, a2i, b2r, b2i = e2
        return (a1r * a2r - a1i * a2i, a1r * a2i + a1i * a2r,
                a2r * b1r - a2i * b1i + b2r, a2r * b1i + a2i * b1r + b2i)

    _, _, x_re, x_im = lax.associative_scan(combine, (a_re, a_im, u_re, u_im), axis=1)
    return x_re, x_im


def s5_mixer(u, a_re, a_im, log_dt, b_re, b_im, c_re, c_im, d_skip, w_glu, b_glu, ctx):
    bsz, n, _ = u.shape
    ug = u.reshape(bsz, n, S5_GROUPS, S5_GROUP)
    bu_re = jnp.einsum('blgp,gnp->blgn', ug, b_re)
    bu_im = jnp.einsum('blgp,gnp->blgn', ug, b_im)
    y = d_skip * u
    finals = []
    for d in range(2):
        ab_re, ab_im, cf_re, cf_im = s5_discretise(a_re[d], a_im[d], log_dt[d])
        ub_re = cf_re * bu_re - cf_im * bu_im
        ub_im = cf_re * bu_im + cf_im * bu_re
        if d == 1:
            ub_re = jnp.flip(ub_re, axis=1)
            ub_im = jnp.flip(ub_im, axis=1)
        if ctx is None:
            x0_re = jnp.zeros((bsz, S5_GROUPS, S5_STATE), u.dtype)
            x0_im = x0_re
        else:
            x0_re, x0_im = ctx[:, d, 0], ctx[:, d, 1]
        x_re, x_im = complex_linear_scan(ab_re, ab_im, ub_re, ub_im, x0_re, x0_im)
        finals.append(jnp.stack([x_re[:, -1], x_im[:, -1]], axis=1))
        if d == 1:
            x_re = jnp.flip(x_re, axis=1)
            x_im = jnp.flip(x_im, axis=1)
        y_ssm = jnp.einsum('blgn,gpn->blgp', x_re, c_re) - jnp.einsum('blgn,gpn->blgp', x_im, c_im)
        y = y + y_ssm.reshape(bsz, n, S5_WIDTH)
    z = jax.nn.gelu(y) @ w_glu + b_glu
    out = z[..., :S5_WIDTH] * jax.nn.sigmoid(z[..., S5_WIDTH:])
    return out, jnp.stack(finals, axis=1)


def trunk_layer(x, cond, rope, ctx, p):
    ada = jax.nn.silu(cond) @ p['w_ada'] + p['b_ada']
    shift, scale, gate = jnp.split(ada[:, None, :], 3, axis=-1)
    h = rms_norm(x, p['norm_w']) * (1.0 + scale) + shift
    (gq, gk, gv, gaf, gab, g_gate, mq, mkv, mkr, m_gate, su, s_gate, merge) = split_cols(h @ p['w_in'], IN_SPLITS)
    if ctx is None:
        ctx_mla, ctx_gla, ctx_s5 = None, None, None
    else:
        ctx_mla, ctx_gla, ctx_s5 = ctx
    o_a, st_gla = gla_mixer(gq, gk, gv, gaf, gab, p['gla_w_a2'], p['gla_b_a'], p['gla_o_norm'], ctx_gla)
    o_b, st_mla = mla_mixer(mq, mkv, mkr, p['mla_q_norm'], p['mla_w_uq'], p['mla_kv_norm'], p['mla_w_uk'],
                            p['mla_w_uv'], p['mla_qh_norm'], p['mla_kh_norm'], rope, ctx_mla)
    o_c, st_s5 = s5_mixer(su, p['s5_a_re'], p['s5_a_im'], p['s5_log_dt'], p['s5_b_re'], p['s5_b_im'],
                          p['s5_c_re'], p['s5_c_im'], p['s5_d'], p['s5_w_glu'], p['s5_b_glu'], ctx_s5)
    g_a, g_b, g_c = jnp.split(jax.nn.sigmoid(merge), 3, axis=-1)
    mixed = (g_a * ((o_a * jax.nn.silu(g_gate)) @ p['w_bo_gla'])
             + g_b * ((o_b * jax.nn.silu(m_gate)) @ p['w_bo_mla'])
             + g_c * ((o_c * jax.nn.silu(s_gate)) @ p['w_bo_s5']))
    y = x + gate * (mixed @ p['w_out'])
    return y, (st_mla, st_gla, st_s5)


def setup_inputs(seed: int = 0) -> dict:
    key = jax.random.key(seed)
    keys = jax.random.split(key, 64)
    counter = [0]

    def nxt():
        counter[0] += 1
        return keys[counter[0] - 1]

    def nrm(shape, scale):
        return scale * jax.random.normal(nxt(), shape, jnp.float32)

    L = DEPTH
    n_idx = jnp.arange(S5_STATE, dtype=jnp.float32)
    log_dt = math.log(DT_MIN) + jax.random.uniform(nxt(), (L, 2, S5_GROUPS), jnp.float32) * (math.log(DT_MAX) - math.log(DT_MIN))
    return {
        'x_prompt': nrm((BATCH, SEQ, D_MODEL), 1.0),
        'x_sample': nrm((DEC_BATCH, DEC_SEQ, D_MODEL), 1.0),
        'c': nrm((DEC_BATCH, D_MODEL), 1.0),
        'c_ctx': nrm((D_MODEL,), 1.0),
        'cache_mla_ckv': nrm((DEC_BATCH, L, PAST_LEN, MLA_KV_LORA), 1.0),
        'cache_mla_krope': nrm((DEC_BATCH, L, PAST_LEN, MLA_ROPE), 1.0),
        'state_gla': nrm((DEC_BATCH, L, 2, GLA_HEADS, GLA_DK, GLA_DV), 0.5),
        'state_s5': nrm((DEC_BATCH, L, 2, 2, S5_GROUPS, S5_STATE), 0.1),
        'norm_w': 1.0 + nrm((L, D_MODEL), 0.02),
        'w_ada': nrm((L, D_MODEL, 3 * D_MODEL), 0.5 * D_MODEL ** -0.5),
        'b_ada': nrm((L, 3 * D_MODEL), 0.02),
        'w_in': nrm((L, D_MODEL, D_IN), D_MODEL ** -0.5),
        'gla_w_a2': nrm((L, 2, GLA_GATE_RANK, GLA_HEADS * GLA_DK), GLA_GATE_RANK ** -0.5),
        'gla_b_a': nrm((L, 2, GLA_HEADS * GLA_DK), 0.1),
        'gla_o_norm': 1.0 + nrm((L, GLA_DV), 0.02),
        'mla_q_norm': 1.0 + nrm((L, MLA_Q_LORA), 0.02),
        'mla_w_uq': nrm((L, MLA_Q_LORA, MLA_HEADS * MLA_QK), MLA_Q_LORA ** -0.5),
        'mla_kv_norm': 1.0 + nrm((L, MLA_KV_LORA), 0.02),
        'mla_w_uk': nrm((L, MLA_KV_LORA, MLA_HEADS * MLA_NOPE), MLA_KV_LORA ** -0.5),
        'mla_w_uv': nrm((L, MLA_KV_LORA, MLA_HEADS * MLA_DV), MLA_KV_LORA ** -0.5),
        'mla_qh_norm': 1.0 + nrm((L, MLA_QK), 0.02),
        'mla_kh_norm': 1.0 + nrm((L, MLA_QK), 0.02),
        's5_a_re': -0.5 * jnp.exp(nrm((L, 2, S5_GROUPS, S5_STATE), 0.05)),
        's5_a_im': jnp.pi * n_idx + nrm((L, 2, S5_GROUPS, S5_STATE), 0.01),
        's5_log_dt': log_dt,
        's5_b_re': nrm((L, S5_GROUPS, S5_STATE, S5_GROUP), (2 * S5_GROUP) ** -0.5),
        's5_b_im': nrm((L, S5_GROUPS, S5_STATE, S5_GROUP), (2 * S5_GROUP) ** -0.5),
        's5_c_re': nrm((L, S5_GROUPS, S5_GROUP, S5_STATE), (2 * S5_STATE) ** -0.5),
        's5_c_im': nrm((L, S5_GROUPS, S5_GROUP, S5_STATE), (2 * S5_STATE) ** -0.5),
        's5_d': nrm((L, S5_WIDTH), 1.0),
        's5_w_glu': nrm((L, S5_WIDTH, 2 * S5_WIDTH), S5_WIDTH ** -0.5),
        's5_b_glu': nrm((L, 2 * S5_WIDTH), 0.02),
        'w_bo_gla': nrm((L, GLA_WIDTH, D_MODEL), GLA_WIDTH ** -0.5),
        'w_bo_mla': nrm((L, MLA_WIDTH, D_MODEL), MLA_WIDTH ** -0.5),
        'w_bo_s5': nrm((L, S5_WIDTH, D_MODEL), S5_WIDTH ** -0.5),
        'w_out': nrm((L, D_MODEL, D_MODEL), D_MODEL ** -0.5),
    }


def reference(x_prompt, x_sample, c, c_ctx, cache_mla_ckv, cache_mla_krope, state_gla, state_s5,
              norm_w, w_ada, b_ada, w_in, gla_w_a2, gla_b_a, gla_o_norm,
              mla_q_norm, mla_w_uq, mla_kv_norm, mla_w_uk, mla_w_uv, mla_qh_norm, mla_kh_norm,
              s5_a_re, s5_a_im, s5_log_dt, s5_b_re, s5_b_im, s5_c_re, s5_c_im, s5_d, s5_w_glu, s5_b_glu,
              w_bo_gla, w_bo_mla, w_bo_s5, w_out):
    cond_ctx = jnp.broadcast_to(c_ctx[None, :], (x_prompt.shape[0], D_MODEL))
    rope = axial_rope_tables(x_sample.shape[1])
    hp = x_prompt
    hs = x_sample
    ckv_l, krope_l, gla_l, s5_l = [], [], [], []
    for l in range(DEPTH):
        p = {
            'norm_w': norm_w[l], 'w_ada': w_ada[l], 'b_ada': b_ada[l], 'w_in': w_in[l],
            'gla_w_a2': gla_w_a2[l], 'gla_b_a': gla_b_a[l], 'gla_o_norm': gla_o_norm[l],
            'mla_q_norm': mla_q_norm[l], 'mla_w_uq': mla_w_uq[l], 'mla_kv_norm': mla_kv_norm[l],
            'mla_w_uk': mla_w_uk[l], 'mla_w_uv': mla_w_uv[l], 'mla_qh_norm': mla_qh_norm[l],
            'mla_kh_norm': mla_kh_norm[l],
            's5_a_re': s5_a_re[l], 's5_a_im': s5_a_im[l], 's5_log_dt': s5_log_dt[l],
            's5_b_re': s5_b_re[l], 's5_b_im': s5_b_im[l], 's5_c_re': s5_c_re[l], 's5_c_im': s5_c_im[l],
            's5_d': s5_d[l], 's5_w_glu': s5_w_glu[l], 's5_b_glu': s5_b_glu[l],
            'w_bo_gla': w_bo_gla[l], 'w_bo_mla': w_bo_mla[l], 'w_bo_s5': w_bo_s5[l], 'w_out': w_out[l],
        }
        hp, (st_mla, st_gla, st_s5) = trunk_layer(hp, cond_ctx, None, None, p)
        ckv_l.append(st_mla[0])
        krope_l.append(st_mla[1])
        gla_l.append(st_gla)
        s5_l.append(st_s5)
        ctx = ((cache_mla_ckv[:, l], cache_mla_krope[:, l]),
               (state_gla[:, l, 0], state_gla[:, l, 1]),
               state_s5[:, l])
        hs, _ = trunk_layer(hs, c, rope, ctx, p)
    new_mla_ckv = jnp.stack(ckv_l, axis=1)
    new_mla_krope = jnp.stack(krope_l, axis=1)
    new_state_gla = jnp.stack(gla_l, axis=1)
    new_state_s5 = jnp.stack(s5_l, axis=1)
    return (hp, hs, new_mla_ckv, new_mla_krope, new_state_gla, new_state_s5)
```

```python
import math
import os
from contextlib import ExitStack

import numpy as np
import concourse.bass as bass
import concourse.mybir as mybir
from concourse.bass_utils import run_bass_kernel_spmd

F32 = mybir.dt.float32
BF16 = mybir.dt.bfloat16
I32 = mybir.dt.int32
ALU = mybir.AluOpType
AF = mybir.ActivationFunctionType
AX = mybir.AxisListType

D = 1024
T = 1024
DEPTH = 2
EPS = 1e-6
PAST = 512
NKEY = PAST + T
D_IN = 6848
C_GQ, C_GK, C_GV, C_GA, C_GG, C_MQ, C_MKV, C_MKR, C_MG, C_SU, C_SG, C_MERGE = (
    0, 256, 512, 1024, 1056, 1568, 1952, 2208, 2240, 2752, 3264, 3776)
MASK_BIG = 2048.0
ATT_SCALE = 96 ** -0.5
PI = math.pi


class Ctx:
    def __init__(self, nc, es):
        self.nc = nc
        self.es = es
        self.eng = {'pe': nc.tensor, 'act': nc.scalar, 'dve': nc.vector, 'pool': nc.gpsimd, 'sp': nc.sync}
        self.sems = {}
        self.cnt = {}
        for e in ('pe', 'act', 'dve', 'pool'):
            self.sems[e] = es.enter_context(nc.semaphore("s_" + e))
            self.cnt[e] = 0
        self.seen = {e: {} for e in self.eng}
        self.lastw = {}
        self.readers = {}
        self.n_ops = 0
        self.bank_rr = 0
        self.fresh = {}

    def _collect(self, reads, writes):
        toks = {}

        def add(t):
            if t is None:
                return
            s, v = t
            if toks.get(s, 0) < v:
                toks[s] = v
        for k in list(reads) + list(writes):
            snap = self.fresh.pop(k, None)
            if snap is not None:
                for s_, v_ in snap.items():
                    if v_ > 0:
                        add((s_, v_))
                self.lastw.pop(k, None)
                self.readers.pop(k, None)
        for k in reads:
            add(self.lastw.get(k))
            if isinstance(k, tuple) and k[0] == 'ps':
                for s, v in self.readers.get(k, {}).items():
                    add((s, v))
        for k in writes:
            add(self.lastw.get(k))
            for s, v in self.readers.get(k, {}).items():
                add((s, v))
        return toks

    def _emit_waits(self, e, toks, skip_own=False):
        eng = self.eng[e]
        seen = self.seen[e]
        for s, v in toks.items():
            if skip_own and s == e:
                continue
            if s not in self.eng:
                v = max(v, self.cnt[s])
            if seen.get(s, 0) >= v:
                continue
            eng.wait_ge(self.sems[s], v)
            seen[s] = v

    def _record(self, tok, reads, writes):
        s, v = tok
        for k in writes:
            self.lastw[k] = tok
            self.readers[k] = {}
        for k in reads:
            r = self.readers.setdefault(k, {})
            if r.get(s, 0) < v:
                r[s] = v

    def op(self, e, fn, reads=(), writes=(), inc=True):
        toks = self._collect(reads, writes)
        self._emit_waits(e, toks, skip_own=(e == 'pe'))
        ins = fn(self.eng[e])
        tok = (e, self.cnt[e] + 1)
        if inc:
            self.cnt[e] += 1
            ins.then_inc(self.sems[e], 1)
        self._record(tok, reads, writes)
        self.n_ops += 1
        return ins

    def dma(self, q, out, in_, reads=(), writes=(), sem=None):
        assert sem is not None
        if sem not in self.sems:
            self.sems[sem] = self.es.enter_context(self.nc.semaphore("d_%d" % len(self.sems)))
            self.cnt[sem] = 0
        toks = self._collect(reads, writes)
        self._emit_waits(q, toks)
        ins = self.eng[q].dma_start(out=out, in_=in_)
        self.cnt[sem] += 16
        ins.then_inc(self.sems[sem], 16)
        self._record((sem, self.cnt[sem]), reads, writes)
        self.n_ops += 1

    def final_wait(self):
        toks = {s: v for s, v in self.cnt.items() if v > 0}
        self._emit_waits('sp', toks)


def build_program(dbg=None, stop=None):
    nc = bass.Bass("TRN2", target_bir_lowering=False)
    es = ExitStack()
    with es:
        cx = _build(nc, es, dbg or {}, stop)
    nc._n_ops = cx.n_ops
    return nc


def _build(nc, es, dbg, stop):
    cx = Ctx(nc, es)

    def tap(name, ap, key):
        if name not in dbg:
            return
        d = nc.dram_tensor("dbg_" + name, list(ap.shape), ap.dtype, kind="ExternalOutput").ap()
        cx.dma('sp', d, ap, reads=[key], sem='dbg')

    def din(name, shape, dt=F32):
        return nc.dram_tensor(name, list(shape), dt, kind="ExternalInput").ap()

    def dout(name, shape, dt=F32):
        return nc.dram_tensor(name, list(shape), dt, kind="ExternalOutput").ap()

    x_d = din("x", [T, D])
    cond_d = din("cond", [D])
    ckvc_d = din("ckv_c", [DEPTH, PAST, 256])
    krc_d = din("kr_c", [DEPTH, PAST, 32])
    sg0_d = din("sg0", [DEPTH, 2, 4, 64, 128])
    s50_d = din("s50", [DEPTH, 2, 2, 32, 64])
    cos_d = din("rope_cos", [T, 16])
    sin_d = din("rope_sin", [T, 16])
    qmask_d = din("qmask", [5, T])
    kmask_d = din("kmask", [5, NKEY])
    rcol_d = din("rcol", [128, 1])
    ident_d = din("ident", [128, 128])
    jmat_d = din("jmat", [128, 128])
    glam_d = din("gla_masks", [2, 128, 128])
    cmask_d = din("chunk_mask", [128, T])
    s5m_d = din("s5_masks", [2, 128, 128])
    selc_d = din("sel_c", [2, 128, 64])
    norm_w = din("norm_w", [DEPTH, D])
    w_ada = din("w_ada", [DEPTH, D, 3 * D])
    b_ada = din("b_ada", [DEPTH, 3 * D])
    w_in = din("w_in", [DEPTH, D, D_IN])
    gla_w_a2 = din("gla_w_a2", [DEPTH, 2, 16, 256])
    gla_b_a = din("gla_b_a", [DEPTH, 2, 256])
    gla_o_norm = din("gla_o_norm", [DEPTH, 128])
    mla_q_norm = din("mla_q_norm", [DEPTH, 384])
    mla_w_uq = din("mla_w_uq", [DEPTH, 384, 384])
    mla_kv_norm = din("mla_kv_norm", [DEPTH, 256])
    mla_w_uk = din("mla_w_uk", [DEPTH, 256, 256])
    mla_w_uv = din("mla_w_uv", [DEPTH, 256, 512])
    mla_qh_norm = din("mla_qh_norm", [DEPTH, 96])
    mla_kh_norm = din("mla_kh_norm", [DEPTH, 96])
    s5_a_re = din("s5_a_re", [DEPTH, 2, 32, 64])
    s5_a_im = din("s5_a_im", [DEPTH, 2, 32, 64])
    s5_log_dt = din("s5_log_dt", [DEPTH, 2, 32])
    s5_b_re = din("s5_b_re", [DEPTH, 32, 64, 16])
    s5_b_im = din("s5_b_im", [DEPTH, 32, 64, 16])
    s5_c_re = din("s5_c_re", [DEPTH, 32, 16, 64])
    s5_c_im = din("s5_c_im", [DEPTH, 32, 16, 64])
    s5_d = din("s5_d", [DEPTH, 512])
    s5_w_glu = din("s5_w_glu", [DEPTH, 512, 1024])
    s5_b_glu = din("s5_b_glu", [DEPTH, 1024])
    w_bo = [din("w_bo_gla", [DEPTH, 512, D]), din("w_bo_mla", [DEPTH, 512, D]), din("w_bo_s5", [DEPTH, 512, D])]
    w_out = din("w_out", [DEPTH, D, D])

    y_d = dout("y", [T, D])
    ockv_d = dout("o_ckv", [DEPTH, T, 256])
    okr_d = dout("o_kr", [DEPTH, T, 32])
    ogla_d = dout("o_gla", [DEPTH, 4, 2, 4, 64, 128])
    os5_d = dout("o_s5", [DEPTH, 4, 2, 2, 32, 64])

    uniq = {'n': 0}

    def sb(name, shape, dt=F32, stack=None):
        uniq['n'] += 1
        if stack is not None:
            cx.fresh[name] = dict(cx.cnt)
        return (stack or es).enter_context(nc.sbuf_tensor("sb%d_%s" % (uniq['n'], name), list(shape), dt))

    psb = [es.enter_context(nc.psum_tensor("psb%d" % i, [128, 512], F32)) for i in range(8)]

    reserved = set()

    def bank():
        while cx.bank_rr in reserved:
            cx.bank_rr = (cx.bank_rr + 1) % 8
        i = cx.bank_rr
        cx.bank_rr = (i + 1) % 8
        return psb[i], ('ps', i)

    def reserve_bank():
        ps, pk = bank()
        reserved.add(pk[1])
        return ps, pk

    def release_bank(pk):
        reserved.discard(pk[1])

    x_sb = sb("x_sb", [128, 8, D])
    hT = sb("hT", [128, 8, T], BF16)
    gate_bc = sb("gate_bc", [128, D])
    ident_bf = sb("ident_bf", [128, 128], BF16)
    jmat_bf = sb("jmat_bf", [128, 128], BF16)
    ident_f = sb("ident_f", [128, 128])
    ones_bf = sb("ones_bf", [128, 128], BF16)
    glam = sb("glam", [128, 2, 128])
    cmask = sb("cmask", [128, T])
    ropec = sb("ropec", [128, 8, 16])
    ropes = sb("ropes", [128, 8, 16])
    rcol = sb("rcol", [128, 1])
    NSLOT = 3
    wring = [sb("wring%d" % i, [128, 8, 512], BF16) for i in range(NSLOT)]
    oT_br = [sb("obr%d" % i, [128, 4, T], BF16) for i in range(3)]

    ring_state = {'i': 0}

    def load_w(src_ap, kt, ncol):
        i = ring_state['i']
        ring_state['i'] = (i + 1) % NSLOT
        slot = wring[i]
        key = ('wring', i)
        cx.dma('pool', slot[:, 0:kt, 0:ncol], src_ap.rearrange("(k p) c -> p k c", p=128),
               writes=[key], sem='wring%d' % i)
        return slot, key

    cx.dma('sp', x_sb[:, :, :], x_d.rearrange("(t p) d -> p t d", p=128), writes=['x'], sem='x')
    cx.dma('pool', ident_bf[:, :], ident_d[:, :], writes=['ident_bf'], sem='c0')
    cx.dma('pool', jmat_bf[:, :], jmat_d[:, :], writes=['jmat_bf'], sem='c0')
    cx.dma('sp', ident_f[:, :], ident_d[:, :], writes=['ident_f'], sem='c1')
    cx.dma('sp', glam[:, :, :], glam_d.rearrange("a p c -> p a c"), writes=['glam'], sem='c1')
    cx.dma('sp', cmask[:, :], cmask_d[:, :], writes=['cmask'], sem='c1')
    cx.dma('sp', ropec[:, :, :], cos_d.rearrange("(t p) c -> p t c", p=128), writes=['ropec'], sem='c1')
    cx.dma('sp', ropes[:, :, :], sin_d.rearrange("(t p) c -> p t c", p=128), writes=['ropes'], sem='c1')
    cx.dma('sp', rcol[:, :], rcol_d[:, :], writes=['rcol'], sem='c1')
    cx.op('dve', lambda e: e.memset(ones_bf[:, :], 1.0), writes=['ones_bf'])

    def act(fn, reads, writes):
        return cx.op('act', fn, reads, writes)

    def dve(fn, reads, writes):
        return cx.op('dve', fn, reads, writes)

    def pool(fn, reads, writes):
        return cx.op('pool', fn, reads, writes)

    def mm(out, lhsT, rhs, start, stop, reads, writes, inc=None, skip=False):
        if inc is None:
            inc = stop
        if skip:
            return cx.op('pe', lambda e: e.matmul(out, lhsT, rhs, start=start, stop=stop, skip_group_check=True),
                         reads, writes, inc=inc)
        return cx.op('pe', lambda e: e.matmul(out, lhsT, rhs, start=start, stop=stop), reads, writes, inc=inc)

    def mm_b(out, lhsT, rhs, start, stop, reads, writes, inc=None, skip=False, base=0):
        if inc is None:
            inc = stop
        if base == 0:
            return mm(out, lhsT, rhs, start, stop, reads, writes, inc=inc, skip=skip)
        mm(out[0:64], lhsT[:, 0:64], rhs, start, stop, reads, writes, inc=False, skip=skip)
        return mm(out[64:128], lhsT[:, 64:128], rhs, start, stop, reads, writes, inc=inc, skip=skip)

    def rstd_from_ss(ss_ap, out_ap, n, key_in, key_out, tmp_ap, key_tmp):
        act(lambda e: e.activation(out=tmp_ap, in_=ss_ap, func=AF.Sqrt, bias=EPS, scale=1.0 / n),
            [key_in], [key_tmp])
        dve(lambda e: e.reciprocal(out=out_ap, in_=tmp_ap), [key_tmp], [key_out])

    def proj_fm(slot, skey, c0, m, evac, kt=8, rhsT=None, rkey='hT'):
        src = hT if rhsT is None else rhsT
        for th in range(2):
            ps, pk = bank()
            for k in range(kt):
                mm(ps[0:m, :], slot[:, k, c0:c0 + m], src[:, k, th * 512:(th + 1) * 512],
                   k == 0, k == kt - 1, [skey, rkey], [pk])
            evac(ps, pk, th)

    def evac_copy(i, out_ap, in_ap, rkeys, wkeys):
        if i % 2 == 0:
            act(lambda e: e.activation(out=out_ap, in_=in_ap, func=AF.Copy), rkeys, wkeys)
        else:
            dve(lambda e: e.tensor_copy(out=out_ap, in_=in_ap), rkeys, wkeys)

    def branch_gla(l):
        with ExitStack() as st:
            v_tm = sb("v_tm", [128, 8, 512], BF16, stack=st)
            ggT = sb("ggT", [128, 4, T], BF16, stack=st)
            alow = sb("alow", [32, T], BF16, stack=st)
            oT = sb("oT", [128, 4, T], F32, stack=st)
            wa2p = sb("wa2p", [32, 2, 256], BF16, stack=st)
            ba = sb("ba", [128, 2, 2], F32, stack=st)
            onw = sb("onw", [128, 1], stack=st)
            wsm = sb("wsm", [128, 8, 32], BF16, stack=st)
            dve(lambda e: e.memset(wa2p[:, :, :], 0.0), [], ['wa2p'])
            dve(lambda e: e.memset(oT[:, :, :], 0.0), [], ['oT'])

            SK = os.environ.get("KDBG_SKIP", "")
            if 'a' not in SK:
                cx.dma('pool', wa2p[0:16, 0, :], gla_w_a2[l, 0], writes=['wa2p'], sem='gsm')
            if 'A' not in SK:
                cx.dma('pool', wa2p[16:32, 1, :], gla_w_a2[l, 1], writes=['wa2p'], sem='gsm')
            if 'b' not in SK:
                cx.dma('pool', wsm[:, :, :], w_in[l][:, C_GA:C_GA + 32].rearrange("(k p) c -> p k c", p=128),
                       writes=['wsm'], sem='gsm')
            with nc.allow_non_contiguous_dma(reason="tiny bias columns"):
                if 'c' not in SK:
                    cx.dma('sp', ba[:, :, :], gla_b_a[l].rearrange("d (hp p) -> p d hp", p=128), writes=['ba'], sem='gsm2')
                if 'C' not in SK:
                    cx.dma('sp', onw[:, :], gla_o_norm[l].rearrange("(p o) -> p o", o=1), writes=['onw'], sem='gsm2')
            dve(lambda e: e.tensor_scalar(out=ba[:, :, :], in0=ba[:, :, :], scalar1=-1.0, scalar2=None, op0=ALU.mult),
                ['ba'], ['ba'])
            slot_qk, k_qk = load_w(w_in[l][:, C_GQ:C_GQ + 512], 8, 512)
            slot_v, k_v = load_w(w_in[l][:, C_GV:C_GV + 512], 8, 512)
            for tt in range(8):
                ps, pk = bank()
                for k in range(8):
                    mm(ps[:, :], hT[:, k, tt * 128:(tt + 1) * 128], slot_v[:, k, :], k == 0, k == 7, ['hT', k_v], [pk])
                evac_copy(tt, v_tm[:, tt, :], ps[:, :], [pk], ['v_tm'])
            slot_gg, k_gg = load_w(w_in[l][:, C_GG:C_GG + 512], 8, 512)
            proj_fm(wsm, 'wsm', 0, 32,
                    lambda ps, pk, th: act(lambda e: e.activation(out=alow[0:32, th * 512:(th + 1) * 512], in_=ps[0:32, :],
                                                                  func=AF.Copy), [pk], ['alow']))
            for m in range(4):
                proj_fm(slot_gg, k_gg, m * 128, 128,
                        lambda ps, pk, th, m=m: act(lambda e: e.activation(
                            out=ggT[:, m, th * 512:(th + 1) * 512], in_=ps[:, :], func=AF.Silu), [pk], ['ggT']))

            tap("g1_ggT", ggT[:, :, :], 'ggT')
            if stop == 'G1':
                return
            for hp in range(2):
                with ExitStack() as s2:
                    q_f = sb("q_f", [128, T], stack=s2)
                    k_f = sb("k_f", [128, T], stack=s2)
                    SP = sb("SP", [128, T], stack=s2)
                    BC = sb("BC", [128, T], stack=s2)
                    E = sb("E", [128, T], stack=s2)
                    TM = sb("TM", [128, T], stack=s2)
                    qd = [sb("qd%d" % d, [128, T], BF16, stack=s2) for d in range(2)]
                    kd = [sb("kd%d" % d, [128, T], BF16, stack=s2) for d in range(2)]
                    kr = [sb("kr%d" % d, [128, T], BF16, stack=s2) for d in range(2)]
                    krtok = [sb("krtok%d" % d, [128, 8, 128], BF16, stack=s2) for d in range(2)]
                    gdec = [sb("gdec%d" % d, [128, 16], stack=s2) for d in range(2)]
                    S = [sb("S%d" % d, [128, 128], stack=s2) for d in range(2)]
                    Sb = [sb("Sb%d" % d, [128, 128], BF16, stack=s2) for d in range(2)]
                    stg = [sb("stg%d" % i, [128, 128], stack=s2) for i in range(2)]
                    attsb = [sb("attsb%d" % i, [128, 2, 128], BF16, stack=s2) for i in range(2)]
                    proj_fm(slot_qk, k_qk, hp * 128, 128,
                            lambda ps, pk, th: act(lambda e: e.mul(out=q_f[:, th * 512:(th + 1) * 512], in_=ps[:, :],
                                                                   mul=0.125), [pk], ['q_f']))
                    proj_fm(slot_qk, k_qk, 256 + hp * 128, 128,
                            lambda ps, pk, th: dve(lambda e: e.tensor_copy(out=k_f[:, th * 512:(th + 1) * 512],
                                                                            in_=ps[:, :]), [pk], ['k_f']))
                    for d in range(2):
                        cx.dma('sp', S[d][:, :], sg0_d[l, d, 2 * hp:2 * hp + 2].rearrange("h k v -> (h k) v"),
                               writes=['S%d' % d], sem='gS%d' % d)
                        act(lambda e, d=d: e.activation(out=Sb[d][:, :], in_=S[d][:, :], func=AF.Copy),
                            ['S%d' % d], ['Sb%d' % d])
                        for th in range(2):
                            ps, pk = bank()
                            mm(ps[:, :], wa2p[:, d, hp * 128:(hp + 1) * 128], alow[0:32, th * 512:(th + 1) * 512],
                               True, True, ['wa2p', 'alow'], [pk])
                            act(lambda e, ps=ps, th=th, d=d: e.activation(
                                out=E[:, th * 512:(th + 1) * 512], in_=ps[:, :], func=AF.Exp,
                                bias=ba[:, d, hp:hp + 1], scale=-1.0), [pk, 'ba'], ['E'])
                        act(lambda e: e.activation(out=SP[:, :], in_=E[:, :], func=AF.Ln, bias=1.0), ['E'], ['SP'])
                        dve(lambda e: e.tensor_tensor_scan(out=BC[:, :], data0=cmask[:, :], data1=SP[:, :], initial=0.0,
                                                           op0=ALU.mult, op1=ALU.add), ['cmask', 'SP'], ['BC'])
                        act(lambda e, d=d: e.activation(out=gdec[d][:, :], in_=BC[:, 63::64], func=AF.Exp,
                                                        scale=-1.0 / 16), ['BC'], ['gdec%d' % d])
                        BC3 = BC[:, :].rearrange("p (c j) -> p c j", j=64)
                        BL = BC3[:, :, 63:64].to_broadcast([128, 16, 64])
                        TM3 = TM[:, :].rearrange("p (c j) -> p c j", j=64)
                        SP3 = SP[:, :].rearrange("p (c j) -> p c j", j=64)
                        kq, kk, kkr = 'qd%d' % d, 'kd%d' % d, 'kr%d' % d
                        if d == 0:
                            src = BC
                            skey = 'BC'
                        else:
                            dve(lambda e: e.tensor_tensor(out=TM3, in0=BL, in1=BC3, op=ALU.subtract), ['BC'], ['TM'])
                            dve(lambda e: e.tensor_tensor(out=TM[:, :], in0=TM[:, :], in1=SP[:, :], op=ALU.add),
                                ['TM', 'SP'], ['TM'])
                            src = TM
                            skey = 'TM'
                        act(lambda e, src=src: e.activation(out=E[:, :], in_=src[:, :], func=AF.Exp, scale=-1.0 / 16),
                            [skey], ['E'])
                        dve(lambda e, d=d: e.tensor_tensor(out=qd[d][:, :], in0=q_f[:, :], in1=E[:, :], op=ALU.mult),
                            ['q_f', 'E'], [kq])
                        act(lambda e, src=src: e.activation(out=E[:, :], in_=src[:, :], func=AF.Exp, scale=1.0 / 16),
                            [skey], ['E'])
                        dve(lambda e, d=d: e.tensor_tensor(out=kd[d][:, :], in0=k_f[:, :], in1=E[:, :], op=ALU.mult),
                            ['k_f', 'E'], [kk])
                        if d == 0:
                            dve(lambda e: e.tensor_tensor(out=TM3, in0=BL, in1=BC3, op=ALU.subtract), ['BC'], ['TM'])
                        else:
                            dve(lambda e: e.tensor_tensor(out=TM[:, :], in0=BC[:, :], in1=SP[:, :], op=ALU.subtract),
                                ['BC', 'SP', 'TM'], ['TM'])
                        act(lambda e: e.activation(out=E[:, :], in_=TM[:, :], func=AF.Exp, scale=-1.0 / 16),
                            ['TM'], ['E'])
                        dve(lambda e, d=d: e.tensor_tensor(out=kr[d][:, :], in0=k_f[:, :], in1=E[:, :], op=ALU.mult),
                            ['k_f', 'E'], [kkr])
                        for half in range(2):
                            ps, pk = bank()
                            for kk4 in range(4):
                                tt = half * 4 + kk4
                                mm(ps[:, kk4 * 128:(kk4 + 1) * 128], kr[d][:, tt * 128:(tt + 1) * 128], ident_bf[:, :],
                                   True, True, [kkr, 'ident_bf'], [pk], inc=(kk4 == 3))
                            evac_copy(half, krtok[d][:, half * 4:half * 4 + 4, :],
                                      ps[:, :].rearrange("p (a b) -> p a b", b=128), [pk], ['krtok%d' % d])

                    tap("g2_kr", krtok[1][:, :, :], 'krtok1')
                    if stop == 'G2':
                        return

                    def gla_step(d, cp, step_i):
                        kq, kk, kS, kSb = 'qd%d' % d, 'kd%d' % d, 'S%d' % d, 'Sb%d' % d
                        cols = slice(cp * 128, (cp + 1) * 128)
                        ab, akey = bank()
                        for h2 in range(2):
                            rows = slice(h2 * 64, (h2 + 1) * 64)
                            mm_b(ab[:, h2 * 128:(h2 + 1) * 128], kd[d][rows, cols], qd[d][rows, cols], True, True,
                                 [kk, kq], [akey], inc=(h2 == 1), base=h2 * 64)
                        lim = int(os.environ.get("KDBG_CUT", "99"))
                        if lim <= 1:
                            return
                        asb = attsb[step_i % 2]
                        askey = 'attsb%d' % (step_i % 2)
                        dve(lambda e: e.tensor_tensor(
                            out=asb[:, :, :], in0=ab[:, 0:256].rearrange("p (a b) -> p a b", b=128),
                            in1=glam[:, d:d + 1, :].to_broadcast([128, 2, 128]), op=ALU.mult),
                            [akey, 'glam'], [askey])
                        if lim <= 2:
                            return
                        ob, okey = bank()
                        for h2 in range(2):
                            h = 2 * hp + h2
                            mm(ob[:, h2 * 128:(h2 + 1) * 128], v_tm[:, cp, h * 128:(h + 1) * 128], asb[:, h2, :],
                               h2 == 0, False, ['v_tm', askey], [okey], inc=False, skip=True)
                        if lim <= 3:
                            return
                        order = [2 * cp, 2 * cp + 1] if d == 0 else [2 * cp + 1, 2 * cp]
                        for idx, c in enumerate(order):
                            ci = c % 2
                            boundary = (c % 4 == 0 and c > 0) if d == 0 else (c % 4 == 3 and c < 15)
                            if boundary:
                                dve(lambda e: e.tensor_scalar(out=S[d][:, :], in0=S[d][:, :], scalar1=rcol[:, 0:1],
                                                              scalar2=None, op0=ALU.mult), [kS, 'rcol'], [kS])
                                act(lambda e: e.activation(out=Sb[d][:, :], in_=S[d][:, :], func=AF.Copy), [kS], [kSb])
                            for h2 in range(2):
                                rows = slice(h2 * 64, (h2 + 1) * 64)
                                mm_b(ob[:, h2 * 128 + ci * 64:h2 * 128 + ci * 64 + 64], Sb[d][rows, :],
                                     qd[d][rows, c * 64:(c + 1) * 64], False, idx == 1, [kSb, kq], [okey],
                                     inc=(idx == 1 and h2 == 1), skip=True, base=h2 * 64)
                            if lim <= 4:
                                return
                            kvb, kvkey = bank()
                            crow = slice(ci * 64, (ci + 1) * 64)
                            mm_b(kvb[:, 0:256], krtok[d][crow, cp, :], v_tm[crow, cp, hp * 256:(hp + 1) * 256], True, True,
                                 ['krtok%d' % d, 'v_tm'], [kvkey], base=ci * 64)
                            if lim <= 5:
                                return
                            for h2 in range(2):
                                rows = slice(h2 * 64, (h2 + 1) * 64)
                                dve(lambda e, rows=rows, h2=h2, c=c: e.scalar_tensor_tensor(
                                    out=S[d][rows, :], in0=S[d][rows, :], scalar=gdec[d][rows, c:c + 1],
                                    in1=kvb[rows, h2 * 128:(h2 + 1) * 128], op0=ALU.mult, op1=ALU.add),
                                    [kS, 'gdec%d' % d, kvkey], [kS])
                            if lim <= 6:
                                return
                            act(lambda e: e.activation(out=Sb[d][:, :], in_=S[d][:, :], func=AF.Copy), [kS], [kSb])
                            seg_end = (c % 4 == 3) if d == 0 else (c % 4 == 0)
                            if seg_end:
                                seg = c // 4
                                sg = stg[seg % 2]
                                sgk = 'stg%d' % (seg % 2)
                                act(lambda e, sg=sg: e.activation(out=sg[:, :], in_=S[d][:, :], func=AF.Copy), [kS], [sgk])
                                cx.dma('sp', ogla_d[l, seg, d, 2 * hp:2 * hp + 2].rearrange("h k v -> (h k) v"), sg[:, :],
                                       reads=[sgk], sem='ogla')
                        dve(lambda e: e.tensor_tensor(
                            out=oT[:, 2 * hp:2 * hp + 2, cols], in0=oT[:, 2 * hp:2 * hp + 2, cols],
                            in1=ob[:, 0:256].rearrange("p (a b) -> p a b", b=128), op=ALU.add), ['oT', okey], ['oT'])

                    nsteps = int(os.environ.get("KDBG_STEPS", "8"))
                    for step in range(min(8, nsteps)):
                        gla_step(0, step, 2 * step)
                        if 'x' not in SK:
                            gla_step(1, 7 - step, 2 * step + 1)
                    if nsteps < 8:
                        tap("g3_oT", oT[:, :, :], 'oT')
                        return
            tap("gla_oT%d" % l, oT[:, :, :], 'oT')
            with ExitStack() as s3:
                sqb = [sb("sqb%d" % i, [128, 512], BF16, stack=s3) for i in range(2)]
                sdv = [sb("sdv%d" % i, [128, 512], stack=s3) for i in range(2)]
                tmv = [sb("tmv%d" % i, [128, 512], stack=s3) for i in range(2)]
                i = 0
                for h in range(4):
                    for th in range(2):
                        cs = slice(th * 512, (th + 1) * 512)
                        sq, sd, tm_ = sqb[i % 2], sdv[i % 2], tmv[i % 2]
                        ksq, ksd, ktm = 'sqb%d' % (i % 2), 'sdv%d' % (i % 2), 'tmv%d' % (i % 2)
                        act(lambda e, sq=sq, h=h, cs=cs: e.activation(out=sq[:, :], in_=oT[:, h, cs], func=AF.Square),
                            ['oT'], [ksq])
                        ps, pk = bank()
                        mm(ps[:, :], ones_bf[:, :], sq[:, :], True, True, ['ones_bf', ksq], [pk])
                        act(lambda e, ps=ps, sd=sd: e.activation(out=sd[:, :], in_=ps[:, :], func=AF.Sqrt, bias=EPS,
                                                                 scale=1.0 / 128), [pk], [ksd])
                        dve(lambda e, sd=sd: e.reciprocal(out=sd[:, :], in_=sd[:, :]), [ksd], [ksd])
                        dve(lambda e, sd=sd, tm_=tm_, h=h, cs=cs: e.scalar_tensor_tensor(
                            out=tm_[:, :], in0=oT[:, h, cs], scalar=onw[:, 0:1], in1=sd[:, :], op0=ALU.mult, op1=ALU.mult),
                            ['oT', 'onw', ksd], [ktm])
                        dve(lambda e, tm_=tm_, h=h, cs=cs: e.tensor_tensor(
                            out=oT_br[0][:, h, cs], in0=tm_[:, :], in1=ggT[:, h, cs], op=ALU.mult),
                            [ktm, 'ggT'], ['obr0'])
                        i += 1
            tap("gla_out%d" % l, oT_br[0][:, :, :], 'obr0')

    def branch_mla(l):
        with ExitStack() as st:
            QT = sb("QT", [128, 4, T], BF16, stack=st)
            KT = sb("KT", [128, 4, NKEY], BF16, stack=st)
            Vt = sb("Vt", [128, 12, 512], BF16, stack=st)
            mgT = sb("mgT", [128, 4, T], BF16, stack=st)
            CKb = sb("CKb", [128, 12, 256], BF16, stack=st)
            KR = sb("KR", [128, 12, 32], stack=st)
            cqn = sb("cqn", [128, 8, 384], BF16, stack=st)
            cqnT = sb("cqnT", [128, 3, T], BF16, stack=st)
            ckvT = sb("ckvT", [128, 2, NKEY], BF16, stack=st)
            wuq = sb("wuq", [128, 3, 384], BF16, stack=st)
            wuqf = sb("wuqf", [128, 3, 384], stack=st)
            qnw = sb("qnw", [128, 3], stack=st)
            wuk = sb("wuk", [128, 2, 256], BF16, stack=st)
            wuv = sb("wuv", [128, 2, 512], BF16, stack=st)
            kvw_bc = sb("kvw_bc", [128, 256], stack=st)
            qhw_bc = sb("qhw_bc", [128, 96], stack=st)
            khw_bc = sb("khw_bc", [128, 96], stack=st)
            ssq = sb("ssq", [128, 8, 2], stack=st)
            rsq = sb("rsq", [128, 8, 2], stack=st)
            tq = sb("tq", [128, 8, 2], stack=st)
            sskr = sb("sskr", [128, 12], stack=st)
            junk = sb("junkm", [128, 512], stack=st)
            stg = [sb("stgkv%d" % i, [128, 288], stack=st) for i in range(2)]
            for h in range(4):
                cx.dma('pool', QT[96:101, h, :], qmask_d[:, :], writes=['QT'], sem='mmask')
                cx.dma('pool', KT[96:101, h, :], kmask_d[:, :], writes=['KT'], sem='mmask')
            cx.dma('sp', wuqf[:, :, :], mla_w_uq[l].rearrange("(k p) c -> p k c", p=128), writes=['wuqf'], sem='mw1')
            with nc.allow_non_contiguous_dma(reason="tiny norm weight column"):
                cx.dma('sp', qnw[:, :], mla_q_norm[l].rearrange("(k p) -> p k", p=128), writes=['qnw'], sem='mw1')
            cx.dma('pool', wuk[:, :, :], mla_w_uk[l].rearrange("(k p) c -> p k c", p=128), writes=['wuk'], sem='mw2')
            cx.dma('pool', wuv[:, :, :], mla_w_uv[l].rearrange("(k p) c -> p k c", p=128), writes=['wuv'], sem='mw2')
            cx.dma('sp', kvw_bc[:, :], mla_kv_norm[l].partition_broadcast(128), writes=['kvw_bc'], sem='mw1')
            cx.dma('sp', qhw_bc[:, :], mla_qh_norm[l].partition_broadcast(128), writes=['qhw_bc'], sem='mw1')
            cx.dma('sp', khw_bc[:, :], mla_kh_norm[l].partition_broadcast(128), writes=['khw_bc'], sem='mw1')
            cx.dma('pool', CKb[:, 0:4, :], ckvc_d[l].rearrange("(t p) c -> p t c", p=128), writes=['CKb'], sem='mw2')
            cx.dma('sp', KR[:, 0:4, :], krc_d[l].rearrange("(t p) c -> p t c", p=128), writes=['KR'], sem='mw1')
            for k in range(3):
                dve(lambda e: e.tensor_scalar(out=wuq[:, k, :], in0=wuqf[:, k, :], scalar1=qnw[:, k:k + 1], scalar2=None,
                                              op0=ALU.mult), ['wuqf', 'qnw'], ['wuq'])
            dve(lambda e: e.memset(ssq[:, :, :], 0.0), [], ['ssq'])
            dve(lambda e: e.memset(sskr[:, :], 0.0), [], ['sskr'])
            slotA, kA = load_w(w_in[l][:, C_MQ:C_MQ + 384], 8, 384)
            slotB, kB = load_w(w_in[l][:, C_MKV:C_MKV + 288], 8, 288)
            for tt in range(8):
                tsl = slice(tt * 128, (tt + 1) * 128)
                psA, pkA = bank()
                for k in range(8):
                    mm(psA[:, 0:384], hT[:, k, tsl], slotA[:, k, 0:384], k == 0, k == 7, ['hT', kA], [pkA])
                psB, pkB = bank()
                for k in range(8):
                    mm(psB[:, 0:288], hT[:, k, tsl], slotB[:, k, 0:288], k == 0, k == 7, ['hT', kB], [pkB])
                act(lambda e: e.activation(out=junk[:, 0:384], in_=psA[:, 0:384], func=AF.Square,
                                           accum_out=ssq[:, tt, 0:1]), [pkA, 'ssq'], ['junkm', 'ssq'])
                act(lambda e: e.activation(out=junk[:, 0:256], in_=psB[:, 0:256], func=AF.Square,
                                           accum_out=ssq[:, tt, 1:2]), [pkB, 'ssq'], ['junkm', 'ssq'])
                act(lambda e: e.activation(out=tq[:, tt, 0:1], in_=ssq[:, tt, 0:1], func=AF.Sqrt, bias=EPS,
                                           scale=1.0 / 384), ['ssq'], ['tq'])
                act(lambda e: e.activation(out=tq[:, tt, 1:2], in_=ssq[:, tt, 1:2], func=AF.Sqrt, bias=EPS,
                                           scale=1.0 / 256), ['ssq'], ['tq'])
                dve(lambda e: e.reciprocal(out=rsq[:, tt, :], in_=tq[:, tt, :]), ['tq'], ['rsq'])
                dve(lambda e: e.tensor_scalar(out=cqn[:, tt, :], in0=psA[:, 0:384], scalar1=rsq[:, tt, 0:1], scalar2=None,
                                              op0=ALU.mult), [pkA, 'rsq'], ['cqn'])
                sg = stg[tt % 2]
                sgk = 'stgkv%d' % (tt % 2)
                dve(lambda e: e.scalar_tensor_tensor(out=sg[:, 0:256], in0=psB[:, 0:256], scalar=rsq[:, tt, 1:2],
                                                     in1=kvw_bc[:, :], op0=ALU.mult, op1=ALU.mult),
                    [pkB, 'rsq', 'kvw_bc'], [sgk])
                act(lambda e: e.activation(out=sg[:, 256:288], in_=psB[:, 256:288], func=AF.Copy), [pkB], [sgk])
                cx.dma('sp', ockv_d[l, tsl, :], sg[:, 0:256], reads=[sgk], sem='ockv')
                cx.dma('sp', okr_d[l, tsl, :], sg[:, 256:288], reads=[sgk], sem='ockv')
                act(lambda e: e.activation(out=CKb[:, 4 + tt, :], in_=sg[:, 0:256], func=AF.Copy), [sgk], ['CKb'])
                dve(lambda e: e.tensor_copy(out=KR[:, 4 + tt, :], in_=sg[:, 256:288]), [sgk], ['KR'])
            for tt in range(8):
                ps, pk = bank()
                for k in range(3):
                    mm(ps[:, k * 128:(k + 1) * 128], cqn[:, tt, k * 128:(k + 1) * 128], ident_bf[:, :], True, True,
                       ['cqn', 'ident_bf'], [pk], inc=(k == 2))
                evac_copy(tt, cqnT[:, :, tt * 128:(tt + 1) * 128], ps[:, 0:384].rearrange("p (a b) -> p a b", b=128),
                          [pk], ['cqnT'])
            for kp in range(6):
                ps, pk = bank()
                for j in range(2):
                    kt = 2 * kp + j
                    for k in range(2):
                        mm(ps[:, (2 * j + k) * 128:(2 * j + k + 1) * 128], CKb[:, kt, k * 128:(k + 1) * 128], ident_bf[:, :],
                           True, True, ['CKb', 'ident_bf'], [pk], inc=(j == 1 and k == 1))
                for j in range(2):
                    kt = 2 * kp + j
                    evac_copy(j, ckvT[:, :, kt * 128:(kt + 1) * 128],
                              ps[:, j * 256:(j + 1) * 256].rearrange("p (a b) -> p a b", b=128), [pk], ['ckvT'])
            slot_mg, k_mg = load_w(w_in[l][:, C_MG:C_MG + 512], 8, 512)
            for m in range(4):
                proj_fm(slot_mg, k_mg, m * 128, 128,
                        lambda ps, pk, th, m=m: act(lambda e: e.activation(
                            out=mgT[:, m, th * 512:(th + 1) * 512], in_=ps[:, :], func=AF.Silu), [pk], ['mgT']))

            def head_finish(src3, skey, nrm_keys, rope_tt, dstT, dkey, col0, tagi):
                i2 = tagi % 2
                fb = hfb[i2]
                fk = 'hfb%d' % i2
                if rope_tt is None:
                    act(lambda e: e.activation(out=fb[:, :, :], in_=src3, func=AF.Copy), [skey], [fk])
                else:
                    rt = rtmp[i2]
                    rk = 'rtmp%d' % i2
                    cb = ropec[:, rope_tt:rope_tt + 1, :].to_broadcast([128, 4, 16])
                    sbb = ropes[:, rope_tt:rope_tt + 1, :].to_broadcast([128, 4, 16])
                    x1 = src3[:, :, 64:80]
                    x2 = src3[:, :, 80:96]
                    act(lambda e: e.activation(out=fb[:, :, 0:64], in_=src3[:, :, 0:64], func=AF.Copy), [skey], [fk])
                    pool(lambda e: e.tensor_tensor(out=rt[:, 0, :, :], in0=x1, in1=cb, op=ALU.mult), [skey, 'ropec'], [rk])
                    pool(lambda e: e.tensor_tensor(out=rt[:, 1, :, :], in0=x2, in1=sbb, op=ALU.mult), [skey, 'ropes'], [rk])
                    pool(lambda e: e.tensor_tensor(out=rt[:, 2, :, :], in0=x1, in1=sbb, op=ALU.mult), [skey, 'ropes'], [rk])
                    pool(lambda e: e.tensor_tensor(out=rt[:, 3, :, :], in0=x2, in1=cb, op=ALU.mult), [skey, 'ropec'], [rk])
                    pool(lambda e: e.tensor_tensor(out=fb[:, :, 64:80], in0=rt[:, 0, :, :], in1=rt[:, 1, :, :],
                                                   op=ALU.subtract), [rk], [fk])
                    pool(lambda e: e.tensor_tensor(out=fb[:, :, 80:96], in0=rt[:, 2, :, :], in1=rt[:, 3, :, :],
                                                   op=ALU.add), [rk], [fk])
                ps, pk = bank()
                for h in range(4):
                    mm(ps[0:96, h * 128:(h + 1) * 128], fb[:, h, :], ident_bf[:, :], True, True, [fk, 'ident_bf'], [pk],
                       inc=(h == 3))
                act(lambda e: e.activation(out=dstT[0:96, :, col0:col0 + 128],
                                           in_=ps[0:96, :].rearrange("p (a b) -> p a b", b=128), func=AF.Copy),
                    [pk], [dkey])

            with ExitStack() as s2:
                hfb = [sb("hfb%d" % i, [128, 4, 96], BF16, stack=s2) for i in range(2)]
                rtmp = [sb("rtmp%d" % i, [128, 4, 4, 16], stack=s2) for i in range(2)]
                sqh = [sb("sqh%d" % i, [128, 384], stack=s2) for i in range(2)]
                ssh = [sb("ssh%d" % i, [128, 4], stack=s2) for i in range(2)]
                hn = [sb("hn%d" % i, [128, 4, 96], stack=s2) for i in range(2)]
                for tt in range(8):
                    i2 = tt % 2
                    psQ, pkQ = bank()
                    for k in range(3):
                        mm(psQ[:, 0:384], cqnT[:, k, tt * 128:(tt + 1) * 128], wuq[:, k, :], k == 0, k == 2,
                           ['cqnT', 'wuq'], [pkQ])
                    q3 = psQ[:, 0:384].rearrange("p (a b) -> p a b", b=96)
                    act(lambda e: e.activation(out=sqh[i2][:, 0:384], in_=psQ[:, 0:384], func=AF.Square),
                        [pkQ], ['sqh%d' % i2])
                    dve(lambda e: e.tensor_reduce(out=ssh[i2][:, :], in_=sqh[i2][:, 0:384].rearrange("p (a b) -> p a b", b=96),
                                                  axis=AX.X, op=ALU.add), ['sqh%d' % i2], ['ssh%d' % i2])
                    act(lambda e: e.activation(out=ssh[i2][:, :], in_=ssh[i2][:, :], func=AF.Sqrt, bias=EPS,
                                               scale=1.0 / 96), ['ssh%d' % i2], ['ssh%d' % i2])
                    dve(lambda e: e.reciprocal(out=ssh[i2][:, :], in_=ssh[i2][:, :]), ['ssh%d' % i2], ['ssh%d' % i2])
                    dve(lambda e: e.tensor_tensor(out=hn[i2][:, :, :], in0=q3,
                                                  in1=ssh[i2][:, :].unsqueeze(2).to_broadcast([128, 4, 96]), op=ALU.mult),
                        [pkQ, 'ssh%d' % i2], ['hn%d' % i2])
                    dve(lambda e: e.tensor_tensor(out=hn[i2][:, :, :], in0=hn[i2][:, :, :],
                                                  in1=qhw_bc[:, :].unsqueeze(1).to_broadcast([128, 4, 96]), op=ALU.mult),
                        ['hn%d' % i2, 'qhw_bc'], ['hn%d' % i2])
                    head_finish(hn[i2][:, :, :], 'hn%d' % i2, None, tt, QT, 'QT', tt * 128, tt)
                for kt in range(12):
                    i2 = kt % 2
                    ksl = slice(kt * 128, (kt + 1) * 128)
                    psK, pkK = bank()
                    for k in range(2):
                        mm(psK[:, 0:256], ckvT[:, k, ksl], wuk[:, k, :], k == 0, k == 1, ['ckvT', 'wuk'], [pkK])
                    psV, pkV = bank()
                    for k in range(2):
                        mm(psV[:, :], ckvT[:, k, ksl], wuv[:, k, :], k == 0, k == 1, ['ckvT', 'wuv'], [pkV])
                    evac_copy(kt, Vt[:, kt, :], psV[:, :], [pkV], ['Vt'])
                    act(lambda e: e.activation(out=sqh[i2][:, 0:256], in_=psK[:, 0:256], func=AF.Square),
                        [pkK], ['sqh%d' % i2])
                    dve(lambda e: e.tensor_reduce(out=ssh[i2][:, :], in_=sqh[i2][:, 0:256].rearrange("p (a b) -> p a b", b=64),
                                                  axis=AX.X, op=ALU.add), ['sqh%d' % i2], ['ssh%d' % i2])
                    act(lambda e: e.activation(out=junk[:, 0:32], in_=KR[:, kt, :], func=AF.Square,
                                               accum_out=sskr[:, kt:kt + 1]), ['KR', 'sskr'], ['junkm', 'sskr'])
                    dve(lambda e: e.tensor_scalar(out=ssh[i2][:, :], in0=ssh[i2][:, :], scalar1=sskr[:, kt:kt + 1],
                                                  scalar2=None, op0=ALU.add), ['ssh%d' % i2, 'sskr'], ['ssh%d' % i2])
                    act(lambda e: e.activation(out=ssh[i2][:, :], in_=ssh[i2][:, :], func=AF.Sqrt, bias=EPS,
                                               scale=1.0 / 96), ['ssh%d' % i2], ['ssh%d' % i2])
                    dve(lambda e: e.reciprocal(out=ssh[i2][:, :], in_=ssh[i2][:, :]), ['ssh%d' % i2], ['ssh%d' % i2])
                    k3 = psK[:, 0:256].rearrange("p (a b) -> p a b", b=64)
                    dve(lambda e: e.tensor_tensor(out=hn[i2][:, :, 0:64], in0=k3,
                                                  in1=ssh[i2][:, :].unsqueeze(2).to_broadcast([128, 4, 64]), op=ALU.mult),
                        [pkK, 'ssh%d' % i2], ['hn%d' % i2])
                    dve(lambda e: e.tensor_tensor(out=hn[i2][:, :, 64:96],
                                                  in0=KR[:, kt:kt + 1, :].to_broadcast([128, 4, 32]),
                                                  in1=ssh[i2][:, :].unsqueeze(2).to_broadcast([128, 4, 32]), op=ALU.mult),
                        ['KR', 'ssh%d' % i2], ['hn%d' % i2])
                    dve(lambda e: e.tensor_tensor(out=hn[i2][:, :, :], in0=hn[i2][:, :, :],
                                                  in1=khw_bc[:, :].unsqueeze(1).to_broadcast([128, 4, 96]), op=ALU.mult),
                        ['hn%d' % i2, 'khw_bc'], ['hn%d' % i2])
                    head_finish(hn[i2][:, :, :], 'hn%d' % i2, None, (kt - 4) if kt >= 4 else None, KT, 'KT', kt * 128, kt)
            tap("mla_QT%d" % l, QT[0:101, :, :], 'QT')
            tap("mla_KT%d" % l, KT[0:101, :, :], 'KT')
            tap("mla_V%d" % l, Vt[:, :, :], 'Vt')
            if stop == 'M1':
                return
            with ExitStack() as s3:
                PT = [sb("PT%d" % i, [128, 512], BF16, stack=s3) for i in range(3)]
                rden = [sb("rden%d" % i, [128, 512], stack=s3) for i in range(2)]
                accs = [(reserve_bank(), reserve_bank()) for _ in range(2)]
                it = 0
                pi = 0
                for h in range(4):
                    for qh in range(2):
                        (ob, okey), (db, dkey) = accs[it % 2]
                        qsl = slice(qh * 512, (qh + 1) * 512)
                        for kt in range(12):
                            sbk, skey = bank()
                            mm(sbk[:, :], KT[0:101, h, kt * 128:(kt + 1) * 128], QT[0:101, h, qsl], True, True,
                               ['KT', 'QT'], [skey])
                            pt = PT[pi % 3]
                            ptk = 'PT%d' % (pi % 3)
                            pi += 1
                            act(lambda e: e.activation(out=pt[:, :], in_=sbk[:, :], func=AF.Exp, scale=ATT_SCALE),
                                [skey], [ptk])
                            mm(ob[:, :], Vt[:, kt, h * 128:(h + 1) * 128], pt[:, :], kt == 0, kt == 11, ['Vt', ptk], [okey])
                            mm(db[:, :], ones_bf[:, :], pt[:, :], kt == 0, kt == 11, ['ones_bf', ptk], [dkey])
                        rd = rden[it % 2]
                        rdk = 'rden%d' % (it % 2)
                        dve(lambda e: e.reciprocal(out=rd[:, :], in_=db[:, :]), [dkey], [rdk])
                        dve(lambda e: e.tensor_tensor(out=rd[:, :], in0=ob[:, :], in1=rd[:, :], op=ALU.mult),
                            [okey, rdk], [rdk])
                        dve(lambda e: e.tensor_tensor(out=oT_br[1][:, h, qsl], in0=rd[:, :], in1=mgT[:, h, qsl],
                                                      op=ALU.mult), [rdk, 'mgT'], ['obr1'])
                        it += 1
                for (a, b) in accs:
                    release_bank(a[1])
                    release_bank(b[1])
            tap("mla_out%d" % l, oT_br[1][:, :, :], 'obr1')

    class Arena:
        def __init__(self, t, nelem, dt):
            self.t = t
            self.dt = dt
            self.ti = t.bitcast(I32) if dt == F32 else None
            self.free_list = [(0, nelem)]
            self.live = {}

        def alloc(self, name, dims, dt=None):
            n = 1
            for d_ in dims:
                n *= d_
            nf = (n + 15) // 16 * 16
            for idx, (o, sz) in enumerate(self.free_list):
                if sz >= nf:
                    if sz == nf:
                        self.free_list.pop(idx)
                    else:
                        self.free_list[idx] = (o + nf, sz - nf)
                    break
            else:
                raise RuntimeError("arena full: %s %s %s" % (name, dims, self.free_list))
            self.live[name] = (o, nf)
            if dt == I32:
                ap = self.ti[:, o:o + n]
            else:
                ap = self.t[:, o:o + n]
            if len(dims) > 1:
                names = ["a%d" % i for i in range(len(dims))]
                pat = "p (%s) -> p %s" % (" ".join(names), " ".join(names))
                ap = ap.rearrange(pat, **{nm: d_ for nm, d_ in zip(names, dims)})
            cx.fresh[name] = dict(cx.cnt)
            return ap

        def free(self, name):
            o, nf = self.live.pop(name)
            fl = sorted(self.free_list + [(o, nf)])
            merged = []
            for a, b in fl:
                if merged and merged[-1][0] + merged[-1][1] == a:
                    merged[-1] = (merged[-1][0], merged[-1][1] + b)
                else:
                    merged.append((a, b))
            self.free_list = merged

    class Arena2:
        def __init__(self, af, ab):
            self.af, self.ab = af, ab
            self.where = {}

        def alloc(self, name, dims, dt=F32):
            a = self.ab if dt == BF16 else self.af
            self.where[name] = a
            return a.alloc(name, dims, dt)

        def free(self, name):
            self.where.pop(name).free(name)

    def branch_s5(l):
        with ExitStack() as st:
            NF, NB = 9600, 29696
            arena_f = sb("s5arena_f", [128, NF], stack=st)
            arena_b = sb("s5arena_b", [128, NB], BF16, stack=st)
            A = Arena2(Arena(arena_f, NF, F32), Arena(arena_b, NB, BF16))
            WoR = A.alloc("WoR", [2, 16, 2, 128], BF16)
            TT = A.alloc("TT", [32, 128], BF16)
            C1 = A.alloc("C1", [16, 2, 2]); C2 = A.alloc("C2", [16, 2, 2])
            C1r = A.alloc("C1r", [16, 2, 2]); C2r = A.alloc("C2r", [16, 2, 2])
            WinT = A.alloc("WinT", [2, 32, 128], BF16)
            s5m = A.alloc("s5m", [2, 128])
            selc = A.alloc("selc", [2, 64])
            cx.dma('sp', s5m, s5m_d.rearrange("a p c -> p a c"), writes=['s5m'], sem='s5c')
            cx.dma('sp', selc, selc_d.rearrange("a p c -> p a c"), writes=['selc'], sem='s5c')

            def tt_(e, o, a, b, op):
                return e.tensor_tensor(out=o, in0=a, in1=b, op=op)

            def D2(o, a, b, op, rk, wk):
                dve(lambda e: tt_(e, o, a, b, op), rk, wk)

            AR = A.alloc("AR", [2, 16]); AI = A.alloc("AI", [2, 16]); LD = A.alloc("LD", [2, 16])
            anat = A.alloc("anat", [2, 128])
            cx.dma('sp', anat[0:32, 0, :], s5_a_re[l].rearrange("d (j g2) n -> (d j) (g2 n)", g2=2), writes=['anat'], sem='s5c')
            cx.dma('sp', anat[0:32, 1, :], s5_a_im[l].rearrange("d (j g2) n -> (d j) (g2 n)", g2=2), writes=['anat'], sem='s5c')
            ps, pk = bank()
            for c_ in range(2):
                mm(ps[:, c_ * 32:(c_ + 1) * 32], anat[0:32, c_, :], ident_f[0:32, 0:32], True, True, ['anat', 'ident_f'], [pk],
                   inc=(c_ == 1))
            dve(lambda e: e.tensor_copy(out=AR, in_=ps[:, 0:32].rearrange("p (d j) -> p d j", d=2)), [pk], ['AR'])
            dve(lambda e: e.tensor_copy(out=AI, in_=ps[:, 32:64].rearrange("p (d j) -> p d j", d=2)), [pk], ['AI'])
            ldf = A.alloc("ldf", [2, 32])
            cx.dma('sp', ldf, s5_log_dt[l].rearrange("d g -> (d g)").partition_broadcast(128).rearrange("p (d g) -> p d g", d=2),
                   writes=['ldf'], sem='s5c')
            for g2 in range(2):
                dve(lambda e: e.tensor_copy(out=LD[g2 * 64:(g2 + 1) * 64], in_=ldf[g2 * 64:(g2 + 1) * 64, :, g2::2]),
                    ['ldf'], ['LD'])
            dcol = A.alloc("dcol", [32])
            with nc.allow_non_contiguous_dma(reason="small S5 parameter gathers"):
                for s_ in range(8):
                    cx.dma('sp', dcol[s_ * 16:(s_ + 1) * 16, :], s5_d[l].rearrange("(g q) -> q g", q=16),
                           writes=['dcol'], sem='s5c')
            BR = A.alloc("BR", [16, 16]); BI = A.alloc("BI", [16, 16])
            for jq in range(4):
                cx.dma('sp', BR[:, 4 * jq:4 * jq + 4, :], s5_b_re[l].rearrange("(j g2) n q -> (g2 n) j q", g2=2)[:, 4 * jq:4 * jq + 4, :],
                       writes=['BR'], sem='s5c')
                cx.dma('sp', BI[:, 4 * jq:4 * jq + 4, :], s5_b_im[l].rearrange("(j g2) n q -> (g2 n) j q", g2=2)[:, 4 * jq:4 * jq + 4, :],
                       writes=['BI'], sem='s5c')
            CNr = A.alloc("CNr", [4, 64]); CNi = A.alloc("CNi", [4, 64])
            cx.dma('sp', CNr, s5_c_re[l].rearrange("(r gl) p n -> (gl p) r n", r=4), writes=['CNr'], sem='s5c')
            cx.dma('sp', CNi, s5_c_im[l].rearrange("(r gl) p n -> (gl p) r n", r=4), writes=['CNi'], sem='s5c')
            CR = A.alloc("CR", [16, 16]); CI = A.alloc("CI", [16, 16])
            for (cn, cnk, cdst, cdk, ei) in ((CNr, 'CNr', CR, 'CR', 0), (CNi, 'CNi', CI, 'CI', 1)):
                ps, pk = bank()
                for r in range(4):
                    for g2 in range(2):
                        mm(ps[g2 * 64:(g2 + 1) * 64, r * 64:(r + 1) * 64], cn[:, r, :], selc[:, g2, :], True, True,
                           [cnk, 'selc'], [pk], inc=(r == 3 and g2 == 1))
                evac_copy(ei, cdst, ps[:, 0:256].rearrange("p (a b) -> p a b", b=16), [pk], [cdk])

            def small(name):
                return A.alloc(name, [2, 16])
            dt_ = small("dt_"); mag = small("mag"); ang = small("ang"); sn = small("sn"); cs = small("cs")
            abr = small("abr"); abi = small("abi"); rden = small("rden"); nre = small("nre")
            cfr = small("cfr"); cfi = small("cfi"); ta = small("ta"); tb = small("tb")
            kI = A.alloc("kI", [2, 16], I32)
            act(lambda e: e.activation(out=dt_, in_=LD, func=AF.Exp), ['LD'], ['dt_'])
            D2(ta, AR, dt_, ALU.mult, ['AR', 'dt_'], ['ta'])
            act(lambda e: e.activation(out=mag, in_=ta, func=AF.Exp), ['ta'], ['mag'])
            D2(ang, AI, dt_, ALU.mult, ['AI', 'dt_'], ['ang'])

            def sin_of(dst, dkey, shift):
                dve(lambda e: e.tensor_scalar(out=ta, in0=ang, scalar1=shift, scalar2=1.0 / (2 * PI), op0=ALU.add,
                                              op1=ALU.mult), ['ang'], ['ta'])
                dve(lambda e: e.tensor_copy(out=kI, in_=ta), ['ta'], ['kI'])
                dve(lambda e: e.tensor_copy(out=tb, in_=kI), ['kI'], ['tb'])
                dve(lambda e: e.tensor_scalar(out=ta, in0=ang, scalar1=shift, scalar2=None, op0=ALU.add), ['ang'], ['ta'])
                dve(lambda e: e.scalar_tensor_tensor(out=ta, in0=tb, scalar=-2 * PI, in1=ta, op0=ALU.mult, op1=ALU.add),
                    ['tb', 'ta'], ['ta'])
                dve(lambda e: e.tensor_scalar(out=tb, in0=ta, scalar1=PI, scalar2=None, op0=ALU.is_gt), ['ta'], ['tb'])
                dve(lambda e: e.scalar_tensor_tensor(out=ta, in0=tb, scalar=-2 * PI, in1=ta, op0=ALU.mult, op1=ALU.add),
                    ['tb', 'ta'], ['ta'])
                dve(lambda e: e.tensor_scalar(out=tb, in0=ta, scalar1=-PI, scalar2=None, op0=ALU.is_lt), ['ta'], ['tb'])
                dve(lambda e: e.scalar_tensor_tensor(out=ta, in0=tb, scalar=2 * PI, in1=ta, op0=ALU.mult, op1=ALU.add),
                    ['tb', 'ta'], ['ta'])
                act(lambda e: e.activation(out=dst, in_=ta, func=AF.Sin), ['ta'], [dkey])
            sin_of(sn, 'sn', 0.0)
            sin_of(cs, 'cs', PI / 2)
            D2(abr, mag, cs, ALU.mult, ['mag', 'cs'], ['abr'])
            D2(abi, mag, sn, ALU.mult, ['mag', 'sn'], ['abi'])
            D2(ta, AR, AR, ALU.mult, ['AR'], ['ta'])
            D2(tb, AI, AI, ALU.mult, ['AI'], ['tb'])
            D2(ta, ta, tb, ALU.add, ['ta', 'tb'], ['ta'])
            dve(lambda e: e.reciprocal(out=rden, in_=ta), ['ta'], ['rden'])
            dve(lambda e: e.tensor_scalar(out=nre, in0=abr, scalar1=-1.0, scalar2=None, op0=ALU.add), ['abr'], ['nre'])
            D2(ta, nre, AR, ALU.mult, ['nre', 'AR'], ['ta'])
            D2(tb, abi, AI, ALU.mult, ['abi', 'AI'], ['tb'])
            D2(ta, ta, tb, ALU.add, ['ta', 'tb'], ['ta'])
            D2(cfr, ta, rden, ALU.mult, ['ta', 'rden'], ['cfr'])
            D2(ta, abi, AR, ALU.mult, ['abi', 'AR'], ['ta'])
            D2(tb, nre, AI, ALU.mult, ['nre', 'AI'], ['tb'])
            D2(ta, ta, tb, ALU.subtract, ['ta', 'tb'], ['ta'])
            D2(cfi, ta, rden, ALU.mult, ['ta', 'rden'], ['cfi'])

            def cmul(o_re, o_im, ok, a_re, a_im, ak, b_re, b_im, bk, t1, t2, tk):
                D2(t1, a_re, b_re, ALU.mult, ak + bk, [tk[0]])
                D2(t2, a_im, b_im, ALU.mult, ak + bk, [tk[1]])
                D2(o_re, t1, t2, ALU.subtract, tk, [ok[0]])
                D2(t1, a_re, b_im, ALU.mult, ak + bk + [ok[0]], [tk[0]])
                D2(t2, a_im, b_re, ALU.mult, ak + bk + [ok[0]], [tk[1]])
                D2(o_im, t1, t2, ALU.add, tk, [ok[1]])

            PWr = A.alloc("PWr", [2, 16, 17]); PWi = A.alloc("PWi", [2, 16, 17])
            pt1 = A.alloc("pt1", [2, 16, 4]); pt2 = A.alloc("pt2", [2, 16, 4])
            dve(lambda e: e.memset(PWr[:, :, :, 8:9], 1.0), [], ['PWr'])
            dve(lambda e: e.memset(PWi[:, :, :, 8:9], 0.0), [], ['PWi'])
            dve(lambda e: e.tensor_copy(out=PWr[:, :, :, 9], in_=abr), ['abr'], ['PWr'])
            dve(lambda e: e.tensor_copy(out=PWi[:, :, :, 9], in_=abi), ['abi'], ['PWi'])
            D2(ta, abr, abr, ALU.mult, ['abr'], ['ta'])
            D2(tb, abi, abi, ALU.mult, ['abi'], ['tb'])
            D2(ta, ta, tb, ALU.add, ['ta', 'tb'], ['ta'])
            dve(lambda e: e.reciprocal(out=tb, in_=ta), ['ta'], ['tb'])
            D2(PWr[:, :, :, 7], abr, tb, ALU.mult, ['abr', 'tb'], ['PWr'])
            dve(lambda e: e.scalar_tensor_tensor(out=PWi[:, :, :, 7], in0=abi, scalar=-1.0, in1=tb, op0=ALU.mult,
                                                 op1=ALU.mult), ['abi', 'tb'], ['PWi'])
            PK = ['PWr', 'PWi']

            def pw_step(o0, o1, i0, i1, m):
                w = o1 - o0
                bre = PWr[:, :, :, m:m + 1].to_broadcast([128, 2, 16, w])
                bim = PWi[:, :, :, m:m + 1].to_broadcast([128, 2, 16, w])
                cmul(PWr[:, :, :, o0:o1], PWi[:, :, :, o0:o1], PK, PWr[:, :, :, i0:i1], PWi[:, :, :, i0:i1], PK,
                     bre, bim, PK, pt1[:, :, :, 0:w], pt2[:, :, :, 0:w], ['pt1', 'pt2'])
            pw_step(10, 11, 9, 10, 9)
            pw_step(11, 13, 9, 11, 10)
            pw_step(13, 17, 9, 13, 12)
            pw_step(6, 7, 7, 8, 7)
            pw_step(4, 6, 6, 8, 6)
            pw_step(0, 4, 4, 8, 4)
            BPr = A.alloc("BPr", [2, 16, 16]); BPi = A.alloc("BPi", [2, 16, 16])
            bt1 = A.alloc("bt1", [2, 16, 16]); bt2 = A.alloc("bt2", [2, 16, 16])
            cmul(BPr, BPi, ['BPr', 'BPi'],
                 cfr.unsqueeze(3).to_broadcast([128, 2, 16, 16]), cfi.unsqueeze(3).to_broadcast([128, 2, 16, 16]),
                 ['cfr', 'cfi'],
                 BR.unsqueeze(1).to_broadcast([128, 2, 16, 16]), BI.unsqueeze(1).to_broadcast([128, 2, 16, 16]),
                 ['BR', 'BI'], bt1, bt2, ['bt1', 'bt2'])
            A.free("bt1"); A.free("bt2")
            for d in range(2):
                for c_ in range(2):
                    dve(lambda e: e.tensor_copy(out=C1[:, :, d, c_], in_=PWr[:, d, :, 16]), ['PWr'], ['C1'])
                dve(lambda e: e.tensor_scalar(out=C2[:, :, d, 0], in0=PWi[:, d, :, 16], scalar1=-1.0, scalar2=None,
                                              op0=ALU.mult), ['PWi'], ['C2'])
                dve(lambda e: e.tensor_copy(out=C2[:, :, d, 1], in_=PWi[:, d, :, 16]), ['PWi'], ['C2'])
            dve(lambda e: e.tensor_scalar(out=C1r, in0=C1, scalar1=rcol[:, 0:1], scalar2=None, op0=ALU.mult),
                ['C1', 'rcol'], ['C1r'])
            dve(lambda e: e.tensor_scalar(out=C2r, in0=C2, scalar1=rcol[:, 0:1], scalar2=None, op0=ALU.mult),
                ['C2', 'rcol'], ['C2r'])

            TTacc = A.alloc("TTacc", [4, 128])
            Wn = [A.alloc("Wn%d" % i, [2, 8, 16]) for i in range(6)]
            wt1 = A.alloc("wt1", [4, 8, 16]); wt2 = A.alloc("wt2", [2, 8, 16])
            WK = ['Wn%d' % i for i in range(6)]
            for qd_ in range(8):
                jsl = slice(2 * qd_, 2 * qd_ + 2)
                for d in range(2):
                    if d == 0:
                        p_in = (slice(15, 7, -1)); p_inv = (slice(7, None, -1)); p_out = slice(9, 17)
                    else:
                        p_in = slice(8, 16); p_inv = slice(0, 8); p_out = slice(16, 8, -1)

                    def pwb(sl, last):
                        return (PWr[:, d, jsl, sl].unsqueeze(3).to_broadcast([128, 2, 8, last]),
                                PWi[:, d, jsl, sl].unsqueeze(3).to_broadcast([128, 2, 8, last]))
                    bpr = BPr[:, d, jsl, :].unsqueeze(2).to_broadcast([128, 2, 8, 16])
                    bpi = BPi[:, d, jsl, :].unsqueeze(2).to_broadcast([128, 2, 8, 16])
                    w1 = wt1[:, 0:2]
                    a_re, a_im = pwb(p_in, 16)
                    cmul(Wn[0], Wn[1], WK[0:2], a_re, a_im, PK, bpr, bpi, ['BPr', 'BPi'], w1, wt2, ['wt1', 'wt2'])
                    a_re, a_im = pwb(p_inv, 16)
                    cmul(Wn[2], Wn[3], WK[2:4], a_re, a_im, PK, bpr, bpi, ['BPr', 'BPi'], w1, wt2, ['wt1', 'wt2'])
                    a_re, a_im = pwb(p_out, 16)
                    cr = CR[:, jsl, :].unsqueeze(2).to_broadcast([128, 2, 8, 16])
                    ci = CI[:, jsl, :].unsqueeze(2).to_broadcast([128, 2, 8, 16])
                    cmul(Wn[4], Wn[5], WK[4:6], a_re, a_im, PK, cr, ci, ['CR', 'CI'], w1, wt2, ['wt1', 'wt2'])
                    dve(lambda e: e.tensor_scalar(out=Wn[5], in0=Wn[5], scalar1=-1.0, scalar2=None, op0=ALU.mult),
                        [WK[5]], [WK[5]])
                    for c2 in range(2):
                        act(lambda e: e.activation(out=WoR[:, d, jsl, c2, :],
                                                   in_=Wn[4 + c2].rearrange("p a b c -> p a (b c)"), func=AF.Copy),
                            [WK[4 + c2]], ['WoR'])
                    ps, pk = bank()
                    for jj in range(2):
                        for c2 in range(2):
                            mm(ps[:, (jj * 2 + c2) * 128:(jj * 2 + c2 + 1) * 128],
                               Wn[c2][:, jj, :, :].rearrange("p b c -> p (b c)"), ident_f[:, :], True, True,
                               [WK[c2], 'ident_f'], [pk], inc=(jj == 1 and c2 == 1))
                    j0 = 2 * qd_
                    for jj in range(2):
                        jg = j0 + jj
                        dst = WinT[:, d, 2 * jg:2 * jg + 2, :].rearrange("p g2 (c n) -> p c g2 n", c=2)
                        evac_copy(jj, dst, ps[:, jj * 256:(jj + 1) * 256].rearrange("p (c g2 n) -> p c g2 n", c=2, g2=2),
                                  [pk], ['WinT'])
                    ps, pk = bank()
                    for gi in range(4):
                        jj = gi // 2
                        g2 = gi % 2
                        rows = slice(g2 * 64, (g2 + 1) * 64)
                        outp = ps[:, gi * 128:(gi + 1) * 128]
                        mm_b(outp, Wn[2][rows, jj, :, :].rearrange("p b c -> p (b c)"),
                             Wn[4][rows, jj, :, :].rearrange("p b c -> p (b c)"), gi == 0, False,
                             [WK[2], WK[4]], [pk], inc=False, skip=True, base=g2 * 64)
                        mm_b(outp, Wn[3][rows, jj, :, :].rearrange("p b c -> p (b c)"),
                             Wn[5][rows, jj, :, :].rearrange("p b c -> p (b c)"), False, True,
                             [WK[3], WK[5]], [pk], inc=(gi == 3), skip=True, base=g2 * 64)
                    g0 = 4 * qd_
                    acc = TTacc[:, 0:4, :]
                    ps3 = ps[:, :].rearrange("p (a b) -> p a b", b=128)
                    mk = s5m[:, d:d + 1, :].to_broadcast([128, 4, 128])
                    if d == 0:
                        D2(acc, ps3, mk, ALU.mult, [pk, 's5m'], ['TTacc'])
                    else:
                        w13 = wt1.rearrange("p a b c -> p a (b c)")
                        D2(w13, ps3, mk, ALU.mult, [pk, 's5m'], ['wt1'])
                        D2(acc, acc, w13, ALU.add, ['TTacc', 'wt1'], ['TTacc'])
                        for gi in range(4):
                            g = g0 + gi
                            dve(lambda e: e.scalar_tensor_tensor(
                                out=TT[:, g, :], in0=ident_f[:, :], scalar=dcol[:, g:g + 1],
                                in1=TTacc[:, gi, :], op0=ALU.mult, op1=ALU.add),
                                ['ident_f', 'dcol', 'TTacc'], ['TT'])
            for nm in ["Wn%d" % i for i in range(6)] + ["wt1", "wt2", "TTacc", "PWr", "PWi", "pt1", "pt2", "BPr", "BPi",
                                                         "CR", "CI", "CNr", "CNi", "BR", "BI", "kI", "s5m", "selc",
                                                         "AR", "AI", "LD", "dcol", "dt_", "mag", "ang", "sn", "cs", "abr",
                                                         "abi", "rden", "nre", "cfr", "cfi", "ta", "tb", "anat", "ldf"]:
                A.free(nm)
            tap("s5_WinT%d" % l, WinT, 'WinT')
            tap("s5_WoR%d" % l, WoR, 'WoR')
            tap("s5_TT%d" % l, TT, 'TT')
            if stop == 'S1':
                return

            Up = A.alloc("Up", [32, 8, 16], BF16)
            UGN = A.alloc("UGN", [32, 128], BF16)
            UGR = [A.alloc("UGR%d" % i, [2, 128], BF16) for i in range(2)]
            VX = A.alloc("VX", [16, 2, 2, 129])
            x0n = A.alloc("x0n", [128])
            slot_su, k_su = load_w(w_in[l][:, C_SU:C_SU + 512], 8, 512)
            for s_ in range(8):
                ps, pk = bank()
                for k in range(8):
                    mm(ps[:, :], hT[:, k, s_::8], slot_su[:, k, :], k == 0, k == 7, ['hT', k_su], [pk])
                evac_copy(s_, Up[:, :, s_, :], ps[:, :].rearrange("p (g q) -> p g q", q=16), [pk], ['Up'])
            tap("s5_Up%d" % l, Up, 'Up')
            if stop == 'S2a':
                return
            cx.dma('sp', x0n[0:64, :], s50_d[l].rearrange("d c (j g2) n -> (d c j) (g2 n)", g2=2), writes=['x0n'], sem='s5x0')
            ps, pk = bank()
            mm(ps[:, 0:64], x0n[0:64, :], ident_f[0:64, 0:64], True, True, ['x0n', 'ident_f'], [pk])
            dve(lambda e: e.tensor_copy(out=VX[:, :, :, :, 0], in_=ps[:, 0:64].rearrange("p (d c j) -> p j d c", d=2, c=2)),
                [pk], ['VX'])
            tap("s5_VX0%d" % l, VX, 'VX')
            if stop == 'S2b':
                return
            for j in range(int(os.environ.get("KDBG_NJ", "16"))):
                ub, ukey = bank()
                ug = UGR[j % 2]
                ugk = 'UGR%d' % (j % 2)
                for g2 in range(2):
                    g = 2 * j + g2
                    src = Up[:, g, :, :].rearrange("p s q -> p (s q)")
                    mm(ub[:, g2 * 128:(g2 + 1) * 128], src, ident_bf[:, :], True, True, ['Up', 'ident_bf'], [ukey], inc=False)
                    mm(ub[:, (2 + g2) * 128:(3 + g2) * 128], src, (ident_bf if os.environ.get("KDBG_J") else jmat_bf)[:, :], True, True, ["Up", "jmat_bf"], [ukey],
                       inc=(g2 == 1))
                SKU = os.environ.get("KDBG_SKIPU", "")
                if 'a' not in SKU:
                    act(lambda e: e.activation(out=UGN[:, 2 * j:2 * j + 2, :],
                                               in_=ub[:, 0:256].rearrange("p (a b) -> p a b", b=128), func=AF.Copy),
                        [ukey], ['UGN'])
                if 'd' not in SKU:
                    dve(lambda e: e.tensor_copy(out=ug, in_=ub[:, 256:512].rearrange("p (a b) -> p a b", b=128)),
                        [ukey], [ugk])
                if stop == 'S2c':
                    continue
                vb, vkey = bank()
                for g2 in range(2):
                    g = 2 * j + g2
                    for d in range(2):
                        rhs = UGN[:, g, :] if d == 0 else ug[:, g2, :]
                        for c2 in range(2):
                            mm(vb[g2 * 64:(g2 + 1) * 64, (2 * d + c2) * 128:(2 * d + c2 + 1) * 128],
                               WinT[:, d, g, c2 * 64:(c2 + 1) * 64], rhs, True, True,
                               ['WinT', 'UGN', ugk], [vkey], inc=(g2 == 1 and d == 1 and c2 == 1))
                evac_copy(j, VX[:, j, :, :, 1:129], vb[:, :].rearrange("p (d c i) -> p d c i", d=2, c=2), [vkey], ['VX'])
            A.free("Up"); A.free("WinT"); A.free("x0n")
            tap("s5_V%d" % l, VX, 'VX')
            if stop in ('S2', 'S2c'):
                return
            ts_ = [A.alloc("ts%d" % i, [16, 2]) for i in range(4)]
            for i in range(128):
                bnd = (i % 32 == 0 and i > 0)
                for d in range(2):
                    eng = 'dve' if d == 0 else 'pool'
                    c1 = (C1r if bnd else C1)[:, :, d, :]
                    c2 = (C2r if bnd else C2)[:, :, d, :]
                    xp = VX[:, :, d, :, i]
                    xsw = VX[:, :, d, ::-1, i]
                    cur = VX[:, :, d, :, i + 1]
                    t1, t2 = ts_[2 * d], ts_[2 * d + 1]
                    k1, k2, kv = 'ts%d' % (2 * d), 'ts%d' % (2 * d + 1), 'VXd%d' % d
                    cx.op(eng, lambda e: tt_(e, t1, xp, c1, ALU.mult), ['VX', kv, 'C1', 'C1r'], [k1])
                    cx.op(eng, lambda e: tt_(e, t2, xsw, c2, ALU.mult), ['VX', kv, 'C2', 'C2r'], [k2])
                    cx.op(eng, lambda e: tt_(e, cur, cur, t1, ALU.add), ['VX', kv, k1], [kv])
                    cx.op(eng, lambda e: tt_(e, cur, cur, t2, ALU.add), ['VX', kv, k2], [kv])
            tap("s5_X%d" % l, VX, 'VXd0')
            tap("s5_Xb%d" % l, VX, 'VXd1')
            fst = A.alloc("fst", [4, 128])
            for d in range(2):
                for c2 in range(2):
                    ps, pk = bank()
                    for sgi in range(4):
                        col = 32 * (sgi + 1)
                        mm(ps[0:16, sgi * 128:(sgi + 1) * 128], VX[:, :, d, c2, col], ident_f[:, :], True, True,
                           ['VXd%d' % d, 'ident_f'], [pk], inc=(sgi == 3))
                    dve(lambda e: e.tensor_copy(out=fst[0:16], in_=ps[0:16, :].rearrange("p (a b) -> p a b", b=128)),
                        [pk], ['fst'])
                    for sgi in range(4):
                        seg = sgi if d == 0 else 3 - sgi
                        cx.dma('sp', os5_d[l, seg, d, c2].rearrange("(j g2) n -> j (g2 n)", g2=2), fst[0:16, sgi, :],
                               reads=['fst'], sem='os5')
            XB = A.alloc("XB", [16, 2, 2, 128], BF16)
            act(lambda e: e.activation(out=XB[:, :, 0, :, :], in_=VX[:, :, 0, :, 0:128], func=AF.Copy), ['VXd0', 'VX'], ['XB'])
            dve(lambda e: e.tensor_copy(out=XB[:, :, 1, :, :], in_=VX[:, :, 1, :, 127::-1]), ['VXd1', 'VX'], ['XB'])
            dve(lambda e: e.tensor_scalar(out=XB[:, :, 0, :, 32:128:32], in0=XB[:, :, 0, :, 32:128:32], scalar1=rcol[:, 0:1],
                                          scalar2=None, op0=ALU.mult), ['XB', 'rcol'], ['XB'])
            dve(lambda e: e.tensor_scalar(out=XB[:, :, 1, :, 31:128:32], in0=XB[:, :, 1, :, 31:128:32], scalar1=rcol[:, 0:1],
                                          scalar2=None, op0=ALU.mult), ['XB', 'rcol'], ['XB'])
            A.free("VX"); A.free("fst")
            for i in range(4):
                A.free("ts%d" % i)
            Yp = A.alloc("Yp", [8, 512])
            for qd_ in range(8):
                yb, ykey = bank()
                for gi in range(4):
                    g = 4 * qd_ + gi
                    j, g2 = g // 2, g % 2
                    rows = slice(g2 * 64, (g2 + 1) * 64)
                    outp = yb[:, gi * 128:(gi + 1) * 128]
                    mm(outp, UGN[:, g, :], TT[:, g, :], gi == 0, False, ['UGN', 'TT'], [ykey], inc=False, skip=True)
                    for d in range(2):
                        for c2 in range(2):
                            last = (d == 1 and c2 == 1)
                            mm_b(outp, XB[rows, j, d, c2, :], WoR[rows, d, j, c2, :], False, last, ['XB', 'WoR'], [ykey],
                                 inc=(last and gi == 3), skip=True, base=g2 * 64)
                evac_copy(qd_, Yp[:, :, 64 * qd_:64 * qd_ + 64].rearrange("p s (g c) -> p s g c", c=16),
                          yb[:, :].rearrange("p (g s c) -> p s g c", g=4, s=8), [ykey], ['Yp'])
            tap("s5_Y%d" % l, Yp, 'Yp')
            A.free("XB"); A.free("UGN"); A.free("WoR"); A.free("TT")
            for i in range(2):
                A.free("UGR%d" % i)
            if stop == 'S3':
                return
            Gp = A.alloc("Gp", [8, 512], BF16)
            gt1 = A.alloc("gt1", [2, 512]); gt2 = A.alloc("gt2", [2, 512])
            for q4 in range(4):
                ysl = Yp[:, 2 * q4:2 * q4 + 2, :]
                act(lambda e: e.activation(out=gt1, in_=ysl, func=AF.Square), ['Yp'], ['gt1'])
                dve(lambda e: e.tensor_scalar(out=gt1, in0=gt1, scalar1=0.044715, scalar2=1.0, op0=ALU.mult, op1=ALU.add),
                    ['gt1'], ['gt1'])
                D2(gt1, gt1, ysl, ALU.mult, ['gt1', 'Yp'], ['gt1'])
                act(lambda e: e.activation(out=gt2, in_=gt1, func=AF.Sigmoid, scale=2.0 * math.sqrt(2.0 / PI)),
                    ['gt1'], ['gt2'])
                D2(Gp[:, 2 * q4:2 * q4 + 2, :], ysl, gt2, ALU.mult, ['Yp', 'gt2'], ['Gp'])
            A.free("Yp"); A.free("gt1"); A.free("gt2")
            gT = A.alloc("gT", [4, T], BF16)
            for s_ in range(8):
                ps, pk = bank()
                for ct in range(4):
                    mm(ps[:, ct * 128:(ct + 1) * 128], Gp[:, s_, ct * 128:(ct + 1) * 128], ident_bf[:, :], True, True,
                       ['Gp', 'ident_bf'], [pk], inc=(ct == 3))
                evac_copy(s_, gT[:, :, s_::8], ps[:, :].rearrange("p (a b) -> p a b", b=128), [pk], ['gT'])
            A.free("Gp")
            sgT = A.alloc("sgT", [4, T], BF16)
            slot_sg, k_sg = load_w(w_in[l][:, C_SG:C_SG + 512], 8, 512)
            for m in range(4):
                proj_fm(slot_sg, k_sg, m * 128, 128,
                        lambda ps, pk, th, m=m: act(lambda e: e.activation(
                            out=sgT[:, m, th * 512:(th + 1) * 512], in_=ps[:, :], func=AF.Silu), [pk], ['sgT']))
            bglu = A.alloc("bglu", [8])
            with nc.allow_non_contiguous_dma(reason="tiny bias columns"):
                cx.dma('sp', bglu, s5_b_glu[l].rearrange("(k p) -> p k", p=128), writes=['bglu'], sem='s5c')
            slot_a, k_a = load_w(s5_w_glu[l][:, 0:512], 4, 512)
            slot_b, k_b = load_w(s5_w_glu[l][:, 512:1024], 4, 512)
            sg_ = [A.alloc("sgm%d" % i, [512]) for i in range(2)]
            it = 0
            for m in range(4):
                for th in range(2):
                    tsl = slice(th * 512, (th + 1) * 512)
                    pa, pka = bank()
                    for k in range(4):
                        mm(pa[:, :], slot_a[:, k, m * 128:(m + 1) * 128], gT[:, k, tsl], k == 0, k == 3, [k_a, 'gT'], [pka])
                    pb, pkb = bank()
                    for k in range(4):
                        mm(pb[:, :], slot_b[:, k, m * 128:(m + 1) * 128], gT[:, k, tsl], k == 0, k == 3, [k_b, 'gT'], [pkb])
                    sg = sg_[it % 2]
                    sgk = 'sgm%d' % (it % 2)
                    it += 1
                    act(lambda e: e.activation(out=sg, in_=pb[:, :], func=AF.Sigmoid, bias=bglu[:, 4 + m:5 + m]),
                        [pkb, 'bglu'], [sgk])
                    dve(lambda e: e.scalar_tensor_tensor(out=sg, in0=pa[:, :], scalar=bglu[:, m:m + 1], in1=sg,
                                                         op0=ALU.add, op1=ALU.mult), [pka, 'bglu', sgk], [sgk])
                    D2(oT_br[2][:, m, tsl], sg, sgT[:, m, tsl], ALU.mult, [sgk, 'sgT'], ['obr2'])
            tap("s5_out%d" % l, oT_br[2][:, :, :], 'obr2')

    def merge_out(l):
        with ExitStack() as st:
            mixedT = sb("mixedT", [128, 8, T], BF16, stack=st)
            wbo = sb("wbo", [128, 3, 4, D], BF16, stack=st)
            gx = [sb("gx%d" % i, [128, 512], stack=st) for i in range(3)]
            t1 = sb("mt1", [128, 512], stack=st)
            t2 = sb("mt2", [128, 512], stack=st)
            for x in range(3):
                cx.dma('pool', wbo[:, x, :, :], w_bo[x][l].rearrange("(k p) c -> p k c", p=128), writes=['wbo'],
                       sem='wbo')
            for fg in range(2):
                slots = [load_w(w_in[l][:, C_MERGE + x * D + fg * 512:C_MERGE + x * D + fg * 512 + 512], 8, 512)
                         for x in range(3)]
                for f4 in range(4):
                    f = fg * 4 + f4
                    for th in range(2):
                        tsl = slice(th * 512, (th + 1) * 512)
                        for x in range(3):
                            slot, skey = slots[x]
                            pg, pgk = bank()
                            for k in range(8):
                                mm(pg[:, :], slot[:, k, f4 * 128:(f4 + 1) * 128], hT[:, k, tsl], k == 0, k == 7,
                                   [skey, 'hT'], [pgk])
                            act(lambda e: e.activation(out=gx[x][:, :], in_=pg[:, :], func=AF.Sigmoid), [pgk], ['gx%d' % x])
                            pp, ppk = bank()
                            for k in range(4):
                                mm(pp[:, :], wbo[:, x, k, f * 128:(f + 1) * 128], oT_br[x][:, k, tsl], k == 0, k == 3,
                                   ['wbo', 'obr%d' % x], [ppk])
                            if x == 0:
                                dve(lambda e: e.tensor_tensor(out=t1[:, :], in0=pp[:, :], in1=gx[x][:, :], op=ALU.mult),
                                    [ppk, 'gx0'], ['mt1'])
                            elif x == 1:
                                dve(lambda e: e.tensor_tensor(out=t2[:, :], in0=pp[:, :], in1=gx[x][:, :], op=ALU.mult),
                                    [ppk, 'gx1'], ['mt2'])
                                dve(lambda e: e.tensor_tensor(out=t1[:, :], in0=t1[:, :], in1=t2[:, :], op=ALU.add),
                                    ['mt1', 'mt2'], ['mt1'])
                            else:
                                dve(lambda e: e.tensor_tensor(out=t2[:, :], in0=pp[:, :], in1=gx[x][:, :], op=ALU.mult),
                                    [ppk, 'gx2'], ['mt2'])
                                dve(lambda e: e.tensor_tensor(out=mixedT[:, f, tsl], in0=t1[:, :], in1=t2[:, :], op=ALU.add),
                                    ['mt1', 'mt2'], ['mixedT'])
            tap("mixedT%d" % l, mixedT[:, :, :], 'mixedT')
            for nh in range(2):
                nsl = slice(nh * 512, (nh + 1) * 512)
                slot, skey = load_w(w_out[l][:, nsl], 8, 512)
                for tt in range(8):
                    ps, pk = bank()
                    for k in range(8):
                        mm(ps[:, :], mixedT[:, k, tt * 128:(tt + 1) * 128], slot[:, k, :], k == 0, k == 7,
                           ['mixedT', skey], [pk])
                    tm = t1 if tt % 2 == 0 else t2
                    tk = 'mt1' if tt % 2 == 0 else 'mt2'
                    dve(lambda e: e.tensor_tensor(out=tm[:, :], in0=ps[:, :], in1=gate_bc[:, nsl], op=ALU.mult),
                        [pk, 'gate_bc'], [tk])
                    dve(lambda e: e.tensor_tensor(out=x_sb[:, tt, nsl], in0=x_sb[:, tt, nsl], in1=tm[:, :], op=ALU.add),
                        ['x', tk], ['x'])
            tap("xout%d" % l, x_sb[:, :, :], 'x')

    for l in range(DEPTH):
        with ExitStack() as st:
            shift_bc = sb("shift_bc", [128, D], stack=st)
            wmod = sb("wmod", [128, D], stack=st)
            bada = sb("bada", [128, 3 * D], stack=st)
            nw_bc = sb("nw_bc", [128, D], stack=st)
            cond_c = sb("cond_c", [128, 8], stack=st)
            scb = sb("scb", [128, 8, 128], BF16, stack=st)
            with nc.allow_non_contiguous_dma(reason="tiny cond column load"):
                cx.dma('sp', cond_c[:, :], cond_d.rearrange("(k p) -> p k", p=128), writes=['cond_c'], sem='c2')
            cx.dma('sp', bada[:, :], b_ada[l].partition_broadcast(128), writes=['bada'], sem='c2')
            cx.dma('sp', nw_bc[:, :], norm_w[l].partition_broadcast(128), writes=['nw_bc'], sem='c2')
            act(lambda e: e.activation(out=cond_c[:, :], in_=cond_c[:, :], func=AF.Silu), ['cond_c'], ['cond_c'])
            dve(lambda e: e.tensor_copy(out=scb[:, :, :], in_=cond_c[:, :].unsqueeze(2).to_broadcast([128, 8, 128])),
                ['cond_c'], ['scb'])
            for ci in range(6):
                slot, skey = load_w(w_ada[l][:, ci * 512:(ci + 1) * 512], 8, 512)
                ps, pk = bank()
                for k in range(8):
                    mm(ps[:, :], scb[:, k, :], slot[:, k, :], k == 0, k == 7, ['scb', skey], [pk])
                dst = (shift_bc, wmod, gate_bc)[ci // 2]
                dkey = ('shift_bc', 'wmod', 'gate_bc')[ci // 2]
                cs = slice((ci % 2) * 512, (ci % 2 + 1) * 512)
                bsl = bada[:, ci * 512:(ci + 1) * 512]
                if ci // 2 == 1:
                    dve(lambda e, ps=ps, dst=dst, cs=cs, bsl=bsl: e.scalar_tensor_tensor(
                        out=dst[:, cs], in0=ps[:, :], scalar=1.0, in1=bsl, op0=ALU.add, op1=ALU.add),
                        [pk, 'bada'], [dkey])
                else:
                    dve(lambda e, ps=ps, dst=dst, cs=cs, bsl=bsl: e.tensor_tensor(
                        out=dst[:, cs], in0=ps[:, :], in1=bsl, op=ALU.add), [pk, 'bada'], [dkey])
            dve(lambda e: e.tensor_tensor(out=wmod[:, :], in0=wmod[:, :], in1=nw_bc[:, :], op=ALU.mult),
                ['wmod', 'nw_bc'], ['wmod'])

            ss = sb("ss_x", [128, 8], stack=st)
            rs = sb("rs_x", [128, 8], stack=st)
            tmp8 = sb("tmp8", [128, 8], stack=st)
            junk = sb("junk_x", [128, D], stack=st)
            hb = [sb("hb%d" % i, [128, D], BF16, stack=st) for i in range(2)]
            tmpf = sb("tmpf", [128, D], stack=st)
            dve(lambda e: e.memset(ss[:, :], 0.0), [], ['ss_x'])
            for tt in range(8):
                act(lambda e, tt=tt: e.activation(out=junk[:, :], in_=x_sb[:, tt, :], func=AF.Square,
                                                  accum_out=ss[:, tt:tt + 1]), ['x', 'ss_x'], ['junk_x', 'ss_x'])
            rstd_from_ss(ss[:, :], rs[:, :], D, 'ss_x', 'rs_x', tmp8[:, :], 'tmp8')
            for tt in range(8):
                hbt = hb[tt % 2]
                hk = 'hb%d' % (tt % 2)
                dve(lambda e, tt=tt: e.scalar_tensor_tensor(out=tmpf[:, :], in0=x_sb[:, tt, :], scalar=rs[:, tt:tt + 1],
                                                            in1=wmod[:, :], op0=ALU.mult, op1=ALU.mult),
                    ['x', 'rs_x', 'wmod'], ['tmpf'])
                dve(lambda e, hbt=hbt: e.tensor_tensor(out=hbt[:, :], in0=tmpf[:, :], in1=shift_bc[:, :], op=ALU.add),
                    ['tmpf', 'shift_bc'], [hk])
                for half in range(2):
                    ps, pk = bank()
                    for kk in range(4):
                        k = half * 4 + kk
                        mm(ps[:, kk * 128:(kk + 1) * 128], hbt[:, k * 128:(k + 1) * 128], ident_bf[:, :], True, True,
                           [hk, 'ident_bf'], [pk], inc=(kk == 3))
                    act(lambda e, ps=ps, half=half, tt=tt: e.activation(
                        out=hT[:, half * 4:half * 4 + 4, tt * 128:(tt + 1) * 128],
                        in_=ps[:, :].rearrange("p (a b) -> p a b", b=128), func=AF.Copy), [pk], ['hT'])

            tap("hT%d" % l, hT[:, :, :], 'hT')
            tap("gate%d" % l, gate_bc[:, :], 'gate_bc')
        if stop == 'B':
            break
        branch_gla(l)
        if stop in ('GLA', 'G1', 'G2', 'G3'):
            break
        branch_mla(l)
        if stop in ('MLA', 'M1'):
            break
        branch_s5(l)
        if stop in ('S5', 'S1', 'S2', 'S3', 'S2a', 'S2b', 'S2c'):
            break
        merge_out(l)
        if stop == 'L0':
            break

    cx.dma('sp', y_d.rearrange("(t p) d -> p t d", p=128), x_sb[:, :, :], reads=['x'], sem='yout')
    cx.final_wait()
    return cx


def _rope_tables():
    rows = T // 64
    r = np.repeat(np.arange(rows, dtype=np.float32), 64)
    col = np.tile(np.arange(64, dtype=np.float32), rows)
    n_freq = 8
    inv = (np.float32(10000.0) ** (-np.arange(n_freq, dtype=np.float32) / np.float32(n_freq))).astype(np.float32)
    ang = np.concatenate([r[:, None] * inv, col[:, None] * inv], axis=-1).astype(np.float32)
    return np.cos(ang).astype(np.float32), np.sin(ang).astype(np.float32)


def _constants():
    c = {}
    c["ident"] = np.eye(128, dtype=np.float32)
    c["jmat"] = np.eye(128, dtype=np.float32)[::-1].copy()
    s_idx = np.arange(128)[:, None]
    t_idx = np.arange(128)[None, :]
    same = (s_idx // 64) == (t_idx // 64)
    c["gla_masks"] = np.stack([(same & (s_idx <= t_idx)), (same & (s_idx >= t_idx))]).astype(np.float32)
    cm = np.ones((128, T), np.float32)
    cm[:, ::64] = 0.0
    c["chunk_mask"] = cm
    sp_ = (np.arange(128) // 16)[:, None]
    s_ = (np.arange(128) // 16)[None, :]
    c["s5_masks"] = np.stack([(s_ >= sp_), (sp_ >= s_)]).astype(np.float32)
    sel = np.zeros((2, 128, 64), np.float32)
    for g2 in range(2):
        for jl in range(4):
            for pch in range(16):
                sel[g2, (2 * jl + g2) * 16 + pch, jl * 16 + pch] = 1.0
    c["sel_c"] = sel
    return c


_W_NAMES = ["norm_w", "w_ada", "b_ada", "w_in", "gla_w_a2", "gla_b_a", "gla_o_norm", "mla_q_norm", "mla_w_uq",
            "mla_kv_norm", "mla_w_uk", "mla_w_uv", "mla_qh_norm", "mla_kh_norm", "s5_a_re", "s5_a_im", "s5_log_dt",
            "s5_b_re", "s5_b_im", "s5_c_re", "s5_c_im", "s5_d", "s5_w_glu", "s5_b_glu", "w_bo_gla", "w_bo_mla",
            "w_bo_s5", "w_out"]


def make_in_maps(inp):
    f = lambda a: np.ascontiguousarray(np.asarray(a, dtype=np.float32))
    consts = _constants()
    cos, sin = _rope_tables()
    weights = {k: f(inp[k]) for k in _W_NAMES}
    maps = []
    for core in range(8):
        m = dict(weights)
        m.update(consts)
        qm = np.zeros((5, T), np.float32)
        km = np.zeros((5, NKEY), np.float32)
        qm[4, :] = 1.0
        km[4, :] = -MASK_BIG
        if core < 4:
            b = core
            m["x"] = f(inp["x_sample"][b])
            m["cond"] = f(inp["c"][b])
            m["ckv_c"] = f(inp["cache_mla_ckv"][b])
            m["kr_c"] = f(inp["cache_mla_krope"][b])
            m["sg0"] = f(inp["state_gla"][b])
            m["s50"] = f(inp["state_s5"][b])
            m["rope_cos"], m["rope_sin"] = cos, sin
            qm[0, :] = 1.0
            km[0, :] = MASK_BIG
            m["rcol"] = np.ones((128, 1), np.float32)
        else:
            j = core - 4
            m["x"] = f(np.asarray(inp["x_prompt"])[4 * j:4 * j + 4].reshape(T, D))
            m["cond"] = f(inp["c_ctx"])
            m["ckv_c"] = np.zeros((DEPTH, PAST, 256), np.float32)
            m["kr_c"] = np.zeros((DEPTH, PAST, 32), np.float32)
            m["sg0"] = np.zeros((DEPTH, 2, 4, 64, 128), np.float32)
            m["s50"] = np.zeros((DEPTH, 2, 2, 32, 64), np.float32)
            m["rope_cos"] = np.ones((T, 16), np.float32)
            m["rope_sin"] = np.zeros((T, 16), np.float32)
            for s in range(4):
                qm[s, s * 256:(s + 1) * 256] = 1.0
                km[s, PAST + s * 256:PAST + (s + 1) * 256] = MASK_BIG
            m["rcol"] = np.zeros((128, 1), np.float32)
        m["qmask"], m["kmask"] = qm, km
        maps.append(m)
    return maps


def kernel(**inputs):
    nc = build_program()
    maps = make_in_maps(inputs)
    res = run_bass_kernel_spmd(nc, maps, core_ids=list(range(8)))
    r = res.results
    y_sample = np.stack([r[b]["y"] for b in range(4)]).astype(np.float32)
    y_prompt = np.concatenate([r[4 + j]["y"].reshape(4, 256, D) for j in range(4)]).astype(np.float32)
    ckv = np.concatenate([r[4 + j]["o_ckv"].reshape(DEPTH, 4, 256, 256).transpose(1, 0, 2, 3) for j in range(4)])
    kr = np.concatenate([r[4 + j]["o_kr"].reshape(DEPTH, 4, 256, 32).transpose(1, 0, 2, 3) for j in range(4)])
    gla = np.concatenate([r[4 + j]["o_gla"].transpose(1, 0, 2, 3, 4, 5) for j in range(4)])
    s5 = np.concatenate([r[4 + j]["o_s5"].transpose(1, 0, 2, 3, 4, 5) for j in range(4)])
    return (y_prompt, y_sample, ckv.astype(np.float32), kr.astype(np.float32), gla.astype(np.float32),
            s5.astype(np.float32))
```

```python
import math
import os
from contextlib import ExitStack

import numpy as np
import concourse.bass as bass
import concourse.mybir as mybir
from concourse.bass_utils import run_bass_kernel_spmd

F32 = mybir.dt.float32
BF16 = mybir.dt.bfloat16
I32 = mybir.dt.int32
ALU = mybir.AluOpType
AF = mybir.ActivationFunctionType
AX = mybir.AxisListType

D = 1024
T = 1024
DEPTH = 2
EPS = 1e-6
PAST = 512
NKEY = PAST + T
D_IN = 6848
C_GQ, C_GK, C_GV, C_GA, C_GG, C_MQ, C_MKV, C_MKR, C_MG, C_SU, C_SG, C_MERGE = (
    0, 256, 512, 1024, 1056, 1568, 1952, 2208, 2240, 2752, 3264, 3776)
MASK_BIG = 2048.0
ATT_SCALE = 96 ** -0.5
PI = math.pi


class Ctx:
    def __init__(self, nc, es):
        self.nc = nc
        self.es = es
        self.eng = {'pe': nc.tensor, 'act': nc.scalar, 'dve': nc.vector, 'pool': nc.gpsimd, 'sp': nc.sync}
        self.sems = {}
        self.cnt = {}
        for e in ('pe', 'act', 'dve', 'pool'):
            self.sems[e] = es.enter_context(nc.semaphore("s_" + e))
            self.cnt[e] = 0
        self.seen = {e: {} for e in self.eng}
        self.lastw = {}
        self.readers = {}
        self.n_ops = 0
        self.bank_rr = 0
        self.fresh = {}

    def _collect(self, reads, writes):
        toks = {}

        def add(t):
            if t is None:
                return
            s, v = t
            if toks.get(s, 0) < v:
                toks[s] = v
        for k in list(reads) + list(writes):
            snap = self.fresh.pop(k, None)
            if snap is not None:
                for s_, v_ in snap.items():
                    if v_ > 0:
                        add((s_, v_))
                self.lastw.pop(k, None)
                self.readers.pop(k, None)
        for k in reads:
            add(self.lastw.get(k))
            if isinstance(k, tuple) and k[0] == 'ps':
                for s, v in self.readers.get(k, {}).items():
                    add((s, v))
        for k in writes:
            add(self.lastw.get(k))
            for s, v in self.readers.get(k, {}).items():
                add((s, v))
        return toks

    def _emit_waits(self, e, toks, skip_own=False, attach=False):
        eng = self.eng[e]
        seen = self.seen[e]
        need = []
        for s, v in toks.items():
            if skip_own and s == e:
                continue
            if s not in self.eng:
                v = max(v, self.cnt[s])
            if seen.get(s, 0) >= v:
                continue
            need.append((s, v))
            seen[s] = v
        last = None
        if attach and need:
            last = need.pop()
        for s, v in need:
            eng.wait_ge(self.sems[s], v)
        return last

    def _record(self, tok, reads, writes):
        s, v = tok
        for k in writes:
            self.lastw[k] = tok
            self.readers[k] = {}
        for k in reads:
            r = self.readers.setdefault(k, {})
            if r.get(s, 0) < v:
                r[s] = v

    def op(self, e, fn, reads=(), writes=(), inc=True):
        toks = self._collect(reads, writes)
        last = self._emit_waits(e, toks, skip_own=(e == 'pe'), attach=True)
        ins = fn(self.eng[e])
        if last is not None:
            ins._wait_ge(self.sems[last[0]], last[1])
        tok = (e, self.cnt[e] + 1)
        if inc:
            self.cnt[e] += 1
            ins.then_inc(self.sems[e], 1)
        self._record(tok, reads, writes)
        self.n_ops += 1
        return ins

    def dma(self, q, out, in_, reads=(), writes=(), sem=None):
        assert sem is not None
        if sem not in self.sems:
            self.sems[sem] = self.es.enter_context(self.nc.semaphore("d_%d" % len(self.sems)))
            self.cnt[sem] = 0
        toks = self._collect(reads, writes)
        self._emit_waits(q, toks)
        ins = self.eng[q].dma_start(out=out, in_=in_)
        self.cnt[sem] += 16
        ins.then_inc(self.sems[sem], 16)
        self._record((sem, self.cnt[sem]), reads, writes)
        self.n_ops += 1

    def final_wait(self):
        toks = {s: v for s, v in self.cnt.items() if v > 0}
        self._emit_waits('sp', toks)


def build_program(dbg=None, stop=None):
    nc = bass.Bass("TRN2", target_bir_lowering=False)
    es = ExitStack()
    with es:
        cx = _build(nc, es, dbg or {}, stop)
    nc._n_ops = cx.n_ops
    return nc


def _build(nc, es, dbg, stop):
    cx = Ctx(nc, es)

    def tap(name, ap, key):
        if name not in dbg:
            return
        d = nc.dram_tensor("dbg_" + name, list(ap.shape), ap.dtype, kind="ExternalOutput").ap()
        cx.dma('sp', d, ap, reads=[key], sem='dbg')

    def din(name, shape, dt=F32):
        return nc.dram_tensor(name, list(shape), dt, kind="ExternalInput").ap()

    def dout(name, shape, dt=F32):
        return nc.dram_tensor(name, list(shape), dt, kind="ExternalOutput").ap()

    x_d = din("x", [T, D])
    cond_d = din("cond", [D])
    ckvc_d = din("ckv_c", [DEPTH, PAST, 256])
    krc_d = din("kr_c", [DEPTH, PAST, 32])
    sg0_d = din("sg0", [DEPTH, 2, 4, 64, 128])
    s50_d = din("s50", [DEPTH, 2, 2, 32, 64])
    cos_d = din("rope_cos", [T, 16])
    sin_d = din("rope_sin", [T, 16])
    qmask_d = din("qmask", [5, T])
    kmask_d = din("kmask", [5, NKEY])
    rcol_d = din("rcol", [128, 1])
    ident_d = din("ident", [128, 128])
    jmat_d = din("jmat", [128, 128])
    glam_d = din("gla_masks", [2, 128, 128])
    cmask_d = din("chunk_mask", [128, T])
    s5m_d = din("s5_masks", [2, 128, 128])
    selc_d = din("sel_c", [2, 128, 64])
    norm_w = din("norm_w", [DEPTH, D])
    w_ada = din("w_ada", [DEPTH, D, 3 * D])
    b_ada = din("b_ada", [DEPTH, 3 * D])
    w_in = din("w_in", [DEPTH, D, D_IN])
    gla_w_a2 = din("gla_w_a2", [DEPTH, 2, 16, 256])
    gla_b_a = din("gla_b_a", [DEPTH, 2, 256])
    gla_o_norm = din("gla_o_norm", [DEPTH, 128])
    mla_q_norm = din("mla_q_norm", [DEPTH, 384])
    mla_w_uq = din("mla_w_uq", [DEPTH, 384, 384])
    mla_kv_norm = din("mla_kv_norm", [DEPTH, 256])
    mla_w_uk = din("mla_w_uk", [DEPTH, 256, 256])
    mla_w_uv = din("mla_w_uv", [DEPTH, 256, 512])
    mla_qh_norm = din("mla_qh_norm", [DEPTH, 96])
    mla_kh_norm = din("mla_kh_norm", [DEPTH, 96])
    s5_a_re = din("s5_a_re", [DEPTH, 2, 32, 64])
    s5_a_im = din("s5_a_im", [DEPTH, 2, 32, 64])
    s5_log_dt = din("s5_log_dt", [DEPTH, 2, 32])
    s5_b_re = din("s5_b_re", [DEPTH, 32, 64, 16])
    s5_b_im = din("s5_b_im", [DEPTH, 32, 64, 16])
    s5_c_re = din("s5_c_re", [DEPTH, 32, 16, 64])
    s5_c_im = din("s5_c_im", [DEPTH, 32, 16, 64])
    s5_d = din("s5_d", [DEPTH, 512])
    s5_w_glu = din("s5_w_glu", [DEPTH, 512, 1024])
    s5_b_glu = din("s5_b_glu", [DEPTH, 1024])
    w_bo = [din("w_bo_gla", [DEPTH, 512, D]), din("w_bo_mla", [DEPTH, 512, D]), din("w_bo_s5", [DEPTH, 512, D])]
    w_out = din("w_out", [DEPTH, D, D])

    y_d = dout("y", [T, D])
    ockv_d = dout("o_ckv", [DEPTH, T, 256])
    okr_d = dout("o_kr", [DEPTH, T, 32])
    ogla_d = dout("o_gla", [DEPTH, 4, 2, 4, 64, 128])
    os5_d = dout("o_s5", [DEPTH, 4, 2, 2, 32, 64])

    uniq = {'n': 0}

    def sb(name, shape, dt=F32, stack=None):
        uniq['n'] += 1
        if stack is not None:
            cx.fresh[name] = dict(cx.cnt)
        return (stack or es).enter_context(nc.sbuf_tensor("sb%d_%s" % (uniq['n'], name), list(shape), dt))

    psb = [es.enter_context(nc.psum_tensor("psb%d" % i, [128, 512], F32)) for i in range(8)]

    reserved = set()

    def bank():
        while cx.bank_rr in reserved:
            cx.bank_rr = (cx.bank_rr + 1) % 8
        i = cx.bank_rr
        cx.bank_rr = (i + 1) % 8
        return psb[i], ('ps', i)

    def reserve_bank():
        ps, pk = bank()
        reserved.add(pk[1])
        return ps, pk

    def release_bank(pk):
        reserved.discard(pk[1])

    x_sb = sb("x_sb", [128, 8, D])
    hT = sb("hT", [128, 8, T], BF16)
    gate_bc = sb("gate_bc", [128, D])
    ident_bf = sb("ident_bf", [128, 128], BF16)
    jmat_bf = sb("jmat_bf", [128, 128], BF16)
    ident_f = sb("ident_f", [128, 128])
    ones_bf = sb("ones_bf", [128, 128], BF16)
    glam = sb("glam", [128, 2, 128])
    cmask = sb("cmask", [128, T])
    ropec = sb("ropec", [128, 8, 16])
    ropes = sb("ropes", [128, 8, 16])
    rcol = sb("rcol", [128, 1])
    NSLOT = 3
    wring = [sb("wring%d" % i, [128, 8, 512], BF16) for i in range(NSLOT)]
    oT_br = [sb("obr%d" % i, [128, 4, T], BF16) for i in range(3)]

    ring_state = {'i': 0}

    def load_w(src_ap, kt, ncol):
        i = ring_state['i']
        ring_state['i'] = (i + 1) % NSLOT
        slot = wring[i]
        key = ('wring', i)
        cx.dma('pool', slot[:, 0:kt, 0:ncol], src_ap.rearrange("(k p) c -> p k c", p=128),
               writes=[key], sem='wring%d' % i)
        return slot, key

    cx.dma('sp', x_sb[:, :, :], x_d.rearrange("(t p) d -> p t d", p=128), writes=['x'], sem='x')
    cx.dma('pool', ident_bf[:, :], ident_d[:, :], writes=['ident_bf'], sem='c0')
    cx.dma('pool', jmat_bf[:, :], jmat_d[:, :], writes=['jmat_bf'], sem='c0')
    cx.dma('sp', ident_f[:, :], ident_d[:, :], writes=['ident_f'], sem='c1')
    cx.dma('sp', glam[:, :, :], glam_d.rearrange("a p c -> p a c"), writes=['glam'], sem='c1')
    cx.dma('sp', cmask[:, :], cmask_d[:, :], writes=['cmask'], sem='c1')
    cx.dma('sp', ropec[:, :, :], cos_d.rearrange("(t p) c -> p t c", p=128), writes=['ropec'], sem='c1')
    cx.dma('sp', ropes[:, :, :], sin_d.rearrange("(t p) c -> p t c", p=128), writes=['ropes'], sem='c1')
    cx.dma('sp', rcol[:, :], rcol_d[:, :], writes=['rcol'], sem='c1')
    cx.op('dve', lambda e: e.memset(ones_bf[:, :], 1.0), writes=['ones_bf'])

    def act(fn, reads, writes):
        return cx.op('act', fn, reads, writes)

    def dve(fn, reads, writes):
        return cx.op('dve', fn, reads, writes)

    def pool(fn, reads, writes):
        return cx.op('pool', fn, reads, writes)

    def mm(out, lhsT, rhs, start, stop, reads, writes, inc=None, skip=False):
        if inc is None:
            inc = stop
        if skip:
            return cx.op('pe', lambda e: e.matmul(out, lhsT, rhs, start=start, stop=stop, skip_group_check=True),
                         reads, writes, inc=inc)
        return cx.op('pe', lambda e: e.matmul(out, lhsT, rhs, start=start, stop=stop), reads, writes, inc=inc)

    def mm_b(out, lhsT, rhs, start, stop, reads, writes, inc=None, skip=False, base=0):
        if inc is None:
            inc = stop
        if base == 0:
            return mm(out, lhsT, rhs, start, stop, reads, writes, inc=inc, skip=skip)
        mm(out[0:64], lhsT[:, 0:64], rhs, start, stop, reads, writes, inc=False, skip=skip)
        return mm(out[64:128], lhsT[:, 64:128], rhs, start, stop, reads, writes, inc=inc, skip=skip)

    def rstd_from_ss(ss_ap, out_ap, n, key_in, key_out, tmp_ap, key_tmp):
        act(lambda e: e.activation(out=tmp_ap, in_=ss_ap, func=AF.Sqrt, bias=EPS, scale=1.0 / n),
            [key_in], [key_tmp])
        dve(lambda e: e.reciprocal(out=out_ap, in_=tmp_ap), [key_tmp], [key_out])

    def proj_fm(slot, skey, c0, m, evac, kt=8, rhsT=None, rkey='hT'):
        src = hT if rhsT is None else rhsT
        for th in range(2):
            ps, pk = bank()
            for k in range(kt):
                mm(ps[0:m, :], slot[:, k, c0:c0 + m], src[:, k, th * 512:(th + 1) * 512],
                   k == 0, k == kt - 1, [skey, rkey], [pk])
            evac(ps, pk, th)

    def evac_copy(i, out_ap, in_ap, rkeys, wkeys):
        if i % 2 == 0:
            act(lambda e: e.activation(out=out_ap, in_=in_ap, func=AF.Copy), rkeys, wkeys)
        else:
            dve(lambda e: e.tensor_copy(out=out_ap, in_=in_ap), rkeys, wkeys)

    def branch_gla(l):
        with ExitStack() as st:
            v_tm = sb("v_tm", [128, 8, 512], BF16, stack=st)
            ggT = sb("ggT", [128, 4, T], BF16, stack=st)
            alow = sb("alow", [32, T], BF16, stack=st)
            oT = sb("oT", [128, 4, T], F32, stack=st)
            wa2p = sb("wa2p", [32, 2, 256], BF16, stack=st)
            ba = sb("ba", [128, 2, 2], F32, stack=st)
            onw = sb("onw", [128, 1], stack=st)
            wsm = sb("wsm", [128, 8, 32], BF16, stack=st)
            dve(lambda e: e.memset(wa2p[:, :, :], 0.0), [], ['wa2p'])
            dve(lambda e: e.memset(oT[:, :, :], 0.0), [], ['oT'])

            SK = os.environ.get("KDBG_SKIP", "")
            if 'a' not in SK:
                cx.dma('pool', wa2p[0:16, 0, :], gla_w_a2[l, 0], writes=['wa2p'], sem='gsm')
            if 'A' not in SK:
                cx.dma('pool', wa2p[16:32, 1, :], gla_w_a2[l, 1], writes=['wa2p'], sem='gsm')
            if 'b' not in SK:
                cx.dma('pool', wsm[:, :, :], w_in[l][:, C_GA:C_GA + 32].rearrange("(k p) c -> p k c", p=128),
                       writes=['wsm'], sem='gsm')
            with nc.allow_non_contiguous_dma(reason="tiny bias columns"):
                if 'c' not in SK:
                    cx.dma('sp', ba[:, :, :], gla_b_a[l].rearrange("d (hp p) -> p d hp", p=128), writes=['ba'], sem='gsm2')
                if 'C' not in SK:
                    cx.dma('sp', onw[:, :], gla_o_norm[l].rearrange("(p o) -> p o", o=1), writes=['onw'], sem='gsm2')
            dve(lambda e: e.tensor_scalar(out=ba[:, :, :], in0=ba[:, :, :], scalar1=-1.0, scalar2=None, op0=ALU.mult),
                ['ba'], ['ba'])
            slot_qk, k_qk = load_w(w_in[l][:, C_GQ:C_GQ + 512], 8, 512)
            slot_v, k_v = load_w(w_in[l][:, C_GV:C_GV + 512], 8, 512)
            for tt in range(8):
                ps, pk = bank()
                for k in range(8):
                    mm(ps[:, :], hT[:, k, tt * 128:(tt + 1) * 128], slot_v[:, k, :], k == 0, k == 7, ['hT', k_v], [pk])
                evac_copy(tt, v_tm[:, tt, :], ps[:, :], [pk], ['v_tm'])
            slot_gg, k_gg = load_w(w_in[l][:, C_GG:C_GG + 512], 8, 512)
            proj_fm(wsm, 'wsm', 0, 32,
                    lambda ps, pk, th: act(lambda e: e.activation(out=alow[0:32, th * 512:(th + 1) * 512], in_=ps[0:32, :],
                                                                  func=AF.Copy), [pk], ['alow']))
            for m in range(4):
                proj_fm(slot_gg, k_gg, m * 128, 128,
                        lambda ps, pk, th, m=m: act(lambda e: e.activation(
                            out=ggT[:, m, th * 512:(th + 1) * 512], in_=ps[:, :], func=AF.Silu), [pk], ['ggT']))

            tap("g1_ggT", ggT[:, :, :], 'ggT')
            if stop == 'G1':
                return
            for hp in range(2):
                with ExitStack() as s2:
                    q_f = sb("q_f", [128, T], stack=s2)
                    k_f = sb("k_f", [128, T], stack=s2)
                    SP = sb("SP", [128, T], stack=s2)
                    BC = sb("BC", [128, T], stack=s2)
                    E = sb("E", [128, T], stack=s2)
                    TM = sb("TM", [128, T], stack=s2)
                    qd = [sb("qd%d" % d, [128, T], BF16, stack=s2) for d in range(2)]
                    kd = [sb("kd%d" % d, [128, T], BF16, stack=s2) for d in range(2)]
                    kr = [sb("kr%d" % d, [128, T], BF16, stack=s2) for d in range(2)]
                    krtok = [sb("krtok%d" % d, [128, 8, 128], BF16, stack=s2) for d in range(2)]
                    gdec = [sb("gdec%d" % d, [128, 16], stack=s2) for d in range(2)]
                    S = [sb("S%d" % d, [128, 128], stack=s2) for d in range(2)]
                    Sb = [sb("Sb%d" % d, [128, 128], BF16, stack=s2) for d in range(2)]
                    stg = [sb("stg%d" % i, [128, 128], stack=s2) for i in range(2)]
                    attsb = [sb("attsb%d" % i, [128, 2, 128], BF16, stack=s2) for i in range(2)]
                    proj_fm(slot_qk, k_qk, hp * 128, 128,
                            lambda ps, pk, th: act(lambda e: e.mul(out=q_f[:, th * 512:(th + 1) * 512], in_=ps[:, :],
                                                                   mul=0.125), [pk], ['q_f']))
                    proj_fm(slot_qk, k_qk, 256 + hp * 128, 128,
                            lambda ps, pk, th: dve(lambda e: e.tensor_copy(out=k_f[:, th * 512:(th + 1) * 512],
                                                                            in_=ps[:, :]), [pk], ['k_f']))
                    for d in range(2):
                        cx.dma('sp', S[d][:, :], sg0_d[l, d, 2 * hp:2 * hp + 2].rearrange("h k v -> (h k) v"),
                               writes=['S%d' % d], sem='gS%d' % d)
                        act(lambda e, d=d: e.activation(out=Sb[d][:, :], in_=S[d][:, :], func=AF.Copy),
                            ['S%d' % d], ['Sb%d' % d])
                        for th in range(2):
                            ps, pk = bank()
                            mm(ps[:, :], wa2p[:, d, hp * 128:(hp + 1) * 128], alow[0:32, th * 512:(th + 1) * 512],
                               True, True, ['wa2p', 'alow'], [pk])
                            act(lambda e, ps=ps, th=th, d=d: e.activation(
                                out=E[:, th * 512:(th + 1) * 512], in_=ps[:, :], func=AF.Exp,
                                bias=ba[:, d, hp:hp + 1], scale=-1.0), [pk, 'ba'], ['E'])
                        act(lambda e: e.activation(out=SP[:, :], in_=E[:, :], func=AF.Ln, bias=1.0), ['E'], ['SP'])
                        dve(lambda e: e.tensor_tensor_scan(out=BC[:, :], data0=cmask[:, :], data1=SP[:, :], initial=0.0,
                                                           op0=ALU.mult, op1=ALU.add), ['cmask', 'SP'], ['BC'])
                        act(lambda e, d=d: e.activation(out=gdec[d][:, :], in_=BC[:, 63::64], func=AF.Exp,
                                                        scale=-1.0 / 16), ['BC'], ['gdec%d' % d])
                        BC3 = BC[:, :].rearrange("p (c j) -> p c j", j=64)
                        BL = BC3[:, :, 63:64].to_broadcast([128, 16, 64])
                        TM3 = TM[:, :].rearrange("p (c j) -> p c j", j=64)
                        SP3 = SP[:, :].rearrange("p (c j) -> p c j", j=64)
                        kq, kk, kkr = 'qd%d' % d, 'kd%d' % d, 'kr%d' % d
                        if d == 0:
                            src = BC
                            skey = 'BC'
                        else:
                            dve(lambda e: e.tensor_tensor(out=TM3, in0=BL, in1=BC3, op=ALU.subtract), ['BC'], ['TM'])
                            dve(lambda e: e.tensor_tensor(out=TM[:, :], in0=TM[:, :], in1=SP[:, :], op=ALU.add),
                                ['TM', 'SP'], ['TM'])
                            src = TM
                            skey = 'TM'
                        act(lambda e, src=src: e.activation(out=E[:, :], in_=src[:, :], func=AF.Exp, scale=-1.0 / 16),
                            [skey], ['E'])
                        dve(lambda e, d=d: e.tensor_tensor(out=qd[d][:, :], in0=q_f[:, :], in1=E[:, :], op=ALU.mult),
                            ['q_f', 'E'], [kq])
                        act(lambda e, src=src: e.activation(out=E[:, :], in_=src[:, :], func=AF.Exp, scale=1.0 / 16),
                            [skey], ['E'])
                        dve(lambda e, d=d: e.tensor_tensor(out=kd[d][:, :], in0=k_f[:, :], in1=E[:, :], op=ALU.mult),
                            ['k_f', 'E'], [kk])
                        if d == 0:
                            dve(lambda e: e.tensor_tensor(out=TM3, in0=BL, in1=BC3, op=ALU.subtract), ['BC'], ['TM'])
                        else:
                            dve(lambda e: e.tensor_tensor(out=TM[:, :], in0=BC[:, :], in1=SP[:, :], op=ALU.subtract),
                                ['BC', 'SP', 'TM'], ['TM'])
                        act(lambda e: e.activation(out=E[:, :], in_=TM[:, :], func=AF.Exp, scale=-1.0 / 16),
                            ['TM'], ['E'])
                        dve(lambda e, d=d: e.tensor_tensor(out=kr[d][:, :], in0=k_f[:, :], in1=E[:, :], op=ALU.mult),
                            ['k_f', 'E'], [kkr])
                        for half in range(2):
                            ps, pk = bank()
                            for kk4 in range(4):
                                tt = half * 4 + kk4
                                mm(ps[:, kk4 * 128:(kk4 + 1) * 128], kr[d][:, tt * 128:(tt + 1) * 128], ident_bf[:, :],
                                   True, True, [kkr, 'ident_bf'], [pk], inc=(kk4 == 3))
                            evac_copy(half, krtok[d][:, half * 4:half * 4 + 4, :],
                                      ps[:, :].rearrange("p (a b) -> p a b", b=128), [pk], ['krtok%d' % d])

                    tap("g2_kr", krtok[1][:, :, :], 'krtok1')
                    if stop == 'G2':
                        return

                    def gla_step(d, cp, step_i):
                        kq, kk, kS, kSb = 'qd%d' % d, 'kd%d' % d, 'S%d' % d, 'Sb%d' % d
                        cols = slice(cp * 128, (cp + 1) * 128)
                        ab, akey = bank()
                        for h2 in range(2):
                            rows = slice(h2 * 64, (h2 + 1) * 64)
                            mm_b(ab[:, h2 * 128:(h2 + 1) * 128], kd[d][rows, cols], qd[d][rows, cols], True, True,
                                 [kk, kq], [akey], inc=(h2 == 1), base=h2 * 64)
                        lim = int(os.environ.get("KDBG_CUT", "99"))
                        if lim <= 1:
                            return
                        asb = attsb[step_i % 2]
                        askey = 'attsb%d' % (step_i % 2)
                        dve(lambda e: e.tensor_tensor(
                            out=asb[:, :, :], in0=ab[:, 0:256].rearrange("p (a b) -> p a b", b=128),
                            in1=glam[:, d:d + 1, :].to_broadcast([128, 2, 128]), op=ALU.mult),
                            [akey, 'glam'], [askey])
                        if lim <= 2:
                            return
                        ob, okey = bank()
                        for h2 in range(2):
                            h = 2 * hp + h2
                            mm(ob[:, h2 * 128:(h2 + 1) * 128], v_tm[:, cp, h * 128:(h + 1) * 128], asb[:, h2, :],
                               h2 == 0, False, ['v_tm', askey], [okey], inc=False, skip=True)
                        if lim <= 3:
                            return
                        order = [2 * cp, 2 * cp + 1] if d == 0 else [2 * cp + 1, 2 * cp]
                        for idx, c in enumerate(order):
                            ci = c % 2
                            boundary = (c % 4 == 0 and c > 0) if d == 0 else (c % 4 == 3 and c < 15)
                            if boundary:
                                dve(lambda e: e.tensor_scalar(out=S[d][:, :], in0=S[d][:, :], scalar1=rcol[:, 0:1],
                                                              scalar2=None, op0=ALU.mult), [kS, 'rcol'], [kS])
                                act(lambda e: e.activation(out=Sb[d][:, :], in_=S[d][:, :], func=AF.Copy), [kS], [kSb])
                            for h2 in range(2):
                                rows = slice(h2 * 64, (h2 + 1) * 64)
                                mm_b(ob[:, h2 * 128 + ci * 64:h2 * 128 + ci * 64 + 64], Sb[d][rows, :],
                                     qd[d][rows, c * 64:(c + 1) * 64], False, idx == 1, [kSb, kq], [okey],
                                     inc=(idx == 1 and h2 == 1), skip=True, base=h2 * 64)
                            if lim <= 4:
                                return
                            kvb, kvkey = bank()
                            crow = slice(ci * 64, (ci + 1) * 64)
                            mm_b(kvb[:, 0:256], krtok[d][crow, cp, :], v_tm[crow, cp, hp * 256:(hp + 1) * 256], True, True,
                                 ['krtok%d' % d, 'v_tm'], [kvkey], base=ci * 64)
                            if lim <= 5:
                                return
                            for h2 in range(2):
                                rows = slice(h2 * 64, (h2 + 1) * 64)
                                dve(lambda e, rows=rows, h2=h2, c=c: e.scalar_tensor_tensor(
                                    out=S[d][rows, :], in0=S[d][rows, :], scalar=gdec[d][rows, c:c + 1],
                                    in1=kvb[rows, h2 * 128:(h2 + 1) * 128], op0=ALU.mult, op1=ALU.add),
                                    [kS, 'gdec%d' % d, kvkey], [kS])
                            if lim <= 6:
                                return
                            act(lambda e: e.activation(out=Sb[d][:, :], in_=S[d][:, :], func=AF.Copy), [kS], [kSb])
                            seg_end = (c % 4 == 3) if d == 0 else (c % 4 == 0)
                            if seg_end:
                                seg = c // 4
                                sg = stg[seg % 2]
                                sgk = 'stg%d' % (seg % 2)
                                act(lambda e, sg=sg: e.activation(out=sg[:, :], in_=S[d][:, :], func=AF.Copy), [kS], [sgk])
                                cx.dma('sp', ogla_d[l, seg, d, 2 * hp:2 * hp + 2].rearrange("h k v -> (h k) v"), sg[:, :],
                                       reads=[sgk], sem='ogla')
                        dve(lambda e: e.tensor_tensor(
                            out=oT[:, 2 * hp:2 * hp + 2, cols], in0=oT[:, 2 * hp:2 * hp + 2, cols],
                            in1=ob[:, 0:256].rearrange("p (a b) -> p a b", b=128), op=ALU.add), ['oT', okey], ['oT'])

                    nsteps = int(os.environ.get("KDBG_STEPS", "8"))
                    for step in range(min(8, nsteps)):
                        gla_step(0, step, 2 * step)
                        if 'x' not in SK:
                            gla_step(1, 7 - step, 2 * step + 1)
                    if nsteps < 8:
                        tap("g3_oT", oT[:, :, :], 'oT')
                        return
            tap("gla_oT%d" % l, oT[:, :, :], 'oT')
            with ExitStack() as s3:
                sqb = [sb("sqb%d" % i, [128, 512], BF16, stack=s3) for i in range(2)]
                sdv = [sb("sdv%d" % i, [128, 512], stack=s3) for i in range(2)]
                tmv = [sb("tmv%d" % i, [128, 512], stack=s3) for i in range(2)]
                i = 0
                for h in range(4):
                    for th in range(2):
                        cs = slice(th * 512, (th + 1) * 512)
                        sq, sd, tm_ = sqb[i % 2], sdv[i % 2], tmv[i % 2]
                        ksq, ksd, ktm = 'sqb%d' % (i % 2), 'sdv%d' % (i % 2), 'tmv%d' % (i % 2)
                        act(lambda e, sq=sq, h=h, cs=cs: e.activation(out=sq[:, :], in_=oT[:, h, cs], func=AF.Square),
                            ['oT'], [ksq])
                        ps, pk = bank()
                        mm(ps[:, :], ones_bf[:, :], sq[:, :], True, True, ['ones_bf', ksq], [pk])
                        act(lambda e, ps=ps, sd=sd: e.activation(out=sd[:, :], in_=ps[:, :], func=AF.Sqrt, bias=EPS,
                                                                 scale=1.0 / 128), [pk], [ksd])
                        dve(lambda e, sd=sd: e.reciprocal(out=sd[:, :], in_=sd[:, :]), [ksd], [ksd])
                        dve(lambda e, sd=sd, tm_=tm_, h=h, cs=cs: e.scalar_tensor_tensor(
                            out=tm_[:, :], in0=oT[:, h, cs], scalar=onw[:, 0:1], in1=sd[:, :], op0=ALU.mult, op1=ALU.mult),
                            ['oT', 'onw', ksd], [ktm])
                        dve(lambda e, tm_=tm_, h=h, cs=cs: e.tensor_tensor(
                            out=oT_br[0][:, h, cs], in0=tm_[:, :], in1=ggT[:, h, cs], op=ALU.mult),
                            [ktm, 'ggT'], ['obr0'])
                        i += 1
            tap("gla_out%d" % l, oT_br[0][:, :, :], 'obr0')

    def branch_mla(l):
        with ExitStack() as st:
            QT = sb("QT", [128, 4, T], BF16, stack=st)
            KT = sb("KT", [128, 4, NKEY], BF16, stack=st)
            Vt = sb("Vt", [128, 12, 512], BF16, stack=st)
            mgT = sb("mgT", [128, 4, T], BF16, stack=st)
            CKb = sb("CKb", [128, 12, 256], BF16, stack=st)
            KR = sb("KR", [128, 12, 32], stack=st)
            cqn = sb("cqn", [128, 8, 384], BF16, stack=st)
            cqnT = sb("cqnT", [128, 3, T], BF16, stack=st)
            ckvT = sb("ckvT", [128, 2, NKEY], BF16, stack=st)
            wuq = sb("wuq", [128, 3, 384], BF16, stack=st)
            wuqf = sb("wuqf", [128, 3, 384], stack=st)
            qnw = sb("qnw", [128, 3], stack=st)
            wuk = sb("wuk", [128, 2, 256], BF16, stack=st)
            wuv = sb("wuv", [128, 2, 512], BF16, stack=st)
            kvw_bc = sb("kvw_bc", [128, 256], stack=st)
            qhw_bc = sb("qhw_bc", [128, 96], stack=st)
            khw_bc = sb("khw_bc", [128, 96], stack=st)
            ssq = sb("ssq", [128, 8, 2], stack=st)
            rsq = sb("rsq", [128, 8, 2], stack=st)
            tq = sb("tq", [128, 8, 2], stack=st)
            sskr = sb("sskr", [128, 12], stack=st)
            junk = sb("junkm", [128, 512], stack=st)
            stg = [sb("stgkv%d" % i, [128, 288], stack=st) for i in range(2)]
            for h in range(4):
                cx.dma('pool', QT[96:101, h, :], qmask_d[:, :], writes=['QT'], sem='mmask')
                cx.dma('pool', KT[96:101, h, :], kmask_d[:, :], writes=['KT'], sem='mmask')
            cx.dma('sp', wuqf[:, :, :], mla_w_uq[l].rearrange("(k p) c -> p k c", p=128), writes=['wuqf'], sem='mw1')
            with nc.allow_non_contiguous_dma(reason="tiny norm weight column"):
                cx.dma('sp', qnw[:, :], mla_q_norm[l].rearrange("(k p) -> p k", p=128), writes=['qnw'], sem='mw1')
            cx.dma('pool', wuk[:, :, :], mla_w_uk[l].rearrange("(k p) c -> p k c", p=128), writes=['wuk'], sem='mw2')
            cx.dma('pool', wuv[:, :, :], mla_w_uv[l].rearrange("(k p) c -> p k c", p=128), writes=['wuv'], sem='mw2')
            cx.dma('sp', kvw_bc[:, :], mla_kv_norm[l].partition_broadcast(128), writes=['kvw_bc'], sem='mw1')
            cx.dma('sp', qhw_bc[:, :], mla_qh_norm[l].partition_broadcast(128), writes=['qhw_bc'], sem='mw1')
            cx.dma('sp', khw_bc[:, :], mla_kh_norm[l].partition_broadcast(128), writes=['khw_bc'], sem='mw1')
            cx.dma('pool', CKb[:, 0:4, :], ckvc_d[l].rearrange("(t p) c -> p t c", p=128), writes=['CKb'], sem='mw2')
            cx.dma('sp', KR[:, 0:4, :], krc_d[l].rearrange("(t p) c -> p t c", p=128), writes=['KR'], sem='mw1')
            for k in range(3):
                dve(lambda e: e.tensor_scalar(out=wuq[:, k, :], in0=wuqf[:, k, :], scalar1=qnw[:, k:k + 1], scalar2=None,
                                              op0=ALU.mult), ['wuqf', 'qnw'], ['wuq'])
            dve(lambda e: e.memset(ssq[:, :, :], 0.0), [], ['ssq'])
            dve(lambda e: e.memset(sskr[:, :], 0.0), [], ['sskr'])
            slotA, kA = load_w(w_in[l][:, C_MQ:C_MQ + 384], 8, 384)
            slotB, kB = load_w(w_in[l][:, C_MKV:C_MKV + 288], 8, 288)
            for tt in range(8):
                tsl = slice(tt * 128, (tt + 1) * 128)
                psA, pkA = bank()
                for k in range(8):
                    mm(psA[:, 0:384], hT[:, k, tsl], slotA[:, k, 0:384], k == 0, k == 7, ['hT', kA], [pkA])
                psB, pkB = bank()
                for k in range(8):
                    mm(psB[:, 0:288], hT[:, k, tsl], slotB[:, k, 0:288], k == 0, k == 7, ['hT', kB], [pkB])
                act(lambda e: e.activation(out=junk[:, 0:384], in_=psA[:, 0:384], func=AF.Square,
                                           accum_out=ssq[:, tt, 0:1]), [pkA, 'ssq'], ['junkm', 'ssq'])
                act(lambda e: e.activation(out=junk[:, 0:256], in_=psB[:, 0:256], func=AF.Square,
                                           accum_out=ssq[:, tt, 1:2]), [pkB, 'ssq'], ['junkm', 'ssq'])
                act(lambda e: e.activation(out=tq[:, tt, 0:1], in_=ssq[:, tt, 0:1], func=AF.Sqrt, bias=EPS,
                                           scale=1.0 / 384), ['ssq'], ['tq'])
                act(lambda e: e.activation(out=tq[:, tt, 1:2], in_=ssq[:, tt, 1:2], func=AF.Sqrt, bias=EPS,
                                           scale=1.0 / 256), ['ssq'], ['tq'])
                dve(lambda e: e.reciprocal(out=rsq[:, tt, :], in_=tq[:, tt, :]), ['tq'], ['rsq'])
                dve(lambda e: e.tensor_scalar(out=cqn[:, tt, :], in0=psA[:, 0:384], scalar1=rsq[:, tt, 0:1], scalar2=None,
                                              op0=ALU.mult), [pkA, 'rsq'], ['cqn'])
                sg = stg[tt % 2]
                sgk = 'stgkv%d' % (tt % 2)
                dve(lambda e: e.scalar_tensor_tensor(out=sg[:, 0:256], in0=psB[:, 0:256], scalar=rsq[:, tt, 1:2],
                                                     in1=kvw_bc[:, :], op0=ALU.mult, op1=ALU.mult),
                    [pkB, 'rsq', 'kvw_bc'], [sgk])
                act(lambda e: e.activation(out=sg[:, 256:288], in_=psB[:, 256:288], func=AF.Copy), [pkB], [sgk])
                cx.dma('sp', ockv_d[l, tsl, :], sg[:, 0:256], reads=[sgk], sem='ockv')
                cx.dma('sp', okr_d[l, tsl, :], sg[:, 256:288], reads=[sgk], sem='ockv')
                act(lambda e: e.activation(out=CKb[:, 4 + tt, :], in_=sg[:, 0:256], func=AF.Copy), [sgk], ['CKb'])
                dve(lambda e: e.tensor_copy(out=KR[:, 4 + tt, :], in_=sg[:, 256:288]), [sgk], ['KR'])
            for tt in range(8):
                ps, pk = bank()
                for k in range(3):
                    mm(ps[:, k * 128:(k + 1) * 128], cqn[:, tt, k * 128:(k + 1) * 128], ident_bf[:, :], True, True,
                       ['cqn', 'ident_bf'], [pk], inc=(k == 2))
                evac_copy(tt, cqnT[:, :, tt * 128:(tt + 1) * 128], ps[:, 0:384].rearrange("p (a b) -> p a b", b=128),
                          [pk], ['cqnT'])
            for kp in range(6):
                ps, pk = bank()
                for j in range(2):
                    kt = 2 * kp + j
                    for k in range(2):
                        mm(ps[:, (2 * j + k) * 128:(2 * j + k + 1) * 128], CKb[:, kt, k * 128:(k + 1) * 128], ident_bf[:, :],
                           True, True, ['CKb', 'ident_bf'], [pk], inc=(j == 1 and k == 1))
                for j in range(2):
                    kt = 2 * kp + j
                    evac_copy(j, ckvT[:, :, kt * 128:(kt + 1) * 128],
                              ps[:, j * 256:(j + 1) * 256].rearrange("p (a b) -> p a b", b=128), [pk], ['ckvT'])
            slot_mg, k_mg = load_w(w_in[l][:, C_MG:C_MG + 512], 8, 512)
            for m in range(4):
                proj_fm(slot_mg, k_mg, m * 128, 128,
                        lambda ps, pk, th, m=m: act(lambda e: e.activation(
                            out=mgT[:, m, th * 512:(th + 1) * 512], in_=ps[:, :], func=AF.Silu), [pk], ['mgT']))

            def head_finish(src3, skey, nrm_keys, rope_tt, dstT, dkey, col0, tagi):
                i2 = tagi % 2
                fb = hfb[i2]
                fk = 'hfb%d' % i2
                if rope_tt is None:
                    act(lambda e: e.activation(out=fb[:, :, :], in_=src3, func=AF.Copy), [skey], [fk])
                else:
                    rt = rtmp[i2]
                    rk = 'rtmp%d' % i2
                    cb = ropec[:, rope_tt:rope_tt + 1, :].to_broadcast([128, 4, 16])
                    sbb = ropes[:, rope_tt:rope_tt + 1, :].to_broadcast([128, 4, 16])
                    x1 = src3[:, :, 64:80]
                    x2 = src3[:, :, 80:96]
                    act(lambda e: e.activation(out=fb[:, :, 0:64], in_=src3[:, :, 0:64], func=AF.Copy), [skey], [fk])
                    pool(lambda e: e.tensor_tensor(out=rt[:, 0, :, :], in0=x1, in1=cb, op=ALU.mult), [skey, 'ropec'], [rk])
                    pool(lambda e: e.tensor_tensor(out=rt[:, 1, :, :], in0=x2, in1=sbb, op=ALU.mult), [skey, 'ropes'], [rk])
                    pool(lambda e: e.tensor_tensor(out=rt[:, 2, :, :], in0=x1, in1=sbb, op=ALU.mult), [skey, 'ropes'], [rk])
                    pool(lambda e: e.tensor_tensor(out=rt[:, 3, :, :], in0=x2, in1=cb, op=ALU.mult), [skey, 'ropec'], [rk])
                    pool(lambda e: e.tensor_tensor(out=fb[:, :, 64:80], in0=rt[:, 0, :, :], in1=rt[:, 1, :, :],
                                                   op=ALU.subtract), [rk], [fk])
                    pool(lambda e: e.tensor_tensor(out=fb[:, :, 80:96], in0=rt[:, 2, :, :], in1=rt[:, 3, :, :],
                                                   op=ALU.add), [rk], [fk])
                ps, pk = bank()
                for h in range(4):
                    mm(ps[0:96, h * 128:(h + 1) * 128], fb[:, h, :], ident_bf[:, :], True, True, [fk, 'ident_bf'], [pk],
                       inc=(h == 3))
                act(lambda e: e.activation(out=dstT[0:96, :, col0:col0 + 128],
                                           in_=ps[0:96, :].rearrange("p (a b) -> p a b", b=128), func=AF.Copy),
                    [pk], [dkey])

            with ExitStack() as s2:
                hfb = [sb("hfb%d" % i, [128, 4, 96], BF16, stack=s2) for i in range(2)]
                rtmp = [sb("rtmp%d" % i, [128, 4, 4, 16], stack=s2) for i in range(2)]
                sqh = [sb("sqh%d" % i, [128, 384], stack=s2) for i in range(2)]
                ssh = [sb("ssh%d" % i, [128, 4], stack=s2) for i in range(2)]
                hn = [sb("hn%d" % i, [128, 4, 96], stack=s2) for i in range(2)]
                for tt in range(8):
                    i2 = tt % 2
                    psQ, pkQ = bank()
                    for k in range(3):
                        mm(psQ[:, 0:384], cqnT[:, k, tt * 128:(tt + 1) * 128], wuq[:, k, :], k == 0, k == 2,
                           ['cqnT', 'wuq'], [pkQ])
                    q3 = psQ[:, 0:384].rearrange("p (a b) -> p a b", b=96)
                    act(lambda e: e.activation(out=sqh[i2][:, 0:384], in_=psQ[:, 0:384], func=AF.Square),
                        [pkQ], ['sqh%d' % i2])
                    dve(lambda e: e.tensor_reduce(out=ssh[i2][:, :], in_=sqh[i2][:, 0:384].rearrange("p (a b) -> p a b", b=96),
                                                  axis=AX.X, op=ALU.add), ['sqh%d' % i2], ['ssh%d' % i2])
                    act(lambda e: e.activation(out=ssh[i2][:, :], in_=ssh[i2][:, :], func=AF.Sqrt, bias=EPS,
                                               scale=1.0 / 96), ['ssh%d' % i2], ['ssh%d' % i2])
                    dve(lambda e: e.reciprocal(out=ssh[i2][:, :], in_=ssh[i2][:, :]), ['ssh%d' % i2], ['ssh%d' % i2])
                    dve(lambda e: e.tensor_tensor(out=hn[i2][:, :, :], in0=q3,
                                                  in1=ssh[i2][:, :].unsqueeze(2).to_broadcast([128, 4, 96]), op=ALU.mult),
                        [pkQ, 'ssh%d' % i2], ['hn%d' % i2])
                    dve(lambda e: e.tensor_tensor(out=hn[i2][:, :, :], in0=hn[i2][:, :, :],
                                                  in1=qhw_bc[:, :].unsqueeze(1).to_broadcast([128, 4, 96]), op=ALU.mult),
                        ['hn%d' % i2, 'qhw_bc'], ['hn%d' % i2])
                    head_finish(hn[i2][:, :, :], 'hn%d' % i2, None, tt, QT, 'QT', tt * 128, tt)
                for kt in range(12):
                    i2 = kt % 2
                    ksl = slice(kt * 128, (kt + 1) * 128)
                    psK, pkK = bank()
                    for k in range(2):
                        mm(psK[:, 0:256], ckvT[:, k, ksl], wuk[:, k, :], k == 0, k == 1, ['ckvT', 'wuk'], [pkK])
                    psV, pkV = bank()
                    for k in range(2):
                        mm(psV[:, :], ckvT[:, k, ksl], wuv[:, k, :], k == 0, k == 1, ['ckvT', 'wuv'], [pkV])
                    evac_copy(kt, Vt[:, kt, :], psV[:, :], [pkV], ['Vt'])
                    act(lambda e: e.activation(out=sqh[i2][:, 0:256], in_=psK[:, 0:256], func=AF.Square),
                        [pkK], ['sqh%d' % i2])
                    dve(lambda e: e.tensor_reduce(out=ssh[i2][:, :], in_=sqh[i2][:, 0:256].rearrange("p (a b) -> p a b", b=64),
                                                  axis=AX.X, op=ALU.add), ['sqh%d' % i2], ['ssh%d' % i2])
                    act(lambda e: e.activation(out=junk[:, 0:32], in_=KR[:, kt, :], func=AF.Square,
                                               accum_out=sskr[:, kt:kt + 1]), ['KR', 'sskr'], ['junkm', 'sskr'])
                    dve(lambda e: e.tensor_scalar(out=ssh[i2][:, :], in0=ssh[i2][:, :], scalar1=sskr[:, kt:kt + 1],
                                                  scalar2=None, op0=ALU.add), ['ssh%d' % i2, 'sskr'], ['ssh%d' % i2])
                    act(lambda e: e.activation(out=ssh[i2][:, :], in_=ssh[i2][:, :], func=AF.Sqrt, bias=EPS,
                                               scale=1.0 / 96), ['ssh%d' % i2], ['ssh%d' % i2])
                    dve(lambda e: e.reciprocal(out=ssh[i2][:, :], in_=ssh[i2][:, :]), ['ssh%d' % i2], ['ssh%d' % i2])
                    k3 = psK[:, 0:256].rearrange("p (a b) -> p a b", b=64)
                    dve(lambda e: e.tensor_tensor(out=hn[i2][:, :, 0:64], in0=k3,
                                                  in1=ssh[i2][:, :].unsqueeze(2).to_broadcast([128, 4, 64]), op=ALU.mult),
                        [pkK, 'ssh%d' % i2], ['hn%d' % i2])
                    dve(lambda e: e.tensor_tensor(out=hn[i2][:, :, 64:96],
                                                  in0=KR[:, kt:kt + 1, :].to_broadcast([128, 4, 32]),
                                                  in1=ssh[i2][:, :].unsqueeze(2).to_broadcast([128, 4, 32]), op=ALU.mult),
                        ['KR', 'ssh%d' % i2], ['hn%d' % i2])
                    dve(lambda e: e.tensor_tensor(out=hn[i2][:, :, :], in0=hn[i2][:, :, :],
                                                  in1=khw_bc[:, :].unsqueeze(1).to_broadcast([128, 4, 96]), op=ALU.mult),
                        ['hn%d' % i2, 'khw_bc'], ['hn%d' % i2])
                    head_finish(hn[i2][:, :, :], 'hn%d' % i2, None, (kt - 4) if kt >= 4 else None, KT, 'KT', kt * 128, kt)
            tap("mla_QT%d" % l, QT[0:101, :, :], 'QT')
            tap("mla_KT%d" % l, KT[0:101, :, :], 'KT')
            tap("mla_V%d" % l, Vt[:, :, :], 'Vt')
            if stop == 'M1':
                return
            with ExitStack() as s3:
                PT = [sb("PT%d" % i, [128, 512], BF16, stack=s3) for i in range(3)]
                rden = [sb("rden%d" % i, [128, 512], stack=s3) for i in range(2)]
                accs = [(reserve_bank(), reserve_bank()) for _ in range(2)]
                it = 0
                pi = 0
                for h in range(4):
                    for qh in range(2):
                        (ob, okey), (db, dkey) = accs[it % 2]
                        qsl = slice(qh * 512, (qh + 1) * 512)
                        for kt in range(12):
                            sbk, skey = bank()
                            mm(sbk[:, :], KT[0:101, h, kt * 128:(kt + 1) * 128], QT[0:101, h, qsl], True, True,
                               ['KT', 'QT'], [skey])
                            pt = PT[pi % 3]
                            ptk = 'PT%d' % (pi % 3)
                            pi += 1
                            act(lambda e: e.activation(out=pt[:, :], in_=sbk[:, :], func=AF.Exp, scale=ATT_SCALE),
                                [skey], [ptk])
                            mm(ob[:, :], Vt[:, kt, h * 128:(h + 1) * 128], pt[:, :], kt == 0, kt == 11, ['Vt', ptk], [okey])
                            mm(db[:, :], ones_bf[:, :], pt[:, :], kt == 0, kt == 11, ['ones_bf', ptk], [dkey])
                        rd = rden[it % 2]
                        rdk = 'rden%d' % (it % 2)
                        dve(lambda e: e.reciprocal(out=rd[:, :], in_=db[:, :]), [dkey], [rdk])
                        dve(lambda e: e.tensor_tensor(out=rd[:, :], in0=ob[:, :], in1=rd[:, :], op=ALU.mult),
                            [okey, rdk], [rdk])
                        dve(lambda e: e.tensor_tensor(out=oT_br[1][:, h, qsl], in0=rd[:, :], in1=mgT[:, h, qsl],
                                                      op=ALU.mult), [rdk, 'mgT'], ['obr1'])
                        it += 1
                for (a, b) in accs:
                    release_bank(a[1])
                    release_bank(b[1])
            tap("mla_out%d" % l, oT_br[1][:, :, :], 'obr1')

    class Arena:
        def __init__(self, t, nelem, dt):
            self.t = t
            self.dt = dt
            self.ti = t.bitcast(I32) if dt == F32 else None
            self.free_list = [(0, nelem)]
            self.live = {}

        def alloc(self, name, dims, dt=None):
            n = 1
            for d_ in dims:
                n *= d_
            nf = (n + 15) // 16 * 16
            for idx, (o, sz) in enumerate(self.free_list):
                if sz >= nf:
                    if sz == nf:
                        self.free_list.pop(idx)
                    else:
                        self.free_list[idx] = (o + nf, sz - nf)
                    break
            else:
                raise RuntimeError("arena full: %s %s %s" % (name, dims, self.free_list))
            self.live[name] = (o, nf)
            if dt == I32:
                ap = self.ti[:, o:o + n]
            else:
                ap = self.t[:, o:o + n]
            if len(dims) > 1:
                names = ["a%d" % i for i in range(len(dims))]
                pat = "p (%s) -> p %s" % (" ".join(names), " ".join(names))
                ap = ap.rearrange(pat, **{nm: d_ for nm, d_ in zip(names, dims)})
            cx.fresh[name] = dict(cx.cnt)
            return ap

        def free(self, name):
            o, nf = self.live.pop(name)
            fl = sorted(self.free_list + [(o, nf)])
            merged = []
            for a, b in fl:
                if merged and merged[-1][0] + merged[-1][1] == a:
                    merged[-1] = (merged[-1][0], merged[-1][1] + b)
                else:
                    merged.append((a, b))
            self.free_list = merged

    class Arena2:
        def __init__(self, af, ab):
            self.af, self.ab = af, ab
            self.where = {}

        def alloc(self, name, dims, dt=F32):
            a = self.ab if dt == BF16 else self.af
            self.where[name] = a
            return a.alloc(name, dims, dt)

        def free(self, name):
            self.where.pop(name).free(name)

    def branch_s5(l):
        with ExitStack() as st:
            NF, NB = 9600, 29696
            arena_f = sb("s5arena_f", [128, NF], stack=st)
            arena_b = sb("s5arena_b", [128, NB], BF16, stack=st)
            A = Arena2(Arena(arena_f, NF, F32), Arena(arena_b, NB, BF16))
            WoR = A.alloc("WoR", [2, 16, 2, 128], BF16)
            TT = A.alloc("TT", [32, 128], BF16)
            C1 = A.alloc("C1", [16, 2, 2]); C2 = A.alloc("C2", [16, 2, 2])
            C1r = A.alloc("C1r", [16, 2, 2]); C2r = A.alloc("C2r", [16, 2, 2])
            WinT = A.alloc("WinT", [2, 32, 128], BF16)
            s5m = A.alloc("s5m", [2, 128])
            selc = A.alloc("selc", [2, 64])
            cx.dma('sp', s5m, s5m_d.rearrange("a p c -> p a c"), writes=['s5m'], sem='s5c')
            cx.dma('sp', selc, selc_d.rearrange("a p c -> p a c"), writes=['selc'], sem='s5c')

            def tt_(e, o, a, b, op):
                return e.tensor_tensor(out=o, in0=a, in1=b, op=op)

            def D2(o, a, b, op, rk, wk):
                dve(lambda e: tt_(e, o, a, b, op), rk, wk)

            AR = A.alloc("AR", [2, 16]); AI = A.alloc("AI", [2, 16]); LD = A.alloc("LD", [2, 16])
            anat = A.alloc("anat", [2, 128])
            cx.dma('sp', anat[0:32, 0, :], s5_a_re[l].rearrange("d (j g2) n -> (d j) (g2 n)", g2=2), writes=['anat'], sem='s5c')
            cx.dma('sp', anat[0:32, 1, :], s5_a_im[l].rearrange("d (j g2) n -> (d j) (g2 n)", g2=2), writes=['anat'], sem='s5c')
            ps, pk = bank()
            for c_ in range(2):
                mm(ps[:, c_ * 32:(c_ + 1) * 32], anat[0:32, c_, :], ident_f[0:32, 0:32], True, True, ['anat', 'ident_f'], [pk],
                   inc=(c_ == 1))
            dve(lambda e: e.tensor_copy(out=AR, in_=ps[:, 0:32].rearrange("p (d j) -> p d j", d=2)), [pk], ['AR'])
            dve(lambda e: e.tensor_copy(out=AI, in_=ps[:, 32:64].rearrange("p (d j) -> p d j", d=2)), [pk], ['AI'])
            ldf = A.alloc("ldf", [2, 32])
            cx.dma('sp', ldf, s5_log_dt[l].rearrange("d g -> (d g)").partition_broadcast(128).rearrange("p (d g) -> p d g", d=2),
                   writes=['ldf'], sem='s5c')
            for g2 in range(2):
                dve(lambda e: e.tensor_copy(out=LD[g2 * 64:(g2 + 1) * 64], in_=ldf[g2 * 64:(g2 + 1) * 64, :, g2::2]),
                    ['ldf'], ['LD'])
            dcol = A.alloc("dcol", [32])
            with nc.allow_non_contiguous_dma(reason="small S5 parameter gathers"):
                for s_ in range(8):
                    cx.dma('sp', dcol[s_ * 16:(s_ + 1) * 16, :], s5_d[l].rearrange("(g q) -> q g", q=16),
                           writes=['dcol'], sem='s5c')
            BR = A.alloc("BR", [16, 16]); BI = A.alloc("BI", [16, 16])
            for jq in range(4):
                cx.dma('sp', BR[:, 4 * jq:4 * jq + 4, :], s5_b_re[l].rearrange("(j g2) n q -> (g2 n) j q", g2=2)[:, 4 * jq:4 * jq + 4, :],
                       writes=['BR'], sem='s5c')
                cx.dma('sp', BI[:, 4 * jq:4 * jq + 4, :], s5_b_im[l].rearrange("(j g2) n q -> (g2 n) j q", g2=2)[:, 4 * jq:4 * jq + 4, :],
                       writes=['BI'], sem='s5c')
            CNr = A.alloc("CNr", [4, 64]); CNi = A.alloc("CNi", [4, 64])
            cx.dma('sp', CNr, s5_c_re[l].rearrange("(r gl) p n -> (gl p) r n", r=4), writes=['CNr'], sem='s5c')
            cx.dma('sp', CNi, s5_c_im[l].rearrange("(r gl) p n -> (gl p) r n", r=4), writes=['CNi'], sem='s5c')
            CR = A.alloc("CR", [16, 16]); CI = A.alloc("CI", [16, 16])
            for (cn, cnk, cdst, cdk, ei) in ((CNr, 'CNr', CR, 'CR', 0), (CNi, 'CNi', CI, 'CI', 1)):
                ps, pk = bank()
                for r in range(4):
                    for g2 in range(2):
                        mm(ps[g2 * 64:(g2 + 1) * 64, r * 64:(r + 1) * 64], cn[:, r, :], selc[:, g2, :], True, True,
                           [cnk, 'selc'], [pk], inc=(r == 3 and g2 == 1))
                evac_copy(ei, cdst, ps[:, 0:256].rearrange("p (a b) -> p a b", b=16), [pk], [cdk])

            def small(name):
                return A.alloc(name, [2, 16])
            dt_ = small("dt_"); mag = small("mag"); ang = small("ang"); sn = small("sn"); cs = small("cs")
            abr = small("abr"); abi = small("abi"); rden = small("rden"); nre = small("nre")
            cfr = small("cfr"); cfi = small("cfi"); ta = small("ta"); tb = small("tb")
            kI = A.alloc("kI", [2, 16], I32)
            act(lambda e: e.activation(out=dt_, in_=LD, func=AF.Exp), ['LD'], ['dt_'])
            D2(ta, AR, dt_, ALU.mult, ['AR', 'dt_'], ['ta'])
            act(lambda e: e.activation(out=mag, in_=ta, func=AF.Exp), ['ta'], ['mag'])
            D2(ang, AI, dt_, ALU.mult, ['AI', 'dt_'], ['ang'])

            def sin_of(dst, dkey, shift):
                dve(lambda e: e.tensor_scalar(out=ta, in0=ang, scalar1=shift, scalar2=1.0 / (2 * PI), op0=ALU.add,
                                              op1=ALU.mult), ['ang'], ['ta'])
                dve(lambda e: e.tensor_copy(out=kI, in_=ta), ['ta'], ['kI'])
                dve(lambda e: e.tensor_copy(out=tb, in_=kI), ['kI'], ['tb'])
                dve(lambda e: e.tensor_scalar(out=ta, in0=ang, scalar1=shift, scalar2=None, op0=ALU.add), ['ang'], ['ta'])
                dve(lambda e: e.scalar_tensor_tensor(out=ta, in0=tb, scalar=-2 * PI, in1=ta, op0=ALU.mult, op1=ALU.add),
                    ['tb', 'ta'], ['ta'])
                dve(lambda e: e.tensor_scalar(out=tb, in0=ta, scalar1=PI, scalar2=None, op0=ALU.is_gt), ['ta'], ['tb'])
                dve(lambda e: e.scalar_tensor_tensor(out=ta, in0=tb, scalar=-2 * PI, in1=ta, op0=ALU.mult, op1=ALU.add),
                    ['tb', 'ta'], ['ta'])
                dve(lambda e: e.tensor_scalar(out=tb, in0=ta, scalar1=-PI, scalar2=None, op0=ALU.is_lt), ['ta'], ['tb'])
                dve(lambda e: e.scalar_tensor_tensor(out=ta, in0=tb, scalar=2 * PI, in1=ta, op0=ALU.mult, op1=ALU.add),
                    ['tb', 'ta'], ['ta'])
                act(lambda e: e.activation(out=dst, in_=ta, func=AF.Sin), ['ta'], [dkey])
            sin_of(sn, 'sn', 0.0)
            sin_of(cs, 'cs', PI / 2)
            D2(abr, mag, cs, ALU.mult, ['mag', 'cs'], ['abr'])
            D2(abi, mag, sn, ALU.mult, ['mag', 'sn'], ['abi'])
            D2(ta, AR, AR, ALU.mult, ['AR'], ['ta'])
            D2(tb, AI, AI, ALU.mult, ['AI'], ['tb'])
            D2(ta, ta, tb, ALU.add, ['ta', 'tb'], ['ta'])
            dve(lambda e: e.reciprocal(out=rden, in_=ta), ['ta'], ['rden'])
            dve(lambda e: e.tensor_scalar(out=nre, in0=abr, scalar1=-1.0, scalar2=None, op0=ALU.add), ['abr'], ['nre'])
            D2(ta, nre, AR, ALU.mult, ['nre', 'AR'], ['ta'])
            D2(tb, abi, AI, ALU.mult, ['abi', 'AI'], ['tb'])
            D2(ta, ta, tb, ALU.add, ['ta', 'tb'], ['ta'])
            D2(cfr, ta, rden, ALU.mult, ['ta', 'rden'], ['cfr'])
            D2(ta, abi, AR, ALU.mult, ['abi', 'AR'], ['ta'])
            D2(tb, nre, AI, ALU.mult, ['nre', 'AI'], ['tb'])
            D2(ta, ta, tb, ALU.subtract, ['ta', 'tb'], ['ta'])
            D2(cfi, ta, rden, ALU.mult, ['ta', 'rden'], ['cfi'])

            def cmul(o_re, o_im, ok, a_re, a_im, ak, b_re, b_im, bk, t1, t2, tk):
                D2(t1, a_re, b_re, ALU.mult, ak + bk, [tk[0]])
                D2(t2, a_im, b_im, ALU.mult, ak + bk, [tk[1]])
                D2(o_re, t1, t2, ALU.subtract, tk, [ok[0]])
                D2(t1, a_re, b_im, ALU.mult, ak + bk + [ok[0]], [tk[0]])
                D2(t2, a_im, b_re, ALU.mult, ak + bk + [ok[0]], [tk[1]])
                D2(o_im, t1, t2, ALU.add, tk, [ok[1]])

            PWr = A.alloc("PWr", [2, 16, 17]); PWi = A.alloc("PWi", [2, 16, 17])
            pt1 = A.alloc("pt1", [2, 16, 4]); pt2 = A.alloc("pt2", [2, 16, 4])
            dve(lambda e: e.memset(PWr[:, :, :, 8:9], 1.0), [], ['PWr'])
            dve(lambda e: e.memset(PWi[:, :, :, 8:9], 0.0), [], ['PWi'])
            dve(lambda e: e.tensor_copy(out=PWr[:, :, :, 9], in_=abr), ['abr'], ['PWr'])
            dve(lambda e: e.tensor_copy(out=PWi[:, :, :, 9], in_=abi), ['abi'], ['PWi'])
            D2(ta, abr, abr, ALU.mult, ['abr'], ['ta'])
            D2(tb, abi, abi, ALU.mult, ['abi'], ['tb'])
            D2(ta, ta, tb, ALU.add, ['ta', 'tb'], ['ta'])
            dve(lambda e: e.reciprocal(out=tb, in_=ta), ['ta'], ['tb'])
            D2(PWr[:, :, :, 7], abr, tb, ALU.mult, ['abr', 'tb'], ['PWr'])
            dve(lambda e: e.scalar_tensor_tensor(out=PWi[:, :, :, 7], in0=abi, scalar=-1.0, in1=tb, op0=ALU.mult,
                                                 op1=ALU.mult), ['abi', 'tb'], ['PWi'])
            PK = ['PWr', 'PWi']

            def pw_step(o0, o1, i0, i1, m):
                w = o1 - o0
                bre = PWr[:, :, :, m:m + 1].to_broadcast([128, 2, 16, w])
                bim = PWi[:, :, :, m:m + 1].to_broadcast([128, 2, 16, w])
                cmul(PWr[:, :, :, o0:o1], PWi[:, :, :, o0:o1], PK, PWr[:, :, :, i0:i1], PWi[:, :, :, i0:i1], PK,
                     bre, bim, PK, pt1[:, :, :, 0:w], pt2[:, :, :, 0:w], ['pt1', 'pt2'])
            pw_step(10, 11, 9, 10, 9)
            pw_step(11, 13, 9, 11, 10)
            pw_step(13, 17, 9, 13, 12)
            pw_step(6, 7, 7, 8, 7)
            pw_step(4, 6, 6, 8, 6)
            pw_step(0, 4, 4, 8, 4)
            BPr = A.alloc("BPr", [2, 16, 16]); BPi = A.alloc("BPi", [2, 16, 16])
            bt1 = A.alloc("bt1", [2, 16, 16]); bt2 = A.alloc("bt2", [2, 16, 16])
            cmul(BPr, BPi, ['BPr', 'BPi'],
                 cfr.unsqueeze(3).to_broadcast([128, 2, 16, 16]), cfi.unsqueeze(3).to_broadcast([128, 2, 16, 16]),
                 ['cfr', 'cfi'],
                 BR.unsqueeze(1).to_broadcast([128, 2, 16, 16]), BI.unsqueeze(1).to_broadcast([128, 2, 16, 16]),
                 ['BR', 'BI'], bt1, bt2, ['bt1', 'bt2'])
            A.free("bt1"); A.free("bt2")
            for d in range(2):
                for c_ in range(2):
                    dve(lambda e: e.tensor_copy(out=C1[:, :, d, c_], in_=PWr[:, d, :, 16]), ['PWr'], ['C1'])
                dve(lambda e: e.tensor_scalar(out=C2[:, :, d, 0], in0=PWi[:, d, :, 16], scalar1=-1.0, scalar2=None,
                                              op0=ALU.mult), ['PWi'], ['C2'])
                dve(lambda e: e.tensor_copy(out=C2[:, :, d, 1], in_=PWi[:, d, :, 16]), ['PWi'], ['C2'])
            dve(lambda e: e.tensor_scalar(out=C1r, in0=C1, scalar1=rcol[:, 0:1], scalar2=None, op0=ALU.mult),
                ['C1', 'rcol'], ['C1r'])
            dve(lambda e: e.tensor_scalar(out=C2r, in0=C2, scalar1=rcol[:, 0:1], scalar2=None, op0=ALU.mult),
                ['C2', 'rcol'], ['C2r'])

            TTacc = A.alloc("TTacc", [4, 128])
            Wn = [A.alloc("Wn%d" % i, [2, 8, 16]) for i in range(6)]
            wt1 = A.alloc("wt1", [4, 8, 16]); wt2 = A.alloc("wt2", [2, 8, 16])
            WK = ['Wn%d' % i for i in range(6)]
            for qd_ in range(8):
                jsl = slice(2 * qd_, 2 * qd_ + 2)
                for d in range(2):
                    if d == 0:
                        p_in = (slice(15, 7, -1)); p_inv = (slice(7, None, -1)); p_out = slice(9, 17)
                    else:
                        p_in = slice(8, 16); p_inv = slice(0, 8); p_out = slice(16, 8, -1)

                    def pwb(sl, last):
                        return (PWr[:, d, jsl, sl].unsqueeze(3).to_broadcast([128, 2, 8, last]),
                                PWi[:, d, jsl, sl].unsqueeze(3).to_broadcast([128, 2, 8, last]))
                    bpr = BPr[:, d, jsl, :].unsqueeze(2).to_broadcast([128, 2, 8, 16])
                    bpi = BPi[:, d, jsl, :].unsqueeze(2).to_broadcast([128, 2, 8, 16])
                    w1 = wt1[:, 0:2]
                    a_re, a_im = pwb(p_in, 16)
                    cmul(Wn[0], Wn[1], WK[0:2], a_re, a_im, PK, bpr, bpi, ['BPr', 'BPi'], w1, wt2, ['wt1', 'wt2'])
                    a_re, a_im = pwb(p_inv, 16)
                    cmul(Wn[2], Wn[3], WK[2:4], a_re, a_im, PK, bpr, bpi, ['BPr', 'BPi'], w1, wt2, ['wt1', 'wt2'])
                    a_re, a_im = pwb(p_out, 16)
                    cr = CR[:, jsl, :].unsqueeze(2).to_broadcast([128, 2, 8, 16])
                    ci = CI[:, jsl, :].unsqueeze(2).to_broadcast([128, 2, 8, 16])
                    cmul(Wn[4], Wn[5], WK[4:6], a_re, a_im, PK, cr, ci, ['CR', 'CI'], w1, wt2, ['wt1', 'wt2'])
                    dve(lambda e: e.tensor_scalar(out=Wn[5], in0=Wn[5], scalar1=-1.0, scalar2=None, op0=ALU.mult),
                        [WK[5]], [WK[5]])
                    for c2 in range(2):
                        act(lambda e: e.activation(out=WoR[:, d, jsl, c2, :],
                                                   in_=Wn[4 + c2].rearrange("p a b c -> p a (b c)"), func=AF.Copy),
                            [WK[4 + c2]], ['WoR'])
                    ps, pk = bank()
                    for jj in range(2):
                        for c2 in range(2):
                            mm(ps[:, (jj * 2 + c2) * 128:(jj * 2 + c2 + 1) * 128],
                               Wn[c2][:, jj, :, :].rearrange("p b c -> p (b c)"), ident_f[:, :], True, True,
                               [WK[c2], 'ident_f'], [pk], inc=(jj == 1 and c2 == 1))
                    j0 = 2 * qd_
                    for jj in range(2):
                        jg = j0 + jj
                        dst = WinT[:, d, 2 * jg:2 * jg + 2, :].rearrange("p g2 (c n) -> p c g2 n", c=2)
                        evac_copy(jj, dst, ps[:, jj * 256:(jj + 1) * 256].rearrange("p (c g2 n) -> p c g2 n", c=2, g2=2),
                                  [pk], ['WinT'])
                    ps, pk = bank()
                    for gi in range(4):
                        jj = gi // 2
                        g2 = gi % 2
                        rows = slice(g2 * 64, (g2 + 1) * 64)
                        outp = ps[:, gi * 128:(gi + 1) * 128]
                        mm_b(outp, Wn[2][rows, jj, :, :].rearrange("p b c -> p (b c)"),
                             Wn[4][rows, jj, :, :].rearrange("p b c -> p (b c)"), gi == 0, False,
                             [WK[2], WK[4]], [pk], inc=False, skip=True, base=g2 * 64)
                        mm_b(outp, Wn[3][rows, jj, :, :].rearrange("p b c -> p (b c)"),
                             Wn[5][rows, jj, :, :].rearrange("p b c -> p (b c)"), False, True,
                             [WK[3], WK[5]], [pk], inc=(gi == 3), skip=True, base=g2 * 64)
                    g0 = 4 * qd_
                    acc = TTacc[:, 0:4, :]
                    ps3 = ps[:, :].rearrange("p (a b) -> p a b", b=128)
                    mk = s5m[:, d:d + 1, :].to_broadcast([128, 4, 128])
                    if d == 0:
                        D2(acc, ps3, mk, ALU.mult, [pk, 's5m'], ['TTacc'])
                    else:
                        w13 = wt1.rearrange("p a b c -> p a (b c)")
                        D2(w13, ps3, mk, ALU.mult, [pk, 's5m'], ['wt1'])
                        D2(acc, acc, w13, ALU.add, ['TTacc', 'wt1'], ['TTacc'])
                        for gi in range(4):
                            g = g0 + gi
                            dve(lambda e: e.scalar_tensor_tensor(
                                out=TT[:, g, :], in0=ident_f[:, :], scalar=dcol[:, g:g + 1],
                                in1=TTacc[:, gi, :], op0=ALU.mult, op1=ALU.add),
                                ['ident_f', 'dcol', 'TTacc'], ['TT'])
            for nm in ["Wn%d" % i for i in range(6)] + ["wt1", "wt2", "TTacc", "PWr", "PWi", "pt1", "pt2", "BPr", "BPi",
                                                         "CR", "CI", "CNr", "CNi", "BR", "BI", "kI", "s5m", "selc",
                                                         "AR", "AI", "LD", "dcol", "dt_", "mag", "ang", "sn", "cs", "abr",
                                                         "abi", "rden", "nre", "cfr", "cfi", "ta", "tb", "anat", "ldf"]:
                A.free(nm)
            tap("s5_WinT%d" % l, WinT, 'WinT')
            tap("s5_WoR%d" % l, WoR, 'WoR')
            tap("s5_TT%d" % l, TT, 'TT')
            if stop == 'S1':
                return

            Up = A.alloc("Up", [32, 8, 16], BF16)
            UGN = A.alloc("UGN", [32, 128], BF16)
            UGR = [A.alloc("UGR%d" % i, [2, 128], BF16) for i in range(2)]
            VX = A.alloc("VX", [16, 2, 2, 129])
            x0n = A.alloc("x0n", [128])
            slot_su, k_su = load_w(w_in[l][:, C_SU:C_SU + 512], 8, 512)
            for s_ in range(8):
                ps, pk = bank()
                for k in range(8):
                    mm(ps[:, :], hT[:, k, s_::8], slot_su[:, k, :], k == 0, k == 7, ['hT', k_su], [pk])
                evac_copy(s_, Up[:, :, s_, :], ps[:, :].rearrange("p (g q) -> p g q", q=16), [pk], ['Up'])
            tap("s5_Up%d" % l, Up, 'Up')
            if stop == 'S2a':
                return
            cx.dma('sp', x0n[0:64, :], s50_d[l].rearrange("d c (j g2) n -> (d c j) (g2 n)", g2=2), writes=['x0n'], sem='s5x0')
            ps, pk = bank()
            mm(ps[:, 0:64], x0n[0:64, :], ident_f[0:64, 0:64], True, True, ['x0n', 'ident_f'], [pk])
            dve(lambda e: e.tensor_copy(out=VX[:, :, :, :, 0], in_=ps[:, 0:64].rearrange("p (d c j) -> p j d c", d=2, c=2)),
                [pk], ['VX'])
            tap("s5_VX0%d" % l, VX, 'VX')
            if stop == 'S2b':
                return
            for j in range(int(os.environ.get("KDBG_NJ", "16"))):
                ub, ukey = bank()
                ug = UGR[j % 2]
                ugk = 'UGR%d' % (j % 2)
                for g2 in range(2):
                    g = 2 * j + g2
                    src = Up[:, g, :, :].rearrange("p s q -> p (s q)")
                    mm(ub[:, g2 * 128:(g2 + 1) * 128], src, ident_bf[:, :], True, True, ['Up', 'ident_bf'], [ukey], inc=False)
                    mm(ub[:, (2 + g2) * 128:(3 + g2) * 128], src, (ident_bf if os.environ.get("KDBG_J") else jmat_bf)[:, :], True, True, ["Up", "jmat_bf"], [ukey],
                       inc=(g2 == 1))
                SKU = os.environ.get("KDBG_SKIPU", "")
                if 'a' not in SKU:
                    act(lambda e: e.activation(out=UGN[:, 2 * j:2 * j + 2, :],
                                               in_=ub[:, 0:256].rearrange("p (a b) -> p a b", b=128), func=AF.Copy),
                        [ukey], ['UGN'])
                if 'd' not in SKU:
                    dve(lambda e: e.tensor_copy(out=ug, in_=ub[:, 256:512].rearrange("p (a b) -> p a b", b=128)),
                        [ukey], [ugk])
                if stop == 'S2c':
                    continue
                vb, vkey = bank()
                for g2 in range(2):
                    g = 2 * j + g2
                    for d in range(2):
                        rhs = UGN[:, g, :] if d == 0 else ug[:, g2, :]
                        for c2 in range(2):
                            mm(vb[g2 * 64:(g2 + 1) * 64, (2 * d + c2) * 128:(2 * d + c2 + 1) * 128],
                               WinT[:, d, g, c2 * 64:(c2 + 1) * 64], rhs, True, True,
                               ['WinT', 'UGN', ugk], [vkey], inc=(g2 == 1 and d == 1 and c2 == 1))
                evac_copy(j, VX[:, j, :, :, 1:129], vb[:, :].rearrange("p (d c i) -> p d c i", d=2, c=2), [vkey], ['VX'])
            A.free("Up"); A.free("WinT"); A.free("x0n")
            tap("s5_V%d" % l, VX, 'VX')
            if stop in ('S2', 'S2c'):
                return
            ts_ = [A.alloc("ts%d" % i, [16, 2, 2]) for i in range(4)]
            for i in range(128):
                bnd = (i % 32 == 0 and i > 0)
                c1 = (C1r if bnd else C1)
                c2 = (C2r if bnd else C2)
                xp = VX[:, :, :, :, i]
                xsw = VX[:, :, :, ::-1, i]
                cur = VX[:, :, :, :, i + 1]
                t1, t2 = ts_[0], ts_[1]
                for kv in ('VXd0', 'VXd1'):
                    pass
                cx.op('dve', lambda e: tt_(e, t1, xp, c1, ALU.mult), ['VX', 'VXd0', 'C1', 'C1r'], ['ts0'])
                cx.op('dve', lambda e: tt_(e, t2, xsw, c2, ALU.mult), ['VX', 'VXd0', 'C2', 'C2r'], ['ts1'])
                cx.op('dve', lambda e: tt_(e, t1, t1, t2, ALU.add), ['ts0', 'ts1'], ['ts0'])
                cx.op('dve', lambda e: tt_(e, cur, cur, t1, ALU.add), ['VX', 'VXd0', 'ts0'], ['VXd0'])
            cx.lastw['VXd1'] = cx.lastw['VXd0']
            tap("s5_X%d" % l, VX, 'VXd0')
            tap("s5_Xb%d" % l, VX, 'VXd1')
            fst = A.alloc("fst", [4, 128])
            for d in range(2):
                for c2 in range(2):
                    ps, pk = bank()
                    for sgi in range(4):
                        col = 32 * (sgi + 1)
                        mm(ps[0:16, sgi * 128:(sgi + 1) * 128], VX[:, :, d, c2, col], ident_f[:, :], True, True,
                           ['VXd%d' % d, 'ident_f'], [pk], inc=(sgi == 3))
                    dve(lambda e: e.tensor_copy(out=fst[0:16], in_=ps[0:16, :].rearrange("p (a b) -> p a b", b=128)),
                        [pk], ['fst'])
                    for sgi in range(4):
                        seg = sgi if d == 0 else 3 - sgi
                        cx.dma('sp', os5_d[l, seg, d, c2].rearrange("(j g2) n -> j (g2 n)", g2=2), fst[0:16, sgi, :],
                               reads=['fst'], sem='os5')
            XB = A.alloc("XB", [16, 2, 2, 128], BF16)
            act(lambda e: e.activation(out=XB[:, :, 0, :, :], in_=VX[:, :, 0, :, 0:128], func=AF.Copy), ['VXd0', 'VX'], ['XB'])
            dve(lambda e: e.tensor_copy(out=XB[:, :, 1, :, :], in_=VX[:, :, 1, :, 127::-1]), ['VXd1', 'VX'], ['XB'])
            dve(lambda e: e.tensor_scalar(out=XB[:, :, 0, :, 32:128:32], in0=XB[:, :, 0, :, 32:128:32], scalar1=rcol[:, 0:1],
                                          scalar2=None, op0=ALU.mult), ['XB', 'rcol'], ['XB'])
            dve(lambda e: e.tensor_scalar(out=XB[:, :, 1, :, 31:128:32], in0=XB[:, :, 1, :, 31:128:32], scalar1=rcol[:, 0:1],
                                          scalar2=None, op0=ALU.mult), ['XB', 'rcol'], ['XB'])
            A.free("VX"); A.free("fst")
            for i in range(4):
                A.free("ts%d" % i)
            Yp = A.alloc("Yp", [8, 512])
            for qd_ in range(8):
                yb, ykey = bank()
                for gi in range(4):
                    g = 4 * qd_ + gi
                    j, g2 = g // 2, g % 2
                    rows = slice(g2 * 64, (g2 + 1) * 64)
                    outp = yb[:, gi * 128:(gi + 1) * 128]
                    mm(outp, UGN[:, g, :], TT[:, g, :], gi == 0, False, ['UGN', 'TT'], [ykey], inc=False, skip=True)
                    for d in range(2):
                        for c2 in range(2):
                            last = (d == 1 and c2 == 1)
                            mm_b(outp, XB[rows, j, d, c2, :], WoR[rows, d, j, c2, :], False, last, ['XB', 'WoR'], [ykey],
                                 inc=(last and gi == 3), skip=True, base=g2 * 64)
                evac_copy(qd_, Yp[:, :, 64 * qd_:64 * qd_ + 64].rearrange("p s (g c) -> p s g c", c=16),
                          yb[:, :].rearrange("p (g s c) -> p s g c", g=4, s=8), [ykey], ['Yp'])
            tap("s5_Y%d" % l, Yp, 'Yp')
            A.free("XB"); A.free("UGN"); A.free("WoR"); A.free("TT")
            for i in range(2):
                A.free("UGR%d" % i)
            if stop == 'S3':
                return
            Gp = A.alloc("Gp", [8, 512], BF16)
            gt1 = A.alloc("gt1", [2, 512]); gt2 = A.alloc("gt2", [2, 512])
            for q4 in range(4):
                ysl = Yp[:, 2 * q4:2 * q4 + 2, :]
                act(lambda e: e.activation(out=gt1, in_=ysl, func=AF.Square), ['Yp'], ['gt1'])
                dve(lambda e: e.tensor_scalar(out=gt1, in0=gt1, scalar1=0.044715, scalar2=1.0, op0=ALU.mult, op1=ALU.add),
                    ['gt1'], ['gt1'])
                D2(gt1, gt1, ysl, ALU.mult, ['gt1', 'Yp'], ['gt1'])
                act(lambda e: e.activation(out=gt2, in_=gt1, func=AF.Sigmoid, scale=2.0 * math.sqrt(2.0 / PI)),
                    ['gt1'], ['gt2'])
                D2(Gp[:, 2 * q4:2 * q4 + 2, :], ysl, gt2, ALU.mult, ['Yp', 'gt2'], ['Gp'])
            A.free("Yp"); A.free("gt1"); A.free("gt2")
            gT = A.alloc("gT", [4, T], BF16)
            for s_ in range(8):
                ps, pk = bank()
                for ct in range(4):
                    mm(ps[:, ct * 128:(ct + 1) * 128], Gp[:, s_, ct * 128:(ct + 1) * 128], ident_bf[:, :], True, True,
                       ['Gp', 'ident_bf'], [pk], inc=(ct == 3))
                evac_copy(s_, gT[:, :, s_::8], ps[:, :].rearrange("p (a b) -> p a b", b=128), [pk], ['gT'])
            A.free("Gp")
            sgT = A.alloc("sgT", [4, T], BF16)
            slot_sg, k_sg = load_w(w_in[l][:, C_SG:C_SG + 512], 8, 512)
            for m in range(4):
                proj_fm(slot_sg, k_sg, m * 128, 128,
                        lambda ps, pk, th, m=m: act(lambda e: e.activation(
                            out=sgT[:, m, th * 512:(th + 1) * 512], in_=ps[:, :], func=AF.Silu), [pk], ['sgT']))
            bglu = A.alloc("bglu", [8])
            with nc.allow_non_contiguous_dma(reason="tiny bias columns"):
                cx.dma('sp', bglu, s5_b_glu[l].rearrange("(k p) -> p k", p=128), writes=['bglu'], sem='s5c')
            slot_a, k_a = load_w(s5_w_glu[l][:, 0:512], 4, 512)
            slot_b, k_b = load_w(s5_w_glu[l][:, 512:1024], 4, 512)
            sg_ = [A.alloc("sgm%d" % i, [512]) for i in range(2)]
            it = 0
            for m in range(4):
                for th in range(2):
                    tsl = slice(th * 512, (th + 1) * 512)
                    pa, pka = bank()
                    for k in range(4):
                        mm(pa[:, :], slot_a[:, k, m * 128:(m + 1) * 128], gT[:, k, tsl], k == 0, k == 3, [k_a, 'gT'], [pka])
                    pb, pkb = bank()
                    for k in range(4):
                        mm(pb[:, :], slot_b[:, k, m * 128:(m + 1) * 128], gT[:, k, tsl], k == 0, k == 3, [k_b, 'gT'], [pkb])
                    sg = sg_[it % 2]
                    sgk = 'sgm%d' % (it % 2)
                    it += 1
                    act(lambda e: e.activation(out=sg, in_=pb[:, :], func=AF.Sigmoid, bias=bglu[:, 4 + m:5 + m]),
                        [pkb, 'bglu'], [sgk])
                    dve(lambda e: e.scalar_tensor_tensor(out=sg, in0=pa[:, :], scalar=bglu[:, m:m + 1], in1=sg,
                                                         op0=ALU.add, op1=ALU.mult), [pka, 'bglu', sgk], [sgk])
                    D2(oT_br[2][:, m, tsl], sg, sgT[:, m, tsl], ALU.mult, [sgk, 'sgT'], ['obr2'])
            tap("s5_out%d" % l, oT_br[2][:, :, :], 'obr2')

    def merge_out(l):
        with ExitStack() as st:
            mixedT = sb("mixedT", [128, 8, T], BF16, stack=st)
            wbo = sb("wbo", [128, 3, 4, D], BF16, stack=st)
            gx = [sb("gx%d" % i, [128, 512], stack=st) for i in range(3)]
            t1 = sb("mt1", [128, 512], stack=st)
            t2 = sb("mt2", [128, 512], stack=st)
            for x in range(3):
                cx.dma('pool', wbo[:, x, :, :], w_bo[x][l].rearrange("(k p) c -> p k c", p=128), writes=['wbo'],
                       sem='wbo')
            for fg in range(2):
                slots = [load_w(w_in[l][:, C_MERGE + x * D + fg * 512:C_MERGE + x * D + fg * 512 + 512], 8, 512)
                         for x in range(3)]
                for f4 in range(4):
                    f = fg * 4 + f4
                    for th in range(2):
                        tsl = slice(th * 512, (th + 1) * 512)
                        for x in range(3):
                            slot, skey = slots[x]
                            pg, pgk = bank()
                            for k in range(8):
                                mm(pg[:, :], slot[:, k, f4 * 128:(f4 + 1) * 128], hT[:, k, tsl], k == 0, k == 7,
                                   [skey, 'hT'], [pgk])
                            act(lambda e: e.activation(out=gx[x][:, :], in_=pg[:, :], func=AF.Sigmoid), [pgk], ['gx%d' % x])
                            pp, ppk = bank()
                            for k in range(4):
                                mm(pp[:, :], wbo[:, x, k, f * 128:(f + 1) * 128], oT_br[x][:, k, tsl], k == 0, k == 3,
                                   ['wbo', 'obr%d' % x], [ppk])
                            if x == 0:
                                dve(lambda e: e.tensor_tensor(out=t1[:, :], in0=pp[:, :], in1=gx[x][:, :], op=ALU.mult),
                                    [ppk, 'gx0'], ['mt1'])
                            elif x == 1:
                                dve(lambda e: e.tensor_tensor(out=t2[:, :], in0=pp[:, :], in1=gx[x][:, :], op=ALU.mult),
                                    [ppk, 'gx1'], ['mt2'])
                                dve(lambda e: e.tensor_tensor(out=t1[:, :], in0=t1[:, :], in1=t2[:, :], op=ALU.add),
                                    ['mt1', 'mt2'], ['mt1'])
                            else:
                                dve(lambda e: e.tensor_tensor(out=t2[:, :], in0=pp[:, :], in1=gx[x][:, :], op=ALU.mult),
                                    [ppk, 'gx2'], ['mt2'])
                                dve(lambda e: e.tensor_tensor(out=mixedT[:, f, tsl], in0=t1[:, :], in1=t2[:, :], op=ALU.add),
                                    ['mt1', 'mt2'], ['mixedT'])
            tap("mixedT%d" % l, mixedT[:, :, :], 'mixedT')
            for nh in range(2):
                nsl = slice(nh * 512, (nh + 1) * 512)
                slot, skey = load_w(w_out[l][:, nsl], 8, 512)
                for tt in range(8):
                    ps, pk = bank()
                    for k in range(8):
                        mm(ps[:, :], mixedT[:, k, tt * 128:(tt + 1) * 128], slot[:, k, :], k == 0, k == 7,
                           ['mixedT', skey], [pk])
                    tm = t1 if tt % 2 == 0 else t2
                    tk = 'mt1' if tt % 2 == 0 else 'mt2'
                    dve(lambda e: e.tensor_tensor(out=tm[:, :], in0=ps[:, :], in1=gate_bc[:, nsl], op=ALU.mult),
                        [pk, 'gate_bc'], [tk])
                    dve(lambda e: e.tensor_tensor(out=x_sb[:, tt, nsl], in0=x_sb[:, tt, nsl], in1=tm[:, :], op=ALU.add),
                        ['x', tk], ['x'])
            tap("xout%d" % l, x_sb[:, :, :], 'x')

    for l in range(DEPTH):
        with ExitStack() as st:
            shift_bc = sb("shift_bc", [128, D], stack=st)
            wmod = sb("wmod", [128, D], stack=st)
            bada = sb("bada", [128, 3 * D], stack=st)
            nw_bc = sb("nw_bc", [128, D], stack=st)
            cond_c = sb("cond_c", [128, 8], stack=st)
            scb = sb("scb", [128, 8, 128], BF16, stack=st)
            with nc.allow_non_contiguous_dma(reason="tiny cond column load"):
                cx.dma('sp', cond_c[:, :], cond_d.rearrange("(k p) -> p k", p=128), writes=['cond_c'], sem='c2')
            cx.dma('sp', bada[:, :], b_ada[l].partition_broadcast(128), writes=['bada'], sem='c2')
            cx.dma('sp', nw_bc[:, :], norm_w[l].partition_broadcast(128), writes=['nw_bc'], sem='c2')
            act(lambda e: e.activation(out=cond_c[:, :], in_=cond_c[:, :], func=AF.Silu), ['cond_c'], ['cond_c'])
            dve(lambda e: e.tensor_copy(out=scb[:, :, :], in_=cond_c[:, :].unsqueeze(2).to_broadcast([128, 8, 128])),
                ['cond_c'], ['scb'])
            for ci in range(6):
                slot, skey = load_w(w_ada[l][:, ci * 512:(ci + 1) * 512], 8, 512)
                ps, pk = bank()
                for k in range(8):
                    mm(ps[:, :], scb[:, k, :], slot[:, k, :], k == 0, k == 7, ['scb', skey], [pk])
                dst = (shift_bc, wmod, gate_bc)[ci // 2]
                dkey = ('shift_bc', 'wmod', 'gate_bc')[ci // 2]
                cs = slice((ci % 2) * 512, (ci % 2 + 1) * 512)
                bsl = bada[:, ci * 512:(ci + 1) * 512]
                if ci // 2 == 1:
                    dve(lambda e, ps=ps, dst=dst, cs=cs, bsl=bsl: e.scalar_tensor_tensor(
                        out=dst[:, cs], in0=ps[:, :], scalar=1.0, in1=bsl, op0=ALU.add, op1=ALU.add),
                        [pk, 'bada'], [dkey])
                else:
                    dve(lambda e, ps=ps, dst=dst, cs=cs, bsl=bsl: e.tensor_tensor(
                        out=dst[:, cs], in0=ps[:, :], in1=bsl, op=ALU.add), [pk, 'bada'], [dkey])
            dve(lambda e: e.tensor_tensor(out=wmod[:, :], in0=wmod[:, :], in1=nw_bc[:, :], op=ALU.mult),
                ['wmod', 'nw_bc'], ['wmod'])

            ss = sb("ss_x", [128, 8], stack=st)
            rs = sb("rs_x", [128, 8], stack=st)
            tmp8 = sb("tmp8", [128, 8], stack=st)
            junk = sb("junk_x", [128, D], stack=st)
            hb = [sb("hb%d" % i, [128, D], BF16, stack=st) for i in range(2)]
            tmpf = sb("tmpf", [128, D], stack=st)
            dve(lambda e: e.memset(ss[:, :], 0.0), [], ['ss_x'])
            for tt in range(8):
                act(lambda e, tt=tt: e.activation(out=junk[:, :], in_=x_sb[:, tt, :], func=AF.Square,
                                                  accum_out=ss[:, tt:tt + 1]), ['x', 'ss_x'], ['junk_x', 'ss_x'])
            rstd_from_ss(ss[:, :], rs[:, :], D, 'ss_x', 'rs_x', tmp8[:, :], 'tmp8')
            for tt in range(8):
                hbt = hb[tt % 2]
                hk = 'hb%d' % (tt % 2)
                dve(lambda e, tt=tt: e.scalar_tensor_tensor(out=tmpf[:, :], in0=x_sb[:, tt, :], scalar=rs[:, tt:tt + 1],
                                                            in1=wmod[:, :], op0=ALU.mult, op1=ALU.mult),
                    ['x', 'rs_x', 'wmod'], ['tmpf'])
                dve(lambda e, hbt=hbt: e.tensor_tensor(out=hbt[:, :], in0=tmpf[:, :], in1=shift_bc[:, :], op=ALU.add),
                    ['tmpf', 'shift_bc'], [hk])
                for half in range(2):
                    ps, pk = bank()
                    for kk in range(4):
                        k = half * 4 + kk
                        mm(ps[:, kk * 128:(kk + 1) * 128], hbt[:, k * 128:(k + 1) * 128], ident_bf[:, :], True, True,
                           [hk, 'ident_bf'], [pk], inc=(kk == 3))
                    act(lambda e, ps=ps, half=half, tt=tt: e.activation(
                        out=hT[:, half * 4:half * 4 + 4, tt * 128:(tt + 1) * 128],
                        in_=ps[:, :].rearrange("p (a b) -> p a b", b=128), func=AF.Copy), [pk], ['hT'])

            tap("hT%d" % l, hT[:, :, :], 'hT')
            tap("gate%d" % l, gate_bc[:, :], 'gate_bc')
        if stop == 'B':
            break
        branch_gla(l)
        if stop in ('GLA', 'G1', 'G2', 'G3'):
            break
        branch_mla(l)
        if stop in ('MLA', 'M1'):
            break
        branch_s5(l)
        if stop in ('S5', 'S1', 'S2', 'S3', 'S2a', 'S2b', 'S2c'):
            break
        merge_out(l)
        if stop == 'L0':
            break

    cx.dma('sp', y_d.rearrange("(t p) d -> p t d", p=128), x_sb[:, :, :], reads=['x'], sem='yout')
    cx.final_wait()
    return cx


def _rope_tables():
    rows = T // 64
    r = np.repeat(np.arange(rows, dtype=np.float32), 64)
    col = np.tile(np.arange(64, dtype=np.float32), rows)
    n_freq = 8
    inv = (np.float32(10000.0) ** (-np.arange(n_freq, dtype=np.float32) / np.float32(n_freq))).astype(np.float32)
    ang = np.concatenate([r[:, None] * inv, col[:, None] * inv], axis=-1).astype(np.float32)
    return np.cos(ang).astype(np.float32), np.sin(ang).astype(np.float32)


def _constants():
    c = {}
    c["ident"] = np.eye(128, dtype=np.float32)
    c["jmat"] = np.eye(128, dtype=np.float32)[::-1].copy()
    s_idx = np.arange(128)[:, None]
    t_idx = np.arange(128)[None, :]
    same = (s_idx // 64) == (t_idx // 64)
    c["gla_masks"] = np.stack([(same & (s_idx <= t_idx)), (same & (s_idx >= t_idx))]).astype(np.float32)
    cm = np.ones((128, T), np.float32)
    cm[:, ::64] = 0.0
    c["chunk_mask"] = cm
    sp_ = (np.arange(128) // 16)[:, None]
    s_ = (np.arange(128) // 16)[None, :]
    c["s5_masks"] = np.stack([(s_ >= sp_), (sp_ >= s_)]).astype(np.float32)
    sel = np.zeros((2, 128, 64), np.float32)
    for g2 in range(2):
        for jl in range(4):
            for pch in range(16):
                sel[g2, (2 * jl + g2) * 16 + pch, jl * 16 + pch] = 1.0
    c["sel_c"] = sel
    return c


_W_NAMES = ["norm_w", "w_ada", "b_ada", "w_in", "gla_w_a2", "gla_b_a", "gla_o_norm", "mla_q_norm", "mla_w_uq",
            "mla_kv_norm", "mla_w_uk", "mla_w_uv", "mla_qh_norm", "mla_kh_norm", "s5_a_re", "s5_a_im", "s5_log_dt",
            "s5_b_re", "s5_b_im", "s5_c_re", "s5_c_im", "s5_d", "s5_w_glu", "s5_b_glu", "w_bo_gla", "w_bo_mla",
            "w_bo_s5", "w_out"]


def make_in_maps(inp):
    f = lambda a: np.ascontiguousarray(np.asarray(a, dtype=np.float32))
    consts = _constants()
    cos, sin = _rope_tables()
    weights = {k: f(inp[k]) for k in _W_NAMES}
    maps = []
    for core in range(8):
        m = dict(weights)
        m.update(consts)
        qm = np.zeros((5, T), np.float32)
        km = np.zeros((5, NKEY), np.float32)
        qm[4, :] = 1.0
        km[4, :] = -MASK_BIG
        if core < 4:
            b = core
            m["x"] = f(inp["x_sample"][b])
            m["cond"] = f(inp["c"][b])
            m["ckv_c"] = f(inp["cache_mla_ckv"][b])
            m["kr_c"] = f(inp["cache_mla_krope"][b])
            m["sg0"] = f(inp["state_gla"][b])
            m["s50"] = f(inp["state_s5"][b])
            m["rope_cos"], m["rope_sin"] = cos, sin
            qm[0, :] = 1.0
            km[0, :] = MASK_BIG
            m["rcol"] = np.ones((128, 1), np.float32)
        else:
            j = core - 4
            m["x"] = f(np.asarray(inp["x_prompt"])[4 * j:4 * j + 4].reshape(T, D))
            m["cond"] = f(inp["c_ctx"])
            m["ckv_c"] = np.zeros((DEPTH, PAST, 256), np.float32)
            m["kr_c"] = np.zeros((DEPTH, PAST, 32), np.float32)
            m["sg0"] = np.zeros((DEPTH, 2, 4, 64, 128), np.float32)
            m["s50"] = np.zeros((DEPTH, 2, 2, 32, 64), np.float32)
            m["rope_cos"] = np.ones((T, 16), np.float32)
            m["rope_sin"] = np.zeros((T, 16), np.float32)
            for s in range(4):
                qm[s, s * 256:(s + 1) * 256] = 1.0
                km[s, PAST + s * 256:PAST + (s + 1) * 256] = MASK_BIG
            m["rcol"] = np.zeros((128, 1), np.float32)
        m["qmask"], m["kmask"] = qm, km
        maps.append(m)
    return maps


def kernel(**inputs):
    nc = build_program()
    maps = make_in_maps(inputs)
    res = run_bass_kernel_spmd(nc, maps, core_ids=list(range(8)))
    r = res.results
    y_sample = np.stack([r[b]["y"] for b in range(4)]).astype(np.float32)
    y_prompt = np.concatenate([r[4 + j]["y"].reshape(4, 256, D) for j in range(4)]).astype(np.float32)
    ckv = np.concatenate([r[4 + j]["o_ckv"].reshape(DEPTH, 4, 256, 256).transpose(1, 0, 2, 3) for j in range(4)])
    kr = np.concatenate([r[4 + j]["o_kr"].reshape(DEPTH, 4, 256, 32).transpose(1, 0, 2, 3) for j in range(4)])
    gla = np.concatenate([r[4 + j]["o_gla"].transpose(1, 0, 2, 3, 4, 5) for j in range(4)])
    s5 = np.concatenate([r[4 + j]["o_s5"].transpose(1, 0, 2, 3, 4, 5) for j in range(4)])
    return (y_prompt, y_sample, ckv.astype(np.float32), kr.astype(np.float32), gla.astype(np.float32),
            s5.astype(np.float32))
```

```python
import math
from contextlib import ExitStack

import numpy as np
import concourse.bass as bass
import concourse.mybir as mybir
from concourse.bass_utils import run_bass_kernel_spmd

F32 = mybir.dt.float32
BF16 = mybir.dt.bfloat16
I32 = mybir.dt.int32
ALU = mybir.AluOpType
AF = mybir.ActivationFunctionType
AX = mybir.AxisListType

D = 1024
T = 1024
DEPTH = 2
EPS = 1e-6
PAST = 512
NKEY = PAST + T
D_IN = 6848
C_GQ, C_GK, C_GV, C_GA, C_GG, C_MQ, C_MKV, C_MKR, C_MG, C_SU, C_SG, C_MERGE = (
    0, 256, 512, 1024, 1056, 1568, 1952, 2208, 2240, 2752, 3264, 3776)
MASK_BIG = 2048.0
ATT_SCALE = 96 ** -0.5
PI = math.pi


class Ctx:
    def __init__(self, nc, es):
        self.nc = nc
        self.es = es
        self.eng = {'pe': nc.tensor, 'act': nc.scalar, 'dve': nc.vector, 'pool': nc.gpsimd, 'sp': nc.sync}
        self.sems = {}
        self.cnt = {}
        for e in ('pe', 'act', 'dve', 'pool'):
            self.sems[e] = es.enter_context(nc.semaphore("s_" + e))
            self.cnt[e] = 0
        self.seen = {e: {} for e in self.eng}
        self.lastw = {}
        self.readers = {}
        self.n_ops = 0
        self.bank_rr = 0
        self.fresh = {}

    def _collect(self, reads, writes):
        toks = {}

        def add(t):
            if t is None:
                return
            s, v = t
            if toks.get(s, 0) < v:
                toks[s] = v
        for k in list(reads) + list(writes):
            snap = self.fresh.pop(k, None)
            if snap is not None:
                for s_, v_ in snap.items():
                    if v_ > 0:
                        add((s_, v_))
                self.lastw.pop(k, None)
                self.readers.pop(k, None)
        for k in reads:
            add(self.lastw.get(k))
            if isinstance(k, tuple) and k[0] == 'ps':
                for s, v in self.readers.get(k, {}).items():
                    add((s, v))
        for k in writes:
            add(self.lastw.get(k))
            for s, v in self.readers.get(k, {}).items():
                add((s, v))
        return toks

    def _emit_waits(self, e, toks, skip_own=False, attach=False):
        eng = self.eng[e]
        seen = self.seen[e]
        need = []
        for s, v in toks.items():
            if skip_own and s == e:
                continue
            if s not in self.eng:
                v = max(v, self.cnt[s])
            if seen.get(s, 0) >= v:
                continue
            need.append((s, v))
            seen[s] = v
        last = None
        if attach and need:
            last = need.pop()
        for s, v in need:
            eng.wait_ge(self.sems[s], v)
        return last

    def _record(self, tok, reads, writes):
        s, v = tok
        for k in writes:
            self.lastw[k] = tok
            self.readers[k] = {}
        for k in reads:
            r = self.readers.setdefault(k, {})
            if r.get(s, 0) < v:
                r[s] = v

    def op(self, e, fn, reads=(), writes=(), inc=True):
        toks = self._collect(reads, writes)
        last = self._emit_waits(e, toks, skip_own=(e == 'pe'), attach=True)
        ins = fn(self.eng[e])
        if last is not None:
            ins._wait_ge(self.sems[last[0]], last[1])
        tok = (e, self.cnt[e] + 1)
        if inc:
            self.cnt[e] += 1
            ins.then_inc(self.sems[e], 1)
        self._record(tok, reads, writes)
        self.n_ops += 1
        return ins

    def dma(self, q, out, in_, reads=(), writes=(), sem=None):
        assert sem is not None
        if sem not in self.sems:
            self.sems[sem] = self.es.enter_context(self.nc.semaphore("d_%d" % len(self.sems)))
            self.cnt[sem] = 0
        toks = self._collect(reads, writes)
        self._emit_waits(q, toks)
        ins = self.eng[q].dma_start(out=out, in_=in_)
        self.cnt[sem] += 16
        ins.then_inc(self.sems[sem], 16)
        self._record((sem, self.cnt[sem]), reads, writes)
        self.n_ops += 1

    def final_wait(self):
        toks = {s: v for s, v in self.cnt.items() if v > 0}
        self._emit_waits('sp', toks)


def build_program(dbg=None, stop=None):
    nc = bass.Bass("TRN2", target_bir_lowering=False)
    es = ExitStack()
    with es:
        cx = _build(nc, es, dbg or {}, stop)
    nc._n_ops = cx.n_ops
    return nc


def _build(nc, es, dbg, stop):
    cx = Ctx(nc, es)

    def tap(name, ap, key):
        if name not in dbg:
            return
        d = nc.dram_tensor("dbg_" + name, list(ap.shape), ap.dtype, kind="ExternalOutput").ap()
        cx.dma('sp', d, ap, reads=[key], sem='dbg')

    def din(name, shape, dt=F32):
        return nc.dram_tensor(name, list(shape), dt, kind="ExternalInput").ap()

    def dout(name, shape, dt=F32):
        return nc.dram_tensor(name, list(shape), dt, kind="ExternalOutput").ap()

    x_d = din("x", [T, D])
    cond_d = din("cond", [D])
    ckvc_d = din("ckv_c", [DEPTH, PAST, 256])
    krc_d = din("kr_c", [DEPTH, PAST, 32])
    sg0_d = din("sg0", [DEPTH, 2, 4, 64, 128])
    s50_d = din("s50", [DEPTH, 2, 2, 32, 64])
    cos_d = din("rope_cos", [T, 16])
    sin_d = din("rope_sin", [T, 16])
    qmask_d = din("qmask", [5, T])
    kmask_d = din("kmask", [5, NKEY])
    rcol_d = din("rcol", [128, 1])
    ident_d = din("ident", [128, 128])
    jmat_d = din("jmat", [128, 128])
    glam_d = din("gla_masks", [2, 128, 128])
    cmask_d = din("chunk_mask", [128, T])
    s5m_d = din("s5_masks", [2, 128, 128])
    selc_d = din("sel_c", [2, 128, 64])
    norm_w = din("norm_w", [DEPTH, D])
    w_ada = din("w_ada", [DEPTH, D, 3 * D])
    b_ada = din("b_ada", [DEPTH, 3 * D])
    w_in = din("w_in", [DEPTH, D, D_IN])
    gla_w_a2 = din("gla_w_a2", [DEPTH, 2, 16, 256])
    gla_b_a = din("gla_b_a", [DEPTH, 2, 256])
    gla_o_norm = din("gla_o_norm", [DEPTH, 128])
    mla_q_norm = din("mla_q_norm", [DEPTH, 384])
    mla_w_uq = din("mla_w_uq", [DEPTH, 384, 384])
    mla_kv_norm = din("mla_kv_norm", [DEPTH, 256])
    mla_w_uk = din("mla_w_uk", [DEPTH, 256, 256])
    mla_w_uv = din("mla_w_uv", [DEPTH, 256, 512])
    mla_qh_norm = din("mla_qh_norm", [DEPTH, 96])
    mla_kh_norm = din("mla_kh_norm", [DEPTH, 96])
    s5_a_re = din("s5_a_re", [DEPTH, 2, 32, 64])
    s5_a_im = din("s5_a_im", [DEPTH, 2, 32, 64])
    s5_log_dt = din("s5_log_dt", [DEPTH, 2, 32])
    s5_b_re = din("s5_b_re", [DEPTH, 32, 64, 16])
    s5_b_im = din("s5_b_im", [DEPTH, 32, 64, 16])
    s5_c_re = din("s5_c_re", [DEPTH, 32, 16, 64])
    s5_c_im = din("s5_c_im", [DEPTH, 32, 16, 64])
    s5_d = din("s5_d", [DEPTH, 512])
    s5_w_glu = din("s5_w_glu", [DEPTH, 512, 1024])
    s5_b_glu = din("s5_b_glu", [DEPTH, 1024])
    w_bo = [din("w_bo_gla", [DEPTH, 512, D]), din("w_bo_mla", [DEPTH, 512, D]), din("w_bo_s5", [DEPTH, 512, D])]
    w_out = din("w_out", [DEPTH, D, D])

    y_d = dout("y", [T, D])
    ockv_d = dout("o_ckv", [DEPTH, T, 256])
    okr_d = dout("o_kr", [DEPTH, T, 32])
    ogla_d = dout("o_gla", [DEPTH, 4, 2, 4, 64, 128])
    os5_d = dout("o_s5", [DEPTH, 4, 2, 2, 32, 64])

    uniq = {'n': 0}

    def sb(name, shape, dt=F32, stack=None):
        uniq['n'] += 1
        if stack is not None:
            cx.fresh[name] = dict(cx.cnt)
        return (stack or es).enter_context(nc.sbuf_tensor("sb%d_%s" % (uniq['n'], name), list(shape), dt))

    psb = [es.enter_context(nc.psum_tensor("psb%d" % i, [128, 512], F32)) for i in range(8)]

    reserved = set()

    def bank():
        while cx.bank_rr in reserved:
            cx.bank_rr = (cx.bank_rr + 1) % 8
        i = cx.bank_rr
        cx.bank_rr = (i + 1) % 8
        return psb[i], ('ps', i)

    def reserve_bank():
        ps, pk = bank()
        reserved.add(pk[1])
        return ps, pk

    def release_bank(pk):
        reserved.discard(pk[1])

    x_sb = sb("x_sb", [128, 8, D])
    hT = sb("hT", [128, 8, T], BF16)
    gate_bc = sb("gate_bc", [128, D])
    ident_bf = sb("ident_bf", [128, 128], BF16)
    jmat_bf = sb("jmat_bf", [128, 128], BF16)
    ident_f = sb("ident_f", [128, 128])
    ones_bf = sb("ones_bf", [128, 128], BF16)
    glam = sb("glam", [128, 2, 128])
    cmask = sb("cmask", [128, T])
    ropec = sb("ropec", [128, 8, 16])
    ropes = sb("ropes", [128, 8, 16])
    rcol = sb("rcol", [128, 1])
    NSLOT = 3
    wring = [sb("wring%d" % i, [128, 8, 512], BF16) for i in range(NSLOT)]
    oT_br = [sb("obr%d" % i, [128, 4, T], BF16) for i in range(3)]

    ring_state = {'i': 0}

    def load_w(src_ap, kt, ncol):
        i = ring_state['i']
        ring_state['i'] = (i + 1) % NSLOT
        slot = wring[i]
        key = ('wring', i)
        cx.dma('pool', slot[:, 0:kt, 0:ncol], src_ap.rearrange("(k p) c -> p k c", p=128),
               writes=[key], sem='wring%d' % i)
        return slot, key

    cx.dma('sp', x_sb[:, :, :], x_d.rearrange("(t p) d -> p t d", p=128), writes=['x'], sem='x')
    cx.dma('pool', ident_bf[:, :], ident_d[:, :], writes=['ident_bf'], sem='c0')
    cx.dma('pool', jmat_bf[:, :], jmat_d[:, :], writes=['jmat_bf'], sem='c0')
    cx.dma('sp', ident_f[:, :], ident_d[:, :], writes=['ident_f'], sem='c1')
    cx.dma('sp', glam[:, :, :], glam_d.rearrange("a p c -> p a c"), writes=['glam'], sem='c1')
    cx.dma('sp', cmask[:, :], cmask_d[:, :], writes=['cmask'], sem='c1')
    cx.dma('sp', ropec[:, :, :], cos_d.rearrange("(t p) c -> p t c", p=128), writes=['ropec'], sem='c1')
    cx.dma('sp', ropes[:, :, :], sin_d.rearrange("(t p) c -> p t c", p=128), writes=['ropes'], sem='c1')
    cx.dma('sp', rcol[:, :], rcol_d[:, :], writes=['rcol'], sem='c1')
    cx.op('dve', lambda e: e.memset(ones_bf[:, :], 1.0), writes=['ones_bf'])

    def act(fn, reads, writes):
        return cx.op('act', fn, reads, writes)

    def dve(fn, reads, writes):
        return cx.op('dve', fn, reads, writes)

    def pool(fn, reads, writes):
        return cx.op('pool', fn, reads, writes)

    def mm(out, lhsT, rhs, start, stop, reads, writes, inc=None, skip=False):
        if inc is None:
            inc = stop
        if skip:
            return cx.op('pe', lambda e: e.matmul(out, lhsT, rhs, start=start, stop=stop, skip_group_check=True),
                         reads, writes, inc=inc)
        return cx.op('pe', lambda e: e.matmul(out, lhsT, rhs, start=start, stop=stop), reads, writes, inc=inc)

    def mm_b(out, lhsT, rhs, start, stop, reads, writes, inc=None, skip=False, base=0):
        if inc is None:
            inc = stop
        if base == 0:
            return mm(out, lhsT, rhs, start, stop, reads, writes, inc=inc, skip=skip)
        mm(out[0:64], lhsT[:, 0:64], rhs, start, stop, reads, writes, inc=False, skip=skip)
        return mm(out[64:128], lhsT[:, 64:128], rhs, start, stop, reads, writes, inc=inc, skip=skip)

    def rstd_from_ss(ss_ap, out_ap, n, key_in, key_out, tmp_ap, key_tmp):
        act(lambda e: e.activation(out=tmp_ap, in_=ss_ap, func=AF.Sqrt, bias=EPS, scale=1.0 / n),
            [key_in], [key_tmp])
        dve(lambda e: e.reciprocal(out=out_ap, in_=tmp_ap), [key_tmp], [key_out])

    def proj_fm(slot, skey, c0, m, evac, kt=8, rhsT=None, rkey='hT'):
        src = hT if rhsT is None else rhsT
        for th in range(2):
            ps, pk = bank()
            for k in range(kt):
                mm(ps[0:m, :], slot[:, k, c0:c0 + m], src[:, k, th * 512:(th + 1) * 512],
                   k == 0, k == kt - 1, [skey, rkey], [pk])
            evac(ps, pk, th)

    def evac_copy(i, out_ap, in_ap, rkeys, wkeys):
        if i % 2 == 0:
            act(lambda e: e.activation(out=out_ap, in_=in_ap, func=AF.Copy), rkeys, wkeys)
        else:
            dve(lambda e: e.tensor_copy(out=out_ap, in_=in_ap), rkeys, wkeys)

    def branch_gla(l):
        with ExitStack() as st:
            v_tm = sb("v_tm", [128, 8, 512], BF16, stack=st)
            ggT = sb("ggT", [128, 4, T], BF16, stack=st)
            alow = sb("alow", [32, T], BF16, stack=st)
            oT = sb("oT", [128, 4, T], F32, stack=st)
            wa2p = sb("wa2p", [32, 2, 256], BF16, stack=st)
            ba = sb("ba", [128, 2, 2], F32, stack=st)
            onw = sb("onw", [128, 1], stack=st)
            wsm = sb("wsm", [128, 8, 32], BF16, stack=st)
            dve(lambda e: e.memset(wa2p[:, :, :], 0.0), [], ['wa2p'])
            dve(lambda e: e.memset(oT[:, :, :], 0.0), [], ['oT'])

            cx.dma('pool', wa2p[0:16, 0, :], gla_w_a2[l, 0], writes=['wa2p'], sem='gsm')
            cx.dma('pool', wa2p[16:32, 1, :], gla_w_a2[l, 1], writes=['wa2p'], sem='gsm')
            cx.dma('pool', wsm[:, :, :], w_in[l][:, C_GA:C_GA + 32].rearrange("(k p) c -> p k c", p=128),
                   writes=['wsm'], sem='gsm')
            with nc.allow_non_contiguous_dma(reason="tiny bias columns"):
                cx.dma('sp', ba[:, :, :], gla_b_a[l].rearrange("d (hp p) -> p d hp", p=128), writes=['ba'], sem='gsm2')
                cx.dma('sp', onw[:, :], gla_o_norm[l].rearrange("(p o) -> p o", o=1), writes=['onw'], sem='gsm2')
            dve(lambda e: e.tensor_scalar(out=ba[:, :, :], in0=ba[:, :, :], scalar1=-1.0, scalar2=None, op0=ALU.mult),
                ['ba'], ['ba'])
            slot_qk, k_qk = load_w(w_in[l][:, C_GQ:C_GQ + 512], 8, 512)
            slot_v, k_v = load_w(w_in[l][:, C_GV:C_GV + 512], 8, 512)
            for tt in range(8):
                ps, pk = bank()
                for k in range(8):
                    mm(ps[:, :], hT[:, k, tt * 128:(tt + 1) * 128], slot_v[:, k, :], k == 0, k == 7, ['hT', k_v], [pk])
                evac_copy(tt, v_tm[:, tt, :], ps[:, :], [pk], ['v_tm'])
            slot_gg, k_gg = load_w(w_in[l][:, C_GG:C_GG + 512], 8, 512)
            proj_fm(wsm, 'wsm', 0, 32,
                    lambda ps, pk, th: act(lambda e: e.activation(out=alow[0:32, th * 512:(th + 1) * 512], in_=ps[0:32, :],
                                                                  func=AF.Copy), [pk], ['alow']))
            for m in range(4):
                proj_fm(slot_gg, k_gg, m * 128, 128,
                        lambda ps, pk, th, m=m: act(lambda e: e.activation(
                            out=ggT[:, m, th * 512:(th + 1) * 512], in_=ps[:, :], func=AF.Silu), [pk], ['ggT']))

            tap("g1_ggT", ggT[:, :, :], 'ggT')
            if stop == 'G1':
                return
            for hp in range(2):
                with ExitStack() as s2:
                    q_f = sb("q_f", [128, T], stack=s2)
                    k_f = sb("k_f", [128, T], stack=s2)
                    SP = sb("SP", [128, T], stack=s2)
                    BC = sb("BC", [128, T], stack=s2)
                    E = sb("E", [128, T], stack=s2)
                    TM = sb("TM", [128, T], stack=s2)
                    qd = [sb("qd%d" % d, [128, T], BF16, stack=s2) for d in range(2)]
                    kd = [sb("kd%d" % d, [128, T], BF16, stack=s2) for d in range(2)]
                    kr = [sb("kr%d" % d, [128, T], BF16, stack=s2) for d in range(2)]
                    krtok = [sb("krtok%d" % d, [128, 8, 128], BF16, stack=s2) for d in range(2)]
                    gdec = [sb("gdec%d" % d, [128, 16], stack=s2) for d in range(2)]
                    S = [sb("S%d" % d, [128, 128], stack=s2) for d in range(2)]
                    Sb = [sb("Sb%d" % d, [128, 128], BF16, stack=s2) for d in range(2)]
                    stg = [sb("stg%d" % i, [128, 128], stack=s2) for i in range(2)]
                    attsb = [sb("attsb%d" % i, [128, 2, 128], BF16, stack=s2) for i in range(2)]
                    proj_fm(slot_qk, k_qk, hp * 128, 128,
                            lambda ps, pk, th: act(lambda e: e.mul(out=q_f[:, th * 512:(th + 1) * 512], in_=ps[:, :],
                                                                   mul=0.125), [pk], ['q_f']))
                    proj_fm(slot_qk, k_qk, 256 + hp * 128, 128,
                            lambda ps, pk, th: dve(lambda e: e.tensor_copy(out=k_f[:, th * 512:(th + 1) * 512],
                                                                            in_=ps[:, :]), [pk], ['k_f']))
                    for d in range(2):
                        cx.dma('sp', S[d][:, :], sg0_d[l, d, 2 * hp:2 * hp + 2].rearrange("h k v -> (h k) v"),
                               writes=['S%d' % d], sem='gS%d' % d)
                        act(lambda e, d=d: e.activation(out=Sb[d][:, :], in_=S[d][:, :], func=AF.Copy),
                            ['S%d' % d], ['Sb%d' % d])
                        for th in range(2):
                            ps, pk = bank()
                            mm(ps[:, :], wa2p[:, d, hp * 128:(hp + 1) * 128], alow[0:32, th * 512:(th + 1) * 512],
                               True, True, ['wa2p', 'alow'], [pk])
                            act(lambda e, ps=ps, th=th, d=d: e.activation(
                                out=E[:, th * 512:(th + 1) * 512], in_=ps[:, :], func=AF.Exp,
                                bias=ba[:, d, hp:hp + 1], scale=-1.0), [pk, 'ba'], ['E'])
                        act(lambda e: e.activation(out=SP[:, :], in_=E[:, :], func=AF.Ln, bias=1.0), ['E'], ['SP'])
                        dve(lambda e: e.tensor_tensor_scan(out=BC[:, :], data0=cmask[:, :], data1=SP[:, :], initial=0.0,
                                                           op0=ALU.mult, op1=ALU.add), ['cmask', 'SP'], ['BC'])
                        act(lambda e, d=d: e.activation(out=gdec[d][:, :], in_=BC[:, 63::64], func=AF.Exp,
                                                        scale=-1.0 / 16), ['BC'], ['gdec%d' % d])
                        BC3 = BC[:, :].rearrange("p (c j) -> p c j", j=64)
                        BL = BC3[:, :, 63:64].to_broadcast([128, 16, 64])
                        TM3 = TM[:, :].rearrange("p (c j) -> p c j", j=64)
                        SP3 = SP[:, :].rearrange("p (c j) -> p c j", j=64)
                        kq, kk, kkr = 'qd%d' % d, 'kd%d' % d, 'kr%d' % d
                        if d == 0:
                            src = BC
                            skey = 'BC'
                        else:
                            dve(lambda e: e.tensor_tensor(out=TM3, in0=BL, in1=BC3, op=ALU.subtract), ['BC'], ['TM'])
                            dve(lambda e: e.tensor_tensor(out=TM[:, :], in0=TM[:, :], in1=SP[:, :], op=ALU.add),
                                ['TM', 'SP'], ['TM'])
                            src = TM
                            skey = 'TM'
                        act(lambda e, src=src: e.activation(out=E[:, :], in_=src[:, :], func=AF.Exp, scale=-1.0 / 16),
                            [skey], ['E'])
                        dve(lambda e, d=d: e.tensor_tensor(out=qd[d][:, :], in0=q_f[:, :], in1=E[:, :], op=ALU.mult),
                            ['q_f', 'E'], [kq])
                        act(lambda e, src=src: e.activation(out=E[:, :], in_=src[:, :], func=AF.Exp, scale=1.0 / 16),
                            [skey], ['E'])
                        dve(lambda e, d=d: e.tensor_tensor(out=kd[d][:, :], in0=k_f[:, :], in1=E[:, :], op=ALU.mult),
                            ['k_f', 'E'], [kk])
                        if d == 0:
                            dve(lambda e: e.tensor_tensor(out=TM3, in0=BL, in1=BC3, op=ALU.subtract), ['BC'], ['TM'])
                        else:
                            dve(lambda e: e.tensor_tensor(out=TM[:, :], in0=BC[:, :], in1=SP[:, :], op=ALU.subtract),
                                ['BC', 'SP', 'TM'], ['TM'])
                        act(lambda e: e.activation(out=E[:, :], in_=TM[:, :], func=AF.Exp, scale=-1.0 / 16),
                            ['TM'], ['E'])
                        dve(lambda e, d=d: e.tensor_tensor(out=kr[d][:, :], in0=k_f[:, :], in1=E[:, :], op=ALU.mult),
                            ['k_f', 'E'], [kkr])
                        for half in range(2):
                            ps, pk = bank()
                            for kk4 in range(4):
                                tt = half * 4 + kk4
                                mm(ps[:, kk4 * 128:(kk4 + 1) * 128], kr[d][:, tt * 128:(tt + 1) * 128], ident_bf[:, :],
                                   True, True, [kkr, 'ident_bf'], [pk], inc=(kk4 == 3))
                            evac_copy(half, krtok[d][:, half * 4:half * 4 + 4, :],
                                      ps[:, :].rearrange("p (a b) -> p a b", b=128), [pk], ['krtok%d' % d])

                    tap("g2_kr", krtok[1][:, :, :], 'krtok1')
                    if stop == 'G2':
                        return

                    def gla_step(d, cp, step_i):
                        kq, kk, kS, kSb = 'qd%d' % d, 'kd%d' % d, 'S%d' % d, 'Sb%d' % d
                        cols = slice(cp * 128, (cp + 1) * 128)
                        ab, akey = bank()
                        for h2 in range(2):
                            rows = slice(h2 * 64, (h2 + 1) * 64)
                            mm_b(ab[:, h2 * 128:(h2 + 1) * 128], kd[d][rows, cols], qd[d][rows, cols], True, True,
                                 [kk, kq], [akey], inc=(h2 == 1), base=h2 * 64)
                        asb = attsb[step_i % 2]
                        askey = 'attsb%d' % (step_i % 2)
                        dve(lambda e: e.tensor_tensor(
                            out=asb[:, :, :], in0=ab[:, 0:256].rearrange("p (a b) -> p a b", b=128),
                            in1=glam[:, d:d + 1, :].to_broadcast([128, 2, 128]), op=ALU.mult),
                            [akey, 'glam'], [askey])
                        ob, okey = bank()
                        for h2 in range(2):
                            h = 2 * hp + h2
                            mm(ob[:, h2 * 128:(h2 + 1) * 128], v_tm[:, cp, h * 128:(h + 1) * 128], asb[:, h2, :],
                               h2 == 0, False, ['v_tm', askey], [okey], inc=False, skip=True)
                        order = [2 * cp, 2 * cp + 1] if d == 0 else [2 * cp + 1, 2 * cp]
                        for idx, c in enumerate(order):
                            ci = c % 2
                            boundary = (c % 4 == 0 and c > 0) if d == 0 else (c % 4 == 3 and c < 15)
                            if boundary:
                                dve(lambda e: e.tensor_scalar(out=S[d][:, :], in0=S[d][:, :], scalar1=rcol[:, 0:1],
                                                              scalar2=None, op0=ALU.mult), [kS, 'rcol'], [kS])
                                act(lambda e: e.activation(out=Sb[d][:, :], in_=S[d][:, :], func=AF.Copy), [kS], [kSb])
                            for h2 in range(2):
                                rows = slice(h2 * 64, (h2 + 1) * 64)
                                mm_b(ob[:, h2 * 128 + ci * 64:h2 * 128 + ci * 64 + 64], Sb[d][rows, :],
                                     qd[d][rows, c * 64:(c + 1) * 64], False, idx == 1, [kSb, kq], [okey],
                                     inc=(idx == 1 and h2 == 1), skip=True, base=h2 * 64)
                            kvb, kvkey = bank()
                            crow = slice(ci * 64, (ci + 1) * 64)
                            mm_b(kvb[:, 0:256], krtok[d][crow, cp, :], v_tm[crow, cp, hp * 256:(hp + 1) * 256], True, True,
                                 ['krtok%d' % d, 'v_tm'], [kvkey], base=ci * 64)
                            for h2 in range(2):
                                rows = slice(h2 * 64, (h2 + 1) * 64)
                                dve(lambda e, rows=rows, h2=h2, c=c: e.scalar_tensor_tensor(
                                    out=S[d][rows, :], in0=S[d][rows, :], scalar=gdec[d][rows, c:c + 1],
                                    in1=kvb[rows, h2 * 128:(h2 + 1) * 128], op0=ALU.mult, op1=ALU.add),
                                    [kS, 'gdec%d' % d, kvkey], [kS])
                            act(lambda e: e.activation(out=Sb[d][:, :], in_=S[d][:, :], func=AF.Copy), [kS], [kSb])
                            seg_end = (c % 4 == 3) if d == 0 else (c % 4 == 0)
                            if seg_end:
                                seg = c // 4
                                sg = stg[seg % 2]
                                sgk = 'stg%d' % (seg % 2)
                                act(lambda e, sg=sg: e.activation(out=sg[:, :], in_=S[d][:, :], func=AF.Copy), [kS], [sgk])
                                cx.dma('sp', ogla_d[l, seg, d, 2 * hp:2 * hp + 2].rearrange("h k v -> (h k) v"), sg[:, :],
                                       reads=[sgk], sem='ogla')
                        dve(lambda e: e.tensor_tensor(
                            out=oT[:, 2 * hp:2 * hp + 2, cols], in0=oT[:, 2 * hp:2 * hp + 2, cols],
                            in1=ob[:, 0:256].rearrange("p (a b) -> p a b", b=128), op=ALU.add), ['oT', okey], ['oT'])

                    for step in range(8):
                        gla_step(0, step, 2 * step)
                        gla_step(1, 7 - step, 2 * step + 1)
            tap("gla_oT%d" % l, oT[:, :, :], 'oT')
            with ExitStack() as s3:
                sqb = [sb("sqb%d" % i, [128, 512], BF16, stack=s3) for i in range(2)]
                sdv = [sb("sdv%d" % i, [128, 512], stack=s3) for i in range(2)]
                tmv = [sb("tmv%d" % i, [128, 512], stack=s3) for i in range(2)]
                i = 0
                for h in range(4):
                    for th in range(2):
                        cs = slice(th * 512, (th + 1) * 512)
                        sq, sd, tm_ = sqb[i % 2], sdv[i % 2], tmv[i % 2]
                        ksq, ksd, ktm = 'sqb%d' % (i % 2), 'sdv%d' % (i % 2), 'tmv%d' % (i % 2)
                        act(lambda e, sq=sq, h=h, cs=cs: e.activation(out=sq[:, :], in_=oT[:, h, cs], func=AF.Square),
                            ['oT'], [ksq])
                        ps, pk = bank()
                        mm(ps[:, :], ones_bf[:, :], sq[:, :], True, True, ['ones_bf', ksq], [pk])
                        act(lambda e, ps=ps, sd=sd: e.activation(out=sd[:, :], in_=ps[:, :], func=AF.Sqrt, bias=EPS,
                                                                 scale=1.0 / 128), [pk], [ksd])
                        dve(lambda e, sd=sd: e.reciprocal(out=sd[:, :], in_=sd[:, :]), [ksd], [ksd])
                        dve(lambda e, sd=sd, tm_=tm_, h=h, cs=cs: e.scalar_tensor_tensor(
                            out=tm_[:, :], in0=oT[:, h, cs], scalar=onw[:, 0:1], in1=sd[:, :], op0=ALU.mult, op1=ALU.mult),
                            ['oT', 'onw', ksd], [ktm])
                        dve(lambda e, tm_=tm_, h=h, cs=cs: e.tensor_tensor(
                            out=oT_br[0][:, h, cs], in0=tm_[:, :], in1=ggT[:, h, cs], op=ALU.mult),
                            [ktm, 'ggT'], ['obr0'])
                        i += 1
            tap("gla_out%d" % l, oT_br[0][:, :, :], 'obr0')

    def branch_mla(l):
        with ExitStack() as st:
            QT = sb("QT", [128, 4, T], BF16, stack=st)
            KT = sb("KT", [128, 4, NKEY], BF16, stack=st)
            Vt = sb("Vt", [128, 12, 512], BF16, stack=st)
            mgT = sb("mgT", [128, 4, T], BF16, stack=st)
            CKb = sb("CKb", [128, 12, 256], BF16, stack=st)
            KR = sb("KR", [128, 12, 32], stack=st)
            cqn = sb("cqn", [128, 8, 384], BF16, stack=st)
            cqnT = sb("cqnT", [128, 3, T], BF16, stack=st)
            ckvT = sb("ckvT", [128, 2, NKEY], BF16, stack=st)
            wuq = sb("wuq", [128, 3, 384], BF16, stack=st)
            wuqf = sb("wuqf", [128, 3, 384], stack=st)
            qnw = sb("qnw", [128, 3], stack=st)
            wuk = sb("wuk", [128, 2, 256], BF16, stack=st)
            wuv = sb("wuv", [128, 2, 512], BF16, stack=st)
            kvw_bc = sb("kvw_bc", [128, 256], stack=st)
            qhw_bc = sb("qhw_bc", [128, 96], stack=st)
            khw_bc = sb("khw_bc", [128, 96], stack=st)
            ssq = sb("ssq", [128, 8, 2], stack=st)
            rsq = sb("rsq", [128, 8, 2], stack=st)
            tq = sb("tq", [128, 8, 2], stack=st)
            sskr = sb("sskr", [128, 12], stack=st)
            junk = sb("junkm", [128, 512], stack=st)
            stg = [sb("stgkv%d" % i, [128, 288], stack=st) for i in range(2)]
            for h in range(4):
                cx.dma('pool', QT[96:101, h, :], qmask_d[:, :], writes=['QT'], sem='mmask')
                cx.dma('pool', KT[96:101, h, :], kmask_d[:, :], writes=['KT'], sem='mmask')
            cx.dma('sp', wuqf[:, :, :], mla_w_uq[l].rearrange("(k p) c -> p k c", p=128), writes=['wuqf'], sem='mw1')
            with nc.allow_non_contiguous_dma(reason="tiny norm weight column"):
                cx.dma('sp', qnw[:, :], mla_q_norm[l].rearrange("(k p) -> p k", p=128), writes=['qnw'], sem='mw1')
            cx.dma('pool', wuk[:, :, :], mla_w_uk[l].rearrange("(k p) c -> p k c", p=128), writes=['wuk'], sem='mw2')
            cx.dma('pool', wuv[:, :, :], mla_w_uv[l].rearrange("(k p) c -> p k c", p=128), writes=['wuv'], sem='mw2')
            cx.dma('sp', kvw_bc[:, :], mla_kv_norm[l].partition_broadcast(128), writes=['kvw_bc'], sem='mw1')
            cx.dma('sp', qhw_bc[:, :], mla_qh_norm[l].partition_broadcast(128), writes=['qhw_bc'], sem='mw1')
            cx.dma('sp', khw_bc[:, :], mla_kh_norm[l].partition_broadcast(128), writes=['khw_bc'], sem='mw1')
            cx.dma('pool', CKb[:, 0:4, :], ckvc_d[l].rearrange("(t p) c -> p t c", p=128), writes=['CKb'], sem='mw2')
            cx.dma('sp', KR[:, 0:4, :], krc_d[l].rearrange("(t p) c -> p t c", p=128), writes=['KR'], sem='mw1')
            for k in range(3):
                dve(lambda e: e.tensor_scalar(out=wuq[:, k, :], in0=wuqf[:, k, :], scalar1=qnw[:, k:k + 1], scalar2=None,
                                              op0=ALU.mult), ['wuqf', 'qnw'], ['wuq'])
            dve(lambda e: e.memset(ssq[:, :, :], 0.0), [], ['ssq'])
            dve(lambda e: e.memset(sskr[:, :], 0.0), [], ['sskr'])
            slotA, kA = load_w(w_in[l][:, C_MQ:C_MQ + 384], 8, 384)
            slotB, kB = load_w(w_in[l][:, C_MKV:C_MKV + 288], 8, 288)
            for tt in range(8):
                tsl = slice(tt * 128, (tt + 1) * 128)
                psA, pkA = bank()
                for k in range(8):
                    mm(psA[:, 0:384], hT[:, k, tsl], slotA[:, k, 0:384], k == 0, k == 7, ['hT', kA], [pkA])
                psB, pkB = bank()
                for k in range(8):
                    mm(psB[:, 0:288], hT[:, k, tsl], slotB[:, k, 0:288], k == 0, k == 7, ['hT', kB], [pkB])
                act(lambda e: e.activation(out=junk[:, 0:384], in_=psA[:, 0:384], func=AF.Square,
                                           accum_out=ssq[:, tt, 0:1]), [pkA, 'ssq'], ['junkm', 'ssq'])
                act(lambda e: e.activation(out=junk[:, 0:256], in_=psB[:, 0:256], func=AF.Square,
                                           accum_out=ssq[:, tt, 1:2]), [pkB, 'ssq'], ['junkm', 'ssq'])
                act(lambda e: e.activation(out=tq[:, tt, 0:1], in_=ssq[:, tt, 0:1], func=AF.Sqrt, bias=EPS,
                                           scale=1.0 / 384), ['ssq'], ['tq'])
                act(lambda e: e.activation(out=tq[:, tt, 1:2], in_=ssq[:, tt, 1:2], func=AF.Sqrt, bias=EPS,
                                           scale=1.0 / 256), ['ssq'], ['tq'])
                dve(lambda e: e.reciprocal(out=rsq[:, tt, :], in_=tq[:, tt, :]), ['tq'], ['rsq'])
                dve(lambda e: e.tensor_scalar(out=cqn[:, tt, :], in0=psA[:, 0:384], scalar1=rsq[:, tt, 0:1], scalar2=None,
                                              op0=ALU.mult), [pkA, 'rsq'], ['cqn'])
                sg = stg[tt % 2]
                sgk = 'stgkv%d' % (tt % 2)
                dve(lambda e: e.scalar_tensor_tensor(out=sg[:, 0:256], in0=psB[:, 0:256], scalar=rsq[:, tt, 1:2],
                                                     in1=kvw_bc[:, :], op0=ALU.mult, op1=ALU.mult),
                    [pkB, 'rsq', 'kvw_bc'], [sgk])
                act(lambda e: e.activation(out=sg[:, 256:288], in_=psB[:, 256:288], func=AF.Copy), [pkB], [sgk])
                cx.dma('sp', ockv_d[l, tsl, :], sg[:, 0:256], reads=[sgk], sem='ockv')
                cx.dma('sp', okr_d[l, tsl, :], sg[:, 256:288], reads=[sgk], sem='ockv')
                act(lambda e: e.activation(out=CKb[:, 4 + tt, :], in_=sg[:, 0:256], func=AF.Copy), [sgk], ['CKb'])
                dve(lambda e: e.tensor_copy(out=KR[:, 4 + tt, :], in_=sg[:, 256:288]), [sgk], ['KR'])
            for tt in range(8):
                ps, pk = bank()
                for k in range(3):
                    mm(ps[:, k * 128:(k + 1) * 128], cqn[:, tt, k * 128:(k + 1) * 128], ident_bf[:, :], True, True,
                       ['cqn', 'ident_bf'], [pk], inc=(k == 2))
                evac_copy(tt, cqnT[:, :, tt * 128:(tt + 1) * 128], ps[:, 0:384].rearrange("p (a b) -> p a b", b=128),
                          [pk], ['cqnT'])
            for kp in range(6):
                ps, pk = bank()
                for j in range(2):
                    kt = 2 * kp + j
                    for k in range(2):
                        mm(ps[:, (2 * j + k) * 128:(2 * j + k + 1) * 128], CKb[:, kt, k * 128:(k + 1) * 128], ident_bf[:, :],
                           True, True, ['CKb', 'ident_bf'], [pk], inc=(j == 1 and k == 1))
                for j in range(2):
                    kt = 2 * kp + j
                    evac_copy(j, ckvT[:, :, kt * 128:(kt + 1) * 128],
                              ps[:, j * 256:(j + 1) * 256].rearrange("p (a b) -> p a b", b=128), [pk], ['ckvT'])
            slot_mg, k_mg = load_w(w_in[l][:, C_MG:C_MG + 512], 8, 512)
            for m in range(4):
                proj_fm(slot_mg, k_mg, m * 128, 128,
                        lambda ps, pk, th, m=m: act(lambda e: e.activation(
                            out=mgT[:, m, th * 512:(th + 1) * 512], in_=ps[:, :], func=AF.Silu), [pk], ['mgT']))

            def head_finish(src3, skey, nrm_keys, rope_tt, dstT, dkey, col0, tagi):
                i2 = tagi % 2
                fb = hfb[i2]
                fk = 'hfb%d' % i2
                if rope_tt is None:
                    act(lambda e: e.activation(out=fb[:, :, :], in_=src3, func=AF.Copy), [skey], [fk])
                else:
                    rt = rtmp[i2]
                    rk = 'rtmp%d' % i2
                    cb = ropec[:, rope_tt:rope_tt + 1, :].to_broadcast([128, 4, 16])
                    sbb = ropes[:, rope_tt:rope_tt + 1, :].to_broadcast([128, 4, 16])
                    x1 = src3[:, :, 64:80]
                    x2 = src3[:, :, 80:96]
                    act(lambda e: e.activation(out=fb[:, :, 0:64], in_=src3[:, :, 0:64], func=AF.Copy), [skey], [fk])
                    pool(lambda e: e.tensor_tensor(out=rt[:, 0, :, :], in0=x1, in1=cb, op=ALU.mult), [skey, 'ropec'], [rk])
                    pool(lambda e: e.tensor_tensor(out=rt[:, 1, :, :], in0=x2, in1=sbb, op=ALU.mult), [skey, 'ropes'], [rk])
                    pool(lambda e: e.tensor_tensor(out=rt[:, 2, :, :], in0=x1, in1=sbb, op=ALU.mult), [skey, 'ropes'], [rk])
                    pool(lambda e: e.tensor_tensor(out=rt[:, 3, :, :], in0=x2, in1=cb, op=ALU.mult), [skey, 'ropec'], [rk])
                    pool(lambda e: e.tensor_tensor(out=fb[:, :, 64:80], in0=rt[:, 0, :, :], in1=rt[:, 1, :, :],
                                                   op=ALU.subtract), [rk], [fk])
                    pool(lambda e: e.tensor_tensor(out=fb[:, :, 80:96], in0=rt[:, 2, :, :], in1=rt[:, 3, :, :],
                                                   op=ALU.add), [rk], [fk])
                ps, pk = bank()
                for h in range(4):
                    mm(ps[0:96, h * 128:(h + 1) * 128], fb[:, h, :], ident_bf[:, :], True, True, [fk, 'ident_bf'], [pk],
                       inc=(h == 3))
                act(lambda e: e.activation(out=dstT[0:96, :, col0:col0 + 128],
                                           in_=ps[0:96, :].rearrange("p (a b) -> p a b", b=128), func=AF.Copy),
                    [pk], [dkey])

            with ExitStack() as s2:
                hfb = [sb("hfb%d" % i, [128, 4, 96], BF16, stack=s2) for i in range(2)]
                rtmp = [sb("rtmp%d" % i, [128, 4, 4, 16], stack=s2) for i in range(2)]
                sqh = [sb("sqh%d" % i, [128, 384], stack=s2) for i in range(2)]
                ssh = [sb("ssh%d" % i, [128, 4], stack=s2) for i in range(2)]
                hn = [sb("hn%d" % i, [128, 4, 96], stack=s2) for i in range(2)]
                for tt in range(8):
                    i2 = tt % 2
                    psQ, pkQ = bank()
                    for k in range(3):
                        mm(psQ[:, 0:384], cqnT[:, k, tt * 128:(tt + 1) * 128], wuq[:, k, :], k == 0, k == 2,
                           ['cqnT', 'wuq'], [pkQ])
                    q3 = psQ[:, 0:384].rearrange("p (a b) -> p a b", b=96)
                    act(lambda e: e.activation(out=sqh[i2][:, 0:384], in_=psQ[:, 0:384], func=AF.Square),
                        [pkQ], ['sqh%d' % i2])
                    dve(lambda e: e.tensor_reduce(out=ssh[i2][:, :], in_=sqh[i2][:, 0:384].rearrange("p (a b) -> p a b", b=96),
                                                  axis=AX.X, op=ALU.add), ['sqh%d' % i2], ['ssh%d' % i2])
                    act(lambda e: e.activation(out=ssh[i2][:, :], in_=ssh[i2][:, :], func=AF.Sqrt, bias=EPS,
                                               scale=1.0 / 96), ['ssh%d' % i2], ['ssh%d' % i2])
                    dve(lambda e: e.reciprocal(out=ssh[i2][:, :], in_=ssh[i2][:, :]), ['ssh%d' % i2], ['ssh%d' % i2])
                    dve(lambda e: e.tensor_tensor(out=hn[i2][:, :, :], in0=q3,
                                                  in1=ssh[i2][:, :].unsqueeze(2).to_broadcast([128, 4, 96]), op=ALU.mult),
                        [pkQ, 'ssh%d' % i2], ['hn%d' % i2])
                    dve(lambda e: e.tensor_tensor(out=hn[i2][:, :, :], in0=hn[i2][:, :, :],
                                                  in1=qhw_bc[:, :].unsqueeze(1).to_broadcast([128, 4, 96]), op=ALU.mult),
                        ['hn%d' % i2, 'qhw_bc'], ['hn%d' % i2])
                    head_finish(hn[i2][:, :, :], 'hn%d' % i2, None, tt, QT, 'QT', tt * 128, tt)
                for kt in range(12):
                    i2 = kt % 2
                    ksl = slice(kt * 128, (kt + 1) * 128)
                    psK, pkK = bank()
                    for k in range(2):
                        mm(psK[:, 0:256], ckvT[:, k, ksl], wuk[:, k, :], k == 0, k == 1, ['ckvT', 'wuk'], [pkK])
                    psV, pkV = bank()
                    for k in range(2):
                        mm(psV[:, :], ckvT[:, k, ksl], wuv[:, k, :], k == 0, k == 1, ['ckvT', 'wuv'], [pkV])
                    evac_copy(kt, Vt[:, kt, :], psV[:, :], [pkV], ['Vt'])
                    act(lambda e: e.activation(out=sqh[i2][:, 0:256], in_=psK[:, 0:256], func=AF.Square),
                        [pkK], ['sqh%d' % i2])
                    dve(lambda e: e.tensor_reduce(out=ssh[i2][:, :], in_=sqh[i2][:, 0:256].rearrange("p (a b) -> p a b", b=64),
                                                  axis=AX.X, op=ALU.add), ['sqh%d' % i2], ['ssh%d' % i2])
                    act(lambda e: e.activation(out=junk[:, 0:32], in_=KR[:, kt, :], func=AF.Square,
                                               accum_out=sskr[:, kt:kt + 1]), ['KR', 'sskr'], ['junkm', 'sskr'])
                    dve(lambda e: e.tensor_scalar(out=ssh[i2][:, :], in0=ssh[i2][:, :], scalar1=sskr[:, kt:kt + 1],
                                                  scalar2=None, op0=ALU.add), ['ssh%d' % i2, 'sskr'], ['ssh%d' % i2])
                    act(lambda e: e.activation(out=ssh[i2][:, :], in_=ssh[i2][:, :], func=AF.Sqrt, bias=EPS,
                                               scale=1.0 / 96), ['ssh%d' % i2], ['ssh%d' % i2])
                    dve(lambda e: e.reciprocal(out=ssh[i2][:, :], in_=ssh[i2][:, :]), ['ssh%d' % i2], ['ssh%d' % i2])
                    k3 = psK[:, 0:256].rearrange("p (a b) -> p a b", b=64)
                    dve(lambda e: e.tensor_tensor(out=hn[i2][:, :, 0:64], in0=k3,
                                                  in1=ssh[i2][:, :].unsqueeze(2).to_broadcast([128, 4, 64]), op=ALU.mult),
                        [pkK, 'ssh%d' % i2], ['hn%d' % i2])
                    dve(lambda e: e.tensor_tensor(out=hn[i2][:, :, 64:96],
                                                  in0=KR[:, kt:kt + 1, :].to_broadcast([128, 4, 32]),
                                                  in1=ssh[i2][:, :].unsqueeze(2).to_broadcast([128, 4, 32]), op=ALU.mult),
                        ['KR', 'ssh%d' % i2], ['hn%d' % i2])
                    dve(lambda e: e.tensor_tensor(out=hn[i2][:, :, :], in0=hn[i2][:, :, :],
                                                  in1=khw_bc[:, :].unsqueeze(1).to_broadcast([128, 4, 96]), op=ALU.mult),
                        ['hn%d' % i2, 'khw_bc'], ['hn%d' % i2])
                    head_finish(hn[i2][:, :, :], 'hn%d' % i2, None, (kt - 4) if kt >= 4 else None, KT, 'KT', kt * 128, kt)
            tap("mla_QT%d" % l, QT[0:101, :, :], 'QT')
            tap("mla_KT%d" % l, KT[0:101, :, :], 'KT')
            tap("mla_V%d" % l, Vt[:, :, :], 'Vt')
            if stop == 'M1':
                return
            with ExitStack() as s3:
                PT = [sb("PT%d" % i, [128, 512], BF16, stack=s3) for i in range(3)]
                rden = [sb("rden%d" % i, [128, 512], stack=s3) for i in range(2)]
                accs = [(reserve_bank(), reserve_bank()) for _ in range(2)]
                it = 0
                pi = 0
                for h in range(4):
                    for qh in range(2):
                        (ob, okey), (db, dkey) = accs[it % 2]
                        qsl = slice(qh * 512, (qh + 1) * 512)
                        for kt in range(12):
                            sbk, skey = bank()
                            mm(sbk[:, :], KT[0:101, h, kt * 128:(kt + 1) * 128], QT[0:101, h, qsl], True, True,
                               ['KT', 'QT'], [skey])
                            pt = PT[pi % 3]
                            ptk = 'PT%d' % (pi % 3)
                            pi += 1
                            act(lambda e: e.activation(out=pt[:, :], in_=sbk[:, :], func=AF.Exp, scale=ATT_SCALE),
                                [skey], [ptk])
                            mm(ob[:, :], Vt[:, kt, h * 128:(h + 1) * 128], pt[:, :], kt == 0, kt == 11, ['Vt', ptk], [okey])
                            mm(db[:, :], ones_bf[:, :], pt[:, :], kt == 0, kt == 11, ['ones_bf', ptk], [dkey])
                        rd = rden[it % 2]
                        rdk = 'rden%d' % (it % 2)
                        dve(lambda e: e.reciprocal(out=rd[:, :], in_=db[:, :]), [dkey], [rdk])
                        dve(lambda e: e.tensor_tensor(out=rd[:, :], in0=ob[:, :], in1=rd[:, :], op=ALU.mult),
                            [okey, rdk], [rdk])
                        dve(lambda e: e.tensor_tensor(out=oT_br[1][:, h, qsl], in0=rd[:, :], in1=mgT[:, h, qsl],
                                                      op=ALU.mult), [rdk, 'mgT'], ['obr1'])
                        it += 1
                for (a, b) in accs:
                    release_bank(a[1])
                    release_bank(b[1])
            tap("mla_out%d" % l, oT_br[1][:, :, :], 'obr1')

    class Arena:
        def __init__(self, t, nelem, dt):
            self.t = t
            self.dt = dt
            self.ti = t.bitcast(I32) if dt == F32 else None
            self.free_list = [(0, nelem)]
            self.live = {}

        def alloc(self, name, dims, dt=None):
            n = 1
            for d_ in dims:
                n *= d_
            nf = (n + 15) // 16 * 16
            for idx, (o, sz) in enumerate(self.free_list):
                if sz >= nf:
                    if sz == nf:
                        self.free_list.pop(idx)
                    else:
                        self.free_list[idx] = (o + nf, sz - nf)
                    break
            else:
                raise RuntimeError("arena full: %s %s %s" % (name, dims, self.free_list))
            self.live[name] = (o, nf)
            if dt == I32:
                ap = self.ti[:, o:o + n]
            else:
                ap = self.t[:, o:o + n]
            if len(dims) > 1:
                names = ["a%d" % i for i in range(len(dims))]
                pat = "p (%s) -> p %s" % (" ".join(names), " ".join(names))
                ap = ap.rearrange(pat, **{nm: d_ for nm, d_ in zip(names, dims)})
            cx.fresh[name] = dict(cx.cnt)
            return ap

        def free(self, name):
            o, nf = self.live.pop(name)
            fl = sorted(self.free_list + [(o, nf)])
            merged = []
            for a, b in fl:
                if merged and merged[-1][0] + merged[-1][1] == a:
                    merged[-1] = (merged[-1][0], merged[-1][1] + b)
                else:
                    merged.append((a, b))
            self.free_list = merged

    class Arena2:
        def __init__(self, af, ab):
            self.af, self.ab = af, ab
            self.where = {}

        def alloc(self, name, dims, dt=F32):
            a = self.ab if dt == BF16 else self.af
            self.where[name] = a
            return a.alloc(name, dims, dt)

        def free(self, name):
            self.where.pop(name).free(name)

    def branch_s5(l):
        with ExitStack() as st:
            NF, NB = 9600, 29696
            arena_f = sb("s5arena_f", [128, NF], stack=st)
            arena_b = sb("s5arena_b", [128, NB], BF16, stack=st)
            A = Arena2(Arena(arena_f, NF, F32), Arena(arena_b, NB, BF16))
            WoR = A.alloc("WoR", [2, 16, 2, 128], BF16)
            TT = A.alloc("TT", [32, 128], BF16)
            C1 = A.alloc("C1", [16, 2, 2]); C2 = A.alloc("C2", [16, 2, 2])
            C1r = A.alloc("C1r", [16, 2, 2]); C2r = A.alloc("C2r", [16, 2, 2])
            WinT = A.alloc("WinT", [2, 32, 128], BF16)
            s5m = A.alloc("s5m", [2, 128])
            selc = A.alloc("selc", [2, 64])
            cx.dma('sp', s5m, s5m_d.rearrange("a p c -> p a c"), writes=['s5m'], sem='s5c')
            cx.dma('sp', selc, selc_d.rearrange("a p c -> p a c"), writes=['selc'], sem='s5c')

            def tt_(e, o, a, b, op):
                return e.tensor_tensor(out=o, in0=a, in1=b, op=op)

            def D2(o, a, b, op, rk, wk):
                dve(lambda e: tt_(e, o, a, b, op), rk, wk)

            AR = A.alloc("AR", [2, 16]); AI = A.alloc("AI", [2, 16]); LD = A.alloc("LD", [2, 16])
            anat = A.alloc("anat", [2, 128])
            cx.dma('sp', anat[0:32, 0, :], s5_a_re[l].rearrange("d (j g2) n -> (d j) (g2 n)", g2=2), writes=['anat'], sem='s5c')
            cx.dma('sp', anat[0:32, 1, :], s5_a_im[l].rearrange("d (j g2) n -> (d j) (g2 n)", g2=2), writes=['anat'], sem='s5c')
            ps, pk = bank()
            for c_ in range(2):
                mm(ps[:, c_ * 32:(c_ + 1) * 32], anat[0:32, c_, :], ident_f[0:32, 0:32], True, True, ['anat', 'ident_f'], [pk],
                   inc=(c_ == 1))
            dve(lambda e: e.tensor_copy(out=AR, in_=ps[:, 0:32].rearrange("p (d j) -> p d j", d=2)), [pk], ['AR'])
            dve(lambda e: e.tensor_copy(out=AI, in_=ps[:, 32:64].rearrange("p (d j) -> p d j", d=2)), [pk], ['AI'])
            ldf = A.alloc("ldf", [2, 32])
            cx.dma('sp', ldf, s5_log_dt[l].rearrange("d g -> (d g)").partition_broadcast(128).rearrange("p (d g) -> p d g", d=2),
                   writes=['ldf'], sem='s5c')
            for g2 in range(2):
                dve(lambda e: e.tensor_copy(out=LD[g2 * 64:(g2 + 1) * 64], in_=ldf[g2 * 64:(g2 + 1) * 64, :, g2::2]),
                    ['ldf'], ['LD'])
            dcol = A.alloc("dcol", [32])
            with nc.allow_non_contiguous_dma(reason="small S5 parameter gathers"):
                for s_ in range(8):
                    cx.dma('sp', dcol[s_ * 16:(s_ + 1) * 16, :], s5_d[l].rearrange("(g q) -> q g", q=16),
                           writes=['dcol'], sem='s5c')
            BR = A.alloc("BR", [16, 16]); BI = A.alloc("BI", [16, 16])
            for jq in range(4):
                cx.dma('sp', BR[:, 4 * jq:4 * jq + 4, :], s5_b_re[l].rearrange("(j g2) n q -> (g2 n) j q", g2=2)[:, 4 * jq:4 * jq + 4, :],
                       writes=['BR'], sem='s5c')
                cx.dma('sp', BI[:, 4 * jq:4 * jq + 4, :], s5_b_im[l].rearrange("(j g2) n q -> (g2 n) j q", g2=2)[:, 4 * jq:4 * jq + 4, :],
                       writes=['BI'], sem='s5c')
            CNr = A.alloc("CNr", [4, 64]); CNi = A.alloc("CNi", [4, 64])
            cx.dma('sp', CNr, s5_c_re[l].rearrange("(r gl) p n -> (gl p) r n", r=4), writes=['CNr'], sem='s5c')
            cx.dma('sp', CNi, s5_c_im[l].rearrange("(r gl) p n -> (gl p) r n", r=4), writes=['CNi'], sem='s5c')
            CR = A.alloc("CR", [16, 16]); CI = A.alloc("CI", [16, 16])
            for (cn, cnk, cdst, cdk, ei) in ((CNr, 'CNr', CR, 'CR', 0), (CNi, 'CNi', CI, 'CI', 1)):
                ps, pk = bank()
                for r in range(4):
                    for g2 in range(2):
                        mm(ps[g2 * 64:(g2 + 1) * 64, r * 64:(r + 1) * 64], cn[:, r, :], selc[:, g2, :], True, True,
                           [cnk, 'selc'], [pk], inc=(r == 3 and g2 == 1))
                evac_copy(ei, cdst, ps[:, 0:256].rearrange("p (a b) -> p a b", b=16), [pk], [cdk])

            def small(name):
                return A.alloc(name, [2, 16])
            dt_ = small("dt_"); mag = small("mag"); ang = small("ang"); sn = small("sn"); cs = small("cs")
            abr = small("abr"); abi = small("abi"); rden = small("rden"); nre = small("nre")
            cfr = small("cfr"); cfi = small("cfi"); ta = small("ta"); tb = small("tb")
            kI = A.alloc("kI", [2, 16], I32)
            act(lambda e: e.activation(out=dt_, in_=LD, func=AF.Exp), ['LD'], ['dt_'])
            D2(ta, AR, dt_, ALU.mult, ['AR', 'dt_'], ['ta'])
            act(lambda e: e.activation(out=mag, in_=ta, func=AF.Exp), ['ta'], ['mag'])
            D2(ang, AI, dt_, ALU.mult, ['AI', 'dt_'], ['ang'])

            def sin_of(dst, dkey, shift):
                dve(lambda e: e.tensor_scalar(out=ta, in0=ang, scalar1=shift, scalar2=1.0 / (2 * PI), op0=ALU.add,
                                              op1=ALU.mult), ['ang'], ['ta'])
                dve(lambda e: e.tensor_copy(out=kI, in_=ta), ['ta'], ['kI'])
                dve(lambda e: e.tensor_copy(out=tb, in_=kI), ['kI'], ['tb'])
                dve(lambda e: e.tensor_scalar(out=ta, in0=ang, scalar1=shift, scalar2=None, op0=ALU.add), ['ang'], ['ta'])
                dve(lambda e: e.scalar_tensor_tensor(out=ta, in0=tb, scalar=-2 * PI, in1=ta, op0=ALU.mult, op1=ALU.add),
                    ['tb', 'ta'], ['ta'])
                dve(lambda e: e.tensor_scalar(out=tb, in0=ta, scalar1=PI, scalar2=None, op0=ALU.is_gt), ['ta'], ['tb'])
                dve(lambda e: e.scalar_tensor_tensor(out=ta, in0=tb, scalar=-2 * PI, in1=ta, op0=ALU.mult, op1=ALU.add),
                    ['tb', 'ta'], ['ta'])
                dve(lambda e: e.tensor_scalar(out=tb, in0=ta, scalar1=-PI, scalar2=None, op0=ALU.is_lt), ['ta'], ['tb'])
                dve(lambda e: e.scalar_tensor_tensor(out=ta, in0=tb, scalar=2 * PI, in1=ta, op0=ALU.mult, op1=ALU.add),
                    ['tb', 'ta'], ['ta'])
                act(lambda e: e.activation(out=dst, in_=ta, func=AF.Sin), ['ta'], [dkey])
            sin_of(sn, 'sn', 0.0)
            sin_of(cs, 'cs', PI / 2)
            D2(abr, mag, cs, ALU.mult, ['mag', 'cs'], ['abr'])
            D2(abi, mag, sn, ALU.mult, ['mag', 'sn'], ['abi'])
            D2(ta, AR, AR, ALU.mult, ['AR'], ['ta'])
            D2(tb, AI, AI, ALU.mult, ['AI'], ['tb'])
            D2(ta, ta, tb, ALU.add, ['ta', 'tb'], ['ta'])
            dve(lambda e: e.reciprocal(out=rden, in_=ta), ['ta'], ['rden'])
            dve(lambda e: e.tensor_scalar(out=nre, in0=abr, scalar1=-1.0, scalar2=None, op0=ALU.add), ['abr'], ['nre'])
            D2(ta, nre, AR, ALU.mult, ['nre', 'AR'], ['ta'])
            D2(tb, abi, AI, ALU.mult, ['abi', 'AI'], ['tb'])
            D2(ta, ta, tb, ALU.add, ['ta', 'tb'], ['ta'])
            D2(cfr, ta, rden, ALU.mult, ['ta', 'rden'], ['cfr'])
            D2(ta, abi, AR, ALU.mult, ['abi', 'AR'], ['ta'])
            D2(tb, nre, AI, ALU.mult, ['nre', 'AI'], ['tb'])
            D2(ta, ta, tb, ALU.subtract, ['ta', 'tb'], ['ta'])
            D2(cfi, ta, rden, ALU.mult, ['ta', 'rden'], ['cfi'])

            def cmul(o_re, o_im, ok, a_re, a_im, ak, b_re, b_im, bk, t1, t2, tk):
                D2(t1, a_re, b_re, ALU.mult, ak + bk, [tk[0]])
                D2(t2, a_im, b_im, ALU.mult, ak + bk, [tk[1]])
                D2(o_re, t1, t2, ALU.subtract, tk, [ok[0]])
                D2(t1, a_re, b_im, ALU.mult, ak + bk + [ok[0]], [tk[0]])
                D2(t2, a_im, b_re, ALU.mult, ak + bk + [ok[0]], [tk[1]])
                D2(o_im, t1, t2, ALU.add, tk, [ok[1]])

            PWr = A.alloc("PWr", [2, 16, 17]); PWi = A.alloc("PWi", [2, 16, 17])
            pt1 = A.alloc("pt1", [2, 16, 4]); pt2 = A.alloc("pt2", [2, 16, 4])
            dve(lambda e: e.memset(PWr[:, :, :, 8:9], 1.0), [], ['PWr'])
            dve(lambda e: e.memset(PWi[:, :, :, 8:9], 0.0), [], ['PWi'])
            dve(lambda e: e.tensor_copy(out=PWr[:, :, :, 9], in_=abr), ['abr'], ['PWr'])
            dve(lambda e: e.tensor_copy(out=PWi[:, :, :, 9], in_=abi), ['abi'], ['PWi'])
            D2(ta, abr, abr, ALU.mult, ['abr'], ['ta'])
            D2(tb, abi, abi, ALU.mult, ['abi'], ['tb'])
            D2(ta, ta, tb, ALU.add, ['ta', 'tb'], ['ta'])
            dve(lambda e: e.reciprocal(out=tb, in_=ta), ['ta'], ['tb'])
            D2(PWr[:, :, :, 7], abr, tb, ALU.mult, ['abr', 'tb'], ['PWr'])
            dve(lambda e: e.scalar_tensor_tensor(out=PWi[:, :, :, 7], in0=abi, scalar=-1.0, in1=tb, op0=ALU.mult,
                                                 op1=ALU.mult), ['abi', 'tb'], ['PWi'])
            PK = ['PWr', 'PWi']

            def pw_step(o0, o1, i0, i1, m):
                w = o1 - o0
                bre = PWr[:, :, :, m:m + 1].to_broadcast([128, 2, 16, w])
                bim = PWi[:, :, :, m:m + 1].to_broadcast([128, 2, 16, w])
                cmul(PWr[:, :, :, o0:o1], PWi[:, :, :, o0:o1], PK, PWr[:, :, :, i0:i1], PWi[:, :, :, i0:i1], PK,
                     bre, bim, PK, pt1[:, :, :, 0:w], pt2[:, :, :, 0:w], ['pt1', 'pt2'])
            pw_step(10, 11, 9, 10, 9)
            pw_step(11, 13, 9, 11, 10)
            pw_step(13, 17, 9, 13, 12)
            pw_step(6, 7, 7, 8, 7)
            pw_step(4, 6, 6, 8, 6)
            pw_step(0, 4, 4, 8, 4)
            BPr = A.alloc("BPr", [2, 16, 16]); BPi = A.alloc("BPi", [2, 16, 16])
            bt1 = A.alloc("bt1", [2, 16, 16]); bt2 = A.alloc("bt2", [2, 16, 16])
            cmul(BPr, BPi, ['BPr', 'BPi'],
                 cfr.unsqueeze(3).to_broadcast([128, 2, 16, 16]), cfi.unsqueeze(3).to_broadcast([128, 2, 16, 16]),
                 ['cfr', 'cfi'],
                 BR.unsqueeze(1).to_broadcast([128, 2, 16, 16]), BI.unsqueeze(1).to_broadcast([128, 2, 16, 16]),
                 ['BR', 'BI'], bt1, bt2, ['bt1', 'bt2'])
            A.free("bt1"); A.free("bt2")
            for d in range(2):
                for c_ in range(2):
                    dve(lambda e: e.tensor_copy(out=C1[:, :, d, c_], in_=PWr[:, d, :, 16]), ['PWr'], ['C1'])
                dve(lambda e: e.tensor_scalar(out=C2[:, :, d, 0], in0=PWi[:, d, :, 16], scalar1=-1.0, scalar2=None,
                                              op0=ALU.mult), ['PWi'], ['C2'])
                dve(lambda e: e.tensor_copy(out=C2[:, :, d, 1], in_=PWi[:, d, :, 16]), ['PWi'], ['C2'])
            dve(lambda e: e.tensor_scalar(out=C1r, in0=C1, scalar1=rcol[:, 0:1], scalar2=None, op0=ALU.mult),
                ['C1', 'rcol'], ['C1r'])
            dve(lambda e: e.tensor_scalar(out=C2r, in0=C2, scalar1=rcol[:, 0:1], scalar2=None, op0=ALU.mult),
                ['C2', 'rcol'], ['C2r'])

            TTacc = A.alloc("TTacc", [4, 128])
            Wn = [A.alloc("Wn%d" % i, [2, 8, 16]) for i in range(6)]
            wt1 = A.alloc("wt1", [4, 8, 16]); wt2 = A.alloc("wt2", [2, 8, 16])
            WK = ['Wn%d' % i for i in range(6)]
            for qd_ in range(8):
                jsl = slice(2 * qd_, 2 * qd_ + 2)
                for d in range(2):
                    if d == 0:
                        p_in = (slice(15, 7, -1)); p_inv = (slice(7, None, -1)); p_out = slice(9, 17)
                    else:
                        p_in = slice(8, 16); p_inv = slice(0, 8); p_out = slice(16, 8, -1)

                    def pwb(sl, last):
                        return (PWr[:, d, jsl, sl].unsqueeze(3).to_broadcast([128, 2, 8, last]),
                                PWi[:, d, jsl, sl].unsqueeze(3).to_broadcast([128, 2, 8, last]))
                    bpr = BPr[:, d, jsl, :].unsqueeze(2).to_broadcast([128, 2, 8, 16])
                    bpi = BPi[:, d, jsl, :].unsqueeze(2).to_broadcast([128, 2, 8, 16])
                    w1 = wt1[:, 0:2]
                    a_re, a_im = pwb(p_in, 16)
                    cmul(Wn[0], Wn[1], WK[0:2], a_re, a_im, PK, bpr, bpi, ['BPr', 'BPi'], w1, wt2, ['wt1', 'wt2'])
                    a_re, a_im = pwb(p_inv, 16)
                    cmul(Wn[2], Wn[3], WK[2:4], a_re, a_im, PK, bpr, bpi, ['BPr', 'BPi'], w1, wt2, ['wt1', 'wt2'])
                    a_re, a_im = pwb(p_out, 16)
                    cr = CR[:, jsl, :].unsqueeze(2).to_broadcast([128, 2, 8, 16])
                    ci = CI[:, jsl, :].unsqueeze(2).to_broadcast([128, 2, 8, 16])
                    cmul(Wn[4], Wn[5], WK[4:6], a_re, a_im, PK, cr, ci, ['CR', 'CI'], w1, wt2, ['wt1', 'wt2'])
                    dve(lambda e: e.tensor_scalar(out=Wn[5], in0=Wn[5], scalar1=-1.0, scalar2=None, op0=ALU.mult),
                        [WK[5]], [WK[5]])
                    for c2 in range(2):
                        act(lambda e: e.activation(out=WoR[:, d, jsl, c2, :],
                                                   in_=Wn[4 + c2].rearrange("p a b c -> p a (b c)"), func=AF.Copy),
                            [WK[4 + c2]], ['WoR'])
                    ps, pk = bank()
                    for jj in range(2):
                        for c2 in range(2):
                            mm(ps[:, (jj * 2 + c2) * 128:(jj * 2 + c2 + 1) * 128],
                               Wn[c2][:, jj, :, :].rearrange("p b c -> p (b c)"), ident_f[:, :], True, True,
                               [WK[c2], 'ident_f'], [pk], inc=(jj == 1 and c2 == 1))
                    j0 = 2 * qd_
                    for jj in range(2):
                        jg = j0 + jj
                        dst = WinT[:, d, 2 * jg:2 * jg + 2, :].rearrange("p g2 (c n) -> p c g2 n", c=2)
                        evac_copy(jj, dst, ps[:, jj * 256:(jj + 1) * 256].rearrange("p (c g2 n) -> p c g2 n", c=2, g2=2),
                                  [pk], ['WinT'])
                    ps, pk = bank()
                    for gi in range(4):
                        jj = gi // 2
                        g2 = gi % 2
                        rows = slice(g2 * 64, (g2 + 1) * 64)
                        outp = ps[:, gi * 128:(gi + 1) * 128]
                        mm_b(outp, Wn[2][rows, jj, :, :].rearrange("p b c -> p (b c)"),
                             Wn[4][rows, jj, :, :].rearrange("p b c -> p (b c)"), gi == 0, False,
                             [WK[2], WK[4]], [pk], inc=False, skip=True, base=g2 * 64)
                        mm_b(outp, Wn[3][rows, jj, :, :].rearrange("p b c -> p (b c)"),
                             Wn[5][rows, jj, :, :].rearrange("p b c -> p (b c)"), False, True,
                             [WK[3], WK[5]], [pk], inc=(gi == 3), skip=True, base=g2 * 64)
                    g0 = 4 * qd_
                    acc = TTacc[:, 0:4, :]
                    ps3 = ps[:, :].rearrange("p (a b) -> p a b", b=128)
                    mk = s5m[:, d:d + 1, :].to_broadcast([128, 4, 128])
                    if d == 0:
                        D2(acc, ps3, mk, ALU.mult, [pk, 's5m'], ['TTacc'])
                    else:
                        w13 = wt1.rearrange("p a b c -> p a (b c)")
                        D2(w13, ps3, mk, ALU.mult, [pk, 's5m'], ['wt1'])
                        D2(acc, acc, w13, ALU.add, ['TTacc', 'wt1'], ['TTacc'])
                        for gi in range(4):
                            g = g0 + gi
                            dve(lambda e: e.scalar_tensor_tensor(
                                out=TT[:, g, :], in0=ident_f[:, :], scalar=dcol[:, g:g + 1],
                                in1=TTacc[:, gi, :], op0=ALU.mult, op1=ALU.add),
                                ['ident_f', 'dcol', 'TTacc'], ['TT'])
            for nm in ["Wn%d" % i for i in range(6)] + ["wt1", "wt2", "TTacc", "PWr", "PWi", "pt1", "pt2", "BPr", "BPi",
                                                         "CR", "CI", "CNr", "CNi", "BR", "BI", "kI", "s5m", "selc",
                                                         "AR", "AI", "LD", "dcol", "dt_", "mag", "ang", "sn", "cs", "abr",
                                                         "abi", "rden", "nre", "cfr", "cfi", "ta", "tb", "anat", "ldf"]:
                A.free(nm)
            tap("s5_WinT%d" % l, WinT, 'WinT')
            tap("s5_WoR%d" % l, WoR, 'WoR')
            tap("s5_TT%d" % l, TT, 'TT')
            if stop == 'S1':
                return

            Up = A.alloc("Up", [32, 8, 16], BF16)
            UGN = A.alloc("UGN", [32, 128], BF16)
            UGR = [A.alloc("UGR%d" % i, [2, 128], BF16) for i in range(2)]
            VX = A.alloc("VX", [16, 2, 2, 129])
            x0n = A.alloc("x0n", [128])
            slot_su, k_su = load_w(w_in[l][:, C_SU:C_SU + 512], 8, 512)
            for s_ in range(8):
                ps, pk = bank()
                for k in range(8):
                    mm(ps[:, :], hT[:, k, s_::8], slot_su[:, k, :], k == 0, k == 7, ['hT', k_su], [pk])
                evac_copy(s_, Up[:, :, s_, :], ps[:, :].rearrange("p (g q) -> p g q", q=16), [pk], ['Up'])
            tap("s5_Up%d" % l, Up, 'Up')
            if stop == 'S2a':
                return
            cx.dma('sp', x0n[0:64, :], s50_d[l].rearrange("d c (j g2) n -> (d c j) (g2 n)", g2=2), writes=['x0n'], sem='s5x0')
            ps, pk = bank()
            mm(ps[:, 0:64], x0n[0:64, :], ident_f[0:64, 0:64], True, True, ['x0n', 'ident_f'], [pk])
            dve(lambda e: e.tensor_copy(out=VX[:, :, :, :, 0], in_=ps[:, 0:64].rearrange("p (d c j) -> p j d c", d=2, c=2)),
                [pk], ['VX'])
            tap("s5_VX0%d" % l, VX, 'VX')
            if stop == 'S2b':
                return
            for j in range(16):
                ub, ukey = bank()
                ug = UGR[j % 2]
                ugk = 'UGR%d' % (j % 2)
                for g2 in range(2):
                    g = 2 * j + g2
                    src = Up[:, g, :, :].rearrange("p s q -> p (s q)")
                    mm(ub[:, g2 * 128:(g2 + 1) * 128], src, ident_bf[:, :], True, True, ['Up', 'ident_bf'], [ukey], inc=False)
                    mm(ub[:, (2 + g2) * 128:(3 + g2) * 128], src, jmat_bf[:, :], True, True, ["Up", "jmat_bf"], [ukey],
                       inc=(g2 == 1))
                act(lambda e: e.activation(out=UGN[:, 2 * j:2 * j + 2, :],
                                           in_=ub[:, 0:256].rearrange("p (a b) -> p a b", b=128), func=AF.Copy),
                    [ukey], ['UGN'])
                dve(lambda e: e.tensor_copy(out=ug, in_=ub[:, 256:512].rearrange("p (a b) -> p a b", b=128)),
                    [ukey], [ugk])
                if stop == 'S2c':
                    continue
                vb, vkey = bank()
                for g2 in range(2):
                    g = 2 * j + g2
                    for d in range(2):
                        rhs = UGN[:, g, :] if d == 0 else ug[:, g2, :]
                        for c2 in range(2):
                            mm(vb[g2 * 64:(g2 + 1) * 64, (2 * d + c2) * 128:(2 * d + c2 + 1) * 128],
                               WinT[:, d, g, c2 * 64:(c2 + 1) * 64], rhs, True, True,
                               ['WinT', 'UGN', ugk], [vkey], inc=(g2 == 1 and d == 1 and c2 == 1))
                evac_copy(j, VX[:, j, :, :, 1:129], vb[:, :].rearrange("p (d c i) -> p d c i", d=2, c=2), [vkey], ['VX'])
            A.free("Up"); A.free("WinT"); A.free("x0n")
            tap("s5_V%d" % l, VX, 'VX')
            if stop in ('S2', 'S2c'):
                return
            ts_ = [A.alloc("ts%d" % i, [16, 2, 2]) for i in range(4)]
            for i in range(128):
                bnd = (i % 32 == 0 and i > 0)
                c1 = (C1r if bnd else C1)
                c2 = (C2r if bnd else C2)
                xp = VX[:, :, :, :, i]
                xsw = VX[:, :, :, ::-1, i]
                cur = VX[:, :, :, :, i + 1]
                t1, t2 = ts_[0], ts_[1]
                cx.op('dve', lambda e: tt_(e, t1, xp, c1, ALU.mult), ['VX', 'VXd0', 'C1', 'C1r'], ['ts0'])
                cx.op('dve', lambda e: tt_(e, t2, xsw, c2, ALU.mult), ['VX', 'VXd0', 'C2', 'C2r'], ['ts1'])
                cx.op('dve', lambda e: tt_(e, t1, t1, t2, ALU.add), ['ts0', 'ts1'], ['ts0'])
                cx.op('dve', lambda e: tt_(e, cur, cur, t1, ALU.add), ['VX', 'VXd0', 'ts0'], ['VXd0'])
            cx.lastw['VXd1'] = cx.lastw['VXd0']
            tap("s5_X%d" % l, VX, 'VXd0')
            tap("s5_Xb%d" % l, VX, 'VXd1')
            fst = A.alloc("fst", [4, 128])
            for d in range(2):
                for c2 in range(2):
                    ps, pk = bank()
                    for sgi in range(4):
                        col = 32 * (sgi + 1)
                        mm(ps[0:16, sgi * 128:(sgi + 1) * 128], VX[:, :, d, c2, col], ident_f[:, :], True, True,
                           ['VXd%d' % d, 'ident_f'], [pk], inc=(sgi == 3))
                    dve(lambda e: e.tensor_copy(out=fst[0:16], in_=ps[0:16, :].rearrange("p (a b) -> p a b", b=128)),
                        [pk], ['fst'])
                    for sgi in range(4):
                        seg = sgi if d == 0 else 3 - sgi
                        cx.dma('sp', os5_d[l, seg, d, c2].rearrange("(j g2) n -> j (g2 n)", g2=2), fst[0:16, sgi, :],
                               reads=['fst'], sem='os5')
            XB = A.alloc("XB", [16, 2, 2, 128], BF16)
            act(lambda e: e.activation(out=XB[:, :, 0, :, :], in_=VX[:, :, 0, :, 0:128], func=AF.Copy), ['VXd0', 'VX'], ['XB'])
            dve(lambda e: e.tensor_copy(out=XB[:, :, 1, :, :], in_=VX[:, :, 1, :, 127::-1]), ['VXd1', 'VX'], ['XB'])
            dve(lambda e: e.tensor_scalar(out=XB[:, :, 0, :, 32:128:32], in0=XB[:, :, 0, :, 32:128:32], scalar1=rcol[:, 0:1],
                                          scalar2=None, op0=ALU.mult), ['XB', 'rcol'], ['XB'])
            dve(lambda e: e.tensor_scalar(out=XB[:, :, 1, :, 31:128:32], in0=XB[:, :, 1, :, 31:128:32], scalar1=rcol[:, 0:1],
                                          scalar2=None, op0=ALU.mult), ['XB', 'rcol'], ['XB'])
            A.free("VX"); A.free("fst")
            for i in range(4):
                A.free("ts%d" % i)
            Yp = A.alloc("Yp", [8, 512])
            for qd_ in range(8):
                yb, ykey = bank()
                for gi in range(4):
                    g = 4 * qd_ + gi
                    j, g2 = g // 2, g % 2
                    rows = slice(g2 * 64, (g2 + 1) * 64)
                    outp = yb[:, gi * 128:(gi + 1) * 128]
                    mm(outp, UGN[:, g, :], TT[:, g, :], gi == 0, False, ['UGN', 'TT'], [ykey], inc=False, skip=True)
                    for d in range(2):
                        for c2 in range(2):
                            last = (d == 1 and c2 == 1)
                            mm_b(outp, XB[rows, j, d, c2, :], WoR[rows, d, j, c2, :], False, last, ['XB', 'WoR'], [ykey],
                                 inc=(last and gi == 3), skip=True, base=g2 * 64)
                evac_copy(qd_, Yp[:, :, 64 * qd_:64 * qd_ + 64].rearrange("p s (g c) -> p s g c", c=16),
                          yb[:, :].rearrange("p (g s c) -> p s g c", g=4, s=8), [ykey], ['Yp'])
            tap("s5_Y%d" % l, Yp, 'Yp')
            A.free("XB"); A.free("UGN"); A.free("WoR"); A.free("TT")
            for i in range(2):
                A.free("UGR%d" % i)
            if stop == 'S3':
                return
            Gp = A.alloc("Gp", [8, 512], BF16)
            gt1 = A.alloc("gt1", [2, 512]); gt2 = A.alloc("gt2", [2, 512])
            for q4 in range(4):
                ysl = Yp[:, 2 * q4:2 * q4 + 2, :]
                act(lambda e: e.activation(out=gt1, in_=ysl, func=AF.Square), ['Yp'], ['gt1'])
                dve(lambda e: e.tensor_scalar(out=gt1, in0=gt1, scalar1=0.044715, scalar2=1.0, op0=ALU.mult, op1=ALU.add),
                    ['gt1'], ['gt1'])
                D2(gt1, gt1, ysl, ALU.mult, ['gt1', 'Yp'], ['gt1'])
                act(lambda e: e.activation(out=gt2, in_=gt1, func=AF.Sigmoid, scale=2.0 * math.sqrt(2.0 / PI)),
                    ['gt1'], ['gt2'])
                D2(Gp[:, 2 * q4:2 * q4 + 2, :], ysl, gt2, ALU.mult, ['Yp', 'gt2'], ['Gp'])
            A.free("Yp"); A.free("gt1"); A.free("gt2")
            gT = A.alloc("gT", [4, T], BF16)
            for s_ in range(8):
                ps, pk = bank()
                for ct in range(4):
                    mm(ps[:, ct * 128:(ct + 1) * 128], Gp[:, s_, ct * 128:(ct + 1) * 128], ident_bf[:, :], True, True,
                       ['Gp', 'ident_bf'], [pk], inc=(ct == 3))
                evac_copy(s_, gT[:, :, s_::8], ps[:, :].rearrange("p (a b) -> p a b", b=128), [pk], ['gT'])
            A.free("Gp")
            sgT = A.alloc("sgT", [4, T], BF16)
            slot_sg, k_sg = load_w(w_in[l][:, C_SG:C_SG + 512], 8, 512)
            for m in range(4):
                proj_fm(slot_sg, k_sg, m * 128, 128,
                        lambda ps, pk, th, m=m: act(lambda e: e.activation(
                            out=sgT[:, m, th * 512:(th + 1) * 512], in_=ps[:, :], func=AF.Silu), [pk], ['sgT']))
            bglu = A.alloc("bglu", [8])
            with nc.allow_non_contiguous_dma(reason="tiny bias columns"):
                cx.dma('sp', bglu, s5_b_glu[l].rearrange("(k p) -> p k", p=128), writes=['bglu'], sem='s5c')
            slot_a, k_a = load_w(s5_w_glu[l][:, 0:512], 4, 512)
            slot_b, k_b = load_w(s5_w_glu[l][:, 512:1024], 4, 512)
            sg_ = [A.alloc("sgm%d" % i, [512]) for i in range(2)]
            it = 0
            for m in range(4):
                for th in range(2):
                    tsl = slice(th * 512, (th + 1) * 512)
                    pa, pka = bank()
                    for k in range(4):
                        mm(pa[:, :], slot_a[:, k, m * 128:(m + 1) * 128], gT[:, k, tsl], k == 0, k == 3, [k_a, 'gT'], [pka])
                    pb, pkb = bank()
                    for k in range(4):
                        mm(pb[:, :], slot_b[:, k, m * 128:(m + 1) * 128], gT[:, k, tsl], k == 0, k == 3, [k_b, 'gT'], [pkb])
                    sg = sg_[it % 2]
                    sgk = 'sgm%d' % (it % 2)
                    it += 1
                    act(lambda e: e.activation(out=sg, in_=pb[:, :], func=AF.Sigmoid, bias=bglu[:, 4 + m:5 + m]),
                        [pkb, 'bglu'], [sgk])
                    dve(lambda e: e.scalar_tensor_tensor(out=sg, in0=pa[:, :], scalar=bglu[:, m:m + 1], in1=sg,
                                                         op0=ALU.add, op1=ALU.mult), [pka, 'bglu', sgk], [sgk])
                    D2(oT_br[2][:, m, tsl], sg, sgT[:, m, tsl], ALU.mult, [sgk, 'sgT'], ['obr2'])
            tap("s5_out%d" % l, oT_br[2][:, :, :], 'obr2')

    def merge_out(l):
        with ExitStack() as st:
            mixedT = sb("mixedT", [128, 8, T], BF16, stack=st)
            wbo = sb("wbo", [128, 3, 4, D], BF16, stack=st)
            gx = [sb("gx%d" % i, [128, 512], stack=st) for i in range(3)]
            t1 = sb("mt1", [128, 512], stack=st)
            t2 = sb("mt2", [128, 512], stack=st)
            for x in range(3):
                cx.dma('pool', wbo[:, x, :, :], w_bo[x][l].rearrange("(k p) c -> p k c", p=128), writes=['wbo'],
                       sem='wbo')
            for fg in range(2):
                slots = [load_w(w_in[l][:, C_MERGE + x * D + fg * 512:C_MERGE + x * D + fg * 512 + 512], 8, 512)
                         for x in range(3)]
                for f4 in range(4):
                    f = fg * 4 + f4
                    for th in range(2):
                        tsl = slice(th * 512, (th + 1) * 512)
                        for x in range(3):
                            slot, skey = slots[x]
                            pg, pgk = bank()
                            for k in range(8):
                                mm(pg[:, :], slot[:, k, f4 * 128:(f4 + 1) * 128], hT[:, k, tsl], k == 0, k == 7,
                                   [skey, 'hT'], [pgk])
                            act(lambda e: e.activation(out=gx[x][:, :], in_=pg[:, :], func=AF.Sigmoid), [pgk], ['gx%d' % x])
                            pp, ppk = bank()
                            for k in range(4):
                                mm(pp[:, :], wbo[:, x, k, f * 128:(f + 1) * 128], oT_br[x][:, k, tsl], k == 0, k == 3,
                                   ['wbo', 'obr%d' % x], [ppk])
                            if x == 0:
                                dve(lambda e: e.tensor_tensor(out=t1[:, :], in0=pp[:, :], in1=gx[x][:, :], op=ALU.mult),
                                    [ppk, 'gx0'], ['mt1'])
                            elif x == 1:
                                dve(lambda e: e.tensor_tensor(out=t2[:, :], in0=pp[:, :], in1=gx[x][:, :], op=ALU.mult),
                                    [ppk, 'gx1'], ['mt2'])
                                dve(lambda e: e.tensor_tensor(out=t1[:, :], in0=t1[:, :], in1=t2[:, :], op=ALU.add),
                                    ['mt1', 'mt2'], ['mt1'])
                            else:
                                dve(lambda e: e.tensor_tensor(out=t2[:, :], in0=pp[:, :], in1=gx[x][:, :], op=ALU.mult),
                                    [ppk, 'gx2'], ['mt2'])
                                dve(lambda e: e.tensor_tensor(out=mixedT[:, f, tsl], in0=t1[:, :], in1=t2[:, :], op=ALU.add),
                                    ['mt1', 'mt2'], ['mixedT'])
            tap("mixedT%d" % l, mixedT[:, :, :], 'mixedT')
            for nh in range(2):
                nsl = slice(nh * 512, (nh + 1) * 512)
                slot, skey = load_w(w_out[l][:, nsl], 8, 512)
                for tt in range(8):
                    ps, pk = bank()
                    for k in range(8):
                        mm(ps[:, :], mixedT[:, k, tt * 128:(tt + 1) * 128], slot[:, k, :], k == 0, k == 7,
                           ['mixedT', skey], [pk])
                    tm = t1 if tt % 2 == 0 else t2
                    tk = 'mt1' if tt % 2 == 0 else 'mt2'
                    dve(lambda e: e.tensor_tensor(out=tm[:, :], in0=ps[:, :], in1=gate_bc[:, nsl], op=ALU.mult),
                        [pk, 'gate_bc'], [tk])
                    dve(lambda e: e.tensor_tensor(out=x_sb[:, tt, nsl], in0=x_sb[:, tt, nsl], in1=tm[:, :], op=ALU.add),
                        ['x', tk], ['x'])
            tap("xout%d" % l, x_sb[:, :, :], 'x')

    for l in range(DEPTH):
        with ExitStack() as st:
            shift_bc = sb("shift_bc", [128, D], stack=st)
            wmod = sb("wmod", [128, D], stack=st)
            bada = sb("bada", [128, 3 * D], stack=st)
            nw_bc = sb("nw_bc", [128, D], stack=st)
            cond_c = sb("cond_c", [128, 8], stack=st)
            scb = sb("scb", [128, 8, 128], BF16, stack=st)
            with nc.allow_non_contiguous_dma(reason="tiny cond column load"):
                cx.dma('sp', cond_c[:, :], cond_d.rearrange("(k p) -> p k", p=128), writes=['cond_c'], sem='c2')
            cx.dma('sp', bada[:, :], b_ada[l].partition_broadcast(128), writes=['bada'], sem='c2')
            cx.dma('sp', nw_bc[:, :], norm_w[l].partition_broadcast(128), writes=['nw_bc'], sem='c2')
            act(lambda e: e.activation(out=cond_c[:, :], in_=cond_c[:, :], func=AF.Silu), ['cond_c'], ['cond_c'])
            dve(lambda e: e.tensor_copy(out=scb[:, :, :], in_=cond_c[:, :].unsqueeze(2).to_broadcast([128, 8, 128])),
                ['cond_c'], ['scb'])
            for ci in range(6):
                slot, skey = load_w(w_ada[l][:, ci * 512:(ci + 1) * 512], 8, 512)
                ps, pk = bank()
                for k in range(8):
                    mm(ps[:, :], scb[:, k, :], slot[:, k, :], k == 0, k == 7, ['scb', skey], [pk])
                dst = (shift_bc, wmod, gate_bc)[ci // 2]
                dkey = ('shift_bc', 'wmod', 'gate_bc')[ci // 2]
                cs = slice((ci % 2) * 512, (ci % 2 + 1) * 512)
                bsl = bada[:, ci * 512:(ci + 1) * 512]
                if ci // 2 == 1:
                    dve(lambda e, ps=ps, dst=dst, cs=cs, bsl=bsl: e.scalar_tensor_tensor(
                        out=dst[:, cs], in0=ps[:, :], scalar=1.0, in1=bsl, op0=ALU.add, op1=ALU.add),
                        [pk, 'bada'], [dkey])
                else:
                    dve(lambda e, ps=ps, dst=dst, cs=cs, bsl=bsl: e.tensor_tensor(
                        out=dst[:, cs], in0=ps[:, :], in1=bsl, op=ALU.add), [pk, 'bada'], [dkey])
            dve(lambda e: e.tensor_tensor(out=wmod[:, :], in0=wmod[:, :], in1=nw_bc[:, :], op=ALU.mult),
                ['wmod', 'nw_bc'], ['wmod'])

            ss = sb("ss_x", [128, 8], stack=st)
            rs = sb("rs_x", [128, 8], stack=st)
            tmp8 = sb("tmp8", [128, 8], stack=st)
            junk = sb("junk_x", [128, D], stack=st)
            hb = [sb("hb%d" % i, [128, D], BF16, stack=st) for i in range(2)]
            tmpf = sb("tmpf", [128, D], stack=st)
            dve(lambda e: e.memset(ss[:, :], 0.0), [], ['ss_x'])
            for tt in range(8):
                act(lambda e, tt=tt: e.activation(out=junk[:, :], in_=x_sb[:, tt, :], func=AF.Square,
                                                  accum_out=ss[:, tt:tt + 1]), ['x', 'ss_x'], ['junk_x', 'ss_x'])
            rstd_from_ss(ss[:, :], rs[:, :], D, 'ss_x', 'rs_x', tmp8[:, :], 'tmp8')
            for tt in range(8):
                hbt = hb[tt % 2]
                hk = 'hb%d' % (tt % 2)
                dve(lambda e, tt=tt: e.scalar_tensor_tensor(out=tmpf[:, :], in0=x_sb[:, tt, :], scalar=rs[:, tt:tt + 1],
                                                            in1=wmod[:, :], op0=ALU.mult, op1=ALU.mult),
                    ['x', 'rs_x', 'wmod'], ['tmpf'])
                dve(lambda e, hbt=hbt: e.tensor_tensor(out=hbt[:, :], in0=tmpf[:, :], in1=shift_bc[:, :], op=ALU.add),
                    ['tmpf', 'shift_bc'], [hk])
                for half in range(2):
                    ps, pk = bank()
                    for kk in range(4):
                        k = half * 4 + kk
                        mm(ps[:, kk * 128:(kk + 1) * 128], hbt[:, k * 128:(k + 1) * 128], ident_bf[:, :], True, True,
                           [hk, 'ident_bf'], [pk], inc=(kk == 3))
                    act(lambda e, ps=ps, half=half, tt=tt: e.activation(
                        out=hT[:, half * 4:half * 4 + 4, tt * 128:(tt + 1) * 128],
                        in_=ps[:, :].rearrange("p (a b) -> p a b", b=128), func=AF.Copy), [pk], ['hT'])

            tap("hT%d" % l, hT[:, :, :], 'hT')
            tap("gate%d" % l, gate_bc[:, :], 'gate_bc')
        if stop == 'B':
            break
        branch_gla(l)
        if stop in ('GLA', 'G1', 'G2', 'G3'):
            break
        branch_mla(l)
        if stop in ('MLA', 'M1'):
            break
        branch_s5(l)
        if stop in ('S5', 'S1', 'S2', 'S3', 'S2a', 'S2b', 'S2c'):
            break
        merge_out(l)
        if stop == 'L0':
            break

    cx.dma('sp', y_d.rearrange("(t p) d -> p t d", p=128), x_sb[:, :, :], reads=['x'], sem='yout')
    cx.final_wait()
    return cx


def _rope_tables():
    rows = T // 64
    r = np.repeat(np.arange(rows, dtype=np.float32), 64)
    col = np.tile(np.arange(64, dtype=np.float32), rows)
    n_freq = 8
    inv = (np.float32(10000.0) ** (-np.arange(n_freq, dtype=np.float32) / np.float32(n_freq))).astype(np.float32)
    ang = np.concatenate([r[:, None] * inv, col[:, None] * inv], axis=-1).astype(np.float32)
    return np.cos(ang).astype(np.float32), np.sin(ang).astype(np.float32)


def _constants():
    c = {}
    c["ident"] = np.eye(128, dtype=np.float32)
    c["jmat"] = np.eye(128, dtype=np.float32)[::-1].copy()
    s_idx = np.arange(128)[:, None]
    t_idx = np.arange(128)[None, :]
    same = (s_idx // 64) == (t_idx // 64)
    c["gla_masks"] = np.stack([(same & (s_idx <= t_idx)), (same & (s_idx >= t_idx))]).astype(np.float32)
    cm = np.ones((128, T), np.float32)
    cm[:, ::64] = 0.0
    c["chunk_mask"] = cm
    sp_ = (np.arange(128) // 16)[:, None]
    s_ = (np.arange(128) // 16)[None, :]
    c["s5_masks"] = np.stack([(s_ >= sp_), (sp_ >= s_)]).astype(np.float32)
    sel = np.zeros((2, 128, 64), np.float32)
    for g2 in range(2):
        for jl in range(4):
            for pch in range(16):
                sel[g2, (2 * jl + g2) * 16 + pch, jl * 16 + pch] = 1.0
    c["sel_c"] = sel
    return c


_W_NAMES = ["norm_w", "w_ada", "b_ada", "w_in", "gla_w_a2", "gla_b_a", "gla_o_norm", "mla_q_norm", "mla_w_uq",
            "mla_kv_norm", "mla_w_uk", "mla_w_uv", "mla_qh_norm", "mla_kh_norm", "s5_a_re", "s5_a_im", "s5_log_dt",
            "s5_b_re", "s5_b_im", "s5_c_re", "s5_c_im", "s5_d", "s5_w_glu", "s5_b_glu", "w_bo_gla", "w_bo_mla",
            "w_bo_s5", "w_out"]


def make_in_maps(inp):
    f = lambda a: np.ascontiguousarray(np.asarray(a, dtype=np.float32))
    consts = _constants()
    cos, sin = _rope_tables()
    weights = {k: f(inp[k]) for k in _W_NAMES}
    maps = []
    for core in range(8):
        m = dict(weights)
        m.update(consts)
        qm = np.zeros((5, T), np.float32)
        km = np.zeros((5, NKEY), np.float32)
        qm[4, :] = 1.0
        km[4, :] = -MASK_BIG
        if core < 4:
            b = core
            m["x"] = f(inp["x_sample"][b])
            m["cond"] = f(inp["c"][b])
            m["ckv_c"] = f(inp["cache_mla_ckv"][b])
            m["kr_c"] = f(inp["cache_mla_krope"][b])
            m["sg0"] = f(inp["state_gla"][b])
            m["s50"] = f(inp["state_s5"][b])
            m["rope_cos"], m["rope_sin"] = cos, sin
            qm[0, :] = 1.0
            km[0, :] = MASK_BIG
            m["rcol"] = np.ones((128, 1), np.float32)
        else:
            j = core - 4
            m["x"] = f(np.asarray(inp["x_prompt"])[4 * j:4 * j + 4].reshape(T, D))
            m["cond"] = f(inp["c_ctx"])
            m["ckv_c"] = np.zeros((DEPTH, PAST, 256), np.float32)
            m["kr_c"] = np.zeros((DEPTH, PAST, 32), np.float32)
            m["sg0"] = np.zeros((DEPTH, 2, 4, 64, 128), np.float32)
            m["s50"] = np.zeros((DEPTH, 2, 2, 32, 64), np.float32)
            m["rope_cos"] = np.ones((T, 16), np.float32)
            m["rope_sin"] = np.zeros((T, 16), np.float32)
            for s in range(4):
                qm[s, s * 256:(s + 1) * 256] = 1.0
                km[s, PAST + s * 256:PAST + (s + 1) * 256] = MASK_BIG
            m["rcol"] = np.zeros((128, 1), np.float32)
        m["qmask"], m["kmask"] = qm, km
        maps.append(m)
    return maps


def kernel(**inputs):
    nc = build_program()
    maps = make_in_maps(inputs)
    res = run_bass_kernel_spmd(nc, maps, core_ids=list(range(8)))
    r = res.results
    y_sample = np.stack([r[b]["y"] for b in range(4)]).astype(np.float32)
    y_prompt = np.concatenate([r[4 + j]["y"].reshape(4, 256, D) for j in range(4)]).astype(np.float32)
    ckv = np.concatenate([r[4 + j]["o_ckv"].reshape(DEPTH, 4, 256, 256).transpose(1, 0, 2, 3) for j in range(4)])
    kr = np.concatenate([r[4 + j]["o_kr"].reshape(DEPTH, 4, 256, 32).transpose(1, 0, 2, 3) for j in range(4)])
    gla = np.concatenate([r[4 + j]["o_gla"].transpose(1, 0, 2, 3, 4, 5) for j in range(4)])
    s5 = np.concatenate([r[4 + j]["o_s5"].transpose(1, 0, 2, 3, 4, 5) for j in range(4)])
    return (y_prompt, y_sample, ckv.astype(np.float32), kr.astype(np.float32), gla.astype(np.float32),
            s5.astype(np.float32))
```

```python
import math
from contextlib import ExitStack

import numpy as np
import concourse.bass as bass
import concourse.mybir as mybir
from concourse.bass_utils import run_bass_kernel_spmd

F32 = mybir.dt.float32
BF16 = mybir.dt.bfloat16
I32 = mybir.dt.int32
ALU = mybir.AluOpType
AF = mybir.ActivationFunctionType
AX = mybir.AxisListType

D = 1024
T = 1024
DEPTH = 2
EPS = 1e-6
PAST = 512
NKEY = PAST + T
D_IN = 6848
C_GQ, C_GK, C_GV, C_GA, C_GG, C_MQ, C_MKV, C_MKR, C_MG, C_SU, C_SG, C_MERGE = (
    0, 256, 512, 1024, 1056, 1568, 1952, 2208, 2240, 2752, 3264, 3776)
MASK_BIG = 2048.0
ATT_SCALE = 96 ** -0.5
PI = math.pi


class Ctx:
    def __init__(self, nc, es):
        self.nc = nc
        self.es = es
        self.eng = {'pe': nc.tensor, 'act': nc.scalar, 'dve': nc.vector, 'pool': nc.gpsimd, 'sp': nc.sync}
        self.sems = {}
        self.cnt = {}
        for e in ('pe', 'act', 'dve', 'pool'):
            self.sems[e] = es.enter_context(nc.semaphore("s_" + e))
            self.cnt[e] = 0
        self.seen = {e: {} for e in self.eng}
        self.lastw = {}
        self.readers = {}
        self.n_ops = 0
        self.bank_rr = 0
        self.fresh = {}
        self.epoch = {}

    def _collect(self, reads, writes):
        toks = {}

        def add(t):
            if t is None:
                return
            s, v = t
            if toks.get(s, 0) < v:
                toks[s] = v
        for k in list(reads) + list(writes):
            snap = self.fresh.pop(k, None)
            if snap is not None:
                for s_, v_ in snap.items():
                    if v_ > 0:
                        add((s_, v_))
                self.lastw.pop(k, None)
                self.readers.pop(k, None)
        for k in reads:
            add(self.lastw.get(k))
            if isinstance(k, tuple) and k[0] == 'ps':
                for s, v in self.readers.get(k, {}).items():
                    add((s, v))
        for k in writes:
            add(self.lastw.get(k))
            for s, v in self.readers.get(k, {}).items():
                add((s, v))
        return toks

    def _emit_waits(self, e, toks, skip_own=False, attach=False):
        eng = self.eng[e]
        seen = self.seen[e]
        need = []
        for s, v in toks.items():
            if skip_own and s == e:
                continue
            if s not in self.eng:
                v = max(v, self.cnt[s])
            if seen.get(s, 0) >= v:
                continue
            need.append((s, v))
            seen[s] = v
        last = None
        if attach and need:
            last = need.pop()
        for s, v in need:
            eng.wait_ge(self.sems[s], v)
        return last

    def _record(self, tok, reads, writes):
        s, v = tok
        for k in writes:
            self.lastw[k] = tok
            self.readers[k] = {}
        for k in reads:
            r = self.readers.setdefault(k, {})
            if r.get(s, 0) < v:
                r[s] = v

    def op(self, e, fn, reads=(), writes=(), inc=True):
        toks = self._collect(reads, writes)
        last = self._emit_waits(e, toks, skip_own=(e == 'pe'), attach=True)
        ins = fn(self.eng[e])
        if last is not None:
            ins._wait_ge(self.sems[last[0]], last[1])
        tok = (e, self.cnt[e] + 1)
        if inc:
            self.cnt[e] += 1
            ins.then_inc(self.sems[e], 1)
        self._record(tok, reads, writes)
        self.n_ops += 1
        return ins

    def dma(self, q, out, in_, reads=(), writes=(), sem=None):
        sem = 'D:' + str(writes[0] if writes else reads[0])
        if sem not in self.sems:
            self.sems[sem] = self.es.enter_context(self.nc.semaphore("d_%d" % len(self.sems)))
            self.cnt[sem] = 0
        toks = self._collect(reads, writes)
        self._emit_waits(q, toks)
        ins = self.eng[q].dma_start(out=out, in_=in_)
        self.cnt[sem] += 16
        ins.then_inc(self.sems[sem], 16)
        self._record((sem, self.cnt[sem]), reads, writes)
        self.n_ops += 1

    def final_wait(self):
        toks = {s: v for s, v in self.cnt.items() if v > 0}
        self._emit_waits('sp', toks)


def build_program(dbg=None, stop=None):
    nc = bass.Bass("TRN2", target_bir_lowering=False)
    es = ExitStack()
    with es:
        cx = _build(nc, es, dbg or {}, stop)
    nc._n_ops = cx.n_ops
    return nc


def _build(nc, es, dbg, stop):
    cx = Ctx(nc, es)

    def tap(name, ap, key):
        if name not in dbg:
            return
        d = nc.dram_tensor("dbg_" + name, list(ap.shape), ap.dtype, kind="ExternalOutput").ap()
        cx.dma('sp', d, ap, reads=[key], sem='dbg')

    def din(name, shape, dt=F32):
        return nc.dram_tensor(name, list(shape), dt, kind="ExternalInput").ap()

    def dout(name, shape, dt=F32):
        return nc.dram_tensor(name, list(shape), dt, kind="ExternalOutput").ap()

    x_d = din("x", [T, D])
    cond_d = din("cond", [D])
    ckvc_d = din("ckv_c", [DEPTH, PAST, 256])
    krc_d = din("kr_c", [DEPTH, PAST, 32])
    sg0_d = din("sg0", [DEPTH, 2, 4, 64, 128])
    s50_d = din("s50", [DEPTH, 2, 2, 32, 64])
    cos_d = din("rope_cos", [T, 16])
    sin_d = din("rope_sin", [T, 16])
    qmask_d = din("qmask", [5, T])
    kmask_d = din("kmask", [5, NKEY])
    rcol_d = din("rcol", [128, 1])
    ident_d = din("ident", [128, 128])
    jmat_d = din("jmat", [128, 128])
    glam_d = din("gla_masks", [2, 128, 128])
    cmask_d = din("chunk_mask", [128, T])
    s5m_d = din("s5_masks", [2, 128, 128])
    selc_d = din("sel_c", [2, 128, 64])
    norm_w = din("norm_w", [DEPTH, D])
    w_ada = din("w_ada", [DEPTH, D, 3 * D])
    b_ada = din("b_ada", [DEPTH, 3 * D])
    w_in = din("w_in", [DEPTH, D, D_IN])
    gla_w_a2 = din("gla_w_a2", [DEPTH, 2, 16, 256])
    gla_b_a = din("gla_b_a", [DEPTH, 2, 256])
    gla_o_norm = din("gla_o_norm", [DEPTH, 128])
    mla_q_norm = din("mla_q_norm", [DEPTH, 384])
    mla_w_uq = din("mla_w_uq", [DEPTH, 384, 384])
    mla_kv_norm = din("mla_kv_norm", [DEPTH, 256])
    mla_w_uk = din("mla_w_uk", [DEPTH, 256, 256])
    mla_w_uv = din("mla_w_uv", [DEPTH, 256, 512])
    mla_qh_norm = din("mla_qh_norm", [DEPTH, 96])
    mla_kh_norm = din("mla_kh_norm", [DEPTH, 96])
    s5_a_re = din("s5_a_re", [DEPTH, 2, 32, 64])
    s5_a_im = din("s5_a_im", [DEPTH, 2, 32, 64])
    s5_log_dt = din("s5_log_dt", [DEPTH, 2, 32])
    s5_b_re = din("s5_b_re", [DEPTH, 32, 64, 16])
    s5_b_im = din("s5_b_im", [DEPTH, 32, 64, 16])
    s5_c_re = din("s5_c_re", [DEPTH, 32, 16, 64])
    s5_c_im = din("s5_c_im", [DEPTH, 32, 16, 64])
    s5_d = din("s5_d", [DEPTH, 512])
    s5_w_glu = din("s5_w_glu", [DEPTH, 512, 1024])
    s5_b_glu = din("s5_b_glu", [DEPTH, 1024])
    w_bo = [din("w_bo_gla", [DEPTH, 512, D]), din("w_bo_mla", [DEPTH, 512, D]), din("w_bo_s5", [DEPTH, 512, D])]
    w_out = din("w_out", [DEPTH, D, D])

    y_d = dout("y", [T, D])
    ockv_d = dout("o_ckv", [DEPTH, T, 256])
    okr_d = dout("o_kr", [DEPTH, T, 32])
    ogla_d = dout("o_gla", [DEPTH, 4, 2, 4, 64, 128])
    os5_d = dout("o_s5", [DEPTH, 4, 2, 2, 32, 64])

    uniq = {'n': 0}

    def _scope_exit():
        cx.epoch = dict(cx.cnt)

    def sb(name, shape, dt=F32, stack=None):
        uniq['n'] += 1
        if stack is not None:
            cx.fresh[name] = dict(cx.epoch)
            if not getattr(stack, '_epoch_hooked', False):
                stack._epoch_hooked = True
                stack.callback(_scope_exit)
        return (stack or es).enter_context(nc.sbuf_tensor("sb%d_%s" % (uniq['n'], name), list(shape), dt))

    psb = [es.enter_context(nc.psum_tensor("psb%d" % i, [128, 512], F32)) for i in range(8)]

    reserved = set()

    def bank():
        while cx.bank_rr in reserved:
            cx.bank_rr = (cx.bank_rr + 1) % 8
        i = cx.bank_rr
        cx.bank_rr = (i + 1) % 8
        return psb[i], ('ps', i)

    def reserve_bank():
        ps, pk = bank()
        reserved.add(pk[1])
        return ps, pk

    def release_bank(pk):
        reserved.discard(pk[1])

    x_sb = sb("x_sb", [128, 8, D])
    hT = sb("hT", [128, 8, T], BF16)
    gate_bc = sb("gate_bc", [128, D])
    ident_bf = sb("ident_bf", [128, 128], BF16)
    jmat_bf = sb("jmat_bf", [128, 128], BF16)
    ident_f = sb("ident_f", [128, 128])
    ones_bf = sb("ones_bf", [128, 128], BF16)
    glam = sb("glam", [128, 2, 128])
    cmask = sb("cmask", [128, T])
    ropec = sb("ropec", [128, 8, 16])
    ropes = sb("ropes", [128, 8, 16])
    rcol = sb("rcol", [128, 1])
    NSLOT = 3
    wring = [sb("wring%d" % i, [128, 8, 512], BF16) for i in range(NSLOT)]
    oT_br = [sb("obr%d" % i, [128, 4, T], BF16) for i in range(3)]

    ring_state = {'i': 0}

    def load_w(src_ap, kt, ncol):
        i = ring_state['i']
        ring_state['i'] = (i + 1) % NSLOT
        slot = wring[i]
        key = ('wring', i)
        cx.dma('pool', slot[:, 0:kt, 0:ncol], src_ap.rearrange("(k p) c -> p k c", p=128),
               writes=[key], sem='wring%d' % i)
        return slot, key

    cx.dma('sp', x_sb[:, :, :], x_d.rearrange("(t p) d -> p t d", p=128), writes=['x'], sem='x')
    cx.dma('pool', ident_bf[:, :], ident_d[:, :], writes=['ident_bf'], sem='c0')
    cx.dma('pool', jmat_bf[:, :], jmat_d[:, :], writes=['jmat_bf'], sem='c0')
    cx.dma('sp', ident_f[:, :], ident_d[:, :], writes=['ident_f'], sem='c1')
    cx.dma('sp', glam[:, :, :], glam_d.rearrange("a p c -> p a c"), writes=['glam'], sem='c1')
    cx.dma('sp', cmask[:, :], cmask_d[:, :], writes=['cmask'], sem='c1')
    cx.dma('sp', ropec[:, :, :], cos_d.rearrange("(t p) c -> p t c", p=128), writes=['ropec'], sem='c1')
    cx.dma('sp', ropes[:, :, :], sin_d.rearrange("(t p) c -> p t c", p=128), writes=['ropes'], sem='c1')
    cx.dma('sp', rcol[:, :], rcol_d[:, :], writes=['rcol'], sem='c1')
    cx.op('dve', lambda e: e.memset(ones_bf[:, :], 1.0), writes=['ones_bf'])

    def act(fn, reads, writes):
        return cx.op('act', fn, reads, writes)

    def dve(fn, reads, writes):
        return cx.op('dve', fn, reads, writes)

    def pool(fn, reads, writes):
        return cx.op('pool', fn, reads, writes)

    def mm(out, lhsT, rhs, start, stop, reads, writes, inc=None, skip=False):
        if inc is None:
            inc = stop
        if skip:
            return cx.op('pe', lambda e: e.matmul(out, lhsT, rhs, start=start, stop=stop, skip_group_check=True),
                         reads, writes, inc=inc)
        return cx.op('pe', lambda e: e.matmul(out, lhsT, rhs, start=start, stop=stop), reads, writes, inc=inc)

    def mm_b(out, lhsT, rhs, start, stop, reads, writes, inc=None, skip=False, base=0):
        if inc is None:
            inc = stop
        if base == 0:
            return mm(out, lhsT, rhs, start, stop, reads, writes, inc=inc, skip=skip)
        mm(out[0:64], lhsT[:, 0:64], rhs, start, stop, reads, writes, inc=False, skip=skip)
        return mm(out[64:128], lhsT[:, 64:128], rhs, start, stop, reads, writes, inc=inc, skip=skip)

    def rstd_from_ss(ss_ap, out_ap, n, key_in, key_out, tmp_ap, key_tmp):
        act(lambda e: e.activation(out=tmp_ap, in_=ss_ap, func=AF.Sqrt, bias=EPS, scale=1.0 / n),
            [key_in], [key_tmp])
        dve(lambda e: e.reciprocal(out=out_ap, in_=tmp_ap), [key_tmp], [key_out])

    def proj_fm(slot, skey, c0, m, evac, kt=8, rhsT=None, rkey='hT'):
        src = hT if rhsT is None else rhsT
        for th in range(2):
            ps, pk = bank()
            for k in range(kt):
                mm(ps[0:m, :], slot[:, k, c0:c0 + m], src[:, k, th * 512:(th + 1) * 512],
                   k == 0, k == kt - 1, [skey, rkey], [pk])
            evac(ps, pk, th)

    def evac_copy(i, out_ap, in_ap, rkeys, wkeys):
        if i % 2 == 0:
            act(lambda e: e.activation(out=out_ap, in_=in_ap, func=AF.Copy), rkeys, wkeys)
        else:
            dve(lambda e: e.tensor_copy(out=out_ap, in_=in_ap), rkeys, wkeys)

    def branch_gla(l):
        with ExitStack() as st:
            v_tm = sb("v_tm", [128, 8, 512], BF16, stack=st)
            ggT = sb("ggT", [128, 4, T], BF16, stack=st)
            alow = sb("alow", [32, T], BF16, stack=st)
            oT = sb("oT", [128, 4, T], F32, stack=st)
            wa2p = sb("wa2p", [32, 2, 256], BF16, stack=st)
            ba = sb("ba", [128, 2, 2], F32, stack=st)
            onw = sb("onw", [128, 1], stack=st)
            wsm = sb("wsm", [128, 8, 32], BF16, stack=st)
            dve(lambda e: e.memset(wa2p[:, :, :], 0.0), [], ['wa2p'])
            dve(lambda e: e.memset(oT[:, :, :], 0.0), [], ['oT'])

            cx.dma('pool', wa2p[0:16, 0, :], gla_w_a2[l, 0], writes=['wa2p'], sem='gsm')
            cx.dma('pool', wa2p[16:32, 1, :], gla_w_a2[l, 1], writes=['wa2p'], sem='gsm')
            cx.dma('pool', wsm[:, :, :], w_in[l][:, C_GA:C_GA + 32].rearrange("(k p) c -> p k c", p=128),
                   writes=['wsm'], sem='gsm')
            with nc.allow_non_contiguous_dma(reason="tiny bias columns"):
                cx.dma('sp', ba[:, :, :], gla_b_a[l].rearrange("d (hp p) -> p d hp", p=128), writes=['ba'], sem='gsm2')
                cx.dma('sp', onw[:, :], gla_o_norm[l].rearrange("(p o) -> p o", o=1), writes=['onw'], sem='gsm2')
            dve(lambda e: e.tensor_scalar(out=ba[:, :, :], in0=ba[:, :, :], scalar1=-1.0, scalar2=None, op0=ALU.mult),
                ['ba'], ['ba'])
            slot_qk, k_qk = load_w(w_in[l][:, C_GQ:C_GQ + 512], 8, 512)
            slot_v, k_v = load_w(w_in[l][:, C_GV:C_GV + 512], 8, 512)
            for tt in range(8):
                ps, pk = bank()
                for k in range(8):
                    mm(ps[:, :], hT[:, k, tt * 128:(tt + 1) * 128], slot_v[:, k, :], k == 0, k == 7, ['hT', k_v], [pk])
                evac_copy(tt, v_tm[:, tt, :], ps[:, :], [pk], ['v_tm'])
            slot_gg, k_gg = load_w(w_in[l][:, C_GG:C_GG + 512], 8, 512)
            proj_fm(wsm, 'wsm', 0, 32,
                    lambda ps, pk, th: act(lambda e: e.activation(out=alow[0:32, th * 512:(th + 1) * 512], in_=ps[0:32, :],
                                                                  func=AF.Copy), [pk], ['alow']))
            for m in range(4):
                proj_fm(slot_gg, k_gg, m * 128, 128,
                        lambda ps, pk, th, m=m: act(lambda e: e.activation(
                            out=ggT[:, m, th * 512:(th + 1) * 512], in_=ps[:, :], func=AF.Silu), [pk], ['ggT']))

            tap("g1_ggT", ggT[:, :, :], 'ggT')
            if stop == 'G1':
                return
            for hp in range(2):
                with ExitStack() as s2:
                    q_f = sb("q_f", [128, T], stack=s2)
                    k_f = sb("k_f", [128, T], stack=s2)
                    SP = sb("SP", [128, T], stack=s2)
                    BC = sb("BC", [128, T], stack=s2)
                    E = sb("E", [128, T], stack=s2)
                    TM = sb("TM", [128, T], stack=s2)
                    qd = [sb("qd%d" % d, [128, T], BF16, stack=s2) for d in range(2)]
                    kd = [sb("kd%d" % d, [128, T], BF16, stack=s2) for d in range(2)]
                    kr = [sb("kr%d" % d, [128, T], BF16, stack=s2) for d in range(2)]
                    krtok = [sb("krtok%d" % d, [128, 8, 128], BF16, stack=s2) for d in range(2)]
                    gdec = [sb("gdec%d" % d, [128, 16], stack=s2) for d in range(2)]
                    S = [sb("S%d" % d, [128, 128], stack=s2) for d in range(2)]
                    Sb = [sb("Sb%d" % d, [128, 128], BF16, stack=s2) for d in range(2)]
                    stg = [sb("stg%d" % i, [128, 128], stack=s2) for i in range(2)]
                    attsb = [sb("attsb%d" % i, [128, 2, 128], BF16, stack=s2) for i in range(2)]
                    proj_fm(slot_qk, k_qk, hp * 128, 128,
                            lambda ps, pk, th: act(lambda e: e.mul(out=q_f[:, th * 512:(th + 1) * 512], in_=ps[:, :],
                                                                   mul=0.125), [pk], ['q_f']))
                    proj_fm(slot_qk, k_qk, 256 + hp * 128, 128,
                            lambda ps, pk, th: dve(lambda e: e.tensor_copy(out=k_f[:, th * 512:(th + 1) * 512],
                                                                            in_=ps[:, :]), [pk], ['k_f']))
                    for d in range(2):
                        cx.dma('sp', S[d][:, :], sg0_d[l, d, 2 * hp:2 * hp + 2].rearrange("h k v -> (h k) v"),
                               writes=['S%d' % d], sem='gS%d' % d)
                        act(lambda e, d=d: e.activation(out=Sb[d][:, :], in_=S[d][:, :], func=AF.Copy),
                            ['S%d' % d], ['Sb%d' % d])
                        for th in range(2):
                            ps, pk = bank()
                            mm(ps[:, :], wa2p[:, d, hp * 128:(hp + 1) * 128], alow[0:32, th * 512:(th + 1) * 512],
                               True, True, ['wa2p', 'alow'], [pk])
                            act(lambda e, ps=ps, th=th, d=d: e.activation(
                                out=E[:, th * 512:(th + 1) * 512], in_=ps[:, :], func=AF.Exp,
                                bias=ba[:, d, hp:hp + 1], scale=-1.0), [pk, 'ba'], ['E'])
                        act(lambda e: e.activation(out=SP[:, :], in_=E[:, :], func=AF.Ln, bias=1.0), ['E'], ['SP'])
                        dve(lambda e: e.tensor_tensor_scan(out=BC[:, :], data0=cmask[:, :], data1=SP[:, :], initial=0.0,
                                                           op0=ALU.mult, op1=ALU.add), ['cmask', 'SP'], ['BC'])
                        act(lambda e, d=d: e.activation(out=gdec[d][:, :], in_=BC[:, 63::64], func=AF.Exp,
                                                        scale=-1.0 / 16), ['BC'], ['gdec%d' % d])
                        BC3 = BC[:, :].rearrange("p (c j) -> p c j", j=64)
                        BL = BC3[:, :, 63:64].to_broadcast([128, 16, 64])
                        TM3 = TM[:, :].rearrange("p (c j) -> p c j", j=64)
                        SP3 = SP[:, :].rearrange("p (c j) -> p c j", j=64)
                        kq, kk, kkr = 'qd%d' % d, 'kd%d' % d, 'kr%d' % d
                        if d == 0:
                            src = BC
                            skey = 'BC'
                        else:
                            dve(lambda e: e.tensor_tensor(out=TM3, in0=BL, in1=BC3, op=ALU.subtract), ['BC'], ['TM'])
                            dve(lambda e: e.tensor_tensor(out=TM[:, :], in0=TM[:, :], in1=SP[:, :], op=ALU.add),
                                ['TM', 'SP'], ['TM'])
                            src = TM
                            skey = 'TM'
                        act(lambda e, src=src: e.activation(out=E[:, :], in_=src[:, :], func=AF.Exp, scale=-1.0 / 16),
                            [skey], ['E'])
                        dve(lambda e, d=d: e.tensor_tensor(out=qd[d][:, :], in0=q_f[:, :], in1=E[:, :], op=ALU.mult),
                            ['q_f', 'E'], [kq])
                        act(lambda e, src=src: e.activation(out=E[:, :], in_=src[:, :], func=AF.Exp, scale=1.0 / 16),
                            [skey], ['E'])
                        dve(lambda e, d=d: e.tensor_tensor(out=kd[d][:, :], in0=k_f[:, :], in1=E[:, :], op=ALU.mult),
                            ['k_f', 'E'], [kk])
                        if d == 0:
                            dve(lambda e: e.tensor_tensor(out=TM3, in0=BL, in1=BC3, op=ALU.subtract), ['BC'], ['TM'])
                        else:
                            dve(lambda e: e.tensor_tensor(out=TM[:, :], in0=BC[:, :], in1=SP[:, :], op=ALU.subtract),
                                ['BC', 'SP', 'TM'], ['TM'])
                        act(lambda e: e.activation(out=E[:, :], in_=TM[:, :], func=AF.Exp, scale=-1.0 / 16),
                            ['TM'], ['E'])
                        dve(lambda e, d=d: e.tensor_tensor(out=kr[d][:, :], in0=k_f[:, :], in1=E[:, :], op=ALU.mult),
                            ['k_f', 'E'], [kkr])
                        for half in range(2):
                            ps, pk = bank()
                            for kk4 in range(4):
                                tt = half * 4 + kk4
                                mm(ps[:, kk4 * 128:(kk4 + 1) * 128], kr[d][:, tt * 128:(tt + 1) * 128], ident_bf[:, :],
                                   True, True, [kkr, 'ident_bf'], [pk], inc=(kk4 == 3))
                            evac_copy(half, krtok[d][:, half * 4:half * 4 + 4, :],
                                      ps[:, :].rearrange("p (a b) -> p a b", b=128), [pk], ['krtok%d' % d])

                    tap("g2_kr", krtok[1][:, :, :], 'krtok1')
                    if stop == 'G2':
                        return

                    def gla_step(d, cp, step_i):
                        kq, kk, kS, kSb = 'qd%d' % d, 'kd%d' % d, 'S%d' % d, 'Sb%d' % d
                        cols = slice(cp * 128, (cp + 1) * 128)
                        ab, akey = bank()
                        for h2 in range(2):
                            rows = slice(h2 * 64, (h2 + 1) * 64)
                            mm_b(ab[:, h2 * 128:(h2 + 1) * 128], kd[d][rows, cols], qd[d][rows, cols], True, True,
                                 [kk, kq], [akey], inc=(h2 == 1), base=h2 * 64)
                        asb = attsb[step_i % 2]
                        askey = 'attsb%d' % (step_i % 2)
                        dve(lambda e: e.tensor_tensor(
                            out=asb[:, :, :], in0=ab[:, 0:256].rearrange("p (a b) -> p a b", b=128),
                            in1=glam[:, d:d + 1, :].to_broadcast([128, 2, 128]), op=ALU.mult),
                            [akey, 'glam'], [askey])
                        ob, okey = bank()
                        for h2 in range(2):
                            h = 2 * hp + h2
                            mm(ob[:, h2 * 128:(h2 + 1) * 128], v_tm[:, cp, h * 128:(h + 1) * 128], asb[:, h2, :],
                               h2 == 0, False, ['v_tm', askey], [okey], inc=False, skip=True)
                        order = [2 * cp, 2 * cp + 1] if d == 0 else [2 * cp + 1, 2 * cp]
                        for idx, c in enumerate(order):
                            ci = c % 2
                            boundary = (c % 4 == 0 and c > 0) if d == 0 else (c % 4 == 3 and c < 15)
                            if boundary:
                                dve(lambda e: e.tensor_scalar(out=S[d][:, :], in0=S[d][:, :], scalar1=rcol[:, 0:1],
                                                              scalar2=None, op0=ALU.mult), [kS, 'rcol'], [kS])
                                act(lambda e: e.activation(out=Sb[d][:, :], in_=S[d][:, :], func=AF.Copy), [kS], [kSb])
                            for h2 in range(2):
                                rows = slice(h2 * 64, (h2 + 1) * 64)
                                mm_b(ob[:, h2 * 128 + ci * 64:h2 * 128 + ci * 64 + 64], Sb[d][rows, :],
                                     qd[d][rows, c * 64:(c + 1) * 64], False, idx == 1, [kSb, kq], [okey],
                                     inc=(idx == 1 and h2 == 1), skip=True, base=h2 * 64)
                            kvb, kvkey = bank()
                            crow = slice(ci * 64, (ci + 1) * 64)
                            mm_b(kvb[:, 0:256], krtok[d][crow, cp, :], v_tm[crow, cp, hp * 256:(hp + 1) * 256], True, True,
                                 ['krtok%d' % d, 'v_tm'], [kvkey], base=ci * 64)
                            for h2 in range(2):
                                rows = slice(h2 * 64, (h2 + 1) * 64)
                                dve(lambda e, rows=rows, h2=h2, c=c: e.scalar_tensor_tensor(
                                    out=S[d][rows, :], in0=S[d][rows, :], scalar=gdec[d][rows, c:c + 1],
                                    in1=kvb[rows, h2 * 128:(h2 + 1) * 128], op0=ALU.mult, op1=ALU.add),
                                    [kS, 'gdec%d' % d, kvkey], [kS])
                            act(lambda e: e.activation(out=Sb[d][:, :], in_=S[d][:, :], func=AF.Copy), [kS], [kSb])
                            seg_end = (c % 4 == 3) if d == 0 else (c % 4 == 0)
                            if seg_end:
                                seg = c // 4
                                sg = stg[seg % 2]
                                sgk = 'stg%d' % (seg % 2)
                                act(lambda e, sg=sg: e.activation(out=sg[:, :], in_=S[d][:, :], func=AF.Copy), [kS], [sgk])
                                cx.dma('sp', ogla_d[l, seg, d, 2 * hp:2 * hp + 2].rearrange("h k v -> (h k) v"), sg[:, :],
                                       reads=[sgk], sem='ogla')
                        dve(lambda e: e.tensor_tensor(
                            out=oT[:, 2 * hp:2 * hp + 2, cols], in0=oT[:, 2 * hp:2 * hp + 2, cols],
                            in1=ob[:, 0:256].rearrange("p (a b) -> p a b", b=128), op=ALU.add), ['oT', okey], ['oT'])

                    for step in range(8):
                        gla_step(0, step, 2 * step)
                        gla_step(1, 7 - step, 2 * step + 1)
            tap("gla_oT%d" % l, oT[:, :, :], 'oT')
            with ExitStack() as s3:
                sqb = [sb("sqb%d" % i, [128, 512], BF16, stack=s3) for i in range(2)]
                sdv = [sb("sdv%d" % i, [128, 512], stack=s3) for i in range(2)]
                tmv = [sb("tmv%d" % i, [128, 512], stack=s3) for i in range(2)]
                i = 0
                for h in range(4):
                    for th in range(2):
                        cs = slice(th * 512, (th + 1) * 512)
                        sq, sd, tm_ = sqb[i % 2], sdv[i % 2], tmv[i % 2]
                        ksq, ksd, ktm = 'sqb%d' % (i % 2), 'sdv%d' % (i % 2), 'tmv%d' % (i % 2)
                        act(lambda e, sq=sq, h=h, cs=cs: e.activation(out=sq[:, :], in_=oT[:, h, cs], func=AF.Square),
                            ['oT'], [ksq])
                        ps, pk = bank()
                        mm(ps[:, :], ones_bf[:, :], sq[:, :], True, True, ['ones_bf', ksq], [pk])
                        act(lambda e, ps=ps, sd=sd: e.activation(out=sd[:, :], in_=ps[:, :], func=AF.Sqrt, bias=EPS,
                                                                 scale=1.0 / 128), [pk], [ksd])
                        dve(lambda e, sd=sd: e.reciprocal(out=sd[:, :], in_=sd[:, :]), [ksd], [ksd])
                        dve(lambda e, sd=sd, tm_=tm_, h=h, cs=cs: e.scalar_tensor_tensor(
                            out=tm_[:, :], in0=oT[:, h, cs], scalar=onw[:, 0:1], in1=sd[:, :], op0=ALU.mult, op1=ALU.mult),
                            ['oT', 'onw', ksd], [ktm])
                        dve(lambda e, tm_=tm_, h=h, cs=cs: e.tensor_tensor(
                            out=oT_br[0][:, h, cs], in0=tm_[:, :], in1=ggT[:, h, cs], op=ALU.mult),
                            [ktm, 'ggT'], ['obr0'])
                        i += 1
            tap("gla_out%d" % l, oT_br[0][:, :, :], 'obr0')

    def branch_mla(l):
        with ExitStack() as st:
            QT = sb("QT", [128, 4, T], BF16, stack=st)
            KT = sb("KT", [128, 4, NKEY], BF16, stack=st)
            Vt = sb("Vt", [128, 12, 512], BF16, stack=st)
            mgT = sb("mgT", [128, 4, T], BF16, stack=st)
            CKb = sb("CKb", [128, 12, 256], BF16, stack=st)
            KR = sb("KR", [128, 12, 32], stack=st)
            cqn = sb("cqn", [128, 8, 384], BF16, stack=st)
            cqnT = sb("cqnT", [128, 3, T], BF16, stack=st)
            ckvT = sb("ckvT", [128, 2, NKEY], BF16, stack=st)
            wuq = sb("wuq", [128, 3, 384], BF16, stack=st)
            wuqf = sb("wuqf", [128, 3, 384], stack=st)
            qnw = sb("qnw", [128, 3], stack=st)
            wuk = sb("wuk", [128, 2, 256], BF16, stack=st)
            wuv = sb("wuv", [128, 2, 512], BF16, stack=st)
            kvw_bc = sb("kvw_bc", [128, 256], stack=st)
            qhw_bc = sb("qhw_bc", [128, 96], stack=st)
            khw_bc = sb("khw_bc", [128, 96], stack=st)
            ssq = sb("ssq", [128, 8, 2], stack=st)
            rsq = sb("rsq", [128, 8, 2], stack=st)
            tq = sb("tq", [128, 8, 2], stack=st)
            sskr = sb("sskr", [128, 12], stack=st)
            junk = sb("junkm", [128, 512], stack=st)
            stg = [sb("stgkv%d" % i, [128, 288], stack=st) for i in range(2)]
            for h in range(4):
                cx.dma('pool', QT[96:101, h, :], qmask_d[:, :], writes=['QT'], sem='mmask')
                cx.dma('pool', KT[96:101, h, :], kmask_d[:, :], writes=['KT'], sem='mmask')
            cx.dma('sp', wuqf[:, :, :], mla_w_uq[l].rearrange("(k p) c -> p k c", p=128), writes=['wuqf'], sem='mw1')
            with nc.allow_non_contiguous_dma(reason="tiny norm weight column"):
                cx.dma('sp', qnw[:, :], mla_q_norm[l].rearrange("(k p) -> p k", p=128), writes=['qnw'], sem='mw1')
            cx.dma('pool', wuk[:, :, :], mla_w_uk[l].rearrange("(k p) c -> p k c", p=128), writes=['wuk'], sem='mw2')
            cx.dma('pool', wuv[:, :, :], mla_w_uv[l].rearrange("(k p) c -> p k c", p=128), writes=['wuv'], sem='mw2')
            cx.dma('sp', kvw_bc[:, :], mla_kv_norm[l].partition_broadcast(128), writes=['kvw_bc'], sem='mw1')
            cx.dma('sp', qhw_bc[:, :], mla_qh_norm[l].partition_broadcast(128), writes=['qhw_bc'], sem='mw1')
            cx.dma('sp', khw_bc[:, :], mla_kh_norm[l].partition_broadcast(128), writes=['khw_bc'], sem='mw1')
            cx.dma('pool', CKb[:, 0:4, :], ckvc_d[l].rearrange("(t p) c -> p t c", p=128), writes=['CKb'], sem='mw2')
            cx.dma('sp', KR[:, 0:4, :], krc_d[l].rearrange("(t p) c -> p t c", p=128), writes=['KR'], sem='mw1')
            for k in range(3):
                dve(lambda e: e.tensor_scalar(out=wuq[:, k, :], in0=wuqf[:, k, :], scalar1=qnw[:, k:k + 1], scalar2=None,
                                              op0=ALU.mult), ['wuqf', 'qnw'], ['wuq'])
            dve(lambda e: e.memset(ssq[:, :, :], 0.0), [], ['ssq'])
            dve(lambda e: e.memset(sskr[:, :], 0.0), [], ['sskr'])
            slotA, kA = load_w(w_in[l][:, C_MQ:C_MQ + 384], 8, 384)
            slotB, kB = load_w(w_in[l][:, C_MKV:C_MKV + 288], 8, 288)
            for tt in range(8):
                tsl = slice(tt * 128, (tt + 1) * 128)
                psA, pkA = bank()
                for k in range(8):
                    mm(psA[:, 0:384], hT[:, k, tsl], slotA[:, k, 0:384], k == 0, k == 7, ['hT', kA], [pkA])
                psB, pkB = bank()
                for k in range(8):
                    mm(psB[:, 0:288], hT[:, k, tsl], slotB[:, k, 0:288], k == 0, k == 7, ['hT', kB], [pkB])
                act(lambda e: e.activation(out=junk[:, 0:384], in_=psA[:, 0:384], func=AF.Square,
                                           accum_out=ssq[:, tt, 0:1]), [pkA, 'ssq'], ['junkm', 'ssq'])
                act(lambda e: e.activation(out=junk[:, 0:256], in_=psB[:, 0:256], func=AF.Square,
                                           accum_out=ssq[:, tt, 1:2]), [pkB, 'ssq'], ['junkm', 'ssq'])
                act(lambda e: e.activation(out=tq[:, tt, 0:1], in_=ssq[:, tt, 0:1], func=AF.Sqrt, bias=EPS,
                                           scale=1.0 / 384), ['ssq'], ['tq'])
                act(lambda e: e.activation(out=tq[:, tt, 1:2], in_=ssq[:, tt, 1:2], func=AF.Sqrt, bias=EPS,
                                           scale=1.0 / 256), ['ssq'], ['tq'])
                dve(lambda e: e.reciprocal(out=rsq[:, tt, :], in_=tq[:, tt, :]), ['tq'], ['rsq'])
                dve(lambda e: e.tensor_scalar(out=cqn[:, tt, :], in0=psA[:, 0:384], scalar1=rsq[:, tt, 0:1], scalar2=None,
                                              op0=ALU.mult), [pkA, 'rsq'], ['cqn'])
                sg = stg[tt % 2]
                sgk = 'stgkv%d' % (tt % 2)
                dve(lambda e: e.scalar_tensor_tensor(out=sg[:, 0:256], in0=psB[:, 0:256], scalar=rsq[:, tt, 1:2],
                                                     in1=kvw_bc[:, :], op0=ALU.mult, op1=ALU.mult),
                    [pkB, 'rsq', 'kvw_bc'], [sgk])
                act(lambda e: e.activation(out=sg[:, 256:288], in_=psB[:, 256:288], func=AF.Copy), [pkB], [sgk])
                cx.dma('sp', ockv_d[l, tsl, :], sg[:, 0:256], reads=[sgk], sem='ockv')
                cx.dma('sp', okr_d[l, tsl, :], sg[:, 256:288], reads=[sgk], sem='ockv')
                act(lambda e: e.activation(out=CKb[:, 4 + tt, :], in_=sg[:, 0:256], func=AF.Copy), [sgk], ['CKb'])
                dve(lambda e: e.tensor_copy(out=KR[:, 4 + tt, :], in_=sg[:, 256:288]), [sgk], ['KR'])
            for tt in range(8):
                ps, pk = bank()
                for k in range(3):
                    mm(ps[:, k * 128:(k + 1) * 128], cqn[:, tt, k * 128:(k + 1) * 128], ident_bf[:, :], True, True,
                       ['cqn', 'ident_bf'], [pk], inc=(k == 2))
                evac_copy(tt, cqnT[:, :, tt * 128:(tt + 1) * 128], ps[:, 0:384].rearrange("p (a b) -> p a b", b=128),
                          [pk], ['cqnT'])
            for kp in range(6):
                ps, pk = bank()
                for j in range(2):
                    kt = 2 * kp + j
                    for k in range(2):
                        mm(ps[:, (2 * j + k) * 128:(2 * j + k + 1) * 128], CKb[:, kt, k * 128:(k + 1) * 128], ident_bf[:, :],
                           True, True, ['CKb', 'ident_bf'], [pk], inc=(j == 1 and k == 1))
                for j in range(2):
                    kt = 2 * kp + j
                    evac_copy(j, ckvT[:, :, kt * 128:(kt + 1) * 128],
                              ps[:, j * 256:(j + 1) * 256].rearrange("p (a b) -> p a b", b=128), [pk], ['ckvT'])
            slot_mg, k_mg = load_w(w_in[l][:, C_MG:C_MG + 512], 8, 512)
            for m in range(4):
                proj_fm(slot_mg, k_mg, m * 128, 128,
                        lambda ps, pk, th, m=m: act(lambda e: e.activation(
                            out=mgT[:, m, th * 512:(th + 1) * 512], in_=ps[:, :], func=AF.Silu), [pk], ['mgT']))

            def head_finish(src3, skey, nrm_keys, rope_tt, dstT, dkey, col0, tagi):
                i2 = tagi % 2
                fb = hfb[i2]
                fk = 'hfb%d' % i2
                if rope_tt is None:
                    act(lambda e: e.activation(out=fb[:, :, :], in_=src3, func=AF.Copy), [skey], [fk])
                else:
                    rt = rtmp[i2]
                    rk = 'rtmp%d' % i2
                    cb = ropec[:, rope_tt:rope_tt + 1, :].to_broadcast([128, 4, 16])
                    sbb = ropes[:, rope_tt:rope_tt + 1, :].to_broadcast([128, 4, 16])
                    x1 = src3[:, :, 64:80]
                    x2 = src3[:, :, 80:96]
                    act(lambda e: e.activation(out=fb[:, :, 0:64], in_=src3[:, :, 0:64], func=AF.Copy), [skey], [fk])
                    pool(lambda e: e.tensor_tensor(out=rt[:, 0, :, :], in0=x1, in1=cb, op=ALU.mult), [skey, 'ropec'], [rk])
                    pool(lambda e: e.tensor_tensor(out=rt[:, 1, :, :], in0=x2, in1=sbb, op=ALU.mult), [skey, 'ropes'], [rk])
                    pool(lambda e: e.tensor_tensor(out=rt[:, 2, :, :], in0=x1, in1=sbb, op=ALU.mult), [skey, 'ropes'], [rk])
                    pool(lambda e: e.tensor_tensor(out=rt[:, 3, :, :], in0=x2, in1=cb, op=ALU.mult), [skey, 'ropec'], [rk])
                    pool(lambda e: e.tensor_tensor(out=fb[:, :, 64:80], in0=rt[:, 0, :, :], in1=rt[:, 1, :, :],
                                                   op=ALU.subtract), [rk], [fk])
                    pool(lambda e: e.tensor_tensor(out=fb[:, :, 80:96], in0=rt[:, 2, :, :], in1=rt[:, 3, :, :],
                                                   op=ALU.add), [rk], [fk])
                ps, pk = bank()
                for h in range(4):
                    mm(ps[0:96, h * 128:(h + 1) * 128], fb[:, h, :], ident_bf[:, :], True, True, [fk, 'ident_bf'], [pk],
                       inc=(h == 3))
                act(lambda e: e.activation(out=dstT[0:96, :, col0:col0 + 128],
                                           in_=ps[0:96, :].rearrange("p (a b) -> p a b", b=128), func=AF.Copy),
                    [pk], [dkey])

            with ExitStack() as s2:
                hfb = [sb("hfb%d" % i, [128, 4, 96], BF16, stack=s2) for i in range(2)]
                rtmp = [sb("rtmp%d" % i, [128, 4, 4, 16], stack=s2) for i in range(2)]
                sqh = [sb("sqh%d" % i, [128, 384], stack=s2) for i in range(2)]
                ssh = [sb("ssh%d" % i, [128, 4], stack=s2) for i in range(2)]
                hn = [sb("hn%d" % i, [128, 4, 96], stack=s2) for i in range(2)]
                for tt in range(8):
                    i2 = tt % 2
                    psQ, pkQ = bank()
                    for k in range(3):
                        mm(psQ[:, 0:384], cqnT[:, k, tt * 128:(tt + 1) * 128], wuq[:, k, :], k == 0, k == 2,
                           ['cqnT', 'wuq'], [pkQ])
                    q3 = psQ[:, 0:384].rearrange("p (a b) -> p a b", b=96)
                    act(lambda e: e.activation(out=sqh[i2][:, 0:384], in_=psQ[:, 0:384], func=AF.Square),
                        [pkQ], ['sqh%d' % i2])
                    dve(lambda e: e.tensor_reduce(out=ssh[i2][:, :], in_=sqh[i2][:, 0:384].rearrange("p (a b) -> p a b", b=96),
                                                  axis=AX.X, op=ALU.add), ['sqh%d' % i2], ['ssh%d' % i2])
                    act(lambda e: e.activation(out=ssh[i2][:, :], in_=ssh[i2][:, :], func=AF.Sqrt, bias=EPS,
                                               scale=1.0 / 96), ['ssh%d' % i2], ['ssh%d' % i2])
                    dve(lambda e: e.reciprocal(out=ssh[i2][:, :], in_=ssh[i2][:, :]), ['ssh%d' % i2], ['ssh%d' % i2])
                    dve(lambda e: e.tensor_tensor(out=hn[i2][:, :, :], in0=q3,
                                                  in1=ssh[i2][:, :].unsqueeze(2).to_broadcast([128, 4, 96]), op=ALU.mult),
                        [pkQ, 'ssh%d' % i2], ['hn%d' % i2])
                    dve(lambda e: e.tensor_tensor(out=hn[i2][:, :, :], in0=hn[i2][:, :, :],
                                                  in1=qhw_bc[:, :].unsqueeze(1).to_broadcast([128, 4, 96]), op=ALU.mult),
                        ['hn%d' % i2, 'qhw_bc'], ['hn%d' % i2])
                    head_finish(hn[i2][:, :, :], 'hn%d' % i2, None, tt, QT, 'QT', tt * 128, tt)
                for kt in range(12):
                    i2 = kt % 2
                    ksl = slice(kt * 128, (kt + 1) * 128)
                    psK, pkK = bank()
                    for k in range(2):
                        mm(psK[:, 0:256], ckvT[:, k, ksl], wuk[:, k, :], k == 0, k == 1, ['ckvT', 'wuk'], [pkK])
                    psV, pkV = bank()
                    for k in range(2):
                        mm(psV[:, :], ckvT[:, k, ksl], wuv[:, k, :], k == 0, k == 1, ['ckvT', 'wuv'], [pkV])
                    evac_copy(kt, Vt[:, kt, :], psV[:, :], [pkV], ['Vt'])
                    act(lambda e: e.activation(out=sqh[i2][:, 0:256], in_=psK[:, 0:256], func=AF.Square),
                        [pkK], ['sqh%d' % i2])
                    dve(lambda e: e.tensor_reduce(out=ssh[i2][:, :], in_=sqh[i2][:, 0:256].rearrange("p (a b) -> p a b", b=64),
                                                  axis=AX.X, op=ALU.add), ['sqh%d' % i2], ['ssh%d' % i2])
                    act(lambda e: e.activation(out=junk[:, 0:32], in_=KR[:, kt, :], func=AF.Square,
                                               accum_out=sskr[:, kt:kt + 1]), ['KR', 'sskr'], ['junkm', 'sskr'])
                    dve(lambda e: e.tensor_scalar(out=ssh[i2][:, :], in0=ssh[i2][:, :], scalar1=sskr[:, kt:kt + 1],
                                                  scalar2=None, op0=ALU.add), ['ssh%d' % i2, 'sskr'], ['ssh%d' % i2])
                    act(lambda e: e.activation(out=ssh[i2][:, :], in_=ssh[i2][:, :], func=AF.Sqrt, bias=EPS,
                                               scale=1.0 / 96), ['ssh%d' % i2], ['ssh%d' % i2])
                    dve(lambda e: e.reciprocal(out=ssh[i2][:, :], in_=ssh[i2][:, :]), ['ssh%d' % i2], ['ssh%d' % i2])
                    k3 = psK[:, 0:256].rearrange("p (a b) -> p a b", b=64)
                    dve(lambda e: e.tensor_tensor(out=hn[i2][:, :, 0:64], in0=k3,
                                                  in1=ssh[i2][:, :].unsqueeze(2).to_broadcast([128, 4, 64]), op=ALU.mult),
                        [pkK, 'ssh%d' % i2], ['hn%d' % i2])
                    dve(lambda e: e.tensor_tensor(out=hn[i2][:, :, 64:96],
                                                  in0=KR[:, kt:kt + 1, :].to_broadcast([128, 4, 32]),
                                                  in1=ssh[i2][:, :].unsqueeze(2).to_broadcast([128, 4, 32]), op=ALU.mult),
                        ['KR', 'ssh%d' % i2], ['hn%d' % i2])
                    dve(lambda e: e.tensor_tensor(out=hn[i2][:, :, :], in0=hn[i2][:, :, :],
                                                  in1=khw_bc[:, :].unsqueeze(1).to_broadcast([128, 4, 96]), op=ALU.mult),
                        ['hn%d' % i2, 'khw_bc'], ['hn%d' % i2])
                    head_finish(hn[i2][:, :, :], 'hn%d' % i2, None, (kt - 4) if kt >= 4 else None, KT, 'KT', kt * 128, kt)
            tap("mla_QT%d" % l, QT[0:101, :, :], 'QT')
            tap("mla_KT%d" % l, KT[0:101, :, :], 'KT')
            tap("mla_V%d" % l, Vt[:, :, :], 'Vt')
            if stop == 'M1':
                return
            with ExitStack() as s3:
                PT = [sb("PT%d" % i, [128, 512], BF16, stack=s3) for i in range(3)]
                rden = [sb("rden%d" % i, [128, 512], stack=s3) for i in range(2)]
                accs = [(reserve_bank(), reserve_bank()) for _ in range(2)]
                it = 0
                pi = 0
                for h in range(4):
                    for qh in range(2):
                        (ob, okey), (db, dkey) = accs[it % 2]
                        qsl = slice(qh * 512, (qh + 1) * 512)
                        for kt in range(12):
                            sbk, skey = bank()
                            mm(sbk[:, :], KT[0:101, h, kt * 128:(kt + 1) * 128], QT[0:101, h, qsl], True, True,
                               ['KT', 'QT'], [skey])
                            pt = PT[pi % 3]
                            ptk = 'PT%d' % (pi % 3)
                            pi += 1
                            act(lambda e: e.activation(out=pt[:, :], in_=sbk[:, :], func=AF.Exp, scale=ATT_SCALE),
                                [skey], [ptk])
                            mm(ob[:, :], Vt[:, kt, h * 128:(h + 1) * 128], pt[:, :], kt == 0, kt == 11, ['Vt', ptk], [okey])
                            mm(db[:, :], ones_bf[:, :], pt[:, :], kt == 0, kt == 11, ['ones_bf', ptk], [dkey])
                        rd = rden[it % 2]
                        rdk = 'rden%d' % (it % 2)
                        dve(lambda e: e.reciprocal(out=rd[:, :], in_=db[:, :]), [dkey], [rdk])
                        dve(lambda e: e.tensor_tensor(out=rd[:, :], in0=ob[:, :], in1=rd[:, :], op=ALU.mult),
                            [okey, rdk], [rdk])
                        dve(lambda e: e.tensor_tensor(out=oT_br[1][:, h, qsl], in0=rd[:, :], in1=mgT[:, h, qsl],
                                                      op=ALU.mult), [rdk, 'mgT'], ['obr1'])
                        it += 1
                for (a, b) in accs:
                    release_bank(a[1])
                    release_bank(b[1])
            tap("mla_out%d" % l, oT_br[1][:, :, :], 'obr1')

    class Arena:
        def __init__(self, t, nelem, dt):
            self.t = t
            self.dt = dt
            self.ti = t.bitcast(I32) if dt == F32 else None
            self.free_list = [(0, nelem)]
            self.live = {}
            self.snap = dict(cx.epoch)

        def alloc(self, name, dims, dt=None):
            n = 1
            for d_ in dims:
                n *= d_
            nf = (n + 15) // 16 * 16
            for idx, (o, sz) in enumerate(self.free_list):
                if sz >= nf:
                    if sz == nf:
                        self.free_list.pop(idx)
                    else:
                        self.free_list[idx] = (o + nf, sz - nf)
                    break
            else:
                raise RuntimeError("arena full: %s %s %s" % (name, dims, self.free_list))
            self.live[name] = (o, nf)
            if dt == I32:
                ap = self.ti[:, o:o + n]
            else:
                ap = self.t[:, o:o + n]
            if len(dims) > 1:
                names = ["a%d" % i for i in range(len(dims))]
                pat = "p (%s) -> p %s" % (" ".join(names), " ".join(names))
                ap = ap.rearrange(pat, **{nm: d_ for nm, d_ in zip(names, dims)})
            cx.fresh[name] = dict(self.snap)
            return ap

        def free(self, name):
            self.snap = dict(cx.cnt)
            o, nf = self.live.pop(name)
            fl = sorted(self.free_list + [(o, nf)])
            merged = []
            for a, b in fl:
                if merged and merged[-1][0] + merged[-1][1] == a:
                    merged[-1] = (merged[-1][0], merged[-1][1] + b)
                else:
                    merged.append((a, b))
            self.free_list = merged

    class Arena2:
        def __init__(self, af, ab):
            self.af, self.ab = af, ab
            self.where = {}

        def alloc(self, name, dims, dt=F32):
            a = self.ab if dt == BF16 else self.af
            self.where[name] = a
            return a.alloc(name, dims, dt)

        def free(self, name):
            self.where.pop(name).free(name)

    def branch_s5(l):
        with ExitStack() as st:
            NF, NB = 9600, 29696
            arena_f = sb("s5arena_f", [128, NF], stack=st)
            arena_b = sb("s5arena_b", [128, NB], BF16, stack=st)
            A = Arena2(Arena(arena_f, NF, F32), Arena(arena_b, NB, BF16))
            WoR = A.alloc("WoR", [2, 16, 2, 128], BF16)
            TT = A.alloc("TT", [32, 128], BF16)
            C1 = A.alloc("C1", [16, 2, 2]); C2 = A.alloc("C2", [16, 2, 2])
            C1r = A.alloc("C1r", [16, 2, 2]); C2r = A.alloc("C2r", [16, 2, 2])
            WinT = A.alloc("WinT", [2, 32, 128], BF16)
            s5m = A.alloc("s5m", [2, 128])
            selc = A.alloc("selc", [2, 64])
            cx.dma('sp', s5m, s5m_d.rearrange("a p c -> p a c"), writes=['s5m'], sem='s5c')
            cx.dma('sp', selc, selc_d.rearrange("a p c -> p a c"), writes=['selc'], sem='s5c')

            def tt_(e, o, a, b, op):
                return e.tensor_tensor(out=o, in0=a, in1=b, op=op)

            def D2(o, a, b, op, rk, wk):
                dve(lambda e: tt_(e, o, a, b, op), rk, wk)

            AR = A.alloc("AR", [2, 16]); AI = A.alloc("AI", [2, 16]); LD = A.alloc("LD", [2, 16])
            anat = A.alloc("anat", [2, 128])
            cx.dma('sp', anat[0:32, 0, :], s5_a_re[l].rearrange("d (j g2) n -> (d j) (g2 n)", g2=2), writes=['anat'], sem='s5c')
            cx.dma('sp', anat[0:32, 1, :], s5_a_im[l].rearrange("d (j g2) n -> (d j) (g2 n)", g2=2), writes=['anat'], sem='s5c')
            ps, pk = bank()
            for c_ in range(2):
                mm(ps[:, c_ * 32:(c_ + 1) * 32], anat[0:32, c_, :], ident_f[0:32, 0:32], True, True, ['anat', 'ident_f'], [pk],
                   inc=(c_ == 1))
            dve(lambda e: e.tensor_copy(out=AR, in_=ps[:, 0:32].rearrange("p (d j) -> p d j", d=2)), [pk], ['AR'])
            dve(lambda e: e.tensor_copy(out=AI, in_=ps[:, 32:64].rearrange("p (d j) -> p d j", d=2)), [pk], ['AI'])
            ldf = A.alloc("ldf", [2, 32])
            cx.dma('sp', ldf, s5_log_dt[l].rearrange("d g -> (d g)").partition_broadcast(128).rearrange("p (d g) -> p d g", d=2),
                   writes=['ldf'], sem='s5c')
            for g2 in range(2):
                dve(lambda e: e.tensor_copy(out=LD[g2 * 64:(g2 + 1) * 64], in_=ldf[g2 * 64:(g2 + 1) * 64, :, g2::2]),
                    ['ldf'], ['LD'])
            dcol = A.alloc("dcol", [32])
            with nc.allow_non_contiguous_dma(reason="small S5 parameter gathers"):
                for s_ in range(8):
                    cx.dma('sp', dcol[s_ * 16:(s_ + 1) * 16, :], s5_d[l].rearrange("(g q) -> q g", q=16),
                           writes=['dcol'], sem='s5c')
            BR = A.alloc("BR", [16, 16]); BI = A.alloc("BI", [16, 16])
            for jq in range(4):
                cx.dma('sp', BR[:, 4 * jq:4 * jq + 4, :], s5_b_re[l].rearrange("(j g2) n q -> (g2 n) j q", g2=2)[:, 4 * jq:4 * jq + 4, :],
                       writes=['BR'], sem='s5c')
                cx.dma('sp', BI[:, 4 * jq:4 * jq + 4, :], s5_b_im[l].rearrange("(j g2) n q -> (g2 n) j q", g2=2)[:, 4 * jq:4 * jq + 4, :],
                       writes=['BI'], sem='s5c')
            CNr = A.alloc("CNr", [4, 64]); CNi = A.alloc("CNi", [4, 64])
            cx.dma('sp', CNr, s5_c_re[l].rearrange("(r gl) p n -> (gl p) r n", r=4), writes=['CNr'], sem='s5c')
            cx.dma('sp', CNi, s5_c_im[l].rearrange("(r gl) p n -> (gl p) r n", r=4), writes=['CNi'], sem='s5c')
            CR = A.alloc("CR", [16, 16]); CI = A.alloc("CI", [16, 16])
            for (cn, cnk, cdst, cdk, ei) in ((CNr, 'CNr', CR, 'CR', 0), (CNi, 'CNi', CI, 'CI', 1)):
                ps, pk = bank()
                for r in range(4):
                    for g2 in range(2):
                        mm(ps[g2 * 64:(g2 + 1) * 64, r * 64:(r + 1) * 64], cn[:, r, :], selc[:, g2, :], True, True,
                           [cnk, 'selc'], [pk], inc=(r == 3 and g2 == 1))
                evac_copy(ei, cdst, ps[:, 0:256].rearrange("p (a b) -> p a b", b=16), [pk], [cdk])

            def small(name):
                return A.alloc(name, [2, 16])
            dt_ = small("dt_"); mag = small("mag"); ang = small("ang"); sn = small("sn"); cs = small("cs")
            abr = small("abr"); abi = small("abi"); rden = small("rden"); nre = small("nre")
            cfr = small("cfr"); cfi = small("cfi"); ta = small("ta"); tb = small("tb")
            kI = A.alloc("kI", [2, 16], I32)
            act(lambda e: e.activation(out=dt_, in_=LD, func=AF.Exp), ['LD'], ['dt_'])
            D2(ta, AR, dt_, ALU.mult, ['AR', 'dt_'], ['ta'])
            act(lambda e: e.activation(out=mag, in_=ta, func=AF.Exp), ['ta'], ['mag'])
            D2(ang, AI, dt_, ALU.mult, ['AI', 'dt_'], ['ang'])

            def sin_of(dst, dkey, shift):
                dve(lambda e: e.tensor_scalar(out=ta, in0=ang, scalar1=shift, scalar2=1.0 / (2 * PI), op0=ALU.add,
                                              op1=ALU.mult), ['ang'], ['ta'])
                dve(lambda e: e.tensor_copy(out=kI, in_=ta), ['ta'], ['kI'])
                dve(lambda e: e.tensor_copy(out=tb, in_=kI), ['kI'], ['tb'])
                dve(lambda e: e.tensor_scalar(out=ta, in0=ang, scalar1=shift, scalar2=None, op0=ALU.add), ['ang'], ['ta'])
                dve(lambda e: e.scalar_tensor_tensor(out=ta, in0=tb, scalar=-2 * PI, in1=ta, op0=ALU.mult, op1=ALU.add),
                    ['tb', 'ta'], ['ta'])
                dve(lambda e: e.tensor_scalar(out=tb, in0=ta, scalar1=PI, scalar2=None, op0=ALU.is_gt), ['ta'], ['tb'])
                dve(lambda e: e.scalar_tensor_tensor(out=ta, in0=tb, scalar=-2 * PI, in1=ta, op0=ALU.mult, op1=ALU.add),
                    ['tb', 'ta'], ['ta'])
                dve(lambda e: e.tensor_scalar(out=tb, in0=ta, scalar1=-PI, scalar2=None, op0=ALU.is_lt), ['ta'], ['tb'])
                dve(lambda e: e.scalar_tensor_tensor(out=ta, in0=tb, scalar=2 * PI, in1=ta, op0=ALU.mult, op1=ALU.add),
                    ['tb', 'ta'], ['ta'])
                act(lambda e: e.activation(out=dst, in_=ta, func=AF.Sin), ['ta'], [dkey])
            sin_of(sn, 'sn', 0.0)
            sin_of(cs, 'cs', PI / 2)
            D2(abr, mag, cs, ALU.mult, ['mag', 'cs'], ['abr'])
            D2(abi, mag, sn, ALU.mult, ['mag', 'sn'], ['abi'])
            D2(ta, AR, AR, ALU.mult, ['AR'], ['ta'])
            D2(tb, AI, AI, ALU.mult, ['AI'], ['tb'])
            D2(ta, ta, tb, ALU.add, ['ta', 'tb'], ['ta'])
            dve(lambda e: e.reciprocal(out=rden, in_=ta), ['ta'], ['rden'])
            dve(lambda e: e.tensor_scalar(out=nre, in0=abr, scalar1=-1.0, scalar2=None, op0=ALU.add), ['abr'], ['nre'])
            D2(ta, nre, AR, ALU.mult, ['nre', 'AR'], ['ta'])
            D2(tb, abi, AI, ALU.mult, ['abi', 'AI'], ['tb'])
            D2(ta, ta, tb, ALU.add, ['ta', 'tb'], ['ta'])
            D2(cfr, ta, rden, ALU.mult, ['ta', 'rden'], ['cfr'])
            D2(ta, abi, AR, ALU.mult, ['abi', 'AR'], ['ta'])
            D2(tb, nre, AI, ALU.mult, ['nre', 'AI'], ['tb'])
            D2(ta, ta, tb, ALU.subtract, ['ta', 'tb'], ['ta'])
            D2(cfi, ta, rden, ALU.mult, ['ta', 'rden'], ['cfi'])

            def cmul(o_re, o_im, ok, a_re, a_im, ak, b_re, b_im, bk, t1, t2, tk):
                D2(t1, a_re, b_re, ALU.mult, ak + bk, [tk[0]])
                D2(t2, a_im, b_im, ALU.mult, ak + bk, [tk[1]])
                D2(o_re, t1, t2, ALU.subtract, tk, [ok[0]])
                D2(t1, a_re, b_im, ALU.mult, ak + bk + [ok[0]], [tk[0]])
                D2(t2, a_im, b_re, ALU.mult, ak + bk + [ok[0]], [tk[1]])
                D2(o_im, t1, t2, ALU.add, tk, [ok[1]])

            PWr = A.alloc("PWr", [2, 16, 17]); PWi = A.alloc("PWi", [2, 16, 17])
            pt1 = A.alloc("pt1", [2, 16, 4]); pt2 = A.alloc("pt2", [2, 16, 4])
            dve(lambda e: e.memset(PWr[:, :, :, 8:9], 1.0), [], ['PWr'])
            dve(lambda e: e.memset(PWi[:, :, :, 8:9], 0.0), [], ['PWi'])
            dve(lambda e: e.tensor_copy(out=PWr[:, :, :, 9], in_=abr), ['abr'], ['PWr'])
            dve(lambda e: e.tensor_copy(out=PWi[:, :, :, 9], in_=abi), ['abi'], ['PWi'])
            D2(ta, abr, abr, ALU.mult, ['abr'], ['ta'])
            D2(tb, abi, abi, ALU.mult, ['abi'], ['tb'])
            D2(ta, ta, tb, ALU.add, ['ta', 'tb'], ['ta'])
            dve(lambda e: e.reciprocal(out=tb, in_=ta), ['ta'], ['tb'])
            D2(PWr[:, :, :, 7], abr, tb, ALU.mult, ['abr', 'tb'], ['PWr'])
            dve(lambda e: e.scalar_tensor_tensor(out=PWi[:, :, :, 7], in0=abi, scalar=-1.0, in1=tb, op0=ALU.mult,
                                                 op1=ALU.mult), ['abi', 'tb'], ['PWi'])
            PK = ['PWr', 'PWi']

            def pw_step(o0, o1, i0, i1, m):
                w = o1 - o0
                bre = PWr[:, :, :, m:m + 1].to_broadcast([128, 2, 16, w])
                bim = PWi[:, :, :, m:m + 1].to_broadcast([128, 2, 16, w])
                cmul(PWr[:, :, :, o0:o1], PWi[:, :, :, o0:o1], PK, PWr[:, :, :, i0:i1], PWi[:, :, :, i0:i1], PK,
                     bre, bim, PK, pt1[:, :, :, 0:w], pt2[:, :, :, 0:w], ['pt1', 'pt2'])
            pw_step(10, 11, 9, 10, 9)
            pw_step(11, 13, 9, 11, 10)
            pw_step(13, 17, 9, 13, 12)
            pw_step(6, 7, 7, 8, 7)
            pw_step(4, 6, 6, 8, 6)
            pw_step(0, 4, 4, 8, 4)
            BPr = A.alloc("BPr", [2, 16, 16]); BPi = A.alloc("BPi", [2, 16, 16])
            bt1 = A.alloc("bt1", [2, 16, 16]); bt2 = A.alloc("bt2", [2, 16, 16])
            cmul(BPr, BPi, ['BPr', 'BPi'],
                 cfr.unsqueeze(3).to_broadcast([128, 2, 16, 16]), cfi.unsqueeze(3).to_broadcast([128, 2, 16, 16]),
                 ['cfr', 'cfi'],
                 BR.unsqueeze(1).to_broadcast([128, 2, 16, 16]), BI.unsqueeze(1).to_broadcast([128, 2, 16, 16]),
                 ['BR', 'BI'], bt1, bt2, ['bt1', 'bt2'])
            A.free("bt1"); A.free("bt2")
            for d in range(2):
                for c_ in range(2):
                    dve(lambda e: e.tensor_copy(out=C1[:, :, d, c_], in_=PWr[:, d, :, 16]), ['PWr'], ['C1'])
                dve(lambda e: e.tensor_scalar(out=C2[:, :, d, 0], in0=PWi[:, d, :, 16], scalar1=-1.0, scalar2=None,
                                              op0=ALU.mult), ['PWi'], ['C2'])
                dve(lambda e: e.tensor_copy(out=C2[:, :, d, 1], in_=PWi[:, d, :, 16]), ['PWi'], ['C2'])
            dve(lambda e: e.tensor_scalar(out=C1r, in0=C1, scalar1=rcol[:, 0:1], scalar2=None, op0=ALU.mult),
                ['C1', 'rcol'], ['C1r'])
            dve(lambda e: e.tensor_scalar(out=C2r, in0=C2, scalar1=rcol[:, 0:1], scalar2=None, op0=ALU.mult),
                ['C2', 'rcol'], ['C2r'])

            TTacc = A.alloc("TTacc", [4, 128])
            Wn = [A.alloc("Wn%d" % i, [2, 8, 16]) for i in range(6)]
            wt1 = A.alloc("wt1", [4, 8, 16]); wt2 = A.alloc("wt2", [2, 8, 16])
            WK = ['Wn%d' % i for i in range(6)]
            for qd_ in range(8):
                jsl = slice(2 * qd_, 2 * qd_ + 2)
                for d in range(2):
                    if d == 0:
                        p_in = (slice(15, 7, -1)); p_inv = (slice(7, None, -1)); p_out = slice(9, 17)
                    else:
                        p_in = slice(8, 16); p_inv = slice(0, 8); p_out = slice(16, 8, -1)

                    def pwb(sl, last):
                        return (PWr[:, d, jsl, sl].unsqueeze(3).to_broadcast([128, 2, 8, last]),
                                PWi[:, d, jsl, sl].unsqueeze(3).to_broadcast([128, 2, 8, last]))
                    bpr = BPr[:, d, jsl, :].unsqueeze(2).to_broadcast([128, 2, 8, 16])
                    bpi = BPi[:, d, jsl, :].unsqueeze(2).to_broadcast([128, 2, 8, 16])
                    w1 = wt1[:, 0:2]
                    a_re, a_im = pwb(p_in, 16)
                    cmul(Wn[0], Wn[1], WK[0:2], a_re, a_im, PK, bpr, bpi, ['BPr', 'BPi'], w1, wt2, ['wt1', 'wt2'])
                    a_re, a_im = pwb(p_inv, 16)
                    cmul(Wn[2], Wn[3], WK[2:4], a_re, a_im, PK, bpr, bpi, ['BPr', 'BPi'], w1, wt2, ['wt1', 'wt2'])
                    a_re, a_im = pwb(p_out, 16)
                    cr = CR[:, jsl, :].unsqueeze(2).to_broadcast([128, 2, 8, 16])
                    ci = CI[:, jsl, :].unsqueeze(2).to_broadcast([128, 2, 8, 16])
                    cmul(Wn[4], Wn[5], WK[4:6], a_re, a_im, PK, cr, ci, ['CR', 'CI'], w1, wt2, ['wt1', 'wt2'])
                    dve(lambda e: e.tensor_scalar(out=Wn[5], in0=Wn[5], scalar1=-1.0, scalar2=None, op0=ALU.mult),
                        [WK[5]], [WK[5]])
                    for c2 in range(2):
                        act(lambda e: e.activation(out=WoR[:, d, jsl, c2, :],
                                                   in_=Wn[4 + c2].rearrange("p a b c -> p a (b c)"), func=AF.Copy),
                            [WK[4 + c2]], ['WoR'])
                    ps, pk = bank()
                    for jj in range(2):
                        for c2 in range(2):
                            mm(ps[:, (jj * 2 + c2) * 128:(jj * 2 + c2 + 1) * 128],
                               Wn[c2][:, jj, :, :].rearrange("p b c -> p (b c)"), ident_f[:, :], True, True,
                               [WK[c2], 'ident_f'], [pk], inc=(jj == 1 and c2 == 1))
                    j0 = 2 * qd_
                    for jj in range(2):
                        jg = j0 + jj
                        dst = WinT[:, d, 2 * jg:2 * jg + 2, :].rearrange("p g2 (c n) -> p c g2 n", c=2)
                        evac_copy(jj, dst, ps[:, jj * 256:(jj + 1) * 256].rearrange("p (c g2 n) -> p c g2 n", c=2, g2=2),
                                  [pk], ['WinT'])
                    ps, pk = bank()
                    for gi in range(4):
                        jj = gi // 2
                        g2 = gi % 2
                        rows = slice(g2 * 64, (g2 + 1) * 64)
                        outp = ps[:, gi * 128:(gi + 1) * 128]
                        mm_b(outp, Wn[2][rows, jj, :, :].rearrange("p b c -> p (b c)"),
                             Wn[4][rows, jj, :, :].rearrange("p b c -> p (b c)"), gi == 0, False,
                             [WK[2], WK[4]], [pk], inc=False, skip=True, base=g2 * 64)
                        mm_b(outp, Wn[3][rows, jj, :, :].rearrange("p b c -> p (b c)"),
                             Wn[5][rows, jj, :, :].rearrange("p b c -> p (b c)"), False, True,
                             [WK[3], WK[5]], [pk], inc=(gi == 3), skip=True, base=g2 * 64)
                    g0 = 4 * qd_
                    acc = TTacc[:, 0:4, :]
                    ps3 = ps[:, :].rearrange("p (a b) -> p a b", b=128)
                    mk = s5m[:, d:d + 1, :].to_broadcast([128, 4, 128])
                    if d == 0:
                        D2(acc, ps3, mk, ALU.mult, [pk, 's5m'], ['TTacc'])
                    else:
                        w13 = wt1.rearrange("p a b c -> p a (b c)")
                        D2(w13, ps3, mk, ALU.mult, [pk, 's5m'], ['wt1'])
                        D2(acc, acc, w13, ALU.add, ['TTacc', 'wt1'], ['TTacc'])
                        for gi in range(4):
                            g = g0 + gi
                            dve(lambda e: e.scalar_tensor_tensor(
                                out=TT[:, g, :], in0=ident_f[:, :], scalar=dcol[:, g:g + 1],
                                in1=TTacc[:, gi, :], op0=ALU.mult, op1=ALU.add),
                                ['ident_f', 'dcol', 'TTacc'], ['TT'])
            for nm in ["Wn%d" % i for i in range(6)] + ["wt1", "wt2", "TTacc", "PWr", "PWi", "pt1", "pt2", "BPr", "BPi",
                                                         "CR", "CI", "CNr", "CNi", "BR", "BI", "kI", "s5m", "selc",
                                                         "AR", "AI", "LD", "dcol", "dt_", "mag", "ang", "sn", "cs", "abr",
                                                         "abi", "rden", "nre", "cfr", "cfi", "ta", "tb", "anat", "ldf"]:
                A.free(nm)
            tap("s5_WinT%d" % l, WinT, 'WinT')
            tap("s5_WoR%d" % l, WoR, 'WoR')
            tap("s5_TT%d" % l, TT, 'TT')
            if stop == 'S1':
                return

            Up = A.alloc("Up", [32, 8, 16], BF16)
            UGN = A.alloc("UGN", [32, 128], BF16)
            UGR = [A.alloc("UGR%d" % i, [2, 128], BF16) for i in range(2)]
            VX = A.alloc("VX", [16, 2, 2, 129])
            x0n = A.alloc("x0n", [128])
            slot_su, k_su = load_w(w_in[l][:, C_SU:C_SU + 512], 8, 512)
            for s_ in range(8):
                ps, pk = bank()
                for k in range(8):
                    mm(ps[:, :], hT[:, k, s_::8], slot_su[:, k, :], k == 0, k == 7, ['hT', k_su], [pk])
                evac_copy(s_, Up[:, :, s_, :], ps[:, :].rearrange("p (g q) -> p g q", q=16), [pk], ['Up'])
            tap("s5_Up%d" % l, Up, 'Up')
            if stop == 'S2a':
                return
            cx.dma('sp', x0n[0:64, :], s50_d[l].rearrange("d c (j g2) n -> (d c j) (g2 n)", g2=2), writes=['x0n'], sem='s5x0')
            ps, pk = bank()
            mm(ps[:, 0:64], x0n[0:64, :], ident_f[0:64, 0:64], True, True, ['x0n', 'ident_f'], [pk])
            dve(lambda e: e.tensor_copy(out=VX[:, :, :, :, 0], in_=ps[:, 0:64].rearrange("p (d c j) -> p j d c", d=2, c=2)),
                [pk], ['VX'])
            tap("s5_VX0%d" % l, VX, 'VX')
            if stop == 'S2b':
                return
            for j in range(16):
                ub, ukey = bank()
                ug = UGR[j % 2]
                ugk = 'UGR%d' % (j % 2)
                for g2 in range(2):
                    g = 2 * j + g2
                    src = Up[:, g, :, :].rearrange("p s q -> p (s q)")
                    mm(ub[:, g2 * 128:(g2 + 1) * 128], src, ident_bf[:, :], True, True, ['Up', 'ident_bf'], [ukey], inc=False)
                    mm(ub[:, (2 + g2) * 128:(3 + g2) * 128], src, jmat_bf[:, :], True, True, ["Up", "jmat_bf"], [ukey],
                       inc=(g2 == 1))
                act(lambda e: e.activation(out=UGN[:, 2 * j:2 * j + 2, :],
                                           in_=ub[:, 0:256].rearrange("p (a b) -> p a b", b=128), func=AF.Copy),
                    [ukey], ['UGN'])
                dve(lambda e: e.tensor_copy(out=ug, in_=ub[:, 256:512].rearrange("p (a b) -> p a b", b=128)),
                    [ukey], [ugk])
                if stop == 'S2c':
                    continue
                vb, vkey = bank()
                for g2 in range(2):
                    g = 2 * j + g2
                    for d in range(2):
                        rhs = UGN[:, g, :] if d == 0 else ug[:, g2, :]
                        for c2 in range(2):
                            mm(vb[g2 * 64:(g2 + 1) * 64, (2 * d + c2) * 128:(2 * d + c2 + 1) * 128],
                               WinT[:, d, g, c2 * 64:(c2 + 1) * 64], rhs, True, True,
                               ['WinT', 'UGN', ugk], [vkey], inc=(g2 == 1 and d == 1 and c2 == 1))
                evac_copy(j, VX[:, j, :, :, 1:129], vb[:, :].rearrange("p (d c i) -> p d c i", d=2, c=2), [vkey], ['VX'])
            A.free("Up"); A.free("WinT"); A.free("x0n")
            tap("s5_V%d" % l, VX, 'VX')
            if stop in ('S2', 'S2c'):
                return
            ts_ = [A.alloc("ts%d" % i, [16, 2, 2]) for i in range(4)]
            for i in range(128):
                bnd = (i % 32 == 0 and i > 0)
                c1 = (C1r if bnd else C1)
                c2 = (C2r if bnd else C2)
                xp = VX[:, :, :, :, i]
                xsw = VX[:, :, :, ::-1, i]
                cur = VX[:, :, :, :, i + 1]
                t1, t2 = ts_[0], ts_[1]
                cx.op('dve', lambda e: tt_(e, t1, xp, c1, ALU.mult), ['VX', 'VXd0', 'C1', 'C1r'], ['ts0'])
                cx.op('dve', lambda e: tt_(e, t2, xsw, c2, ALU.mult), ['VX', 'VXd0', 'C2', 'C2r'], ['ts1'])
                cx.op('dve', lambda e: tt_(e, t1, t1, t2, ALU.add), ['ts0', 'ts1'], ['ts0'])
                cx.op('dve', lambda e: tt_(e, cur, cur, t1, ALU.add), ['VX', 'VXd0', 'ts0'], ['VXd0'])
            cx.lastw['VXd1'] = cx.lastw['VXd0']
            tap("s5_X%d" % l, VX, 'VXd0')
            tap("s5_Xb%d" % l, VX, 'VXd1')
            fst = A.alloc("fst", [4, 128])
            for d in range(2):
                for c2 in range(2):
                    ps, pk = bank()
                    for sgi in range(4):
                        col = 32 * (sgi + 1)
                        mm(ps[0:16, sgi * 128:(sgi + 1) * 128], VX[:, :, d, c2, col], ident_f[:, :], True, True,
                           ['VXd%d' % d, 'ident_f'], [pk], inc=(sgi == 3))
                    dve(lambda e: e.tensor_copy(out=fst[0:16], in_=ps[0:16, :].rearrange("p (a b) -> p a b", b=128)),
                        [pk], ['fst'])
                    for sgi in range(4):
                        seg = sgi if d == 0 else 3 - sgi
                        cx.dma('sp', os5_d[l, seg, d, c2].rearrange("(j g2) n -> j (g2 n)", g2=2), fst[0:16, sgi, :],
                               reads=['fst'], sem='os5')
            XB = A.alloc("XB", [16, 2, 2, 128], BF16)
            act(lambda e: e.activation(out=XB[:, :, 0, :, :], in_=VX[:, :, 0, :, 0:128], func=AF.Copy), ['VXd0', 'VX'], ['XB'])
            dve(lambda e: e.tensor_copy(out=XB[:, :, 1, :, :], in_=VX[:, :, 1, :, 127::-1]), ['VXd1', 'VX'], ['XB'])
            dve(lambda e: e.tensor_scalar(out=XB[:, :, 0, :, 32:128:32], in0=XB[:, :, 0, :, 32:128:32], scalar1=rcol[:, 0:1],
                                          scalar2=None, op0=ALU.mult), ['XB', 'rcol'], ['XB'])
            dve(lambda e: e.tensor_scalar(out=XB[:, :, 1, :, 31:128:32], in0=XB[:, :, 1, :, 31:128:32], scalar1=rcol[:, 0:1],
                                          scalar2=None, op0=ALU.mult), ['XB', 'rcol'], ['XB'])
            A.free("VX"); A.free("fst")
            for i in range(4):
                A.free("ts%d" % i)
            Yp = A.alloc("Yp", [8, 512])
            for qd_ in range(8):
                yb, ykey = bank()
                for gi in range(4):
                    g = 4 * qd_ + gi
                    j, g2 = g // 2, g % 2
                    rows = slice(g2 * 64, (g2 + 1) * 64)
                    outp = yb[:, gi * 128:(gi + 1) * 128]
                    mm(outp, UGN[:, g, :], TT[:, g, :], gi == 0, False, ['UGN', 'TT'], [ykey], inc=False, skip=True)
                    for d in range(2):
                        for c2 in range(2):
                            last = (d == 1 and c2 == 1)
                            mm_b(outp, XB[rows, j, d, c2, :], WoR[rows, d, j, c2, :], False, last, ['XB', 'WoR'], [ykey],
                                 inc=(last and gi == 3), skip=True, base=g2 * 64)
                evac_copy(qd_, Yp[:, :, 64 * qd_:64 * qd_ + 64].rearrange("p s (g c) -> p s g c", c=16),
                          yb[:, :].rearrange("p (g s c) -> p s g c", g=4, s=8), [ykey], ['Yp'])
            tap("s5_Y%d" % l, Yp, 'Yp')
            A.free("XB"); A.free("UGN"); A.free("WoR"); A.free("TT")
            for i in range(2):
                A.free("UGR%d" % i)
            if stop == 'S3':
                return
            Gp = A.alloc("Gp", [8, 512], BF16)
            gt1 = A.alloc("gt1", [2, 512]); gt2 = A.alloc("gt2", [2, 512])
            for q4 in range(4):
                ysl = Yp[:, 2 * q4:2 * q4 + 2, :]
                act(lambda e: e.activation(out=gt1, in_=ysl, func=AF.Square), ['Yp'], ['gt1'])
                dve(lambda e: e.tensor_scalar(out=gt1, in0=gt1, scalar1=0.044715, scalar2=1.0, op0=ALU.mult, op1=ALU.add),
                    ['gt1'], ['gt1'])
                D2(gt1, gt1, ysl, ALU.mult, ['gt1', 'Yp'], ['gt1'])
                act(lambda e: e.activation(out=gt2, in_=gt1, func=AF.Sigmoid, scale=2.0 * math.sqrt(2.0 / PI)),
                    ['gt1'], ['gt2'])
                D2(Gp[:, 2 * q4:2 * q4 + 2, :], ysl, gt2, ALU.mult, ['Yp', 'gt2'], ['Gp'])
            A.free("Yp"); A.free("gt1"); A.free("gt2")
            gT = A.alloc("gT", [4, T], BF16)
            for s_ in range(8):
                ps, pk = bank()
                for ct in range(4):
                    mm(ps[:, ct * 128:(ct + 1) * 128], Gp[:, s_, ct * 128:(ct + 1) * 128], ident_bf[:, :], True, True,
                       ['Gp', 'ident_bf'], [pk], inc=(ct == 3))
                evac_copy(s_, gT[:, :, s_::8], ps[:, :].rearrange("p (a b) -> p a b", b=128), [pk], ['gT'])
            A.free("Gp")
            sgT = A.alloc("sgT", [4, T], BF16)
            slot_sg, k_sg = load_w(w_in[l][:, C_SG:C_SG + 512], 8, 512)
            for m in range(4):
                proj_fm(slot_sg, k_sg, m * 128, 128,
                        lambda ps, pk, th, m=m: act(lambda e: e.activation(
                            out=sgT[:, m, th * 512:(th + 1) * 512], in_=ps[:, :], func=AF.Silu), [pk], ['sgT']))
            bglu = A.alloc("bglu", [8])
            with nc.allow_non_contiguous_dma(reason="tiny bias columns"):
                cx.dma('sp', bglu, s5_b_glu[l].rearrange("(k p) -> p k", p=128), writes=['bglu'], sem='s5c')
            slot_a, k_a = load_w(s5_w_glu[l][:, 0:512], 4, 512)
            slot_b, k_b = load_w(s5_w_glu[l][:, 512:1024], 4, 512)
            sg_ = [A.alloc("sgm%d" % i, [512]) for i in range(2)]
            it = 0
            for m in range(4):
                for th in range(2):
                    tsl = slice(th * 512, (th + 1) * 512)
                    pa, pka = bank()
                    for k in range(4):
                        mm(pa[:, :], slot_a[:, k, m * 128:(m + 1) * 128], gT[:, k, tsl], k == 0, k == 3, [k_a, 'gT'], [pka])
                    pb, pkb = bank()
                    for k in range(4):
                        mm(pb[:, :], slot_b[:, k, m * 128:(m + 1) * 128], gT[:, k, tsl], k == 0, k == 3, [k_b, 'gT'], [pkb])
                    sg = sg_[it % 2]
                    sgk = 'sgm%d' % (it % 2)
                    it += 1
                    act(lambda e: e.activation(out=sg, in_=pb[:, :], func=AF.Sigmoid, bias=bglu[:, 4 + m:5 + m]),
                        [pkb, 'bglu'], [sgk])
                    dve(lambda e: e.scalar_tensor_tensor(out=sg, in0=pa[:, :], scalar=bglu[:, m:m + 1], in1=sg,
                                                         op0=ALU.add, op1=ALU.mult), [pka, 'bglu', sgk], [sgk])
                    D2(oT_br[2][:, m, tsl], sg, sgT[:, m, tsl], ALU.mult, [sgk, 'sgT'], ['obr2'])
            tap("s5_out%d" % l, oT_br[2][:, :, :], 'obr2')

    def merge_out(l):
        with ExitStack() as st:
            mixedT = sb("mixedT", [128, 8, T], BF16, stack=st)
            wbo = sb("wbo", [128, 3, 4, D], BF16, stack=st)
            gx = [sb("gx%d" % i, [128, 512], stack=st) for i in range(3)]
            t1 = sb("mt1", [128, 512], stack=st)
            t2 = sb("mt2", [128, 512], stack=st)
            for x in range(3):
                cx.dma('pool', wbo[:, x, :, :], w_bo[x][l].rearrange("(k p) c -> p k c", p=128), writes=['wbo'],
                       sem='wbo')
            for fg in range(2):
                slots = [load_w(w_in[l][:, C_MERGE + x * D + fg * 512:C_MERGE + x * D + fg * 512 + 512], 8, 512)
                         for x in range(3)]
                for f4 in range(4):
                    f = fg * 4 + f4
                    for th in range(2):
                        tsl = slice(th * 512, (th + 1) * 512)
                        for x in range(3):
                            slot, skey = slots[x]
                            pg, pgk = bank()
                            for k in range(8):
                                mm(pg[:, :], slot[:, k, f4 * 128:(f4 + 1) * 128], hT[:, k, tsl], k == 0, k == 7,
                                   [skey, 'hT'], [pgk])
                            act(lambda e: e.activation(out=gx[x][:, :], in_=pg[:, :], func=AF.Sigmoid), [pgk], ['gx%d' % x])
                            pp, ppk = bank()
                            for k in range(4):
                                mm(pp[:, :], wbo[:, x, k, f * 128:(f + 1) * 128], oT_br[x][:, k, tsl], k == 0, k == 3,
                                   ['wbo', 'obr%d' % x], [ppk])
                            if x == 0:
                                dve(lambda e: e.tensor_tensor(out=t1[:, :], in0=pp[:, :], in1=gx[x][:, :], op=ALU.mult),
                                    [ppk, 'gx0'], ['mt1'])
                            elif x == 1:
                                dve(lambda e: e.tensor_tensor(out=t2[:, :], in0=pp[:, :], in1=gx[x][:, :], op=ALU.mult),
                                    [ppk, 'gx1'], ['mt2'])
                                dve(lambda e: e.tensor_tensor(out=t1[:, :], in0=t1[:, :], in1=t2[:, :], op=ALU.add),
                                    ['mt1', 'mt2'], ['mt1'])
                            else:
                                dve(lambda e: e.tensor_tensor(out=t2[:, :], in0=pp[:, :], in1=gx[x][:, :], op=ALU.mult),
                                    [ppk, 'gx2'], ['mt2'])
                                dve(lambda e: e.tensor_tensor(out=mixedT[:, f, tsl], in0=t1[:, :], in1=t2[:, :], op=ALU.add),
                                    ['mt1', 'mt2'], ['mixedT'])
            tap("mixedT%d" % l, mixedT[:, :, :], 'mixedT')
            for nh in range(2):
                nsl = slice(nh * 512, (nh + 1) * 512)
                slot, skey = load_w(w_out[l][:, nsl], 8, 512)
                for tt in range(8):
                    ps, pk = bank()
                    for k in range(8):
                        mm(ps[:, :], mixedT[:, k, tt * 128:(tt + 1) * 128], slot[:, k, :], k == 0, k == 7,
                           ['mixedT', skey], [pk])
                    tm = t1 if tt % 2 == 0 else t2
                    tk = 'mt1' if tt % 2 == 0 else 'mt2'
                    dve(lambda e: e.tensor_tensor(out=tm[:, :], in0=ps[:, :], in1=gate_bc[:, nsl], op=ALU.mult),
                        [pk, 'gate_bc'], [tk])
                    dve(lambda e: e.tensor_tensor(out=x_sb[:, tt, nsl], in0=x_sb[:, tt, nsl], in1=tm[:, :], op=ALU.add),
                        ['x', tk], ['x'])
            tap("xout%d" % l, x_sb[:, :, :], 'x')

    for l in range(DEPTH):
        with ExitStack() as st:
            shift_bc = sb("shift_bc", [128, D], stack=st)
            wmod = sb("wmod", [128, D], stack=st)
            bada = sb("bada", [128, 3 * D], stack=st)
            nw_bc = sb("nw_bc", [128, D], stack=st)
            cond_c = sb("cond_c", [128, 8], stack=st)
            scb = sb("scb", [128, 8, 128], BF16, stack=st)
            with nc.allow_non_contiguous_dma(reason="tiny cond column load"):
                cx.dma('sp', cond_c[:, :], cond_d.rearrange("(k p) -> p k", p=128), writes=['cond_c'], sem='c2')
            cx.dma('sp', bada[:, :], b_ada[l].partition_broadcast(128), writes=['bada'], sem='c2')
            cx.dma('sp', nw_bc[:, :], norm_w[l].partition_broadcast(128), writes=['nw_bc'], sem='c2')
            act(lambda e: e.activation(out=cond_c[:, :], in_=cond_c[:, :], func=AF.Silu), ['cond_c'], ['cond_c'])
            dve(lambda e: e.tensor_copy(out=scb[:, :, :], in_=cond_c[:, :].unsqueeze(2).to_broadcast([128, 8, 128])),
                ['cond_c'], ['scb'])
            for ci in range(6):
                slot, skey = load_w(w_ada[l][:, ci * 512:(ci + 1) * 512], 8, 512)
                ps, pk = bank()
                for k in range(8):
                    mm(ps[:, :], scb[:, k, :], slot[:, k, :], k == 0, k == 7, ['scb', skey], [pk])
                dst = (shift_bc, wmod, gate_bc)[ci // 2]
                dkey = ('shift_bc', 'wmod', 'gate_bc')[ci // 2]
                cs = slice((ci % 2) * 512, (ci % 2 + 1) * 512)
                bsl = bada[:, ci * 512:(ci + 1) * 512]
                if ci // 2 == 1:
                    dve(lambda e, ps=ps, dst=dst, cs=cs, bsl=bsl: e.scalar_tensor_tensor(
                        out=dst[:, cs], in0=ps[:, :], scalar=1.0, in1=bsl, op0=ALU.add, op1=ALU.add),
                        [pk, 'bada'], [dkey])
                else:
                    dve(lambda e, ps=ps, dst=dst, cs=cs, bsl=bsl: e.tensor_tensor(
                        out=dst[:, cs], in0=ps[:, :], in1=bsl, op=ALU.add), [pk, 'bada'], [dkey])
            dve(lambda e: e.tensor_tensor(out=wmod[:, :], in0=wmod[:, :], in1=nw_bc[:, :], op=ALU.mult),
                ['wmod', 'nw_bc'], ['wmod'])

            ss = sb("ss_x", [128, 8], stack=st)
            rs = sb("rs_x", [128, 8], stack=st)
            tmp8 = sb("tmp8", [128, 8], stack=st)
            junk = sb("junk_x", [128, D], stack=st)
            hb = [sb("hb%d" % i, [128, D], BF16, stack=st) for i in range(2)]
            tmpf = sb("tmpf", [128, D], stack=st)
            dve(lambda e: e.memset(ss[:, :], 0.0), [], ['ss_x'])
            for tt in range(8):
                act(lambda e, tt=tt: e.activation(out=junk[:, :], in_=x_sb[:, tt, :], func=AF.Square,
                                                  accum_out=ss[:, tt:tt + 1]), ['x', 'ss_x'], ['junk_x', 'ss_x'])
            rstd_from_ss(ss[:, :], rs[:, :], D, 'ss_x', 'rs_x', tmp8[:, :], 'tmp8')
            for tt in range(8):
                hbt = hb[tt % 2]
                hk = 'hb%d' % (tt % 2)
                dve(lambda e, tt=tt: e.scalar_tensor_tensor(out=tmpf[:, :], in0=x_sb[:, tt, :], scalar=rs[:, tt:tt + 1],
                                                            in1=wmod[:, :], op0=ALU.mult, op1=ALU.mult),
                    ['x', 'rs_x', 'wmod'], ['tmpf'])
                dve(lambda e, hbt=hbt: e.tensor_tensor(out=hbt[:, :], in0=tmpf[:, :], in1=shift_bc[:, :], op=ALU.add),
                    ['tmpf', 'shift_bc'], [hk])
                for half in range(2):
                    ps, pk = bank()
                    for kk in range(4):
                        k = half * 4 + kk
                        mm(ps[:, kk * 128:(kk + 1) * 128], hbt[:, k * 128:(k + 1) * 128], ident_bf[:, :], True, True,
                           [hk, 'ident_bf'], [pk], inc=(kk == 3))
                    act(lambda e, ps=ps, half=half, tt=tt: e.activation(
                        out=hT[:, half * 4:half * 4 + 4, tt * 128:(tt + 1) * 128],
                        in_=ps[:, :].rearrange("p (a b) -> p a b", b=128), func=AF.Copy), [pk], ['hT'])

            tap("hT%d" % l, hT[:, :, :], 'hT')
            tap("gate%d" % l, gate_bc[:, :], 'gate_bc')
        if stop == 'B':
            break
        branch_gla(l)
        if stop in ('GLA', 'G1', 'G2', 'G3'):
            break
        branch_mla(l)
        if stop in ('MLA', 'M1'):
            break
        branch_s5(l)
        if stop in ('S5', 'S1', 'S2', 'S3', 'S2a', 'S2b', 'S2c'):
            break
        merge_out(l)
        if stop == 'L0':
            break

    cx.dma('sp', y_d.rearrange("(t p) d -> p t d", p=128), x_sb[:, :, :], reads=['x'], sem='yout')
    cx.final_wait()
    return cx


def _rope_tables():
    rows = T // 64
    r = np.repeat(np.arange(rows, dtype=np.float32), 64)
    col = np.tile(np.arange(64, dtype=np.float32), rows)
    n_freq = 8
    inv = (np.float32(10000.0) ** (-np.arange(n_freq, dtype=np.float32) / np.float32(n_freq))).astype(np.float32)
    ang = np.concatenate([r[:, None] * inv, col[:, None] * inv], axis=-1).astype(np.float32)
    return np.cos(ang).astype(np.float32), np.sin(ang).astype(np.float32)


def _constants():
    c = {}
    c["ident"] = np.eye(128, dtype=np.float32)
    c["jmat"] = np.eye(128, dtype=np.float32)[::-1].copy()
    s_idx = np.arange(128)[:, None]
    t_idx = np.arange(128)[None, :]
    same = (s_idx // 64) == (t_idx // 64)
    c["gla_masks"] = np.stack([(same & (s_idx <= t_idx)), (same & (s_idx >= t_idx))]).astype(np.float32)
    cm = np.ones((128, T), np.float32)
    cm[:, ::64] = 0.0
    c["chunk_mask"] = cm
    sp_ = (np.arange(128) // 16)[:, None]
    s_ = (np.arange(128) // 16)[None, :]
    c["s5_masks"] = np.stack([(s_ >= sp_), (sp_ >= s_)]).astype(np.float32)
    sel = np.zeros((2, 128, 64), np.float32)
    for g2 in range(2):
        for jl in range(4):
            for pch in range(16):
                sel[g2, (2 * jl + g2) * 16 + pch, jl * 16 + pch] = 1.0
    c["sel_c"] = sel
    return c


_W_NAMES = ["norm_w", "w_ada", "b_ada", "w_in", "gla_w_a2", "gla_b_a", "gla_o_norm", "mla_q_norm", "mla_w_uq",
            "mla_kv_norm", "mla_w_uk", "mla_w_uv", "mla_qh_norm", "mla_kh_norm", "s5_a_re", "s5_a_im", "s5_log_dt",
            "s5_b_re", "s5_b_im", "s5_c_re", "s5_c_im", "s5_d", "s5_w_glu", "s5_b_glu", "w_bo_gla", "w_bo_mla",
            "w_bo_s5", "w_out"]


def make_in_maps(inp):
    f = lambda a: np.ascontiguousarray(np.asarray(a, dtype=np.float32))
    consts = _constants()
    cos, sin = _rope_tables()
    weights = {k: f(inp[k]) for k in _W_NAMES}
    maps = []
    for core in range(8):
        m = dict(weights)
        m.update(consts)
        qm = np.zeros((5, T), np.float32)
        km = np.zeros((5, NKEY), np.float32)
        qm[4, :] = 1.0
        km[4, :] = -MASK_BIG
        if core < 4:
            b = core
            m["x"] = f(inp["x_sample"][b])
            m["cond"] = f(inp["c"][b])
            m["ckv_c"] = f(inp["cache_mla_ckv"][b])
            m["kr_c"] = f(inp["cache_mla_krope"][b])
            m["sg0"] = f(inp["state_gla"][b])
            m["s50"] = f(inp["state_s5"][b])
            m["rope_cos"], m["rope_sin"] = cos, sin
            qm[0, :] = 1.0
            km[0, :] = MASK_BIG
            m["rcol"] = np.ones((128, 1), np.float32)
        else:
            j = core - 4
            m["x"] = f(np.asarray(inp["x_prompt"])[4 * j:4 * j + 4].reshape(T, D))
            m["cond"] = f(inp["c_ctx"])
            m["ckv_c"] = np.zeros((DEPTH, PAST, 256), np.float32)
            m["kr_c"] = np.zeros((DEPTH, PAST, 32), np.float32)
            m["sg0"] = np.zeros((DEPTH, 2, 4, 64, 128), np.float32)
            m["s50"] = np.zeros((DEPTH, 2, 2, 32, 64), np.float32)
            m["rope_cos"] = np.ones((T, 16), np.float32)
            m["rope_sin"] = np.zeros((T, 16), np.float32)
            for s in range(4):
                qm[s, s * 256:(s + 1) * 256] = 1.0
                km[s, PAST + s * 256:PAST + (s + 1) * 256] = MASK_BIG
            m["rcol"] = np.zeros((128, 1), np.float32)
        m["qmask"], m["kmask"] = qm, km
        maps.append(m)
    return maps


def kernel(**inputs):
    nc = build_program()
    maps = make_in_maps(inputs)
    res = run_bass_kernel_spmd(nc, maps, core_ids=list(range(8)))
    r = res.results
    y_sample = np.stack([r[b]["y"] for b in range(4)]).astype(np.float32)
    y_prompt = np.concatenate([r[4 + j]["y"].reshape(4, 256, D) for j in range(4)]).astype(np.float32)
    ckv = np.concatenate([r[4 + j]["o_ckv"].reshape(DEPTH, 4, 256, 256).transpose(1, 0, 2, 3) for j in range(4)])
    kr = np.concatenate([r[4 + j]["o_kr"].reshape(DEPTH, 4, 256, 32).transpose(1, 0, 2, 3) for j in range(4)])
    gla = np.concatenate([r[4 + j]["o_gla"].transpose(1, 0, 2, 3, 4, 5) for j in range(4)])
    s5 = np.concatenate([r[4 + j]["o_s5"].transpose(1, 0, 2, 3, 4, 5) for j in range(4)])
    return (y_prompt, y_sample, ckv.astype(np.float32), kr.astype(np.float32), gla.astype(np.float32),
            s5.astype(np.float32))
```
